# Optimizing a Trainium2 kernel written in Bass

```python
import jax, jax.numpy as jnp
from jax import lax
import numpy as np

D_MODEL = 1024
BATCH = 8
SEQ = 2048
DEPTH = 4
DEC_BATCH = 128
DEC_SEQ = 8
PAST_LEN = 16384
PAGE_SIZE = 128

N_MIXERS = 2
N_GDN = (DEPTH + 1) // 2
N_HGRN = DEPTH // 2
GDN_HEADS = 8
GDN_DK = D_MODEL // GDN_HEADS
GDN_DV = D_MODEL // GDN_HEADS
GDN_CONV = 4
GDN_CHUNK = 64
HGRN_HEADS = 8
HGRN_DEXP = D_MODEL // HGRN_HEADS
HGRN_DV = D_MODEL // HGRN_HEADS
HGRN_CHUNK = 16
D_FF = 2816
FFN_CONV = 3
N_MOD = 6
EPS = 1e-6
LB_FLOOR = 1e-30

GDN_QK = GDN_HEADS * GDN_DK
GDN_V = GDN_HEADS * GDN_DV
GDN_IN = 2 * GDN_QK + 2 * GDN_V + 2 * GDN_HEADS
HGRN_QF = HGRN_HEADS * HGRN_DEXP
HGRN_V = HGRN_HEADS * HGRN_DV
HGRN_IN = 2 * HGRN_QF + 2 * HGRN_V

kernel_name = "hybrid_gdn_hgrn2_convffn_adaln_step"

F32 = jnp.float32


def _rmsnorm(x, g):
    xf = x.astype(F32)
    y = xf * lax.rsqrt(jnp.mean(xf * xf, axis=-1, keepdims=True) + EPS)
    return (y * g.astype(F32)).astype(x.dtype)


def _l2norm(x):
    xf = x.astype(F32)
    return xf * lax.rsqrt(jnp.sum(xf * xf, axis=-1, keepdims=True) + EPS)


def _masked_exp(mask, d):
    return jnp.where(mask, jnp.exp(jnp.where(mask, d, 0.0)), 0.0)


def _causal_dwconv(x, buf, w, b):
    width = w.shape[0]
    L = x.shape[1]
    xp = jnp.concatenate([buf.astype(x.dtype), x], axis=1)
    y = xp[:, 0:L] * w[0]
    for j in range(1, width):
        y = y + xp[:, j:j + L] * w[j]
    return y + b, xp[:, xp.shape[1] - (width - 1):]


def _chunk(t, C):
    B, L = t.shape[:2]
    n = -(-L // C)
    t = jnp.pad(t, [(0, 0), (0, n * C - L)] + [(0, 0)] * (t.ndim - 2))
    t = t.reshape((B, n, C) + t.shape[2:])
    return t.transpose((1, 0, 3, 2) + tuple(range(4, t.ndim)))


def _unchunk(o, L):
    n, B, H, C, V = o.shape
    return o.transpose(1, 0, 3, 2, 4).reshape(B, n * C, H, V)[:, :L]


def _gated_delta(q, k, v, beta, g, S0):
    L = q.shape[1]
    C = min(GDN_CHUNK, L)
    qc, kc, vc = [_chunk(t.astype(F32), C) for t in (q, k, v)]
    bc = _chunk(beta.astype(F32), C)
    G = jnp.cumsum(_chunk(g.astype(F32), C), axis=-1)
    incl = jnp.tril(jnp.ones((C, C), bool))
    strict = jnp.tril(jnp.ones((C, C), bool), -1)
    decay = _masked_exp(incl, G[..., :, None] - G[..., None, :])
    kb = kc * bc[..., None]
    A = jnp.where(strict, jnp.einsum('nbhik,nbhjk->nbhij', kb, kc) * decay, 0.0)
    eye = jnp.eye(C, dtype=F32)
    rhs = jnp.concatenate([vc * bc[..., None], kb * jnp.exp(G)[..., None]], axis=-1)
    sol = lax.linalg.triangular_solve(A + eye, rhs, left_side=True, lower=True, unit_diagonal=True)
    dv = vc.shape[-1]
    uv, w = sol[..., :dv], sol[..., dv:]
    attn = jnp.einsum('nbhik,nbhjk->nbhij', qc, kc) * decay
    qg = qc * jnp.exp(G)[..., None]
    GL = G[..., -1]
    kd = kc * jnp.exp(GL[..., None] - G)[..., None]

    def step(S, xs):
        qg_, uv_, w_, kd_, gl_, attn_ = xs
        u = uv_ - jnp.einsum('bhck,bhkv->bhcv', w_, S)
        o = jnp.einsum('bhck,bhkv->bhcv', qg_, S) + jnp.einsum('bhij,bhjv->bhiv', attn_, u)
        S = S * gl_[..., None, None] + jnp.einsum('bhck,bhcv->bhkv', kd_, u)
        return S, o

    S, o = lax.scan(step, S0.astype(F32), (qg, uv, w, kd, jnp.exp(GL), attn))
    return _unchunk(o, L), S.astype(S0.dtype)


def _hgrn_chunked(q, k, logf, v, S0):
    L = q.shape[1]
    C = min(HGRN_CHUNK, L)
    qc, kc, lc, vc = [_chunk(t.astype(F32), C) for t in (q, k, logf, v)]
    incl = jnp.tril(jnp.ones((C, C), bool))[:, :, None]

    def step(S, xs):
        qb, kb, lb, vb = xs
        G = jnp.cumsum(lb, axis=2)
        dec = _masked_exp(incl, G[:, :, :, None, :] - G[:, :, None, :, :])
        attn = jnp.einsum('bhik,bhjk,bhijk->bhij', qb, kb, dec)
        o = jnp.einsum('bhik,bhkv->bhiv', qb * jnp.exp(G), S) + jnp.einsum('bhij,bhjv->bhiv', attn, vb)
        GL = G[:, :, -1:, :]
        S = jnp.exp(GL[:, :, 0, :])[..., None] * S + jnp.einsum('bhjk,bhjv->bhkv', kb * jnp.exp(GL - G), vb)
        return S, o

    S, o = lax.scan(step, S0.astype(F32), (qc, kc, lc, vc))
    return _unchunk(o, L), S.astype(S0.dtype)


def _gdn_mixer(h, S0, buf, w_in, conv_w, conv_b, a_log, dt_bias, norm_g, w_out):
    B, L, _ = h.shape
    proj = h @ w_in
    c1 = 2 * GDN_QK + GDN_V
    qkv, gt, br, ar = jnp.split(proj, [c1, c1 + GDN_V, c1 + GDN_V + GDN_HEADS], axis=-1)
    qkv, new_buf = _causal_dwconv(qkv, buf, conv_w, conv_b)
    qkv = jax.nn.silu(qkv)
    q, k, v = jnp.split(qkv, [GDN_QK, 2 * GDN_QK], axis=-1)
    shp = (B, L, GDN_HEADS, -1)
    q = _l2norm(q.reshape(shp)) * (GDN_DK ** -0.5)
    k = _l2norm(k.reshape(shp))
    beta = jax.nn.sigmoid(br.astype(F32))
    g = -jnp.exp(a_log.astype(F32)) * jax.nn.softplus(ar.astype(F32) + dt_bias.astype(F32))
    o, S = _gated_delta(q, k, v.reshape(shp), beta, g, S0)
    o = _rmsnorm(o.astype(h.dtype), norm_g) * jax.nn.silu(gt.reshape(shp))
    return o.reshape(B, L, GDN_V) @ w_out, S, new_buf


def _hgrn_mixer(h, S0, lb, w_in, norm_g, w_out):
    B, L, _ = h.shape
    proj = h @ w_in
    q, fr, i, gt = jnp.split(proj, [HGRN_QF, 2 * HGRN_QF, 2 * HGRN_QF + HGRN_V], axis=-1)
    fr = fr.astype(F32)
    lb = lb.astype(F32)
    logf = jnp.logaddexp(jnp.log(jnp.maximum(lb, LB_FLOOR)), jnp.log1p(-lb) + jax.nn.log_sigmoid(fr))
    k = (1.0 - lb) * jax.nn.sigmoid(-fr)
    shp = (B, L, HGRN_HEADS, -1)
    o, S = _hgrn_chunked(jax.nn.silu(q).reshape(shp), k.reshape(shp), logf.reshape(shp), i.reshape(shp), S0)
    o = _rmsnorm(o.astype(h.dtype), norm_g) * jax.nn.silu(gt.reshape(shp))
    return o.reshape(B, L, HGRN_V) @ w_out, S


def _conv_ffn(h, buf, w_gu, conv_w, conv_b, w_down):
    gt, up = jnp.split(h @ w_gu, 2, axis=-1)
    gt, new_buf = _causal_dwconv(gt, buf, conv_w, conv_b)
    return (jax.nn.silu(gt) * up) @ w_down, new_buf


def _trunk(x, c, s_gdn, s_gconv, s_hgrn, s_fconv,
           ada_w, ada_b, norm_pre_mix, norm_post_mix, norm_pre_ffn, norm_post_ffn,
           gdn_w_in, gdn_conv_w, gdn_conv_b, gdn_a_log, gdn_dt_bias, gdn_norm, gdn_w_out,
           hgrn_lb, hgrn_w_in, hgrn_norm, hgrn_w_out,
           ffn_w_gu, ffn_conv_w, ffn_conv_b, ffn_w_down):
    p = jax.nn.softmax(hgrn_lb.astype(F32), axis=0)
    lower = jnp.cumsum(p, axis=0) - p[0]
    cs = jax.nn.silu(c)
    new_gdn, new_gconv, new_hgrn, new_fconv = [], [], [], []
    for layer in range(DEPTH):
        mod = (cs @ ada_w[layer] + ada_b[layer])[:, None, :]
        sh1, sc1, g1, sh2, sc2, g2 = jnp.split(mod, N_MOD, axis=-1)
        h = _rmsnorm(x, norm_pre_mix[layer]) * (1 + sc1) + sh1
        j = layer // N_MIXERS
        if layer % N_MIXERS == 0:
            out, S, buf = _gdn_mixer(h, s_gdn[j], s_gconv[j], gdn_w_in[j], gdn_conv_w[j], gdn_conv_b[j],
                                     gdn_a_log[j], gdn_dt_bias[j], gdn_norm[j], gdn_w_out[j])
            new_gdn.append(S)
            new_gconv.append(buf)
        else:
            out, S = _hgrn_mixer(h, s_hgrn[j], lower[j], hgrn_w_in[j], hgrn_norm[j], hgrn_w_out[j])
            new_hgrn.append(S)
        x = x + (1 + g1) * _rmsnorm(out, norm_post_mix[layer])
        h = _rmsnorm(x, norm_pre_ffn[layer]) * (1 + sc2) + sh2
        out, fbuf = _conv_ffn(h, s_fconv[layer], ffn_w_gu[layer], ffn_conv_w[layer], ffn_conv_b[layer],
                              ffn_w_down[layer])
        new_fconv.append(fbuf)
        x = x + (1 + g2) * _rmsnorm(out, norm_post_ffn[layer])
    return x, jnp.stack(new_gdn), jnp.stack(new_gconv), jnp.stack(new_hgrn), jnp.stack(new_fconv)


def setup_inputs(seed: int = 0) -> dict:
    key = jax.random.key(seed)
    ks = iter(jax.random.split(key, 40))

    def nrm(shape, s):
        return jax.random.normal(next(ks), shape, F32) * s

    cqkv = 2 * GDN_QK + GDN_V
    dt = jnp.exp(jax.random.uniform(next(ks), (N_GDN, GDN_HEADS), F32, float(np.log(1e-3)), float(np.log(1e-1))))
    return {
        "x_prompt": nrm((BATCH, SEQ, D_MODEL), 1.0),
        "x_sample": nrm((DEC_BATCH, DEC_SEQ, D_MODEL), 1.0),
        "state_gdn": nrm((N_GDN, DEC_BATCH, GDN_HEADS, GDN_DK, GDN_DV), 0.1),
        "state_gdn_conv": nrm((N_GDN, DEC_BATCH, GDN_CONV - 1, cqkv), 1.0),
        "state_hgrn": nrm((N_HGRN, DEC_BATCH, HGRN_HEADS, HGRN_DEXP, HGRN_DV), 0.3),
        "state_ffn_conv": nrm((DEPTH, DEC_BATCH, FFN_CONV - 1, D_FF), 1.0),
        "c_prompt": nrm((BATCH, D_MODEL), 1.0),
        "c_sample": nrm((DEC_BATCH, D_MODEL), 1.0),
        "ada_w": nrm((DEPTH, D_MODEL, N_MOD * D_MODEL), 0.1 * D_MODEL ** -0.5),
        "ada_b": nrm((DEPTH, N_MOD * D_MODEL), 0.01),
        "norm_pre_mix": 1.0 + nrm((DEPTH, D_MODEL), 0.05),
        "norm_post_mix": 1.0 + nrm((DEPTH, D_MODEL), 0.05),
        "norm_pre_ffn": 1.0 + nrm((DEPTH, D_MODEL), 0.05),
        "norm_post_ffn": 1.0 + nrm((DEPTH, D_MODEL), 0.05),
        "gdn_w_in": nrm((N_GDN, D_MODEL, GDN_IN), D_MODEL ** -0.5),
        "gdn_conv_w": nrm((N_GDN, GDN_CONV, cqkv), GDN_CONV ** -0.5),
        "gdn_conv_b": nrm((N_GDN, cqkv), 0.01),
        "gdn_a_log": jnp.log(jax.random.uniform(next(ks), (N_GDN, GDN_HEADS), F32, 1.0, 16.0)),
        "gdn_dt_bias": dt + jnp.log(-jnp.expm1(-dt)),
        "gdn_norm": 1.0 + nrm((N_GDN, GDN_DV), 0.05),
        "gdn_w_out": nrm((N_GDN, GDN_V, D_MODEL), GDN_V ** -0.5),
        "hgrn_lb": nrm((N_HGRN, HGRN_QF), 1.0),
        "hgrn_w_in": nrm((N_HGRN, D_MODEL, HGRN_IN), D_MODEL ** -0.5),
        "hgrn_norm": 1.0 + nrm((N_HGRN, HGRN_DV), 0.05),
        "hgrn_w_out": nrm((N_HGRN, HGRN_V, D_MODEL), HGRN_V ** -0.5),
        "ffn_w_gu": nrm((DEPTH, D_MODEL, 2 * D_FF), D_MODEL ** -0.5),
        "ffn_conv_w": nrm((DEPTH, FFN_CONV, D_FF), FFN_CONV ** -0.5),
        "ffn_conv_b": nrm((DEPTH, D_FF), 0.01),
        "ffn_w_down": nrm((DEPTH, D_FF, D_MODEL), D_FF ** -0.5),
    }


def reference(x_prompt, x_sample, state_gdn, state_gdn_conv, state_hgrn, state_ffn_conv, c_prompt, c_sample,
              ada_w, ada_b, norm_pre_mix, norm_post_mix, norm_pre_ffn, norm_post_ffn,
              gdn_w_in, gdn_conv_w, gdn_conv_b, gdn_a_log, gdn_dt_bias, gdn_norm, gdn_w_out,
              hgrn_lb, hgrn_w_in, hgrn_norm, hgrn_w_out,
              ffn_w_gu, ffn_conv_w, ffn_conv_b, ffn_w_down):
    weights = (ada_w, ada_b, norm_pre_mix, norm_post_mix, norm_pre_ffn, norm_post_ffn,
               gdn_w_in, gdn_conv_w, gdn_conv_b, gdn_a_log, gdn_dt_bias, gdn_norm, gdn_w_out,
               hgrn_lb, hgrn_w_in, hgrn_norm, hgrn_w_out,
               ffn_w_gu, ffn_conv_w, ffn_conv_b, ffn_w_down)
    bp = x_prompt.shape[0]
    dt = x_prompt.dtype
    z_gdn = jnp.zeros((N_GDN, bp) + state_gdn.shape[2:], dt)
    z_gconv = jnp.zeros((N_GDN, bp) + state_gdn_conv.shape[2:], dt)
    z_hgrn = jnp.zeros((N_HGRN, bp) + state_hgrn.shape[2:], dt)
    z_fconv = jnp.zeros((DEPTH, bp) + state_ffn_conv.shape[2:], dt)
    y_prompt, p_gdn, p_gconv, p_hgrn, p_fconv = _trunk(x_prompt, c_prompt, z_gdn, z_gconv, z_hgrn, z_fconv, *weights)
    y_sample, s_gdn, s_gconv, s_hgrn, s_fconv = _trunk(x_sample, c_sample, state_gdn, state_gdn_conv, state_hgrn,
                                                       state_ffn_conv, *weights)
    return (y_prompt, y_sample, p_gdn, p_gconv, p_hgrn, p_fconv, s_gdn, s_gconv, s_hgrn, s_fconv)
```

```python
import contextlib
import numpy as np
import concourse.bass as bass
import concourse.mybir as mybir
from concourse.bass_utils import run_bass_kernel_spmd

F32 = mybir.dt.float32
BF16 = mybir.dt.bfloat16
AF = mybir.ActivationFunctionType
ALU = mybir.AluOpType

NCORE = 8
D = 1024
KC = 8
H = 8
DFF = 2816
NFF = 22
DEPTH = 4
TP = 1024
SPP = 8
LS = 8
TS = SPP * LS
TPH = TP + TS
TILES = [(0, 512), (512, 512), (1024, 64)]
EPS = 1e-6
NPRM = 48 + 32 + 96 + 24 + 66 + 22 + 16 + 2

P_ADAB = 0
P_NPRE_MIX = 48
P_NPOST_MIX = 56
P_NPRE_FFN = 64
P_NPOST_FFN = 72
P_GCW = 80
P_GCB = 176
P_FCW = 200
P_FCB = 266
P_HLB = 288
P_ALOG = 304
P_DTB = 305

C_IDENT = 0
C_PENP = 128
C_PENS = 256
C_CM2 = 320
C_CM4 = 576
C_CM8 = 1088
C_RM2 = 1600
C_RM4 = 1602
C_RM8 = 1606
C_MIP = 1614
C_MIS = 1742
C_ESEL = 1806
NCST = 1806 + 1024


class Tok:
    __slots__ = ("w", "r", "dsem", "dval", "excl")

    def __init__(self, excl=False):
        self.w = {}
        self.r = {}
        self.dsem = None
        self.dval = 0
        self.excl = excl


class Sched:
    ENGS = ("pe", "act", "dve", "pool", "sp")

    def __init__(self, nc, es):
        self.nc = nc
        self.es = es
        self.q = {k: [] for k in self.ENGS}
        self.n = {k: 0 for k in self.ENGS}
        self.seen = {k: {} for k in self.ENGS}
        self.sem = {}
        for k in ("pe", "act", "dve", "pool"):
            self.sem[k] = es.enter_context(nc.semaphore("s_" + k))
        self.dma_latest = {}
        self.nd = 0

    def _waits(self, en, R, W):
        need = {}

        def add(ev):
            key, sem, val = ev
            if key not in need or need[key][1] < val:
                need[key] = (sem, val)

        for t in R:
            for ev in t.w.values():
                add(ev)
            if t.excl:
                for ev in t.r.values():
                    add(ev)
        for t in W:
            for ev in t.w.values():
                add(ev)
            for ev in t.r.values():
                add(ev)
        out = []
        seen = self.seen[en]
        for key, (sem, val) in need.items():
            if key == en and en == "pe":
                continue
            if seen.get(key, 0) >= val:
                continue
            seen[key] = val
            out.append((sem, val))
        return out

    def op(self, en, fn, R=(), W=()):
        if _STAGE_LIMIT[1]:
            return
        waits = self._waits(en, R, W)
        self.n[en] += 1
        sem = self.sem[en]
        self.q[en].append((waits, fn, sem, 1))
        ev = (en, sem, self.n[en])
        for t in W:
            t.w[en] = ev
        for t in R:
            t.r[en] = ev

    def dma(self, qn, out, in_, R=(), W=(), semtok=None):
        if _STAGE_LIMIT[1]:
            return
        waits = self._waits(qn, R, W)
        t = semtok
        if t.dsem is None:
            t.dsem = {}
        if qn not in t.dsem:
            self.nd += 1
            t.dsem[qn] = [self.es.enter_context(self.nc.semaphore("d%d" % self.nd)), 0]
        ent = t.dsem[qn]
        ent[1] += 16
        dsem, dval = ent[0], ent[1]
        key = "d%d_%s" % (id(t), qn)
        self.q[qn].append((waits, (lambda e: e.dma_start(out=out, in_=in_)), dsem, 16))
        ev = (key, dsem, dval)
        for x in W:
            x.w[key] = ev
        for x in R:
            x.r[key] = ev
        self.dma_latest[key] = (dsem, dval)

    def barrier(self):
        for en in self.ENGS:
            waits = []
            seen = self.seen[en]
            for k in ("pe", "act", "dve", "pool"):
                if k == en and en == "pe":
                    continue
                v = self.n[k]
                if v > 0 and seen.get(k, 0) < v:
                    seen[k] = v
                    waits.append((self.sem[k], v))
            for key, (sem, val) in self.dma_latest.items():
                if seen.get(key, 0) < val:
                    seen[key] = val
                    waits.append((sem, val))
            if waits:
                self.q[en].append((waits, None, None, 0))

    def final_wait(self, en="sp"):
        waits = []
        for key, (sem, val) in self.dma_latest.items():
            waits.append((sem, val))
        for k in ("pe", "act", "dve", "pool"):
            if self.n[k] > 0:
                waits.append((self.sem[k], self.n[k]))
        self.q[en].append((waits, None, None, 0))

    def replay(self, block):
        q = self.q

        def run(e, lst):
            for waits, fn, sem, inc in lst:
                for s, v in waits:
                    e.wait_ge(s, v)
                if fn is not None:
                    fn(e).then_inc(sem, inc)

        @block.tensor
        def _(e):
            run(e, q["pe"])

        @block.scalar
        def _(e):
            run(e, q["act"])

        @block.vector
        def _(e):
            run(e, q["dve"])

        @block.gpsimd
        def _(e):
            run(e, q["pool"])

        @block.sync
        def _(e):
            run(e, q["sp"])


class StopBuild(Exception):
    pass


_STAGE_LIMIT = [None, False]


def stage(n):
    if _STAGE_LIMIT[0] is not None and n > _STAGE_LIMIT[0]:
        _STAGE_LIMIT[1] = True


class Ring:
    def __init__(self, items):
        self.items = items
        self.i = 0

    def get(self):
        r = self.items[self.i]
        self.i = (self.i + 1) % len(self.items)
        return r


def build_nc(depth=DEPTH):
    nc = bass.Bass("TRN2", target_bir_lowering=False)

    def din(name, shape):
        return nc.dram_tensor(name, list(shape), F32, kind="ExternalInput").ap()

    def dout(name, shape):
        return nc.dram_tensor(name, list(shape), F32, kind="ExternalOutput").ap()

    xT_d = din("xT", [128, KC * 2 * TPH])
    cT_d = din("cT", [128, KC * 17])
    sgdn_d = din("sgdn", [2, 16, H, 128, 128])
    shgrn_d = din("shgrn", [2, 16, H, 128, 128])
    gci_d = din("gci", [2, 2, 128, 24 * 8 * 3])
    fci_d = din("fci", [4, 2, 128, NFF * 8 * 2])
    prm_d = din("prm", [128, 4 * NPRM])
    nrow_d = din("nrow", [128, 4 * 128])
    cst_d = din("cst", [128, NCST])
    ada_w = din("ada_w", [4, D, 6 * D])
    gdn_w_in = din("gdn_w_in", [2, D, 4112])
    gdn_w_out = din("gdn_w_out", [2, D, D])
    hgrn_w_in = din("hgrn_w_in", [2, D, 4096])
    hgrn_w_out = din("hgrn_w_out", [2, D, D])
    ffn_w_gu = din("ffn_w_gu", [4, D, 2 * DFF])
    ffn_w_down = din("ffn_w_down", [4, DFF, D])

    yT_d = dout("yT", [128, KC * 2 * TPH])
    pgdn_d = dout("pgdn", [2, H, 128, 128])
    sgdn_o = dout("sgdn_o", [2, 16, H, 128, 128])
    phgrn_d = dout("phgrn", [2, H, 128, 128])
    shgrn_o = dout("shgrn_o", [2, 16, H, 128, 128])
    gco_d = dout("gco", [2, 2, 128, 24 * 9 * 3])
    fco_d = dout("fco", [4, 2, 128, NFF * 9 * 2])

    with contextlib.ExitStack() as es:
        K = Sched(nc, es)

        cnt = [0]

        def sb(stack, name, shape, dt):
            cnt[0] += 1
            return stack.enter_context(nc.sbuf_tensor("sb%d_%s" % (cnt[0], name), list(shape), dt))

        xT = sb(es, "xT", [128, KC, 2, TPH], F32)
        xtok = [Tok(), Tok()]
        cst = sb(es, "cst", [128, NCST], F32)
        cst_tok = Tok()
        prm = sb(es, "prm", [128, 4, NPRM], F32)
        prm_tok = Tok()
        nrow = sb(es, "nrow", [128, 4, 128], F32)
        nrow_tok = Tok()
        ident_bf = sb(es, "ident_bf", [128, 128], BF16)
        ones_bf = sb(es, "ones_bf", [128, 128], BF16)
        cb = sb(es, "cb", [128, 8], F32)
        misc_tok = Tok()
        csT = sb(es, "csT", [128, KC, 17], BF16)
        cs_tok = Tok()
        modA = [sb(es, "modA%d" % i, [128, KC, 17], F32) for i in range(6)]
        mod_tok = Tok()
        Sf_all = sb(es, "Sf_all", [128, H, 128], F32)
        Sf_tok = [Tok() for _ in range(H)]
        gcar = sb(es, "gcar", [128, 24, 3], F32)
        gcar_tok = Tok()
        fcar = sb(es, "fcar", [128, NFF, 2], F32)
        fcar_tok = Tok()
        lbT = sb(es, "lbT", [128, H, 2], F32)
        lb_tok = Tok()
        wun = [(sb(es, "wun%d" % i, [128, KC, 128], BF16), Tok()) for i in range(5)]
        wring = Ring(wun)

        psF = [es.enter_context(nc.psum_tensor("psF%d" % i, [128, 512], F32)) for i in range(7)]
        psB = es.enter_context(nc.psum_tensor("psB", [128, 1024], BF16))
        bkt = [Tok(excl=True) for _ in range(8)]
        pbig = Ring([(psF[0], bkt[0]), (psF[1], bkt[1])])
        psmall = Ring([(psF[2 + i % 3][:, (i // 3) * 128:(i // 3) * 128 + 128], bkt[2 + i % 3]) for i in range(12)])
        pbf = Ring([(psB[:, i * 128:(i + 1) * 128], bkt[7]) for i in range(8)])
        o_ps, o_ptok = psF[5], bkt[5]
        u_ps, u_ptok = psF[6], bkt[6]

        def mm(out, lhsT, rhs, start=True, stop=True, R=(), W=()):
            K.op("pe", lambda e: e.matmul(out, lhsT, rhs, start=start, stop=stop), R, W)

        def tr(out, in_, idn, R=(), W=()):
            K.op("pe", lambda e: e.transpose(out, in_, idn), R, W)

        def act(out, in_, func, bias=None, scale=None, accum=None, R=(), W=()):
            kw = {}
            if bias is not None:
                kw["bias"] = bias
            if scale is not None:
                kw["scale"] = scale
            if accum is not None:
                kw["accum_out"] = accum
            K.op("act", lambda e: e.activation(out=out, in_=in_, func=func, **kw), R, W)

        def tt(out, a, b, op, R=(), W=(), en="dve"):
            K.op(en, lambda e: e.tensor_tensor(out=out, in0=a, in1=b, op=op), R, W)

        def ts(out, a, s1, s2, op0, op1=None, R=(), W=(), en="dve"):
            if op1 is None:
                K.op(en, lambda e: e.tensor_scalar(out=out, in0=a, scalar1=s1, scalar2=None, op0=op0), R, W)
            else:
                K.op(en, lambda e: e.tensor_scalar(out=out, in0=a, scalar1=s1, scalar2=s2, op0=op0, op1=op1), R, W)

        def stt(out, a, sc, b, op0, op1, R=(), W=(), en="dve"):
            K.op(en, lambda e: e.scalar_tensor_tensor(out=out, in0=a, scalar=sc, in1=b, op0=op0, op1=op1), R, W)

        def cp(out, in_, R=(), W=(), en="dve"):
            K.op(en, lambda e: e.tensor_copy(out=out, in_=in_), R, W)

        def recip(out, in_, R=(), W=()):
            K.op("dve", lambda e: e.reciprocal(out=out, in_=in_), R, W)

        def memset(ap, val, W=(), en="dve"):
            K.op(en, lambda e: e.memset(ap, val), (), W)

        def wload(W2d, c0, ncols=128):
            t, tok = wring.get()
            K.dma("pool", t[:, :, 0:ncols], W2d[:, c0:c0 + ncols].rearrange("(k p) n -> p k n", p=128), W=[tok], semtok=tok)
            return t, tok

        K.dma("sp", cst[:], cst_d[:, :], W=[cst_tok], semtok=cst_tok)
        K.dma("sp", prm[:].rearrange("p l n -> p (l n)"), prm_d[:, :], W=[prm_tok], semtok=prm_tok)
        K.dma("sp", nrow[:].rearrange("p l n -> p (l n)"), nrow_d[:, :], W=[nrow_tok], semtok=nrow_tok)
        for ph in range(2):
            K.dma("sp", xT[:, :, ph, :], xT_d.rearrange("p (k h t) -> p k h t", k=KC, h=2)[:, :, ph, :], W=[xtok[ph]], semtok=xtok[ph])
        ident = cst[:, C_IDENT:C_IDENT + 128]
        cp(ident_bf[:], ident, R=[cst_tok], W=[misc_tok])
        memset(ones_bf[:], 1.0, W=[misc_tok])
        memset(cb[:, 0:1], 1024.0 * EPS, W=[misc_tok])
        memset(cb[:, 1:2], EPS, W=[misc_tok])
        memset(cb[:, 2:3], 128.0 * EPS, W=[misc_tok])
        memset(cb[:, 3:4], 1.0, W=[misc_tok])
        memset(cb[:, 4:5], 0.0, W=[misc_tok])
        ts(nrow[:], nrow[:], float(np.sqrt(128.0)), None, ALU.mult, R=[nrow_tok], W=[nrow_tok])
        for l in range(4):
            ts(prm[:, l, P_NPRE_MIX:P_NPRE_MIX + 32], prm[:, l, P_NPRE_MIX:P_NPRE_MIX + 32], 32.0, None, ALU.mult, R=[prm_tok], W=[prm_tok])
        with contextlib.ExitStack() as s0:
            cTt = sb(s0, "cTt", [128, KC, 17], F32)
            ctok = Tok()
            K.dma("sp", cTt[:].rearrange("p k q -> p (k q)"), cT_d[:, :], W=[ctok], semtok=ctok)
            act(csT[:], cTt[:], AF.Silu, R=[ctok], W=[cs_tok])
            K.barrier()

        def adaln(l, S):
            mod = sb(S, "mod", [128, 48, 17], F32)
            mtok = Tok()
            slots = [pbig.get(), pbig.get()]
            for cc in range(48):
                wt, wtok = wload(ada_w[l], cc * 128)
                ps, ptok = slots[cc // 24]
                col = (cc % 24) * 17
                for kc in range(KC):
                    mm(ps[:, col:col + 17], wt[:, kc, :], csT[:, kc, :], start=(kc == 0), stop=(kc == KC - 1),
                       R=[wtok, cs_tok], W=[ptok])
            for bk in range(2):
                ps, ptok = slots[bk]
                tt(mod[:, 24 * bk:24 * bk + 24, :], ps[:, 0:408].rearrange("p (c q) -> p c q", q=17),
                   prm[:, l, P_ADAB + 24 * bk:P_ADAB + 24 * bk + 24].unsqueeze(2).broadcast_to([128, 24, 17]),
                   ALU.add, R=[ptok, prm_tok], W=[mtok])
            def nb(off):
                return prm[:, l, off:off + 8].unsqueeze(2).broadcast_to([128, 8, 17])
            stt(modA[0][:], mod[:, 8:16, :], 1.0, nb(P_NPRE_MIX), ALU.add, ALU.mult, R=[mtok, prm_tok], W=[mod_tok])
            cp(modA[1][:], mod[:, 0:8, :], R=[mtok], W=[mod_tok])
            stt(modA[2][:], mod[:, 16:24, :], 1.0, nb(P_NPOST_MIX), ALU.add, ALU.mult, R=[mtok, prm_tok], W=[mod_tok])
            stt(modA[3][:], mod[:, 32:40, :], 1.0, nb(P_NPRE_FFN), ALU.add, ALU.mult, R=[mtok, prm_tok], W=[mod_tok])
            cp(modA[4][:], mod[:, 24:32, :], R=[mtok], W=[mod_tok])
            stt(modA[5][:], mod[:, 40:48, :], 1.0, nb(P_NPOST_FFN), ALU.add, ALU.mult, R=[mtok, prm_tok], W=[mod_tok])

        def seqcols(ph):
            return slice(1 + SPP * ph, 1 + SPP * ph + SPP)

        def rstd_fm(src3, n, R, sqt, rs):
            sq, sqtok = sqt
            rst, rstok = rs
            act(sq[:, :, 0:n], src3, AF.Square, R=R, W=[sqtok])
            ps, ptok = pbig.get()
            for kc in range(KC):
                mm(ps[:, 0:n], ones_bf[:], sq[:, kc, 0:n], start=(kc == 0), stop=(kc == KC - 1), R=[sqtok, misc_tok], W=[ptok])
            act(rst[:, 0:n], ps[:, 0:n], AF.Sqrt, bias=cb[:, 0:1], scale=1.0, R=[ptok, misc_tok], W=[rstok])
            recip(rst[:, 0:n], rst[:, 0:n], R=[rstok], W=[rstok])

        def prenorm(ph, A, B, hT, htoks, S):
            sqt = (sb(S, "pn_sq", [128, KC, 512], BF16), Tok())
            rs = (sb(S, "pn_rs", [128, 512], F32), Tok())
            tmps = Ring([(sb(S, "pn_t%d" % i, [128, 512], F32), Tok()) for i in range(3)])
            for ti, (t0, n) in enumerate(TILES):
                rstd_fm(xT[:, :, ph, t0:t0 + n], n, [xtok[ph]], sqt, rs)
                for kc in range(KC):
                    tmp, ttok = tmps.get()
                    tt(tmp[:, 0:n], xT[:, kc, ph, t0:t0 + n], rs[0][:, 0:n], ALU.mult, R=[xtok[ph], rs[1]], W=[ttok])
                    if ti < 2:
                        act(hT[:, kc, t0:t0 + n], tmp[:, 0:n], AF.Identity, bias=B[:, kc, 0:1], scale=A[:, kc, 0:1],
                            R=[ttok, mod_tok], W=[htoks[ti]])
                    else:
                        sc = seqcols(ph)
                        tt(tmp[:, 0:n].rearrange("p (s j) -> p s j", j=LS), tmp[:, 0:n].rearrange("p (s j) -> p s j", j=LS),
                           A[:, kc, sc].unsqueeze(2).broadcast_to([128, SPP, LS]), ALU.mult, R=[ttok, mod_tok], W=[ttok])
                        tt(hT[:, kc, t0:t0 + n].rearrange("p (s j) -> p s j", j=LS), tmp[:, 0:n].rearrange("p (s j) -> p s j", j=LS),
                           B[:, kc, sc].unsqueeze(2).broadcast_to([128, SPP, LS]), ALU.add, R=[ttok, mod_tok], W=[htoks[ti]])

        def proj_tile(wt, wtok, hT, htoks, ti, M=128):
            t0, n = TILES[ti]
            ps, ptok = pbig.get()
            for kc in range(KC):
                mm(ps[0:M, 0:n], wt[:, kc, 0:M], hT[:, kc, t0:t0 + n], start=(kc == 0), stop=(kc == KC - 1),
                   R=[wtok, htoks[ti]], W=[ptok])
            return ps, ptok, n

        def outproj_postnorm(l, ph, Wout2d, oT, otoks, G, S):
            y = sb(S, "op_y", [128, KC, 512], F32)
            ytok = Tok()
            sqt = (sb(S, "op_sq", [128, KC, 512], BF16), Tok())
            rs = (sb(S, "op_rs", [128, 512], F32), Tok())
            tmps = Ring([(sb(S, "op_t%d" % i, [128, 512], F32), Tok()) for i in range(2)])
            for ti, (t0, n) in enumerate(TILES):
                for oc in range(KC):
                    wt, wtok = wload(Wout2d, oc * 128)
                    ps, ptok = pbig.get()
                    for kc in range(KC):
                        mm(ps[:, 0:n], wt[:, kc, :], oT[:, kc, t0:t0 + n], start=(kc == 0), stop=(kc == KC - 1),
                           R=[wtok, otoks[ti]], W=[ptok])
                    cp(y[:, oc, 0:n], ps[:, 0:n], R=[ptok], W=[ytok])
                resid_update(ph, t0, n, y, ytok, G, sqt, rs, tmps)

        def resid_update(ph, t0, n, y, ytok, G, sqt, rs, tmps):
            rstd_fm(y[:, :, 0:n], n, [ytok], sqt, rs)
            for kc in range(KC):
                tmp, ttok = tmps.get()
                tt(tmp[:, 0:n], y[:, kc, 0:n], rs[0][:, 0:n], ALU.mult, R=[ytok, rs[1]], W=[ttok])
                if t0 < TP:
                    stt(xT[:, kc, ph, t0:t0 + n], tmp[:, 0:n], G[:, kc, 0:1], xT[:, kc, ph, t0:t0 + n], ALU.mult, ALU.add,
                        R=[ttok, mod_tok], W=[xtok[ph]])
                else:
                    sc = seqcols(ph)
                    v3 = tmp[:, 0:n].rearrange("p (s j) -> p s j", j=LS)
                    tt(v3, v3, G[:, kc, sc].unsqueeze(2).broadcast_to([128, SPP, LS]), ALU.mult, R=[ttok, mod_tok], W=[ttok])
                    x3 = xT[:, kc, ph, t0:t0 + n].rearrange("p (s j) -> p s j", j=LS)
                    tt(x3, x3, v3, ALU.add, R=[ttok], W=[xtok[ph]])

        def blocks():
            return [(b * 128, 128, "P") for b in range(8)] + [(TP, 64, "S")]

        def o_post(l, o_rows, BS, h, c0, oT, otok_t, sgT, sgtok, tmpf, tmpb):
            junk, jtok = tmpf.get()
            ss, sstok = tmpf.get()
            act(junk[0:BS, :], o_ps[0:BS, 0:128], AF.Square, accum=ss[0:BS, 0:1], R=[o_ptok], W=[jtok, sstok])
            act(ss[0:BS, 0:1], ss[0:BS, 0:1], AF.Sqrt, bias=cb[0:BS, 2:3], scale=1.0, R=[sstok, misc_tok], W=[sstok])
            recip(ss[0:BS, 0:1], ss[0:BS, 0:1], R=[sstok], W=[sstok])
            on, ontok = tmpb.get()
            stt(on[0:BS, :], o_ps[0:BS, 0:128], ss[0:BS, 0:1], nrow[0:BS, l, :], ALU.mult, ALU.mult,
                R=[o_ptok, sstok, nrow_tok], W=[ontok])
            pb, pbtok = pbf.get()
            tr(pb[:, 0:BS], on[0:BS, :], ident_bf[0:BS, 0:BS], R=[ontok, misc_tok], W=[pbtok])
            tt(oT[:, h, c0:c0 + BS], pb[:, 0:BS], sgT[:, c0:c0 + BS], ALU.mult, R=[pbtok, sgtok], W=[otok_t])

        def gdn_phase(l, ph, hT, htoks, oT, otoks, S):
            j = l // 2
            Win = gdn_w_in[j]
            stage(3)
            dB = sb(S, "g_dB", [8, TPH], F32)
            dG = sb(S, "g_dG", [8, TPH], F32)
            dL = sb(S, "g_dL", [8, TPH], F32)
            dg = sb(S, "g_dg", [8, TPH], F32)
            rmask = dL
            dtok_ = Tok()
            nega = sb(S, "g_nega", [8, 1], F32)
            dtk = sb(S, "g_dtk", [128, 9, 3, 8], F32)
            ex = sb(S, "g_ex", [128, 9, 4, 8], F32)
            glb = sb(S, "g_glb", [128, H, 24], F32)
            dk_tok = Tok()
            gci = sb(S, "g_gci", [128, 24, 8, 3], F32)
            gci_tok = Tok()
            gco = sb(S, "g_gco", [128, 24, 9, 3], F32)
            gco_tok = Tok()
            K.dma("sp", gci[:].rearrange("p a s r -> p (a s r)"), gci_d[j, ph], W=[gci_tok], semtok=gci_tok)
            memset(gco[:], 0.0, W=[gco_tok])
            memset(rmask[:], 1.0, W=[dtok_])
            memset(rmask[:, 0:TP].rearrange("p (c t) -> p c t", t=64)[:, :, 0:1], 0.0, W=[dtok_])
            memset(rmask[:, TP:TPH].rearrange("p (c t) -> p c t", t=8)[:, :, 0:1], 0.0, W=[dtok_])
            act(nega[:], prm[0:8, l, P_ALOG:P_ALOG + 1], AF.Exp, R=[prm_tok], W=[dtok_])
            ts(nega[:], nega[:], -1.0, None, ALU.mult, R=[dtok_], W=[dtok_])
            wb, wbtok = wload(Win, 4096, 8)
            for ti in range(3):
                ps, ptok, n = proj_tile(wb, wbtok, hT, htoks, ti, M=8)
                t0 = TILES[ti][0]
                act(dB[:, t0:t0 + n], ps[0:8, 0:n], AF.Sigmoid, R=[ptok], W=[dtok_])
            wa, watok = wload(Win, 4104, 8)
            for ti in range(3):
                ps, ptok, n = proj_tile(wa, watok, hT, htoks, ti, M=8)
                t0 = TILES[ti][0]
                act(dg[:, t0:t0 + n], ps[0:8, 0:n], AF.Exp, bias=prm[0:8, l, P_DTB:P_DTB + 1], scale=1.0, R=[ptok, prm_tok], W=[dtok_])
            act(dg[:], dg[:], AF.Ln, bias=cb[0:8, 3:4], scale=1.0, R=[dtok_, misc_tok], W=[dtok_])
            ts(dg[:], dg[:], nega[:, 0:1], None, ALU.mult, R=[dtok_], W=[dtok_])
            K.op("dve", lambda e: e.tensor_tensor_scan(out=dG[:], data0=rmask[:], data1=dg[:], initial=0.0, op0=ALU.mult, op1=ALU.add),
                 [dtok_], [dtok_])
            gp = dG[:, 0:TP].rearrange("p (c t) -> p c t", t=64)
            tt(dL[:, 0:TP].rearrange("p (c t) -> p c t", t=64), gp[:, :, 63:64].broadcast_to([8, 16, 64]), gp, ALU.subtract, R=[dtok_], W=[dtok_])
            gs = dG[:, TP:TPH].rearrange("p (c t) -> p c t", t=8)
            tt(dL[:, TP:TPH].rearrange("p (c t) -> p c t", t=8), gs[:, :, 7:8].broadcast_to([8, 8, 8]), gs, ALU.subtract, R=[dtok_], W=[dtok_])
            ps, ptok = pbig.get()
            for bi, (c0, BS, kind) in enumerate(blocks()):
                for qi, src in enumerate((dB, dG, dL)):
                    col = (bi * 3 + qi) * 8
                    tr(ps[0:BS, col:col + 8], src[:, c0:c0 + BS], cst[0:8, C_IDENT:C_IDENT + 8], R=[dtok_, cst_tok], W=[ptok])
            memset(dtk[:], 0.0, W=[dk_tok])
            cp(dtk[:, 0:8].rearrange("p b q h -> p (b q h)"), ps[:, 0:192], R=[ptok], W=[dk_tok])
            cp(dtk[0:64, 8].rearrange("p q h -> p (q h)"), ps[0:64, 192:216], R=[ptok], W=[dk_tok])
            act(ex[:, :, 0:2, :], dtk[:, :, 1:3, :], AF.Exp, R=[dk_tok], W=[dk_tok])
            tt(ex[:, :, 2, :], dtk[:, :, 0, :], ex[:, :, 0, :], ALU.mult, R=[dk_tok], W=[dk_tok])
            ts(ex[:, :, 3, :], dtk[:, :, 0, :], -1.0, None, ALU.mult, R=[dk_tok], W=[dk_tok])
            ps, ptok = pbig.get()
            for h in range(H):
                esel_h = cst[0:8, C_ESEL + h * 128:C_ESEL + (h + 1) * 128]
                mm(ps[:, h * 24:h * 24 + 16], esel_h, dG[:, 0:TP].rearrange("p (c t) -> p c t", t=64)[:, :, 63], R=[dtok_, cst_tok], W=[ptok])
                mm(ps[:, h * 24 + 16:h * 24 + 24], esel_h, dG[:, TP:TPH].rearrange("p (c t) -> p c t", t=8)[:, :, 7], R=[dtok_, cst_tok], W=[ptok])
            act(glb[:].rearrange("p h c -> p (h c)"), ps[:, 0:192], AF.Exp, R=[ptok], W=[dk_tok])

            with contextlib.ExitStack() as S1:
                pre = sb(S1, "g_pre", [128, 3 + TP + SPP * 11], F32)
                pre_tok = Tok()
                cv = sb(S1, "g_cv", [128, TPH], F32)
                cv_tok = Tok()
                sqb = sb(S1, "g_sqb", [128, 512], BF16)
                sqb_tok = Tok()
                rinv = sb(S1, "g_rinv", [128, 512], F32)
                rinv_tok = Tok()
                qT = sb(S1, "g_qT", [128, TPH], BF16)
                kT = sb(S1, "g_kT", [128, TPH], BF16)
                vT = sb(S1, "g_vT", [128, TPH], BF16)
                sgT = sb(S1, "g_sgT", [128, TPH], BF16)
                q_tok, k_tok, v_tok, sg_tok = Tok(), Tok(), Tok(), Tok()
                Sbf = sb(S1, "g_Sbf", [128, 128], BF16)
                Sbf_tok = Tok()
                Ss = sb(S1, "g_Ss", [128, SPP, 128], F32)
                Ss_tok = Tok()
                Ssb = sb(S1, "g_Ssb", [128, SPP, 128], BF16)
                Ssb_tok = Tok()
                Sn, Sn_tok = Ss, Ss_tok
                u_sb = sb(S1, "g_usb", [128, 128], BF16)
                usb_tok = Tok()
                Kpad = sb(S1, "g_Kpad", [128, 8, 128], BF16)
                Wpad = sb(S1, "g_Wpad", [128, 8, 128], BF16)
                Qpad = sb(S1, "g_Qpad", [128, 8, 128], BF16)
                kp_tok, wp_tok, qp_tok = Tok(), Tok(), Tok()
                tmpf = Ring([(sb(S1, "g_tf%d" % i, [128, 128], F32), Tok()) for i in range(10)])
                ttr = Ring([(sb(S1, "g_tt%d" % i, [128, 128], F32), Tok()) for i in range(2)])
                tmpb = Ring([(sb(S1, "g_tb%d" % i, [128, 128], BF16), Tok()) for i in range(8)])
                tmpw = Ring([(sb(S1, "g_tw%d" % i, [128, 8, 128], BF16), Tok()) for i in range(1)])
                memset(u_sb[:], 0.0, W=[usb_tok])
                pv = pre[:, 3 + TP:3 + TP + SPP * 11].rearrange("p (s r) -> p s r", r=11)

                for h in range(H):
                    stage(4 + 0.1 * h)
                    for role in range(4):
                        wt, wtok = wload(Win, role * 1024 + h * 128)
                        if role < 3:
                            ch = role * 8 + h
                            if ph == 0:
                                memset(pre[:, 0:3], 0.0, W=[pre_tok])
                            else:
                                cp(pre[:, 0:3], gcar[:, ch, :], R=[gcar_tok], W=[pre_tok])
                            cp(pv[:, :, 0:3], gci[:, ch, :, :], R=[gci_tok], W=[pre_tok])
                        for ti in range(3):
                            ps, ptok, n = proj_tile(wt, wtok, hT, htoks, ti)
                            t0 = TILES[ti][0]
                            if role == 3:
                                act(sgT[:, t0:t0 + n], ps[:, 0:n], AF.Silu, R=[ptok], W=[sg_tok])
                            elif ti < 2:
                                act(pre[:, 3 + t0:3 + t0 + n], ps[:, 0:n], AF.Copy, R=[ptok], W=[pre_tok])
                            else:
                                act(pv[:, :, 3:11], ps[:, 0:n].rearrange("p (s r) -> p s r", r=LS), AF.Copy, R=[ptok], W=[pre_tok])
                        if role == 3:
                            continue
                        if ph == 0:
                            cp(gcar[:, ch, :], pre[:, TP:TP + 3], R=[pre_tok], W=[gcar_tok])
                        else:
                            cp(gco[:, ch, 8, :], pre[:, TP:TP + 3], R=[pre_tok], W=[gco_tok])
                        cp(gco[:, ch, 0:8, :], pv[:, :, 8:11], R=[pre_tok], W=[gco_tok])
                        wc = prm[:, l, P_GCW + ch * 4:P_GCW + ch * 4 + 4]
                        bc = prm[:, l, P_GCB + ch:P_GCB + ch + 1]
                        cvs = cv[:, TP:TPH].rearrange("p (s r) -> p s r", r=LS)
                        ts(cv[:, 0:TP], pre[:, 0:TP], wc[:, 0:1], bc, ALU.mult, ALU.add, R=[pre_tok, prm_tok], W=[cv_tok])
                        ts(cvs, pv[:, :, 0:8], wc[:, 0:1], bc, ALU.mult, ALU.add, R=[pre_tok, prm_tok], W=[cv_tok])
                        for tap in range(1, 4):
                            stt(cv[:, 0:TP], pre[:, tap:tap + TP], wc[:, tap:tap + 1], cv[:, 0:TP], ALU.mult, ALU.add,
                                R=[pre_tok, prm_tok], W=[cv_tok])
                            stt(cvs, pv[:, :, tap:tap + 8], wc[:, tap:tap + 1], cvs, ALU.mult, ALU.add, R=[pre_tok, prm_tok], W=[cv_tok])
                        if role == 2:
                            act(vT[:], cv[:], AF.Silu, R=[cv_tok], W=[v_tok])
                            continue
                        act(cv[:], cv[:], AF.Silu, R=[cv_tok], W=[cv_tok])
                        for ti, (t0, n) in enumerate(TILES):
                            act(sqb[:, 0:n], cv[:, t0:t0 + n], AF.Square, R=[cv_tok], W=[sqb_tok])
                            ps, ptok = pbig.get()
                            mm(ps[:, 0:n], ones_bf[:], sqb[:, 0:n], R=[sqb_tok, misc_tok], W=[ptok])
                            act(rinv[:, 0:n], ps[:, 0:n], AF.Sqrt, bias=cb[:, 1:2], scale=1.0, R=[ptok, misc_tok], W=[rinv_tok])
                            recip(rinv[:, 0:n], rinv[:, 0:n], R=[rinv_tok], W=[rinv_tok])
                            if role == 0:
                                stt(qT[:, t0:t0 + n], cv[:, t0:t0 + n], float(128.0 ** -0.5), rinv[:, 0:n], ALU.mult, ALU.mult,
                                    R=[cv_tok, rinv_tok], W=[q_tok])
                            else:
                                tt(kT[:, t0:t0 + n], cv[:, t0:t0 + n], rinv[:, 0:n], ALU.mult, R=[cv_tok, rinv_tok], W=[k_tok])

                    stage(4.05 + 0.1 * h)
                    K.dma("pool", Ss[:], sgdn_d[j, SPP * ph:SPP * ph + SPP, h].rearrange("s k v -> k s v"), W=[Ss_tok], semtok=Ss_tok)
                    act(Ssb[:], Ss[:], AF.Copy, R=[Ss_tok], W=[Ssb_tok])
                    if ph == 0:
                        memset(Sf_all[:, h, :], 0.0, W=[Sf_tok[h]])
                    cp(Sbf[:], Sf_all[:, h, :], R=[Sf_tok[h]], W=[Sbf_tok])

                    for bi, (c0, BS, kind) in enumerate(blocks()):
                        stage(4.06 + 0.1 * h + 0.001 * bi)
                        if kind == "P":
                            ncb, C, nlev = 2, 64, 5
                            pen = cst[:, C_PENP:C_PENP + 128]
                            cmask = cst[:, C_CM2:C_CM2 + 256].rearrange("p (c t) -> p c t", c=2)
                            rmk = cst[:, C_RM2:C_RM2 + 2]
                        else:
                            ncb, C, nlev = 8, 8, 2
                            pen = cst[0:64, C_PENS:C_PENS + 64]
                            cmask = cst[:, C_CM8:C_CM8 + 512].rearrange("p (c t) -> p c t", c=8)
                            rmk = cst[0:64, C_RM8:C_RM8 + 8]
                        cols = slice(c0, c0 + BS)
                        Gcol = dtk[0:BS, bi, 1, h:h + 1]
                        beta_c = dtk[0:BS, bi, 0, h:h + 1]
                        eGL_c = ex[0:BS, bi, 1, h:h + 1]
                        bG_c = ex[0:BS, bi, 2, h:h + 1]
                        nbeta_c = ex[0:BS, bi, 3, h:h + 1]
                        esel_h = cst[0:8, C_ESEL + h * 128:C_ESEL + (h + 1) * 128]
                        gps, gptok = psmall.get()
                        mm(gps[:, 0:BS], esel_h, dG[:, cols], R=[dtok_, cst_tok], W=[gptok])
                        stage(4.06 + 0.1 * h + 0.001 * bi + 0.00001 * 1)
                        r_, rtok = tmpf.get()
                        stt(r_[0:BS, 0:BS], gps[0:BS, 0:BS], Gcol, pen, ALU.subtract, ALU.max, R=[gptok, dk_tok, cst_tok], W=[rtok])
                        stage(4.06 + 0.1 * h + 0.001 * bi + 0.00001 * 2)
                        Ds, dstok = tmpf.get()
                        act(Ds[0:BS, 0:BS], r_[0:BS, 0:BS], AF.Exp, scale=-1.0, R=[rtok], W=[dstok])
                        stage(4.06 + 0.1 * h + 0.001 * bi + 0.00001 * 3)
                        eGr, egtok = tmpf.get()
                        act(eGr[:, 0:BS], gps[:, 0:BS], AF.Exp, R=[gptok], W=[egtok])
                        stage(4.06 + 0.1 * h + 0.001 * bi + 0.00001 * 4)
                        tw, twtok = tmpw.get()
                        tt(tw[:, 0:ncb, 0:BS], cmask[:, :, 0:BS], eGr[:, 0:BS].unsqueeze(1).broadcast_to([128, ncb, BS]), ALU.mult,
                           R=[cst_tok, egtok], W=[twtok])
                        stage(4.06 + 0.1 * h + 0.001 * bi + 0.00001 * 5)
                        tt(Qpad[:, 0:ncb, 0:BS], tw[:, 0:ncb, 0:BS], qT[:, cols].unsqueeze(1).broadcast_to([128, ncb, BS]), ALU.mult,
                           R=[twtok, q_tok], W=[qp_tok])
                        stage(4.06 + 0.1 * h + 0.001 * bi + 0.0001 * 1)
                        dps, dptok = psmall.get()
                        tr(dps[0:BS, 0:BS], Ds[0:BS, 0:BS], ident[0:BS, 0:BS], R=[dstok, cst_tok], W=[dptok])
                        stage(4.06 + 0.1 * h + 0.001 * bi + 0.00013)
                        DmT, dmtok = tmpf.get()
                        tt(DmT[0:BS, 0:BS], dps[0:BS, 0:BS], ident[0:BS, 0:BS], ALU.add, R=[dptok, cst_tok], W=[dmtok])
                        stage(4.06 + 0.1 * h + 0.001 * bi + 0.0001 * 2)
                        kkps, kktok = psmall.get()
                        mm(kkps[0:BS, 0:BS], kT[:, cols], kT[:, cols], R=[k_tok], W=[kktok])
                        stage(4.06 + 0.1 * h + 0.001 * bi + 0.00022)
                        qkps, qktok = psmall.get()
                        mm(qkps[0:BS, 0:BS], kT[:, cols], qT[:, cols], R=[k_tok, q_tok], W=[qktok])
                        stage(4.06 + 0.1 * h + 0.001 * bi + 0.00024)
                        B, btok = tmpf.get()
                        stt(B[0:BS, 0:BS], kkps[0:BS, 0:BS], nbeta_c, Ds[0:BS, 0:BS], ALU.mult, ALU.mult, R=[kktok, dk_tok, dstok], W=[btok])
                        stage(4.06 + 0.1 * h + 0.001 * bi + 0.00026)
                        attnT, attok = tmpb.get()
                        tt(attnT[0:BS, 0:BS], qkps[0:BS, 0:BS], DmT[0:BS, 0:BS], ALU.mult, R=[qktok, dmtok], W=[attok])
                        stage(4.06 + 0.1 * h + 0.001 * bi + 0.0001 * 3)
                        btps, btptok = psmall.get()
                        tr(btps[0:BS, 0:BS], B[0:BS, 0:BS], ident[0:BS, 0:BS], R=[btok, cst_tok], W=[btptok])
                        stage(4.06 + 0.1 * h + 0.001 * bi + 0.000301)
                        Bt, bttok = tmpf.get()
                        act(Bt[0:BS, 0:BS], btps[0:BS, 0:BS], AF.Copy, R=[btptok], W=[bttok])
                        stage(4.06 + 0.1 * h + 0.001 * bi + 0.000302)
                        TT, tttok = ttr.get()
                        tt(TT[0:BS, 0:BS], btps[0:BS, 0:BS], ident[0:BS, 0:BS], ALU.add, R=[btptok, cst_tok, bttok], W=[tttok])
                        for lev in range(nlev):
                            stage(4.06 + 0.1 * h + 0.001 * bi + 0.00031 + 0.00001 * lev)
                            last = (lev == nlev - 1)
                            p2, p2tok = psmall.get()
                            mm(p2[0:BS, 0:BS], Bt[0:BS, 0:BS], B[0:BS, 0:BS], R=[bttok, btok], W=[p2tok])
                            if not last:
                                p1, p1tok = psmall.get()
                                mm(p1[0:BS, 0:BS], B[0:BS, 0:BS], Bt[0:BS, 0:BS], R=[bttok, btok], W=[p1tok])
                            Bn, bntok = tmpf.get()
                            cp(Bn[0:BS, 0:BS], p2[0:BS, 0:BS], R=[p2tok], W=[bntok])
                            if not last:
                                Btn, btntok = tmpf.get()
                                act(Btn[0:BS, 0:BS], p1[0:BS, 0:BS], AF.Copy, R=[p1tok], W=[btntok])
                            stage(4.06 + 0.1 * h + 0.001 * bi + 0.00031 + 0.00001 * lev + 0.000005)
                            p3, p3tok = psmall.get()
                            mm(p3[0:BS, 0:BS], Bn[0:BS, 0:BS], TT[0:BS, 0:BS], R=[bntok, tttok], W=[p3tok])
                            tt(TT[0:BS, 0:BS], p3[0:BS, 0:BS], TT[0:BS, 0:BS], ALU.add, R=[p3tok, tttok], W=[tttok])
                            B, btok = Bn, bntok
                            if not last:
                                Bt, bttok = Btn, btntok
                        stage(4.06 + 0.1 * h + 0.001 * bi + 0.0001 * 4)
                        TTb, ttbtok = tmpb.get()
                        act(TTb[0:BS, 0:BS], TT[0:BS, 0:BS], AF.Copy, R=[tttok], W=[ttbtok])
                        stage(4.06 + 0.1 * h + 0.001 * bi + 0.0001 * 5)
                        pk, pktok = pbf.get()
                        tr(pk[0:BS, :], kT[:, cols], ident_bf[:], R=[k_tok, misc_tok], W=[pktok])
                        pvv, pvtok = pbf.get()
                        tr(pvv[0:BS, :], vT[:, cols], ident_bf[:], R=[v_tok, misc_tok], W=[pvtok])
                        vb, vbtok = tmpb.get()
                        act(vb[0:BS, :], pvv[0:BS, :], AF.Copy, scale=beta_c, R=[pvtok, dk_tok], W=[vbtok])
                        kbg, kbgtok = tmpb.get()
                        act(kbg[0:BS, :], pk[0:BS, :], AF.Copy, scale=bG_c, R=[pktok, dk_tok], W=[kbgtok])
                        kdm, kdmtok = tmpf.get()
                        ts(kdm[0:BS, 0:ncb], rmk, eGL_c, None, ALU.mult, R=[cst_tok, dk_tok], W=[kdmtok])
                        tt(Kpad[0:BS, 0:ncb, :], pk[0:BS, :].unsqueeze(1).broadcast_to([BS, ncb, 128]),
                           kdm[0:BS, 0:ncb].unsqueeze(2).broadcast_to([BS, ncb, 128]), ALU.mult, R=[pktok, kdmtok], W=[kp_tok])
                        stage(4.06 + 0.1 * h + 0.001 * bi + 0.0001 * 6)
                        wps, wptok = psmall.get()
                        mm(wps[:, 0:BS], kbg[0:BS, :], TTb[0:BS, 0:BS], R=[kbgtok, ttbtok], W=[wptok])
                        stt(Wpad[:, 0:ncb, 0:BS], wps[:, 0:BS].unsqueeze(1).broadcast_to([128, ncb, BS]), -1.0, cmask[:, :, 0:BS],
                            ALU.mult, ALU.mult, R=[wptok, cst_tok], W=[wp_tok])
                        stage(4.06 + 0.1 * h + 0.001 * bi + 0.0001 * 7)
                        if kind == "P":
                            for c in range(ncb):
                                rows = slice(c * C, (c + 1) * C)
                                mm(u_ps[rows, 0:128], TTb[rows, rows], vb[rows, :], start=True, stop=False, R=[ttbtok, vbtok], W=[u_ptok])
                                mm(u_ps[rows, 0:128], Wpad[:, c, rows], Sbf[:], start=False, stop=True, R=[wp_tok, Sbf_tok], W=[u_ptok])
                                act(u_sb[rows, :], u_ps[rows, 0:128], AF.Copy, R=[u_ptok], W=[usb_tok])
                                mm(o_ps[rows, 0:128], Qpad[:, c, rows], Sbf[:], start=True, stop=False, R=[qp_tok, Sbf_tok], W=[o_ptok])
                                mm(o_ps[rows, 0:128], attnT[rows, rows], u_sb[rows, :], start=False, stop=True, R=[attok, usb_tok], W=[o_ptok])
                                sps, sptok = psmall.get()
                                mm(sps[:, :], Kpad[0:BS, c, :], u_sb[0:BS, :], R=[kp_tok, usb_tok], W=[sptok])
                                stt(Sf_all[:, h, :], Sf_all[:, h, :], glb[:, h, (bi * 2 + c):(bi * 2 + c) + 1], sps[:, :], ALU.mult, ALU.add,
                                    R=[sptok, dk_tok], W=[Sf_tok[h]])
                                act(Sbf[:], Sf_all[:, h, :], AF.Copy, R=[Sf_tok[h]], W=[Sbf_tok])
                        else:
                            mm(u_ps[0:BS, 0:128], TTb[0:BS, 0:BS], vb[0:BS, :], start=True, stop=False, R=[ttbtok, vbtok], W=[u_ptok])
                            for c in range(ncb):
                                mm(u_ps[0:BS, 0:128], Wpad[:, c, 0:BS], Ssb[:, c, :], start=False, stop=(c == ncb - 1), R=[wp_tok, Ssb_tok], W=[u_ptok])
                            act(u_sb[0:BS, :], u_ps[0:BS, 0:128], AF.Copy, R=[u_ptok], W=[usb_tok])
                            for c in range(ncb):
                                mm(o_ps[0:BS, 0:128], Qpad[:, c, 0:BS], Ssb[:, c, :], start=(c == 0), stop=False, R=[qp_tok, Ssb_tok], W=[o_ptok])
                            mm(o_ps[0:BS, 0:128], attnT[0:BS, 0:BS], u_sb[0:BS, :], start=False, stop=True, R=[attok, usb_tok], W=[o_ptok])
                            sl = [pbig.get(), pbig.get()]
                            for c in range(ncb):
                                ps, ptok = sl[c // 4]
                                mm(ps[:, (c % 4) * 128:(c % 4) * 128 + 128], Kpad[0:BS, c, :], u_sb[0:BS, :], R=[kp_tok, usb_tok], W=[ptok])
                            tt(Sn[:], Ss[:], glb[:, h, 16:24].unsqueeze(2).broadcast_to([128, 8, 128]), ALU.mult, R=[Ss_tok, dk_tok], W=[Sn_tok])
                            for hb in range(2):
                                ps, ptok = sl[hb]
                                tt(Sn[:, hb * 4:hb * 4 + 4, :], Sn[:, hb * 4:hb * 4 + 4, :], ps[:, :].rearrange("p (s v) -> p s v", v=128), ALU.add,
                                   R=[ptok], W=[Sn_tok])
                            K.dma("sp", sgdn_o[j, SPP * ph:SPP * ph + SPP, h].rearrange("s k v -> k s v"), Sn[:], R=[Sn_tok], semtok=Sn_tok)
                        stage(4.06 + 0.1 * h + 0.001 * bi + 0.0001 * 8)
                        o_post(l, None, BS, h, c0, oT, otoks[min(c0 // 512, 2)], sgT, sg_tok, tmpf, tmpb)
                    if ph == 1:
                        K.dma("sp", pgdn_d[j, h], Sf_all[:, h, :], R=[Sf_tok[h]], semtok=Sf_tok[h])
                K.dma("sp", gco_d[j, ph], gco[:].rearrange("p a s r -> p (a s r)"), R=[gco_tok], semtok=gco_tok)
                K.barrier()

        def hgrn_phase(l, ph, hT, htoks, oT, otoks, S):
            j = l // 2
            Win = hgrn_w_in[j]
            with contextlib.ExitStack() as S1:
                rmask = sb(S1, "h_rm", [128, TPH], F32)
                rm_tok = Tok()
                fa = sb(S1, "h_fa", [128, TPH], F32)
                fa_tok = Tok()
                fk = sb(S1, "h_fk", [128, TPH], F32)
                fk_tok = Tok()
                fG = sb(S1, "h_fG", [128, TPH], F32)
                fG_tok = Tok()
                fe = sb(S1, "h_fe", [128, TPH], F32)
                fe_tok = Tok()
                eG = sb(S1, "h_eG", [128, TPH], F32)
                eG_tok = Tok()
                sq_ = sb(S1, "h_sq", [128, TPH], F32)
                sq_tok = Tok()
                qgT = sb(S1, "h_qgT", [128, TPH], BF16)
                kiT = sb(S1, "h_kiT", [128, TPH], BF16)
                kdT = sb(S1, "h_kdT", [128, TPH], BF16)
                vT = sb(S1, "h_vT", [128, TPH], BF16)
                sgT = sb(S1, "h_sgT", [128, TPH], BF16)
                qg_tok, ki_tok, kd_tok, v_tok, sg_tok = Tok(), Tok(), Tok(), Tok(), Tok()
                Sbf = sb(S1, "h_Sbf", [128, 128], BF16)
                Sbf_tok = Tok()
                Ss = sb(S1, "h_Ss", [128, SPP, 128], F32)
                Ss_tok = Tok()
                Ssb = sb(S1, "h_Ssb", [128, SPP, 128], BF16)
                Ssb_tok = Tok()
                Sn, Sn_tok = Ss, Ss_tok
                Kpad = sb(S1, "h_Kpad", [128, 8, 128], BF16)
                Qpad = sb(S1, "h_Qpad", [128, 8, 128], BF16)
                kp_tok, qp_tok = Tok(), Tok()
                tmpf = Ring([(sb(S1, "h_tf%d" % i, [128, 128], F32), Tok()) for i in range(4)])
                tmpb = Ring([(sb(S1, "h_tb%d" % i, [128, 128], BF16), Tok()) for i in range(6)])
                memset(rmask[:], 1.0, W=[rm_tok])
                memset(rmask[:, 0:TP].rearrange("p (c t) -> p c t", t=32)[:, :, 0:1], 0.0, W=[rm_tok])
                memset(rmask[:, TP:TPH].rearrange("p (c t) -> p c t", t=8)[:, :, 0:1], 0.0, W=[rm_tok])
                for h in range(H):
                    lb = lbT[:, h, 0:1]
                    oml = lbT[:, h, 1:2]
                    for role in range(4):
                        wt, wtok = wload(Win, role * 1024 + h * 128)
                        for ti in range(3):
                            ps, ptok, n = proj_tile(wt, wtok, hT, htoks, ti)
                            t0 = TILES[ti][0]
                            if role == 0:
                                act(sq_[:, t0:t0 + n], ps[:, 0:n], AF.Silu, R=[ptok], W=[sq_tok])
                            elif role == 1:
                                act(fa[:, t0:t0 + n], ps[:, 0:n], AF.Sigmoid, R=[ptok], W=[fa_tok])
                            elif role == 2:
                                act(vT[:, t0:t0 + n], ps[:, 0:n], AF.Copy, R=[ptok], W=[v_tok])
                            else:
                                act(sgT[:, t0:t0 + n], ps[:, 0:n], AF.Silu, R=[ptok], W=[sg_tok])
                    ts(fa[:], fa[:], oml, lb, ALU.mult, ALU.add, R=[fa_tok, lb_tok], W=[fa_tok])
                    ts(fk[:], fa[:], -1.0, 1.0, ALU.mult, ALU.add, R=[fa_tok], W=[fk_tok])
                    act(fa[:], fa[:], AF.Ln, R=[fa_tok], W=[fa_tok])
                    K.op("dve", lambda e: e.tensor_tensor_scan(out=fG[:], data0=rmask[:], data1=fa[:], initial=0.0, op0=ALU.mult, op1=ALU.add),
                         [fa_tok, rm_tok], [fG_tok])
                    act(eG[:], fG[:], AF.Exp, R=[fG_tok], W=[eG_tok])
                    tt(qgT[:], sq_[:], eG[:], ALU.mult, R=[sq_tok, eG_tok], W=[qg_tok])
                    act(fe[:], fG[:], AF.Exp, scale=-1.0, R=[fG_tok], W=[fe_tok])
                    tt(kiT[:], fk[:], fe[:], ALU.mult, R=[fk_tok, fe_tok], W=[ki_tok])
                    gp = fG[:, 0:TP].rearrange("p (c t) -> p c t", t=32)
                    tt(fa[:, 0:TP].rearrange("p (c t) -> p c t", t=32), gp[:, :, 31:32].broadcast_to([128, 32, 32]), gp, ALU.subtract,
                       R=[fG_tok], W=[fa_tok])
                    gs = fG[:, TP:TPH].rearrange("p (c t) -> p c t", t=8)
                    tt(fa[:, TP:TPH].rearrange("p (c t) -> p c t", t=8), gs[:, :, 7:8].broadcast_to([128, 8, 8]), gs, ALU.subtract,
                       R=[fG_tok], W=[fa_tok])
                    act(fe[:], fa[:], AF.Exp, R=[fa_tok], W=[fe_tok])
                    tt(kdT[:], fk[:], fe[:], ALU.mult, R=[fk_tok, fe_tok], W=[kd_tok])

                    K.dma("pool", Ss[:], shgrn_d[j, SPP * ph:SPP * ph + SPP, h].rearrange("s k v -> k s v"), W=[Ss_tok], semtok=Ss_tok)
                    act(Ssb[:], Ss[:], AF.Copy, R=[Ss_tok], W=[Ssb_tok])
                    if ph == 0:
                        memset(Sf_all[:, h, :], 0.0, W=[Sf_tok[h]])
                    cp(Sbf[:], Sf_all[:, h, :], R=[Sf_tok[h]], W=[Sbf_tok])

                    for bi, (c0, BS, kind) in enumerate(blocks()):
                        if kind == "P":
                            ncb, C = 4, 32
                            mi = cst[:, C_MIP:C_MIP + 128]
                            cmask = cst[:, C_CM4:C_CM4 + 512].rearrange("p (c t) -> p c t", c=4)
                            rmk = cst[:, C_RM4:C_RM4 + 4]
                        else:
                            ncb, C = 8, 8
                            mi = cst[0:64, C_MIS:C_MIS + 64]
                            cmask = cst[:, C_CM8:C_CM8 + 512].rearrange("p (c t) -> p c t", c=8)
                            rmk = cst[0:64, C_RM8:C_RM8 + 8]
                        cols = slice(c0, c0 + BS)
                        aps, aptok = psmall.get()
                        mm(aps[0:BS, 0:BS], kiT[:, cols], qgT[:, cols], R=[ki_tok, qg_tok], W=[aptok])
                        attnT, attok = tmpb.get()
                        tt(attnT[0:BS, 0:BS], aps[0:BS, 0:BS], mi, ALU.mult, R=[aptok, cst_tok], W=[attok])
                        pvv, pvtok = pbf.get()
                        tr(pvv[0:BS, :], vT[:, cols], ident_bf[:], R=[v_tok, misc_tok], W=[pvtok])
                        vtk, vtktok = tmpb.get()
                        act(vtk[0:BS, :], pvv[0:BS, :], AF.Copy, R=[pvtok], W=[vtktok])
                        pk, pktok = pbf.get()
                        tr(pk[0:BS, :], kdT[:, cols], ident_bf[:], R=[kd_tok, misc_tok], W=[pktok])
                        tt(Kpad[0:BS, 0:ncb, :], pk[0:BS, :].unsqueeze(1).broadcast_to([BS, ncb, 128]),
                           rmk.unsqueeze(2).broadcast_to([BS, ncb, 128]), ALU.mult, R=[pktok, cst_tok], W=[kp_tok])
                        tt(Qpad[:, 0:ncb, 0:BS], cmask[:, :, 0:BS], qgT[:, cols].unsqueeze(1).broadcast_to([128, ncb, BS]), ALU.mult,
                           R=[cst_tok, qg_tok], W=[qp_tok])
                        if kind == "P":
                            for c in range(ncb):
                                mm(o_ps[0:BS, 0:128], Qpad[:, c, 0:BS], Sbf[:], start=(c == 0), stop=False, R=[qp_tok, Sbf_tok], W=[o_ptok])
                                sps, sptok = psmall.get()
                                mm(sps[:, :], Kpad[0:BS, c, :], vtk[0:BS, :], R=[kp_tok, vtktok], W=[sptok])
                                ce = c0 + (c + 1) * C - 1
                                stt(Sf_all[:, h, :], Sf_all[:, h, :], eG[:, ce:ce + 1], sps[:, :], ALU.mult, ALU.add,
                                    R=[sptok, eG_tok], W=[Sf_tok[h]])
                                act(Sbf[:], Sf_all[:, h, :], AF.Copy, R=[Sf_tok[h]], W=[Sbf_tok])
                            mm(o_ps[0:BS, 0:128], attnT[0:BS, 0:BS], vtk[0:BS, :], start=False, stop=True, R=[attok, vtktok], W=[o_ptok])
                        else:
                            for c in range(ncb):
                                mm(o_ps[0:BS, 0:128], Qpad[:, c, 0:BS], Ssb[:, c, :], start=(c == 0), stop=False, R=[qp_tok, Ssb_tok], W=[o_ptok])
                            mm(o_ps[0:BS, 0:128], attnT[0:BS, 0:BS], vtk[0:BS, :], start=False, stop=True, R=[attok, vtktok], W=[o_ptok])
                            sl = [pbig.get(), pbig.get()]
                            for c in range(ncb):
                                ps, ptok = sl[c // 4]
                                mm(ps[:, (c % 4) * 128:(c % 4) * 128 + 128], Kpad[0:BS, c, :], vtk[0:BS, :], R=[kp_tok, vtktok], W=[ptok])
                            ege = eG[:, TP:TPH].rearrange("p (s t) -> p s t", t=8)[:, :, 7:8]
                            tt(Sn[:], Ss[:], ege.broadcast_to([128, 8, 128]), ALU.mult, R=[Ss_tok, eG_tok], W=[Sn_tok])
                            for hb in range(2):
                                ps, ptok = sl[hb]
                                tt(Sn[:, hb * 4:hb * 4 + 4, :], Sn[:, hb * 4:hb * 4 + 4, :], ps[:, :].rearrange("p (s v) -> p s v", v=128), ALU.add,
                                   R=[ptok], W=[Sn_tok])
                            K.dma("sp", shgrn_o[j, SPP * ph:SPP * ph + SPP, h].rearrange("s k v -> k s v"), Sn[:], R=[Sn_tok], semtok=Sn_tok)
                        o_post(l, None, BS, h, c0, oT, otoks[min(c0 // 512, 2)], sgT, sg_tok, tmpf, tmpb)
                    if ph == 1:
                        K.dma("sp", phgrn_d[j, h], Sf_all[:, h, :], R=[Sf_tok[h]], semtok=Sf_tok[h])
                K.barrier()

        def ffn_phase(l, ph):
            A2, B2, G2 = modA[3], modA[4], modA[5]
            with contextlib.ExitStack() as SF:
                aT = sb(SF, "f_aT", [128, NFF, TPH], BF16)
                a_toks = [Tok() for _ in range(3)]
                fci = sb(SF, "f_fci", [128, NFF, 8, 2], F32)
                fci_tok = Tok()
                fco = sb(SF, "f_fco", [128, NFF, 9, 2], F32)
                fco_tok = Tok()
                K.dma("sp", fci[:].rearrange("p a s r -> p (a s r)"), fci_d[l, ph], W=[fci_tok], semtok=fci_tok)
                memset(fco[:], 0.0, W=[fco_tok])
                with contextlib.ExitStack() as S1:
                    hT = sb(S1, "f_hT", [128, KC, TPH], BF16)
                    htoks = [Tok() for _ in range(3)]
                    prenorm(ph, A2, B2, hT, htoks, S1)
                    gpre = sb(S1, "f_gpre", [128, 2 + TP + SPP * 10], F32)
                    gp_tok = Tok()
                    up = sb(S1, "f_up", [128, TPH], F32)
                    up_tok = Tok()
                    gc = sb(S1, "f_gc", [128, TPH], F32)
                    gc_tok = Tok()
                    pv = gpre[:, 2 + TP:2 + TP + SPP * 10].rearrange("p (s r) -> p s r", r=10)
                    for jj in range(NFF):
                        wg, wgtok = wload(ffn_w_gu[l], jj * 128)
                        wu, wutok = wload(ffn_w_gu[l], DFF + jj * 128)
                        if ph == 0:
                            memset(gpre[:, 0:2], 0.0, W=[gp_tok])
                        else:
                            cp(gpre[:, 0:2], fcar[:, jj, :], R=[fcar_tok], W=[gp_tok])
                        cp(pv[:, :, 0:2], fci[:, jj, :, :], R=[fci_tok], W=[gp_tok])
                        for ti in range(3):
                            t0 = TILES[ti][0]
                            ps, ptok, n = proj_tile(wg, wgtok, hT, htoks, ti)
                            if ti < 2:
                                act(gpre[:, 2 + t0:2 + t0 + n], ps[:, 0:n], AF.Copy, R=[ptok], W=[gp_tok])
                            else:
                                act(pv[:, :, 2:10], ps[:, 0:n].rearrange("p (s r) -> p s r", r=LS), AF.Copy, R=[ptok], W=[gp_tok])
                            ps, ptok, n = proj_tile(wu, wutok, hT, htoks, ti)
                            cp(up[:, t0:t0 + n], ps[:, 0:n], R=[ptok], W=[up_tok])
                        if ph == 0:
                            cp(fcar[:, jj, :], gpre[:, TP:TP + 2], R=[gp_tok], W=[fcar_tok])
                        else:
                            cp(fco[:, jj, 8, :], gpre[:, TP:TP + 2], R=[gp_tok], W=[fco_tok])
                        cp(fco[:, jj, 0:8, :], pv[:, :, 8:10], R=[gp_tok], W=[fco_tok])
                        wc = prm[:, l, P_FCW + jj * 3:P_FCW + jj * 3 + 3]
                        bc = prm[:, l, P_FCB + jj:P_FCB + jj + 1]
                        gcs = gc[:, TP:TPH].rearrange("p (s r) -> p s r", r=LS)
                        ts(gc[:, 0:TP], gpre[:, 0:TP], wc[:, 0:1], bc, ALU.mult, ALU.add, R=[gp_tok, prm_tok], W=[gc_tok])
                        ts(gcs, pv[:, :, 0:8], wc[:, 0:1], bc, ALU.mult, ALU.add, R=[gp_tok, prm_tok], W=[gc_tok])
                        for tap in range(1, 3):
                            stt(gc[:, 0:TP], gpre[:, tap:tap + TP], wc[:, tap:tap + 1], gc[:, 0:TP], ALU.mult, ALU.add,
                                R=[gp_tok, prm_tok], W=[gc_tok])
                            stt(gcs, pv[:, :, tap:tap + 8], wc[:, tap:tap + 1], gcs, ALU.mult, ALU.add, R=[gp_tok, prm_tok], W=[gc_tok])
                        act(gc[:], gc[:], AF.Silu, R=[gc_tok], W=[gc_tok])
                        for ti, (t0, n) in enumerate(TILES):
                            tt(aT[:, jj, t0:t0 + n], gc[:, t0:t0 + n], up[:, t0:t0 + n], ALU.mult, R=[gc_tok, up_tok], W=[a_toks[ti]])
                    K.dma("sp", fco_d[l, ph], fco[:].rearrange("p a s r -> p (a s r)"), R=[fco_tok], semtok=fco_tok)
                    K.barrier()
                with contextlib.ExitStack() as S2:
                    y = sb(S2, "f_y", [128, KC, TPH], F32)
                    ytok = Tok()
                    wdn = Ring([(sb(S2, "f_wd%d" % i, [128, NFF, 128], BF16), Tok()) for i in range(2)])
                    sqt = (sb(S2, "f_sq", [128, KC, 256], BF16), Tok())
                    rs = (sb(S2, "f_rs", [128, 256], F32), Tok())
                    tmps = Ring([(sb(S2, "f_t%d" % i, [128, 256], F32), Tok()) for i in range(2)])
                    for oc in range(KC):
                        wt, wtok = wdn.get()
                        K.dma("pool", wt[:], ffn_w_down[l][:, oc * 128:(oc + 1) * 128].rearrange("(j p) n -> p j n", p=128), W=[wtok], semtok=wtok)
                        for ti, (t0, n) in enumerate(TILES):
                            ps, ptok = pbig.get()
                            for jj in range(NFF):
                                mm(ps[:, 0:n], wt[:, jj, :], aT[:, jj, t0:t0 + n], start=(jj == 0), stop=(jj == NFF - 1),
                                   R=[wtok, a_toks[ti]], W=[ptok])
                            cp(y[:, oc, t0:t0 + n], ps[:, 0:n], R=[ptok], W=[ytok])
                    for ti, (t0, n) in enumerate([(0, 256), (256, 256), (512, 256), (768, 256), (1024, 64)]):
                        rstd_fm(y[:, :, t0:t0 + n], n, [ytok], sqt, rs)
                        for kc in range(KC):
                            tmp, ttok = tmps.get()
                            tt(tmp[:, 0:n], y[:, kc, t0:t0 + n], rs[0][:, 0:n], ALU.mult, R=[ytok, rs[1]], W=[ttok])
                            if t0 < TP:
                                stt(xT[:, kc, ph, t0:t0 + n], tmp[:, 0:n], G2[:, kc, 0:1], xT[:, kc, ph, t0:t0 + n], ALU.mult, ALU.add,
                                    R=[ttok, mod_tok], W=[xtok[ph]])
                            else:
                                sc = seqcols(ph)
                                v3 = tmp[:, 0:n].rearrange("p (s j) -> p s j", j=LS)
                                tt(v3, v3, G2[:, kc, sc].unsqueeze(2).broadcast_to([128, SPP, LS]), ALU.mult, R=[ttok, mod_tok], W=[ttok])
                                x3 = xT[:, kc, ph, t0:t0 + n].rearrange("p (s j) -> p s j", j=LS)
                                tt(x3, x3, v3, ALU.add, R=[ttok], W=[xtok[ph]])
                    K.barrier()

        def main_program():
            for l in range(depth):
                with contextlib.ExitStack() as SL:
                    stage(1)
                    adaln(l, SL)
                    if l % 2 == 1:
                        jj_ = l // 2
                        if jj_ == 0:
                            memset(lbT[:, :, 0:1], 0.0, W=[lb_tok])
                            memset(lbT[:, :, 1:2], 1.0, W=[lb_tok])
                        else:
                            hl = prm[:, l, P_HLB:P_HLB + 16].rearrange("p (h t) -> p h t", t=2)
                            tt(lbT[:, :, 0:1], hl[:, :, 1:2], hl[:, :, 0:1], ALU.subtract, R=[prm_tok], W=[lb_tok])
                            act(lbT[:, :, 0:1], lbT[:, :, 0:1], AF.Sigmoid, R=[lb_tok], W=[lb_tok])
                            ts(lbT[:, :, 1:2], lbT[:, :, 0:1], -1.0, 1.0, ALU.mult, ALU.add, R=[lb_tok], W=[lb_tok])
                    K.barrier()
                for ph in range(2):
                    with contextlib.ExitStack() as SA:
                        hT = sb(SA, "m_hT", [128, KC, TPH], BF16)
                        htoks = [Tok() for _ in range(3)]
                        oT = sb(SA, "m_oT", [128, KC, TPH], BF16)
                        otoks = [Tok() for _ in range(3)]
                        with contextlib.ExitStack() as SP:
                            stage(2)
                            prenorm(ph, modA[0], modA[1], hT, htoks, SP)
                            K.barrier()
                        with contextlib.ExitStack() as SM:
                            if l % 2 == 0:
                                gdn_phase(l, ph, hT, htoks, oT, otoks, SM)
                            else:
                                hgrn_phase(l, ph, hT, htoks, oT, otoks, SM)
                            K.barrier()
                        with contextlib.ExitStack() as SO:
                            stage(5)
                            Wout = gdn_w_out[l // 2] if l % 2 == 0 else hgrn_w_out[l // 2]
                            outproj_postnorm(l, ph, Wout, oT, otoks, modA[2], SO)
                            K.barrier()
                    stage(6)
                    ffn_phase(l, ph)
            for ph in range(2):
                K.dma("sp", yT_d.rearrange("p (k h t) -> p k h t", k=KC, h=2)[:, :, ph, :], xT[:, :, ph, :], R=[xtok[ph]], semtok=xtok[ph])

        try:
            main_program()
        except StopBuild:
            K.barrier()
        K.final_wait("sp")

        with nc.Block() as block:
            K.replay(block)
    return nc


def _consts():
    c = np.zeros((128, NCST), np.float32)
    c[:, C_IDENT:C_IDENT + 128] = np.eye(128, dtype=np.float32)
    i = np.arange(128)[:, None]
    jx = np.arange(128)[None, :]
    BIG = 1.0e4
    valid = (i // 64 == jx // 64) & (i > jx)
    c[:, C_PENP:C_PENP + 128] = np.where(valid, 0.0, BIG)
    i8 = np.arange(64)[:, None]
    j8 = np.arange(64)[None, :]
    valid = (i8 // 8 == j8 // 8) & (i8 > j8)
    c[0:64, C_PENS:C_PENS + 64] = np.where(valid, 0.0, BIG)
    for ncb, off, bs in ((2, C_CM2, 128), (4, C_CM4, 128), (8, C_CM8, 64)):
        C = bs // ncb
        m = np.zeros((ncb, bs), np.float32)
        for cc in range(ncb):
            m[cc, cc * C:(cc + 1) * C] = 1.0
        c[:, off:off + ncb * bs] = m.reshape(1, -1)
    for ncb, off, bs in ((2, C_RM2, 128), (4, C_RM4, 128), (8, C_RM8, 64)):
        C = bs // ncb
        m = np.zeros((bs, ncb), np.float32)
        for cc in range(ncb):
            m[cc * C:(cc + 1) * C, cc] = 1.0
        c[0:bs, off:off + ncb] = m
    c[:, C_MIP:C_MIP + 128] = ((i // 32 == jx // 32) & (i <= jx)).astype(np.float32)
    c[0:64, C_MIS:C_MIS + 64] = ((i8 // 8 == j8 // 8) & (i8 <= j8)).astype(np.float32)
    e = np.zeros((8, 8, 128), np.float32)
    for h in range(8):
        e[h, h, :] = 1.0
    c[0:8, C_ESEL:C_ESEL + 1024] = e.reshape(8, 1024)
    return c


def _fm(v):
    sh = v.shape
    k = sh[-1] // 128
    v = v.reshape(sh[:-1] + (k, 128))
    return np.moveaxis(np.moveaxis(v, -1, 0), -1, 1)


_NC_CACHE = {}


def kernel(x_prompt, x_sample, state_gdn, state_gdn_conv, state_hgrn, state_ffn_conv, c_prompt, c_sample,
           ada_w, ada_b, norm_pre_mix, norm_post_mix, norm_pre_ffn, norm_post_ffn,
           gdn_w_in, gdn_conv_w, gdn_conv_b, gdn_a_log, gdn_dt_bias, gdn_norm, gdn_w_out,
           hgrn_lb, hgrn_w_in, hgrn_norm, hgrn_w_out,
           ffn_w_gu, ffn_conv_w, ffn_conv_b, ffn_w_down, _depth=DEPTH, _stage=None):
    f32 = np.float32
    A = lambda a: np.ascontiguousarray(np.asarray(a, dtype=f32))
    x_prompt, x_sample = A(x_prompt), A(x_sample)
    state_gdn, state_gdn_conv, state_hgrn, state_ffn_conv = A(state_gdn), A(state_gdn_conv), A(state_hgrn), A(state_ffn_conv)
    c_prompt, c_sample = A(c_prompt), A(c_sample)
    prm = np.zeros((128, 4, NPRM), f32)
    nrow = np.zeros((128, 4, 128), f32)
    ada_b_, gcw, gcb = A(ada_b), A(gdn_conv_w), A(gdn_conv_b)
    fcw, fcb = A(ffn_conv_w), A(ffn_conv_b)
    hlb = A(hgrn_lb)
    for l in range(4):
        prm[:, l, P_ADAB:P_ADAB + 48] = ada_b_[l].reshape(48, 128).T
        prm[:, l, P_NPRE_MIX:P_NPRE_MIX + 8] = A(norm_pre_mix)[l].reshape(8, 128).T
        prm[:, l, P_NPOST_MIX:P_NPOST_MIX + 8] = A(norm_post_mix)[l].reshape(8, 128).T
        prm[:, l, P_NPRE_FFN:P_NPRE_FFN + 8] = A(norm_pre_ffn)[l].reshape(8, 128).T
        prm[:, l, P_NPOST_FFN:P_NPOST_FFN + 8] = A(norm_post_ffn)[l].reshape(8, 128).T
        j = l // 2
        if l % 2 == 0:
            prm[:, l, P_GCW:P_GCW + 96] = gcw[j].reshape(4, 24, 128).transpose(2, 1, 0).reshape(128, 96)
            prm[:, l, P_GCB:P_GCB + 24] = gcb[j].reshape(24, 128).T
            prm[0:8, l, P_ALOG] = A(gdn_a_log)[j]
            prm[0:8, l, P_DTB] = A(gdn_dt_bias)[j]
            nrow[:, l, :] = A(gdn_norm)[j][None, :]
        else:
            nrow[:, l, :] = A(hgrn_norm)[j][None, :]
        prm[:, l, P_FCW:P_FCW + 66] = fcw[l].reshape(3, 22, 128).transpose(2, 1, 0).reshape(128, 66)
        prm[:, l, P_FCB:P_FCB + 22] = fcb[l].reshape(22, 128).T
        prm[:, l, P_HLB:P_HLB + 16] = hlb.reshape(2, 8, 128).transpose(2, 1, 0).reshape(128, 16)
    cst = _consts()
    shared = dict(prm=prm.reshape(128, -1), nrow=nrow.reshape(128, -1), cst=cst,
                  ada_w=A(ada_w), gdn_w_in=A(gdn_w_in), gdn_w_out=A(gdn_w_out), hgrn_w_in=A(hgrn_w_in),
                  hgrn_w_out=A(hgrn_w_out), ffn_w_gu=A(ffn_w_gu), ffn_w_down=A(ffn_w_down))
    in_maps = []
    for c in range(NCORE):
        xt = np.zeros((128, KC, 2, TPH), f32)
        xp = _fm(x_prompt[c])
        xs = _fm(x_sample[16 * c:16 * c + 16])
        for ph in range(2):
            xt[:, :, ph, 0:TP] = xp[:, :, TP * ph:TP * ph + TP]
            xt[:, :, ph, TP:] = xs[:, :, 8 * ph:8 * ph + 8, :].reshape(128, KC, TS)
        cc = np.concatenate([c_prompt[c:c + 1], c_sample[16 * c:16 * c + 16]], 0)
        cT = _fm(cc)
        gc = _fm(state_gdn_conv[:, 16 * c:16 * c + 16])
        gci = np.zeros((2, 2, 128, 24, 8, 3), f32)
        fc = _fm(state_ffn_conv[:, 16 * c:16 * c + 16])
        fci = np.zeros((4, 2, 128, NFF, 8, 2), f32)
        for ph in range(2):
            gci[:, ph] = gc[:, :, :, 8 * ph:8 * ph + 8, :].transpose(2, 0, 1, 3, 4)
            fci[:, ph] = fc[:, :, :, 8 * ph:8 * ph + 8, :].transpose(2, 0, 1, 3, 4)
        m = dict(shared)
        m.update(xT=xt.reshape(128, -1), cT=np.ascontiguousarray(cT).reshape(128, -1),
                 sgdn=np.ascontiguousarray(state_gdn[:, 16 * c:16 * c + 16]),
                 shgrn=np.ascontiguousarray(state_hgrn[:, 16 * c:16 * c + 16]),
                 gci=gci.reshape(2, 2, 128, -1), fci=fci.reshape(4, 2, 128, -1))
        in_maps.append(m)
    ck = (_depth, _stage)
    if ck not in _NC_CACHE:
        _STAGE_LIMIT[0] = _stage
        _STAGE_LIMIT[1] = False
        _NC_CACHE[ck] = build_nc(_depth)
        _STAGE_LIMIT[0] = None
        _STAGE_LIMIT[1] = False
    nc = _NC_CACHE[ck]
    res = run_bass_kernel_spmd(nc, in_maps, core_ids=list(range(NCORE)))
    R = res.results
    y_prompt = np.zeros((8, 2048, D), f32)
    y_sample = np.zeros((128, 8, D), f32)
    p_gdn = np.zeros((2, 8, H, 128, 128), f32)
    p_hgrn = np.zeros((2, 8, H, 128, 128), f32)
    s_gdn = np.zeros((2, 128, H, 128, 128), f32)
    s_hgrn = np.zeros((2, 128, H, 128, 128), f32)
    p_gconv = np.zeros((2, 8, 3, 3072), f32)
    s_gconv = np.zeros((2, 128, 3, 3072), f32)
    p_fconv = np.zeros((4, 8, 2, DFF), f32)
    s_fconv = np.zeros((4, 128, 2, DFF), f32)
    for c in range(NCORE):
        r = R[c]
        yt = r["yT"].reshape(128, KC, 2, TPH)
        for ph in range(2):
            y_prompt[c, TP * ph:TP * ph + TP] = yt[:, :, ph, 0:TP].transpose(2, 1, 0).reshape(TP, D)
            ys = yt[:, :, ph, TP:].reshape(128, KC, 8, 8)
            y_sample[16 * c + 8 * ph:16 * c + 8 * ph + 8] = ys.transpose(2, 3, 1, 0).reshape(8, 8, D)
        p_gdn[:, c] = r["pgdn"]
        p_hgrn[:, c] = r["phgrn"]
        s_gdn[:, 16 * c:16 * c + 16] = r["sgdn_o"]
        s_hgrn[:, 16 * c:16 * c + 16] = r["shgrn_o"]
        g = r["gco"].reshape(2, 2, 128, 24, 9, 3)
        f = r["fco"].reshape(4, 2, 128, NFF, 9, 2)
        for ph in range(2):
            gs = g[:, ph, :, :, 0:8, :].transpose(0, 3, 4, 2, 1).reshape(2, 8, 3, 3072)
            s_gconv[:, 16 * c + 8 * ph:16 * c + 8 * ph + 8] = gs
            fs = f[:, ph, :, :, 0:8, :].transpose(0, 3, 4, 2, 1).reshape(4, 8, 2, DFF)
            s_fconv[:, 16 * c + 8 * ph:16 * c + 8 * ph + 8] = fs
        p_gconv[:, c] = g[:, 1, :, :, 8, :].transpose(0, 3, 2, 1).reshape(2, 3, 3072)
        p_fconv[:, c] = f[:, 1, :, :, 8, :].transpose(0, 3, 2, 1).reshape(4, 2, DFF)
    return (y_prompt, y_sample, p_gdn, p_gconv, p_hgrn, p_fconv, s_gdn, s_gconv, s_hgrn, s_fconv)
```

```python
import contextlib
import numpy as np
import concourse.bass as bass
import concourse.mybir as mybir
from concourse.bass_utils import run_bass_kernel_spmd

F32 = mybir.dt.float32
BF16 = mybir.dt.bfloat16
AF = mybir.ActivationFunctionType
ALU = mybir.AluOpType

NCORE = 8
D = 1024
KC = 8
H = 8
DFF = 2816
NFF = 22
DEPTH = 4
TP = 1024
SPP = 8
LS = 8
TS = SPP * LS
TPH = TP + TS
TILES = [(0, 512), (512, 512), (1024, 64)]
EPS = 1e-6
NPRM = 48 + 32 + 96 + 24 + 66 + 22 + 16 + 2

P_ADAB = 0
P_NPRE_MIX = 48
P_NPOST_MIX = 56
P_NPRE_FFN = 64
P_NPOST_FFN = 72
P_GCW = 80
P_GCB = 176
P_FCW = 200
P_FCB = 266
P_HLB = 288
P_ALOG = 304
P_DTB = 305

C_IDENT = 0
C_PENP = 128
C_PENS = 256
C_CM2 = 320
C_CM4 = 576
C_CM8 = 1088
C_RM2 = 1600
C_RM4 = 1602
C_RM8 = 1606
C_MIP = 1614
C_MIS = 1742
C_ESEL = 1806
NCST = 1806 + 1024


class Tok:
    __slots__ = ("w", "r", "dsem", "dval", "excl")

    def __init__(self, excl=False):
        self.w = {}
        self.r = {}
        self.dsem = None
        self.dval = 0
        self.excl = excl


class Sched:
    ENGS = ("pe", "act", "dve", "pool", "sp")

    def __init__(self, nc, es):
        self.nc = nc
        self.es = es
        self.q = {k: [] for k in self.ENGS}
        self.n = {k: 0 for k in self.ENGS}
        self.seen = {k: {} for k in self.ENGS}
        self.sem = {}
        for k in ("pe", "act", "dve", "pool"):
            self.sem[k] = es.enter_context(nc.semaphore("s_" + k))
        self.dma_latest = {}
        self.nd = 0

    def _waits(self, en, R, W):
        need = {}

        def add(ev):
            key, sem, val = ev
            if key not in need or need[key][1] < val:
                need[key] = (sem, val)

        for t in R:
            for ev in t.w.values():
                add(ev)
            if t.excl:
                for ev in t.r.values():
                    add(ev)
        for t in W:
            for ev in t.w.values():
                add(ev)
            for ev in t.r.values():
                add(ev)
        out = []
        seen = self.seen[en]
        for key, (sem, val) in need.items():
            if key == en and (en == "pe" or _NO_SAME_ENGINE_WAIT[0]):
                continue
            if seen.get(key, 0) >= val:
                continue
            seen[key] = val
            out.append((sem, val))
        return out

    def op(self, en, fn, R=(), W=()):
        if _STAGE_LIMIT[1]:
            return
        waits = self._waits(en, R, W)
        self.n[en] += 1
        sem = self.sem[en]
        self.q[en].append((waits, fn, sem, 1))
        ev = (en, sem, self.n[en])
        for t in W:
            t.w[en] = ev
        for t in R:
            t.r[en] = ev

    def dma(self, qn, out, in_, R=(), W=(), semtok=None):
        if _STAGE_LIMIT[1]:
            return
        waits = self._waits(qn, R, W)
        t = semtok
        if t.dsem is None:
            t.dsem = {}
        if qn not in t.dsem:
            self.nd += 1
            t.dsem[qn] = [self.es.enter_context(self.nc.semaphore("d%d" % self.nd)), 0]
        ent = t.dsem[qn]
        ent[1] += 16
        dsem, dval = ent[0], ent[1]
        key = "d%d_%s" % (id(t), qn)
        self.q[qn].append((waits, (lambda e: e.dma_start(out=out, in_=in_)), dsem, 16))
        ev = (key, dsem, dval)
        for x in W:
            x.w[key] = ev
        for x in R:
            x.r[key] = ev
        self.dma_latest[key] = (dsem, dval)

    def barrier(self):
        for en in self.ENGS:
            waits = []
            seen = self.seen[en]
            for k in ("pe", "act", "dve", "pool"):
                if k == en and en == "pe":
                    continue
                v = self.n[k]
                if v > 0 and seen.get(k, 0) < v:
                    seen[k] = v
                    waits.append((self.sem[k], v))
            for key, (sem, val) in self.dma_latest.items():
                if seen.get(key, 0) < val:
                    seen[key] = val
                    waits.append((sem, val))
            if waits:
                self.q[en].append((waits, None, None, 0))

    def final_wait(self, en="sp"):
        waits = []
        for key, (sem, val) in self.dma_latest.items():
            waits.append((sem, val))
        for k in ("pe", "act", "dve", "pool"):
            if self.n[k] > 0:
                waits.append((self.sem[k], self.n[k]))
        self.q[en].append((waits, None, None, 0))

    def replay(self, block):
        q = self.q

        def run(e, lst):
            for waits, fn, sem, inc in lst:
                if fn is None or not _INLINE_WAIT[0] or not waits:
                    for s, v in waits:
                        e.wait_ge(s, v)
                    if fn is not None:
                        fn(e).then_inc(sem, inc)
                else:
                    for s, v in waits[:-1]:
                        e.wait_ge(s, v)
                    s, v = waits[-1]
                    fn(e)._wait_ge(s, v).then_inc(sem, inc)

        @block.tensor
        def _(e):
            run(e, q["pe"])

        @block.scalar
        def _(e):
            run(e, q["act"])

        @block.vector
        def _(e):
            run(e, q["dve"])

        @block.gpsimd
        def _(e):
            run(e, q["pool"])

        @block.sync
        def _(e):
            run(e, q["sp"])


class StopBuild(Exception):
    pass


_STAGE_LIMIT = [None, False]
_NO_SAME_ENGINE_WAIT = [False]
_INLINE_WAIT = [True]


def stage(n):
    if _STAGE_LIMIT[0] is not None and n > _STAGE_LIMIT[0]:
        _STAGE_LIMIT[1] = True


class Ring:
    def __init__(self, items):
        self.items = items
        self.i = 0

    def get(self):
        r = self.items[self.i]
        self.i = (self.i + 1) % len(self.items)
        return r


def build_nc(depth=DEPTH):
    nc = bass.Bass("TRN2", target_bir_lowering=False)

    def din(name, shape):
        return nc.dram_tensor(name, list(shape), F32, kind="ExternalInput").ap()

    def dout(name, shape):
        return nc.dram_tensor(name, list(shape), F32, kind="ExternalOutput").ap()

    xT_d = din("xT", [128, KC * 2 * TPH])
    cT_d = din("cT", [128, KC * 17])
    sgdn_d = din("sgdn", [2, 16, H, 128, 128])
    shgrn_d = din("shgrn", [2, 16, H, 128, 128])
    gci_d = din("gci", [2, 2, 128, 24 * 8 * 3])
    fci_d = din("fci", [4, 2, 128, NFF * 8 * 2])
    prm_d = din("prm", [128, 4 * NPRM])
    nrow_d = din("nrow", [128, 4 * 128])
    cst_d = din("cst", [128, NCST])
    ada_w = din("ada_w", [4, D, 6 * D])
    gdn_w_in = din("gdn_w_in", [2, D, 4112])
    gdn_w_out = din("gdn_w_out", [2, D, D])
    hgrn_w_in = din("hgrn_w_in", [2, D, 4096])
    hgrn_w_out = din("hgrn_w_out", [2, D, D])
    ffn_w_gu = din("ffn_w_gu", [4, D, 2 * DFF])
    ffn_w_down = din("ffn_w_down", [4, DFF, D])

    yT_d = dout("yT", [128, KC * 2 * TPH])
    pgdn_d = dout("pgdn", [2, H, 128, 128])
    sgdn_o = dout("sgdn_o", [2, 16, H, 128, 128])
    phgrn_d = dout("phgrn", [2, H, 128, 128])
    shgrn_o = dout("shgrn_o", [2, 16, H, 128, 128])
    gco_d = dout("gco", [2, 2, 128, 24 * 9 * 3])
    fco_d = dout("fco", [4, 2, 128, NFF * 9 * 2])

    with contextlib.ExitStack() as es:
        K = Sched(nc, es)

        cnt = [0]

        def sb(stack, name, shape, dt):
            cnt[0] += 1
            return stack.enter_context(nc.sbuf_tensor("sb%d_%s" % (cnt[0], name), list(shape), dt))

        xT = sb(es, "xT", [128, KC, 2, TPH], F32)
        xtok = [Tok(), Tok()]
        cst = sb(es, "cst", [128, NCST], F32)
        cst_tok = Tok()
        prm = sb(es, "prm", [128, 4, NPRM], F32)
        prm_tok = Tok()
        nrow = sb(es, "nrow", [128, 4, 128], F32)
        nrow_tok = Tok()
        ident_bf = sb(es, "ident_bf", [128, 128], BF16)
        ones_bf = sb(es, "ones_bf", [128, 128], BF16)
        cb = sb(es, "cb", [128, 8], F32)
        misc_tok = Tok()
        csT = sb(es, "csT", [128, KC, 17], BF16)
        cs_tok = Tok()
        modA = [sb(es, "modA%d" % i, [128, KC, 17], F32) for i in range(6)]
        mod_tok = Tok()
        Sf_all = sb(es, "Sf_all", [128, H, 128], F32)
        Sf_tok = [Tok() for _ in range(H)]
        gcar = sb(es, "gcar", [128, 24, 3], F32)
        gcar_tok = Tok()
        fcar = sb(es, "fcar", [128, NFF, 2], F32)
        fcar_tok = Tok()
        lbT = sb(es, "lbT", [128, H, 2], F32)
        lb_tok = Tok()
        wun = [(sb(es, "wun%d" % i, [128, KC, 128], BF16), Tok()) for i in range(5)]
        wring = Ring(wun)

        psF = [es.enter_context(nc.psum_tensor("psF%d" % i, [128, 512], F32)) for i in range(7)]
        psB = es.enter_context(nc.psum_tensor("psB", [128, 1024], BF16))
        bkt = [Tok(excl=True) for _ in range(8)]
        pbig = Ring([(psF[0], bkt[0]), (psF[1], bkt[1])])
        psmall = Ring([(psF[2 + i % 3][:, (i // 3) * 128:(i // 3) * 128 + 128], bkt[2 + i % 3]) for i in range(12)])
        pbf = Ring([(psB[:, i * 128:(i + 1) * 128], bkt[7]) for i in range(8)])
        o_ps, o_ptok = psF[5], bkt[5]
        u_ps, u_ptok = psF[6], bkt[6]
        pwide = Ring([(psF[i], bkt[i]) for i in range(7)])

        def mm(out, lhsT, rhs, start=True, stop=True, R=(), W=()):
            K.op("pe", lambda e: e.matmul(out, lhsT, rhs, start=start, stop=stop), R, W)

        def tr(out, in_, idn, R=(), W=()):
            K.op("pe", lambda e: e.transpose(out, in_, idn), R, W)

        def act(out, in_, func, bias=None, scale=None, accum=None, R=(), W=()):
            kw = {}
            if bias is not None:
                kw["bias"] = bias
            if scale is not None:
                kw["scale"] = scale
            if accum is not None:
                kw["accum_out"] = accum
            K.op("act", lambda e: e.activation(out=out, in_=in_, func=func, **kw), R, W)

        def tt(out, a, b, op, R=(), W=(), en="dve"):
            K.op(en, lambda e: e.tensor_tensor(out=out, in0=a, in1=b, op=op), R, W)

        def ts(out, a, s1, s2, op0, op1=None, R=(), W=(), en="dve"):
            if op1 is None:
                K.op(en, lambda e: e.tensor_scalar(out=out, in0=a, scalar1=s1, scalar2=None, op0=op0), R, W)
            else:
                K.op(en, lambda e: e.tensor_scalar(out=out, in0=a, scalar1=s1, scalar2=s2, op0=op0, op1=op1), R, W)

        def stt(out, a, sc, b, op0, op1, R=(), W=(), en="dve"):
            K.op(en, lambda e: e.scalar_tensor_tensor(out=out, in0=a, scalar=sc, in1=b, op0=op0, op1=op1), R, W)

        def cp(out, in_, R=(), W=(), en="dve"):
            K.op(en, lambda e: e.tensor_copy(out=out, in_=in_), R, W)

        def recip(out, in_, R=(), W=()):
            K.op("dve", lambda e: e.reciprocal(out=out, in_=in_), R, W)

        def memset(ap, val, W=(), en="dve"):
            K.op(en, lambda e: e.memset(ap, val), (), W)

        def wload(W2d, c0, ncols=128):
            t, tok = wring.get()
            K.dma("pool", t[:, :, 0:ncols], W2d[:, c0:c0 + ncols].rearrange("(k p) n -> p k n", p=128), W=[tok], semtok=tok)
            return t, tok

        K.dma("sp", cst[:], cst_d[:, :], W=[cst_tok], semtok=cst_tok)
        K.dma("sp", prm[:].rearrange("p l n -> p (l n)"), prm_d[:, :], W=[prm_tok], semtok=prm_tok)
        K.dma("sp", nrow[:].rearrange("p l n -> p (l n)"), nrow_d[:, :], W=[nrow_tok], semtok=nrow_tok)
        for ph in range(2):
            K.dma("sp", xT[:, :, ph, :], xT_d.rearrange("p (k h t) -> p k h t", k=KC, h=2)[:, :, ph, :], W=[xtok[ph]], semtok=xtok[ph])
        ident = cst[:, C_IDENT:C_IDENT + 128]
        cp(ident_bf[:], ident, R=[cst_tok], W=[misc_tok])
        memset(ones_bf[:], 1.0, W=[misc_tok])
        memset(cb[:, 0:1], 1024.0 * EPS, W=[misc_tok])
        memset(cb[:, 1:2], EPS, W=[misc_tok])
        memset(cb[:, 2:3], 128.0 * EPS, W=[misc_tok])
        memset(cb[:, 3:4], 1.0, W=[misc_tok])
        memset(cb[:, 4:5], 0.0, W=[misc_tok])
        ts(nrow[:], nrow[:], float(np.sqrt(128.0)), None, ALU.mult, R=[nrow_tok], W=[nrow_tok])
        for l in range(4):
            ts(prm[:, l, P_NPRE_MIX:P_NPRE_MIX + 32], prm[:, l, P_NPRE_MIX:P_NPRE_MIX + 32], 32.0, None, ALU.mult, R=[prm_tok], W=[prm_tok])
        with contextlib.ExitStack() as s0:
            cTt = sb(s0, "cTt", [128, KC, 17], F32)
            ctok = Tok()
            K.dma("sp", cTt[:].rearrange("p k q -> p (k q)"), cT_d[:, :], W=[ctok], semtok=ctok)
            act(csT[:], cTt[:], AF.Silu, R=[ctok], W=[cs_tok])
            K.barrier()

        def adaln(l, S):
            mod = sb(S, "mod", [128, 48, 17], F32)
            mtok = Tok()
            slots = [pbig.get(), pbig.get()]
            for cc in range(48):
                wt, wtok = wload(ada_w[l], cc * 128)
                ps, ptok = slots[cc // 24]
                col = (cc % 24) * 17
                for kc in range(KC):
                    mm(ps[:, col:col + 17], wt[:, kc, :], csT[:, kc, :], start=(kc == 0), stop=(kc == KC - 1),
                       R=[wtok, cs_tok], W=[ptok])
            for bk in range(2):
                ps, ptok = slots[bk]
                tt(mod[:, 24 * bk:24 * bk + 24, :], ps[:, 0:408].rearrange("p (c q) -> p c q", q=17),
                   prm[:, l, P_ADAB + 24 * bk:P_ADAB + 24 * bk + 24].unsqueeze(2).broadcast_to([128, 24, 17]),
                   ALU.add, R=[ptok, prm_tok], W=[mtok])
            def nb(off):
                return prm[:, l, off:off + 8].unsqueeze(2).broadcast_to([128, 8, 17])
            stt(modA[0][:], mod[:, 8:16, :], 1.0, nb(P_NPRE_MIX), ALU.add, ALU.mult, R=[mtok, prm_tok], W=[mod_tok])
            cp(modA[1][:], mod[:, 0:8, :], R=[mtok], W=[mod_tok])
            stt(modA[2][:], mod[:, 16:24, :], 1.0, nb(P_NPOST_MIX), ALU.add, ALU.mult, R=[mtok, prm_tok], W=[mod_tok])
            stt(modA[3][:], mod[:, 32:40, :], 1.0, nb(P_NPRE_FFN), ALU.add, ALU.mult, R=[mtok, prm_tok], W=[mod_tok])
            cp(modA[4][:], mod[:, 24:32, :], R=[mtok], W=[mod_tok])
            stt(modA[5][:], mod[:, 40:48, :], 1.0, nb(P_NPOST_FFN), ALU.add, ALU.mult, R=[mtok, prm_tok], W=[mod_tok])

        def seqcols(ph):
            return slice(1 + SPP * ph, 1 + SPP * ph + SPP)

        def rstd_fm(src3, n, R, sqt, rs):
            sq, sqtok = sqt
            rst, rstok = rs
            act(sq[:, :, 0:n], src3, AF.Square, R=R, W=[sqtok])
            ps, ptok = pbig.get()
            for kc in range(KC):
                mm(ps[:, 0:n], ones_bf[:], sq[:, kc, 0:n], start=(kc == 0), stop=(kc == KC - 1), R=[sqtok, misc_tok], W=[ptok])
            act(rst[:, 0:n], ps[:, 0:n], AF.Sqrt, bias=cb[:, 0:1], scale=1.0, R=[ptok, misc_tok], W=[rstok])
            recip(rst[:, 0:n], rst[:, 0:n], R=[rstok], W=[rstok])

        def prenorm(ph, A, B, hT, htoks, S):
            sqt = (sb(S, "pn_sq", [128, KC, 512], BF16), Tok())
            rs = (sb(S, "pn_rs", [128, 512], F32), Tok())
            tmps = Ring([(sb(S, "pn_t%d" % i, [128, 512], F32), Tok()) for i in range(3)])
            for ti, (t0, n) in enumerate(TILES):
                rstd_fm(xT[:, :, ph, t0:t0 + n], n, [xtok[ph]], sqt, rs)
                for kc in range(KC):
                    tmp, ttok = tmps.get()
                    tt(tmp[:, 0:n], xT[:, kc, ph, t0:t0 + n], rs[0][:, 0:n], ALU.mult, R=[xtok[ph], rs[1]], W=[ttok])
                    if ti < 2:
                        act(hT[:, kc, t0:t0 + n], tmp[:, 0:n], AF.Identity, bias=B[:, kc, 0:1], scale=A[:, kc, 0:1],
                            R=[ttok, mod_tok], W=[htoks[ti]])
                    else:
                        sc = seqcols(ph)
                        tt(tmp[:, 0:n].rearrange("p (s j) -> p s j", j=LS), tmp[:, 0:n].rearrange("p (s j) -> p s j", j=LS),
                           A[:, kc, sc].unsqueeze(2).broadcast_to([128, SPP, LS]), ALU.mult, R=[ttok, mod_tok], W=[ttok])
                        tt(hT[:, kc, t0:t0 + n].rearrange("p (s j) -> p s j", j=LS), tmp[:, 0:n].rearrange("p (s j) -> p s j", j=LS),
                           B[:, kc, sc].unsqueeze(2).broadcast_to([128, SPP, LS]), ALU.add, R=[ttok, mod_tok], W=[htoks[ti]])

        def proj_tile(wt, wtok, hT, htoks, ti, M=128, ring=None):
            t0, n = TILES[ti]
            ps, ptok = (ring or pbig).get()
            for kc in range(KC):
                mm(ps[0:M, 0:n], wt[:, kc, 0:M], hT[:, kc, t0:t0 + n], start=(kc == 0), stop=(kc == KC - 1),
                   R=[wtok, htoks[ti]], W=[ptok])
            return ps, ptok, n

        def outproj_postnorm(l, ph, Wout2d, oT, otoks, G, S):
            y = sb(S, "op_y", [128, KC, 512], F32)
            ytok = Tok()
            sqt = (sb(S, "op_sq", [128, KC, 512], BF16), Tok())
            rs = (sb(S, "op_rs", [128, 512], F32), Tok())
            tmps = Ring([(sb(S, "op_t%d" % i, [128, 512], F32), Tok()) for i in range(2)])
            for ti, (t0, n) in enumerate(TILES):
                for oc in range(KC):
                    wt, wtok = wload(Wout2d, oc * 128)
                    ps, ptok = pbig.get()
                    for kc in range(KC):
                        mm(ps[:, 0:n], wt[:, kc, :], oT[:, kc, t0:t0 + n], start=(kc == 0), stop=(kc == KC - 1),
                           R=[wtok, otoks[ti]], W=[ptok])
                    cp(y[:, oc, 0:n], ps[:, 0:n], R=[ptok], W=[ytok])
                resid_update(ph, t0, n, y, ytok, G, sqt, rs, tmps)

        def resid_update(ph, t0, n, y, ytok, G, sqt, rs, tmps):
            rstd_fm(y[:, :, 0:n], n, [ytok], sqt, rs)
            for kc in range(KC):
                tmp, ttok = tmps.get()
                tt(tmp[:, 0:n], y[:, kc, 0:n], rs[0][:, 0:n], ALU.mult, R=[ytok, rs[1]], W=[ttok])
                if t0 < TP:
                    stt(xT[:, kc, ph, t0:t0 + n], tmp[:, 0:n], G[:, kc, 0:1], xT[:, kc, ph, t0:t0 + n], ALU.mult, ALU.add,
                        R=[ttok, mod_tok], W=[xtok[ph]])
                else:
                    sc = seqcols(ph)
                    v3 = tmp[:, 0:n].rearrange("p (s j) -> p s j", j=LS)
                    tt(v3, v3, G[:, kc, sc].unsqueeze(2).broadcast_to([128, SPP, LS]), ALU.mult, R=[ttok, mod_tok], W=[ttok])
                    x3 = xT[:, kc, ph, t0:t0 + n].rearrange("p (s j) -> p s j", j=LS)
                    tt(x3, x3, v3, ALU.add, R=[ttok], W=[xtok[ph]])

        def blocks():
            return [(b * 128, 128, "P") for b in range(8)] + [(TP, 64, "S")]

        def o_post(l, o_rows, BS, h, c0, oT, otok_t, sgT, sgtok, tmpf, tmpb):
            junk, jtok = tmpf.get()
            ss, sstok = tmpf.get()
            act(junk[0:BS, :], o_ps[0:BS, 0:128], AF.Square, accum=ss[0:BS, 0:1], R=[o_ptok], W=[jtok, sstok])
            act(ss[0:BS, 0:1], ss[0:BS, 0:1], AF.Sqrt, bias=cb[0:BS, 2:3], scale=1.0, R=[sstok, misc_tok], W=[sstok])
            recip(ss[0:BS, 0:1], ss[0:BS, 0:1], R=[sstok], W=[sstok])
            on, ontok = tmpb.get()
            stt(on[0:BS, :], o_ps[0:BS, 0:128], ss[0:BS, 0:1], nrow[0:BS, l, :], ALU.mult, ALU.mult,
                R=[o_ptok, sstok, nrow_tok], W=[ontok])
            pb, pbtok = pbf.get()
            tr(pb[:, 0:BS], on[0:BS, :], ident_bf[0:BS, 0:BS], R=[ontok, misc_tok], W=[pbtok])
            tt(oT[:, h, c0:c0 + BS], pb[:, 0:BS], sgT[:, c0:c0 + BS], ALU.mult, R=[pbtok, sgtok], W=[otok_t])

        def gdn_phase(l, ph, hT, htoks, oT, otoks, S):
            j = l // 2
            Win = gdn_w_in[j]
            stage(3)
            G = 4
            dG = sb(S, "g_dG", [8, TPH], F32)
            dtok_ = Tok()
            dtk = sb(S, "g_dtk", [128, 9, 3, 8], F32)
            ex = sb(S, "g_ex", [128, 9, 4, 8], F32)
            glb = sb(S, "g_glb", [128, H, 24], F32)
            dk_tok = Tok()
            gci = sb(S, "g_gci", [128, 24, 8, 3], F32)
            gci_tok = Tok()
            gco = sb(S, "g_gco", [128, 24, 9, 3], F32)
            gco_tok = Tok()
            K.dma("sp", gci[:].rearrange("p a s r -> p (a s r)"), gci_d[j, ph], W=[gci_tok], semtok=gci_tok)
            memset(gco[:], 0.0, W=[gco_tok])
            with contextlib.ExitStack() as SD:
                dB = sb(SD, "g_dB", [8, TPH], F32)
                dL = sb(SD, "g_dL", [8, TPH], F32)
                dg = sb(SD, "g_dg", [8, TPH], F32)
                rmask = dL
                nega = sb(SD, "g_nega", [8, 1], F32)
                memset(rmask[:], 1.0, W=[dtok_])
                memset(rmask[:, 0:TP].rearrange("p (c t) -> p c t", t=64)[:, :, 0:1], 0.0, W=[dtok_])
                memset(rmask[:, TP:TPH].rearrange("p (c t) -> p c t", t=8)[:, :, 0:1], 0.0, W=[dtok_])
                act(nega[:], prm[0:8, l, P_ALOG:P_ALOG + 1], AF.Exp, R=[prm_tok], W=[dtok_])
                ts(nega[:], nega[:], -1.0, None, ALU.mult, R=[dtok_], W=[dtok_])
                wb, wbtok = wload(Win, 4096, 8)
                for ti in range(3):
                    ps, ptok, n = proj_tile(wb, wbtok, hT, htoks, ti, M=8)
                    t0 = TILES[ti][0]
                    act(dB[:, t0:t0 + n], ps[0:8, 0:n], AF.Sigmoid, R=[ptok], W=[dtok_])
                wa, watok = wload(Win, 4104, 8)
                for ti in range(3):
                    ps, ptok, n = proj_tile(wa, watok, hT, htoks, ti, M=8)
                    t0 = TILES[ti][0]
                    act(dg[:, t0:t0 + n], ps[0:8, 0:n], AF.Exp, bias=prm[0:8, l, P_DTB:P_DTB + 1], scale=1.0, R=[ptok, prm_tok], W=[dtok_])
                act(dg[:], dg[:], AF.Ln, bias=cb[0:8, 3:4], scale=1.0, R=[dtok_, misc_tok], W=[dtok_])
                ts(dg[:], dg[:], nega[:, 0:1], None, ALU.mult, R=[dtok_], W=[dtok_])
                K.op("dve", lambda e: e.tensor_tensor_scan(out=dG[:], data0=rmask[:], data1=dg[:], initial=0.0, op0=ALU.mult, op1=ALU.add),
                     [dtok_], [dtok_])
                gp = dG[:, 0:TP].rearrange("p (c t) -> p c t", t=64)
                tt(dL[:, 0:TP].rearrange("p (c t) -> p c t", t=64), gp[:, :, 63:64].broadcast_to([8, 16, 64]), gp, ALU.subtract, R=[dtok_], W=[dtok_])
                gs = dG[:, TP:TPH].rearrange("p (c t) -> p c t", t=8)
                tt(dL[:, TP:TPH].rearrange("p (c t) -> p c t", t=8), gs[:, :, 7:8].broadcast_to([8, 8, 8]), gs, ALU.subtract, R=[dtok_], W=[dtok_])
                ps, ptok = pbig.get()
                for bi, (c0, BS, kind) in enumerate(blocks()):
                    for qi, src in enumerate((dB, dG, dL)):
                        col = (bi * 3 + qi) * 8
                        tr(ps[0:BS, col:col + 8], src[:, c0:c0 + BS], cst[0:8, C_IDENT:C_IDENT + 8], R=[dtok_, cst_tok], W=[ptok])
                memset(dtk[:], 0.0, W=[dk_tok])
                cp(dtk[:, 0:8].rearrange("p b q h -> p (b q h)"), ps[:, 0:192], R=[ptok], W=[dk_tok])
                cp(dtk[0:64, 8].rearrange("p q h -> p (q h)"), ps[0:64, 192:216], R=[ptok], W=[dk_tok])
                act(ex[:, :, 0:2, :], dtk[:, :, 1:3, :], AF.Exp, R=[dk_tok], W=[dk_tok])
                tt(ex[:, :, 2, :], dtk[:, :, 0, :], ex[:, :, 0, :], ALU.mult, R=[dk_tok], W=[dk_tok])
                ts(ex[:, :, 3, :], dtk[:, :, 0, :], -1.0, None, ALU.mult, R=[dk_tok], W=[dk_tok])
                ps, ptok = pbig.get()
                for h in range(H):
                    esel_h = cst[0:8, C_ESEL + h * 128:C_ESEL + (h + 1) * 128]
                    mm(ps[:, h * 24:h * 24 + 16], esel_h, dG[:, 0:TP].rearrange("p (c t) -> p c t", t=64)[:, :, 63], R=[dtok_, cst_tok], W=[ptok])
                    mm(ps[:, h * 24 + 16:h * 24 + 24], esel_h, dG[:, TP:TPH].rearrange("p (c t) -> p c t", t=8)[:, :, 7], R=[dtok_, cst_tok], W=[ptok])
                act(glb[:].rearrange("p h c -> p (h c)"), ps[:, 0:192], AF.Exp, R=[ptok], W=[dk_tok])
                K.barrier()

            with contextlib.ExitStack() as S1:
                qT = sb(S1, "g_qT", [128, TPH], BF16)
                kT = sb(S1, "g_kT", [128, TPH], BF16)
                vT = sb(S1, "g_vT", [128, TPH], BF16)
                sgT = sb(S1, "g_sgT", [128, TPH], BF16)
                q_tok, k_tok, v_tok, sg_tok = Tok(), Tok(), Tok(), Tok()
                Sbf = sb(S1, "g_Sbf", [128, 128], BF16)
                Sbf_tok = Tok()
                Ss = sb(S1, "g_Ss", [128, SPP, 128], F32)
                Ss_tok = Tok()
                Ssb = sb(S1, "g_Ssb", [128, SPP, 128], BF16)
                Ssb_tok = Tok()
                Sn, Sn_tok = Ss, Ss_tok
                u_sb = sb(S1, "g_usb", [128, 128], BF16)
                usb_tok = Tok()
                memset(u_sb[:], 0.0, W=[usb_tok])
                psblk = Ring([(psF[i % 5][:, (i // 5) * 128:(i // 5) * 128 + 128], bkt[i % 5]) for i in range(20)])

                for h in range(H):
                    stage(4 + 0.1 * h)
                    with contextlib.ExitStack() as SPJ:
                        pre = sb(SPJ, "g_pre", [128, 3 + TP + SPP * 11], F32)
                        pre_tok = Tok()
                        cv = sb(SPJ, "g_cv", [128, TPH], F32)
                        cv_tok = Tok()
                        sqb = sb(SPJ, "g_sqb", [128, 512], BF16)
                        sqb_tok = Tok()
                        rinv = sb(SPJ, "g_rinv", [128, 512], F32)
                        rinv_tok = Tok()
                        pv = pre[:, 3 + TP:3 + TP + SPP * 11].rearrange("p (s r) -> p s r", r=11)
                        for role in range(4):
                            wt, wtok = wload(Win, role * 1024 + h * 128)
                            if role < 3:
                                ch = role * 8 + h
                                if ph == 0:
                                    memset(pre[:, 0:3], 0.0, W=[pre_tok])
                                else:
                                    cp(pre[:, 0:3], gcar[:, ch, :], R=[gcar_tok], W=[pre_tok])
                                cp(pv[:, :, 0:3], gci[:, ch, :, :], R=[gci_tok], W=[pre_tok])
                            for ti in range(3):
                                ps, ptok, n = proj_tile(wt, wtok, hT, htoks, ti, ring=pwide)
                                t0 = TILES[ti][0]
                                if role == 3:
                                    act(sgT[:, t0:t0 + n], ps[:, 0:n], AF.Silu, R=[ptok], W=[sg_tok])
                                elif ti < 2:
                                    act(pre[:, 3 + t0:3 + t0 + n], ps[:, 0:n], AF.Copy, R=[ptok], W=[pre_tok])
                                else:
                                    act(pv[:, :, 3:11], ps[:, 0:n].rearrange("p (s r) -> p s r", r=LS), AF.Copy, R=[ptok], W=[pre_tok])
                            if role == 3:
                                continue
                            if ph == 0:
                                cp(gcar[:, ch, :], pre[:, TP:TP + 3], R=[pre_tok], W=[gcar_tok])
                            else:
                                cp(gco[:, ch, 8, :], pre[:, TP:TP + 3], R=[pre_tok], W=[gco_tok])
                            cp(gco[:, ch, 0:8, :], pv[:, :, 8:11], R=[pre_tok], W=[gco_tok])
                            wc = prm[:, l, P_GCW + ch * 4:P_GCW + ch * 4 + 4]
                            bc = prm[:, l, P_GCB + ch:P_GCB + ch + 1]
                            cvs = cv[:, TP:TPH].rearrange("p (s r) -> p s r", r=LS)
                            ts(cv[:, 0:TP], pre[:, 0:TP], wc[:, 0:1], bc, ALU.mult, ALU.add, R=[pre_tok, prm_tok], W=[cv_tok])
                            ts(cvs, pv[:, :, 0:8], wc[:, 0:1], bc, ALU.mult, ALU.add, R=[pre_tok, prm_tok], W=[cv_tok])
                            for tap in range(1, 4):
                                stt(cv[:, 0:TP], pre[:, tap:tap + TP], wc[:, tap:tap + 1], cv[:, 0:TP], ALU.mult, ALU.add,
                                    R=[pre_tok, prm_tok], W=[cv_tok])
                                stt(cvs, pv[:, :, tap:tap + 8], wc[:, tap:tap + 1], cvs, ALU.mult, ALU.add, R=[pre_tok, prm_tok], W=[cv_tok])
                            if role == 2:
                                act(vT[:], cv[:], AF.Silu, R=[cv_tok], W=[v_tok])
                                continue
                            act(cv[:], cv[:], AF.Silu, R=[cv_tok], W=[cv_tok])
                            for ti, (t0, n) in enumerate(TILES):
                                act(sqb[:, 0:n], cv[:, t0:t0 + n], AF.Square, R=[cv_tok], W=[sqb_tok])
                                ps, ptok = pwide.get()
                                mm(ps[:, 0:n], ones_bf[:], sqb[:, 0:n], R=[sqb_tok, misc_tok], W=[ptok])
                                act(rinv[:, 0:n], ps[:, 0:n], AF.Sqrt, bias=cb[:, 1:2], scale=1.0, R=[ptok, misc_tok], W=[rinv_tok])
                                recip(rinv[:, 0:n], rinv[:, 0:n], R=[rinv_tok], W=[rinv_tok])
                                if role == 0:
                                    stt(qT[:, t0:t0 + n], cv[:, t0:t0 + n], float(128.0 ** -0.5), rinv[:, 0:n], ALU.mult, ALU.mult,
                                        R=[cv_tok, rinv_tok], W=[q_tok])
                                else:
                                    tt(kT[:, t0:t0 + n], cv[:, t0:t0 + n], rinv[:, 0:n], ALU.mult, R=[cv_tok, rinv_tok], W=[k_tok])
                        K.barrier()

                    stage(4.05 + 0.1 * h)
                    K.dma("pool", Ss[:], sgdn_d[j, SPP * ph:SPP * ph + SPP, h].rearrange("s k v -> k s v"), W=[Ss_tok], semtok=Ss_tok)
                    act(Ssb[:], Ss[:], AF.Copy, R=[Ss_tok], W=[Ssb_tok])
                    if ph == 0:
                        memset(Sf_all[:, h, :], 0.0, W=[Sf_tok[h]])
                    cp(Sbf[:], Sf_all[:, h, :], R=[Sf_tok[h]], W=[Sbf_tok])

                    with contextlib.ExitStack() as SBK:
                        slots = []
                        for k in range(G):
                            sl = {}
                            sl["X"] = [(sb(SBK, "g_x%d_%d" % (k, i), [128, 128], F32), Tok()) for i in range(6)]
                            sl["Y"] = [(sb(SBK, "g_y%d_%d" % (k, i), [128, 128], BF16), Tok()) for i in range(4)]
                            sl["kdm"] = (sb(SBK, "g_kdm%d" % k, [128, 8], F32), Tok())
                            npad = 8 if k == 0 else 2
                            for nm in ("Kpad", "Wpad", "Qpad", "tw"):
                                sl[nm] = (sb(SBK, "g_%s%d" % (nm, k), [128, npad, 128], BF16), Tok())
                            slots.append(sl)
                        opf = Ring([(sb(SBK, "g_of%d" % i, [128, 128], F32), Tok()) for i in range(3)])
                        opb = Ring([(sb(SBK, "g_ob%d" % i, [128, 128], BF16), Tok()) for i in range(2)])

                        def pre_state(bi, c0, BS, kind, sl):
                            if kind == "P":
                                ncb, C, nlev = 2, 64, 5
                                pen = cst[:, C_PENP:C_PENP + 128]
                                cmask = cst[:, C_CM2:C_CM2 + 256].rearrange("p (c t) -> p c t", c=2)
                                rmk = cst[:, C_RM2:C_RM2 + 2]
                            else:
                                ncb, C, nlev = 8, 8, 2
                                pen = cst[0:64, C_PENS:C_PENS + 64]
                                cmask = cst[:, C_CM8:C_CM8 + 512].rearrange("p (c t) -> p c t", c=8)
                                rmk = cst[0:64, C_RM8:C_RM8 + 8]
                            cols = slice(c0, c0 + BS)
                            X, Y = sl["X"], sl["Y"]
                            Kpad, kp_tok = sl["Kpad"]
                            Wpad, wp_tok = sl["Wpad"]
                            Qpad, qp_tok = sl["Qpad"]
                            tw, twtok = sl["tw"]
                            Gcol = dtk[0:BS, bi, 1, h:h + 1]
                            beta_c = dtk[0:BS, bi, 0, h:h + 1]
                            eGL_c = ex[0:BS, bi, 1, h:h + 1]
                            bG_c = ex[0:BS, bi, 2, h:h + 1]
                            nbeta_c = ex[0:BS, bi, 3, h:h + 1]
                            esel_h = cst[0:8, C_ESEL + h * 128:C_ESEL + (h + 1) * 128]
                            gps, gptok = psblk.get()
                            mm(gps[:, 0:BS], esel_h, dG[:, cols], R=[dtok_, cst_tok], W=[gptok])
                            yield
                            r_, rtok = X[0]
                            stt(r_[0:BS, 0:BS], gps[0:BS, 0:BS], Gcol, pen, ALU.subtract, ALU.max, R=[gptok, dk_tok, cst_tok], W=[rtok])
                            eGr, egtok = X[2]
                            act(eGr[:, 0:BS], gps[:, 0:BS], AF.Exp, R=[gptok], W=[egtok])
                            yield
                            Ds, dstok = X[1]
                            act(Ds[0:BS, 0:BS], r_[0:BS, 0:BS], AF.Exp, scale=-1.0, R=[rtok], W=[dstok])
                            tt(tw[:, 0:ncb, 0:BS], cmask[:, :, 0:BS], eGr[:, 0:BS].unsqueeze(1).broadcast_to([128, ncb, BS]), ALU.mult,
                               R=[cst_tok, egtok], W=[twtok])
                            yield
                            tt(Qpad[:, 0:ncb, 0:BS], tw[:, 0:ncb, 0:BS], qT[:, cols].unsqueeze(1).broadcast_to([128, ncb, BS]), ALU.mult,
                               R=[twtok, q_tok], W=[qp_tok])
                            dps, dptok = psblk.get()
                            tr(dps[0:BS, 0:BS], Ds[0:BS, 0:BS], ident[0:BS, 0:BS], R=[dstok, cst_tok], W=[dptok])
                            kkps, kktok = psblk.get()
                            mm(kkps[0:BS, 0:BS], kT[:, cols], kT[:, cols], R=[k_tok], W=[kktok])
                            qkps, qktok = psblk.get()
                            mm(qkps[0:BS, 0:BS], kT[:, cols], qT[:, cols], R=[k_tok, q_tok], W=[qktok])
                            yield
                            DmT, dmtok = X[0]
                            tt(DmT[0:BS, 0:BS], dps[0:BS, 0:BS], ident[0:BS, 0:BS], ALU.add, R=[dptok, cst_tok], W=[dmtok])
                            B, btok = X[2]
                            stt(B[0:BS, 0:BS], kkps[0:BS, 0:BS], nbeta_c, Ds[0:BS, 0:BS], ALU.mult, ALU.mult, R=[kktok, dk_tok, dstok], W=[btok])
                            yield
                            attnT, attok = Y[0]
                            tt(attnT[0:BS, 0:BS], qkps[0:BS, 0:BS], DmT[0:BS, 0:BS], ALU.mult, R=[qktok, dmtok], W=[attok])
                            btps, btptok = psblk.get()
                            tr(btps[0:BS, 0:BS], B[0:BS, 0:BS], ident[0:BS, 0:BS], R=[btok, cst_tok], W=[btptok])
                            yield
                            Bt, bttok = X[3]
                            act(Bt[0:BS, 0:BS], btps[0:BS, 0:BS], AF.Copy, R=[btptok], W=[bttok])
                            TT, tttok = X[4]
                            tt(TT[0:BS, 0:BS], btps[0:BS, 0:BS], ident[0:BS, 0:BS], ALU.add, R=[btptok, cst_tok], W=[tttok])
                            yield
                            cur = (X[2], X[3])
                            nxt = (X[5], X[1])
                            for lev in range(nlev):
                                last = (lev == nlev - 1)
                                (B, btok), (Bt, bttok) = cur
                                (Bn, bntok), (Btn, btntok) = nxt
                                p2, p2tok = psblk.get()
                                mm(p2[0:BS, 0:BS], Bt[0:BS, 0:BS], B[0:BS, 0:BS], R=[bttok, btok], W=[p2tok])
                                if not last:
                                    p1, p1tok = psblk.get()
                                    mm(p1[0:BS, 0:BS], B[0:BS, 0:BS], Bt[0:BS, 0:BS], R=[bttok, btok], W=[p1tok])
                                yield
                                cp(Bn[0:BS, 0:BS], p2[0:BS, 0:BS], R=[p2tok], W=[bntok])
                                if not last:
                                    act(Btn[0:BS, 0:BS], p1[0:BS, 0:BS], AF.Copy, R=[p1tok], W=[btntok])
                                yield
                                p3, p3tok = psblk.get()
                                mm(p3[0:BS, 0:BS], Bn[0:BS, 0:BS], TT[0:BS, 0:BS], R=[bntok, tttok], W=[p3tok])
                                yield
                                tt(TT[0:BS, 0:BS], p3[0:BS, 0:BS], TT[0:BS, 0:BS], ALU.add, R=[p3tok, tttok], W=[tttok])
                                yield
                                cur, nxt = nxt, cur
                            TTb, ttbtok = Y[1]
                            act(TTb[0:BS, 0:BS], TT[0:BS, 0:BS], AF.Copy, R=[tttok], W=[ttbtok])
                            pk, pktok = pbf.get()
                            tr(pk[0:BS, :], kT[:, cols], ident_bf[:], R=[k_tok, misc_tok], W=[pktok])
                            pvv, pvtok = pbf.get()
                            tr(pvv[0:BS, :], vT[:, cols], ident_bf[:], R=[v_tok, misc_tok], W=[pvtok])
                            yield
                            vb, vbtok = Y[2]
                            act(vb[0:BS, :], pvv[0:BS, :], AF.Copy, scale=beta_c, R=[pvtok, dk_tok], W=[vbtok])
                            kbg, kbgtok = Y[3]
                            act(kbg[0:BS, :], pk[0:BS, :], AF.Copy, scale=bG_c, R=[pktok, dk_tok], W=[kbgtok])
                            kdm, kdmtok = sl["kdm"]
                            ts(kdm[0:BS, 0:ncb], rmk, eGL_c, None, ALU.mult, R=[cst_tok, dk_tok], W=[kdmtok])
                            tt(Kpad[0:BS, 0:ncb, :], pk[0:BS, :].unsqueeze(1).broadcast_to([BS, ncb, 128]),
                               kdm[0:BS, 0:ncb].unsqueeze(2).broadcast_to([BS, ncb, 128]), ALU.mult, R=[pktok, kdmtok], W=[kp_tok])
                            yield
                            wps, wptok = psblk.get()
                            mm(wps[:, 0:BS], kbg[0:BS, :], TTb[0:BS, 0:BS], R=[kbgtok, ttbtok], W=[wptok])
                            yield
                            stt(Wpad[:, 0:ncb, 0:BS], wps[:, 0:BS].unsqueeze(1).broadcast_to([128, ncb, BS]), -1.0, cmask[:, :, 0:BS],
                                ALU.mult, ALU.mult, R=[wptok, cst_tok], W=[wp_tok])

                        def state_part(bi, c0, BS, kind, sl):
                            if kind == "P":
                                ncb, C = 2, 64
                            else:
                                ncb, C = 8, 8
                            Y = sl["Y"]
                            attnT, attok = Y[0]
                            TTb, ttbtok = Y[1]
                            vb, vbtok = Y[2]
                            Kpad, kp_tok = sl["Kpad"]
                            Wpad, wp_tok = sl["Wpad"]
                            Qpad, qp_tok = sl["Qpad"]
                            if kind == "P":
                                for c in range(ncb):
                                    rows = slice(c * C, (c + 1) * C)
                                    mm(u_ps[rows, 0:128], TTb[rows, rows], vb[rows, :], start=True, stop=False, R=[ttbtok, vbtok], W=[u_ptok])
                                    mm(u_ps[rows, 0:128], Wpad[:, c, rows], Sbf[:], start=False, stop=True, R=[wp_tok, Sbf_tok], W=[u_ptok])
                                    act(u_sb[rows, :], u_ps[rows, 0:128], AF.Copy, R=[u_ptok], W=[usb_tok])
                                    mm(o_ps[rows, 0:128], Qpad[:, c, rows], Sbf[:], start=True, stop=False, R=[qp_tok, Sbf_tok], W=[o_ptok])
                                    mm(o_ps[rows, 0:128], attnT[rows, rows], u_sb[rows, :], start=False, stop=True, R=[attok, usb_tok], W=[o_ptok])
                                    sps, sptok = psblk.get()
                                    mm(sps[:, :], Kpad[0:BS, c, :], u_sb[0:BS, :], R=[kp_tok, usb_tok], W=[sptok])
                                    stt(Sf_all[:, h, :], Sf_all[:, h, :], glb[:, h, (bi * 2 + c):(bi * 2 + c) + 1], sps[:, :], ALU.mult, ALU.add,
                                        R=[sptok, dk_tok], W=[Sf_tok[h]])
                                    act(Sbf[:], Sf_all[:, h, :], AF.Copy, R=[Sf_tok[h]], W=[Sbf_tok])
                            else:
                                mm(u_ps[0:BS, 0:128], TTb[0:BS, 0:BS], vb[0:BS, :], start=True, stop=False, R=[ttbtok, vbtok], W=[u_ptok])
                                for c in range(ncb):
                                    mm(u_ps[0:BS, 0:128], Wpad[:, c, 0:BS], Ssb[:, c, :], start=False, stop=(c == ncb - 1), R=[wp_tok, Ssb_tok], W=[u_ptok])
                                act(u_sb[0:BS, :], u_ps[0:BS, 0:128], AF.Copy, R=[u_ptok], W=[usb_tok])
                                for c in range(ncb):
                                    mm(o_ps[0:BS, 0:128], Qpad[:, c, 0:BS], Ssb[:, c, :], start=(c == 0), stop=False, R=[qp_tok, Ssb_tok], W=[o_ptok])
                                mm(o_ps[0:BS, 0:128], attnT[0:BS, 0:BS], u_sb[0:BS, :], start=False, stop=True, R=[attok, usb_tok], W=[o_ptok])
                                sl2 = [pbig.get(), pbig.get()]
                                for c in range(ncb):
                                    ps, ptok = sl2[c // 4]
                                    mm(ps[:, (c % 4) * 128:(c % 4) * 128 + 128], Kpad[0:BS, c, :], u_sb[0:BS, :], R=[kp_tok, usb_tok], W=[ptok])
                                tt(Sn[:], Ss[:], glb[:, h, 16:24].unsqueeze(2).broadcast_to([128, 8, 128]), ALU.mult, R=[Ss_tok, dk_tok], W=[Sn_tok])
                                for hb in range(2):
                                    ps, ptok = sl2[hb]
                                    tt(Sn[:, hb * 4:hb * 4 + 4, :], Sn[:, hb * 4:hb * 4 + 4, :], ps[:, :].rearrange("p (s v) -> p s v", v=128), ALU.add,
                                       R=[ptok], W=[Sn_tok])
                                K.dma("sp", sgdn_o[j, SPP * ph:SPP * ph + SPP, h].rearrange("s k v -> k s v"), Sn[:], R=[Sn_tok], semtok=Sn_tok)
                            o_post(l, None, BS, h, c0, oT, otoks[min(c0 // 512, 2)], sgT, sg_tok, opf, opb)

                        blks = blocks()
                        groups = [[0, 1, 2, 3], [4, 5, 6, 7], [8]]
                        for grp in groups:
                            stage(4.06 + 0.1 * h + 0.001 * grp[0])
                            gens = [pre_state(bi, blks[bi][0], blks[bi][1], blks[bi][2], slots[k]) for k, bi in enumerate(grp)]
                            alive = gens
                            while alive:
                                nx = []
                                for g in alive:
                                    try:
                                        next(g)
                                        nx.append(g)
                                    except StopIteration:
                                        pass
                                alive = nx
                            for k, bi in enumerate(grp):
                                state_part(bi, blks[bi][0], blks[bi][1], blks[bi][2], slots[k])
                        if ph == 1:
                            K.dma("sp", pgdn_d[j, h], Sf_all[:, h, :], R=[Sf_tok[h]], semtok=Sf_tok[h])
                        K.barrier()
                K.dma("sp", gco_d[j, ph], gco[:].rearrange("p a s r -> p (a s r)"), R=[gco_tok], semtok=gco_tok)
                K.barrier()

        def hgrn_phase(l, ph, hT, htoks, oT, otoks, S):
            j = l // 2
            Win = hgrn_w_in[j]
            with contextlib.ExitStack() as S1:
                rmask = sb(S1, "h_rm", [128, TPH], F32)
                rm_tok = Tok()
                fa = sb(S1, "h_fa", [128, TPH], F32)
                fa_tok = Tok()
                fk = sb(S1, "h_fk", [128, TPH], F32)
                fk_tok = Tok()
                fG = sb(S1, "h_fG", [128, TPH], F32)
                fG_tok = Tok()
                fe = sb(S1, "h_fe", [128, TPH], F32)
                fe_tok = Tok()
                eG = sb(S1, "h_eG", [128, TPH], F32)
                eG_tok = Tok()
                sq_ = sb(S1, "h_sq", [128, TPH], F32)
                sq_tok = Tok()
                qgT = sb(S1, "h_qgT", [128, TPH], BF16)
                kiT = sb(S1, "h_kiT", [128, TPH], BF16)
                kdT = sb(S1, "h_kdT", [128, TPH], BF16)
                vT = sb(S1, "h_vT", [128, TPH], BF16)
                sgT = sb(S1, "h_sgT", [128, TPH], BF16)
                qg_tok, ki_tok, kd_tok, v_tok, sg_tok = Tok(), Tok(), Tok(), Tok(), Tok()
                Sbf = sb(S1, "h_Sbf", [128, 128], BF16)
                Sbf_tok = Tok()
                Ss = sb(S1, "h_Ss", [128, SPP, 128], F32)
                Ss_tok = Tok()
                Ssb = sb(S1, "h_Ssb", [128, SPP, 128], BF16)
                Ssb_tok = Tok()
                Sn, Sn_tok = Ss, Ss_tok
                Kpad = sb(S1, "h_Kpad", [128, 8, 128], BF16)
                Qpad = sb(S1, "h_Qpad", [128, 8, 128], BF16)
                kp_tok, qp_tok = Tok(), Tok()
                tmpf = Ring([(sb(S1, "h_tf%d" % i, [128, 128], F32), Tok()) for i in range(4)])
                tmpb = Ring([(sb(S1, "h_tb%d" % i, [128, 128], BF16), Tok()) for i in range(6)])
                memset(rmask[:], 1.0, W=[rm_tok])
                memset(rmask[:, 0:TP].rearrange("p (c t) -> p c t", t=32)[:, :, 0:1], 0.0, W=[rm_tok])
                memset(rmask[:, TP:TPH].rearrange("p (c t) -> p c t", t=8)[:, :, 0:1], 0.0, W=[rm_tok])
                for h in range(H):
                    lb = lbT[:, h, 0:1]
                    oml = lbT[:, h, 1:2]
                    for role in range(4):
                        wt, wtok = wload(Win, role * 1024 + h * 128)
                        for ti in range(3):
                            ps, ptok, n = proj_tile(wt, wtok, hT, htoks, ti)
                            t0 = TILES[ti][0]
                            if role == 0:
                                act(sq_[:, t0:t0 + n], ps[:, 0:n], AF.Silu, R=[ptok], W=[sq_tok])
                            elif role == 1:
                                act(fa[:, t0:t0 + n], ps[:, 0:n], AF.Sigmoid, R=[ptok], W=[fa_tok])
                            elif role == 2:
                                act(vT[:, t0:t0 + n], ps[:, 0:n], AF.Copy, R=[ptok], W=[v_tok])
                            else:
                                act(sgT[:, t0:t0 + n], ps[:, 0:n], AF.Silu, R=[ptok], W=[sg_tok])
                    ts(fa[:], fa[:], oml, lb, ALU.mult, ALU.add, R=[fa_tok, lb_tok], W=[fa_tok])
                    ts(fk[:], fa[:], -1.0, 1.0, ALU.mult, ALU.add, R=[fa_tok], W=[fk_tok])
                    act(fa[:], fa[:], AF.Ln, R=[fa_tok], W=[fa_tok])
                    K.op("dve", lambda e: e.tensor_tensor_scan(out=fG[:], data0=rmask[:], data1=fa[:], initial=0.0, op0=ALU.mult, op1=ALU.add),
                         [fa_tok, rm_tok], [fG_tok])
                    act(eG[:], fG[:], AF.Exp, R=[fG_tok], W=[eG_tok])
                    tt(qgT[:], sq_[:], eG[:], ALU.mult, R=[sq_tok, eG_tok], W=[qg_tok])
                    act(fe[:], fG[:], AF.Exp, scale=-1.0, R=[fG_tok], W=[fe_tok])
                    tt(kiT[:], fk[:], fe[:], ALU.mult, R=[fk_tok, fe_tok], W=[ki_tok])
                    gp = fG[:, 0:TP].rearrange("p (c t) -> p c t", t=32)
                    tt(fa[:, 0:TP].rearrange("p (c t) -> p c t", t=32), gp[:, :, 31:32].broadcast_to([128, 32, 32]), gp, ALU.subtract,
                       R=[fG_tok], W=[fa_tok])
                    gs = fG[:, TP:TPH].rearrange("p (c t) -> p c t", t=8)
                    tt(fa[:, TP:TPH].rearrange("p (c t) -> p c t", t=8), gs[:, :, 7:8].broadcast_to([128, 8, 8]), gs, ALU.subtract,
                       R=[fG_tok], W=[fa_tok])
                    act(fe[:], fa[:], AF.Exp, R=[fa_tok], W=[fe_tok])
                    tt(kdT[:], fk[:], fe[:], ALU.mult, R=[fk_tok, fe_tok], W=[kd_tok])

                    K.dma("pool", Ss[:], shgrn_d[j, SPP * ph:SPP * ph + SPP, h].rearrange("s k v -> k s v"), W=[Ss_tok], semtok=Ss_tok)
                    act(Ssb[:], Ss[:], AF.Copy, R=[Ss_tok], W=[Ssb_tok])
                    if ph == 0:
                        memset(Sf_all[:, h, :], 0.0, W=[Sf_tok[h]])
                    cp(Sbf[:], Sf_all[:, h, :], R=[Sf_tok[h]], W=[Sbf_tok])

                    for bi, (c0, BS, kind) in enumerate(blocks()):
                        if kind == "P":
                            ncb, C = 4, 32
                            mi = cst[:, C_MIP:C_MIP + 128]
                            cmask = cst[:, C_CM4:C_CM4 + 512].rearrange("p (c t) -> p c t", c=4)
                            rmk = cst[:, C_RM4:C_RM4 + 4]
                        else:
                            ncb, C = 8, 8
                            mi = cst[0:64, C_MIS:C_MIS + 64]
                            cmask = cst[:, C_CM8:C_CM8 + 512].rearrange("p (c t) -> p c t", c=8)
                            rmk = cst[0:64, C_RM8:C_RM8 + 8]
                        cols = slice(c0, c0 + BS)
                        aps, aptok = psmall.get()
                        mm(aps[0:BS, 0:BS], kiT[:, cols], qgT[:, cols], R=[ki_tok, qg_tok], W=[aptok])
                        attnT, attok = tmpb.get()
                        tt(attnT[0:BS, 0:BS], aps[0:BS, 0:BS], mi, ALU.mult, R=[aptok, cst_tok], W=[attok])
                        pvv, pvtok = pbf.get()
                        tr(pvv[0:BS, :], vT[:, cols], ident_bf[:], R=[v_tok, misc_tok], W=[pvtok])
                        vtk, vtktok = tmpb.get()
                        act(vtk[0:BS, :], pvv[0:BS, :], AF.Copy, R=[pvtok], W=[vtktok])
                        pk, pktok = pbf.get()
                        tr(pk[0:BS, :], kdT[:, cols], ident_bf[:], R=[kd_tok, misc_tok], W=[pktok])
                        tt(Kpad[0:BS, 0:ncb, :], pk[0:BS, :].unsqueeze(1).broadcast_to([BS, ncb, 128]),
                           rmk.unsqueeze(2).broadcast_to([BS, ncb, 128]), ALU.mult, R=[pktok, cst_tok], W=[kp_tok])
                        tt(Qpad[:, 0:ncb, 0:BS], cmask[:, :, 0:BS], qgT[:, cols].unsqueeze(1).broadcast_to([128, ncb, BS]), ALU.mult,
                           R=[cst_tok, qg_tok], W=[qp_tok])
                        if kind == "P":
                            for c in range(ncb):
                                mm(o_ps[0:BS, 0:128], Qpad[:, c, 0:BS], Sbf[:], start=(c == 0), stop=False, R=[qp_tok, Sbf_tok], W=[o_ptok])
                                sps, sptok = psmall.get()
                                mm(sps[:, :], Kpad[0:BS, c, :], vtk[0:BS, :], R=[kp_tok, vtktok], W=[sptok])
                                ce = c0 + (c + 1) * C - 1
                                stt(Sf_all[:, h, :], Sf_all[:, h, :], eG[:, ce:ce + 1], sps[:, :], ALU.mult, ALU.add,
                                    R=[sptok, eG_tok], W=[Sf_tok[h]])
                                act(Sbf[:], Sf_all[:, h, :], AF.Copy, R=[Sf_tok[h]], W=[Sbf_tok])
                            mm(o_ps[0:BS, 0:128], attnT[0:BS, 0:BS], vtk[0:BS, :], start=False, stop=True, R=[attok, vtktok], W=[o_ptok])
                        else:
                            for c in range(ncb):
                                mm(o_ps[0:BS, 0:128], Qpad[:, c, 0:BS], Ssb[:, c, :], start=(c == 0), stop=False, R=[qp_tok, Ssb_tok], W=[o_ptok])
                            mm(o_ps[0:BS, 0:128], attnT[0:BS, 0:BS], vtk[0:BS, :], start=False, stop=True, R=[attok, vtktok], W=[o_ptok])
                            sl = [pbig.get(), pbig.get()]
                            for c in range(ncb):
                                ps, ptok = sl[c // 4]
                                mm(ps[:, (c % 4) * 128:(c % 4) * 128 + 128], Kpad[0:BS, c, :], vtk[0:BS, :], R=[kp_tok, vtktok], W=[ptok])
                            ege = eG[:, TP:TPH].rearrange("p (s t) -> p s t", t=8)[:, :, 7:8]
                            tt(Sn[:], Ss[:], ege.broadcast_to([128, 8, 128]), ALU.mult, R=[Ss_tok, eG_tok], W=[Sn_tok])
                            for hb in range(2):
                                ps, ptok = sl[hb]
                                tt(Sn[:, hb * 4:hb * 4 + 4, :], Sn[:, hb * 4:hb * 4 + 4, :], ps[:, :].rearrange("p (s v) -> p s v", v=128), ALU.add,
                                   R=[ptok], W=[Sn_tok])
                            K.dma("sp", shgrn_o[j, SPP * ph:SPP * ph + SPP, h].rearrange("s k v -> k s v"), Sn[:], R=[Sn_tok], semtok=Sn_tok)
                        o_post(l, None, BS, h, c0, oT, otoks[min(c0 // 512, 2)], sgT, sg_tok, tmpf, tmpb)
                    if ph == 1:
                        K.dma("sp", phgrn_d[j, h], Sf_all[:, h, :], R=[Sf_tok[h]], semtok=Sf_tok[h])
                K.barrier()

        def ffn_phase(l, ph):
            A2, B2, G2 = modA[3], modA[4], modA[5]
            with contextlib.ExitStack() as SF:
                aT = sb(SF, "f_aT", [128, NFF, TPH], BF16)
                a_toks = [Tok() for _ in range(3)]
                fci = sb(SF, "f_fci", [128, NFF, 8, 2], F32)
                fci_tok = Tok()
                fco = sb(SF, "f_fco", [128, NFF, 9, 2], F32)
                fco_tok = Tok()
                K.dma("sp", fci[:].rearrange("p a s r -> p (a s r)"), fci_d[l, ph], W=[fci_tok], semtok=fci_tok)
                memset(fco[:], 0.0, W=[fco_tok])
                with contextlib.ExitStack() as S1:
                    hT = sb(S1, "f_hT", [128, KC, TPH], BF16)
                    htoks = [Tok() for _ in range(3)]
                    with contextlib.ExitStack() as SPN:
                        prenorm(ph, A2, B2, hT, htoks, SPN)
                        K.barrier()
                    gpre_r = Ring([(sb(S1, "f_gpre%d" % i, [128, 2 + TP + SPP * 10], F32), Tok()) for i in range(2)])
                    up_r = Ring([(sb(S1, "f_up%d" % i, [128, TPH], F32), Tok()) for i in range(2)])
                    gc_r = Ring([(sb(S1, "f_gc%d" % i, [128, TPH], F32), Tok()) for i in range(2)])
                    for jj in range(NFF):
                        gpre, gp_tok = gpre_r.get()
                        up, up_tok = up_r.get()
                        gc, gc_tok = gc_r.get()
                        pv = gpre[:, 2 + TP:2 + TP + SPP * 10].rearrange("p (s r) -> p s r", r=10)
                        wg, wgtok = wload(ffn_w_gu[l], jj * 128)
                        wu, wutok = wload(ffn_w_gu[l], DFF + jj * 128)
                        if ph == 0:
                            memset(gpre[:, 0:2], 0.0, W=[gp_tok])
                        else:
                            cp(gpre[:, 0:2], fcar[:, jj, :], R=[fcar_tok], W=[gp_tok])
                        cp(pv[:, :, 0:2], fci[:, jj, :, :], R=[fci_tok], W=[gp_tok])
                        for ti in range(3):
                            t0 = TILES[ti][0]
                            ps, ptok, n = proj_tile(wg, wgtok, hT, htoks, ti, ring=pwide)
                            if ti < 2:
                                act(gpre[:, 2 + t0:2 + t0 + n], ps[:, 0:n], AF.Copy, R=[ptok], W=[gp_tok])
                            else:
                                act(pv[:, :, 2:10], ps[:, 0:n].rearrange("p (s r) -> p s r", r=LS), AF.Copy, R=[ptok], W=[gp_tok])
                            ps, ptok, n = proj_tile(wu, wutok, hT, htoks, ti, ring=pwide)
                            act(up[:, t0:t0 + n], ps[:, 0:n], AF.Copy, R=[ptok], W=[up_tok])
                        if ph == 0:
                            cp(fcar[:, jj, :], gpre[:, TP:TP + 2], R=[gp_tok], W=[fcar_tok])
                        else:
                            cp(fco[:, jj, 8, :], gpre[:, TP:TP + 2], R=[gp_tok], W=[fco_tok])
                        cp(fco[:, jj, 0:8, :], pv[:, :, 8:10], R=[gp_tok], W=[fco_tok])
                        wc = prm[:, l, P_FCW + jj * 3:P_FCW + jj * 3 + 3]
                        bc = prm[:, l, P_FCB + jj:P_FCB + jj + 1]
                        gcs = gc[:, TP:TPH].rearrange("p (s r) -> p s r", r=LS)
                        ts(gc[:, 0:TP], gpre[:, 0:TP], wc[:, 0:1], bc, ALU.mult, ALU.add, R=[gp_tok, prm_tok], W=[gc_tok])
                        ts(gcs, pv[:, :, 0:8], wc[:, 0:1], bc, ALU.mult, ALU.add, R=[gp_tok, prm_tok], W=[gc_tok])
                        for tap in range(1, 3):
                            stt(gc[:, 0:TP], gpre[:, tap:tap + TP], wc[:, tap:tap + 1], gc[:, 0:TP], ALU.mult, ALU.add,
                                R=[gp_tok, prm_tok], W=[gc_tok])
                            stt(gcs, pv[:, :, tap:tap + 8], wc[:, tap:tap + 1], gcs, ALU.mult, ALU.add, R=[gp_tok, prm_tok], W=[gc_tok])
                        act(gc[:], gc[:], AF.Silu, R=[gc_tok], W=[gc_tok])
                        for ti, (t0, n) in enumerate(TILES):
                            tt(aT[:, jj, t0:t0 + n], gc[:, t0:t0 + n], up[:, t0:t0 + n], ALU.mult, R=[gc_tok, up_tok], W=[a_toks[ti]])
                    K.dma("sp", fco_d[l, ph], fco[:].rearrange("p a s r -> p (a s r)"), R=[fco_tok], semtok=fco_tok)
                    K.barrier()
                with contextlib.ExitStack() as S2:
                    y = sb(S2, "f_y", [128, KC, TPH], F32)
                    ytok = Tok()
                    wdn = Ring([(sb(S2, "f_wd%d" % i, [128, NFF, 128], BF16), Tok()) for i in range(2)])
                    sqt = (sb(S2, "f_sq", [128, KC, 256], BF16), Tok())
                    rs = (sb(S2, "f_rs", [128, 256], F32), Tok())
                    tmps = Ring([(sb(S2, "f_t%d" % i, [128, 256], F32), Tok()) for i in range(2)])
                    for oc in range(KC):
                        wt, wtok = wdn.get()
                        K.dma("pool", wt[:], ffn_w_down[l][:, oc * 128:(oc + 1) * 128].rearrange("(j p) n -> p j n", p=128), W=[wtok], semtok=wtok)
                        for ti, (t0, n) in enumerate(TILES):
                            ps, ptok = pwide.get()
                            for jj in range(NFF):
                                mm(ps[:, 0:n], wt[:, jj, :], aT[:, jj, t0:t0 + n], start=(jj == 0), stop=(jj == NFF - 1),
                                   R=[wtok, a_toks[ti]], W=[ptok])
                            cp(y[:, oc, t0:t0 + n], ps[:, 0:n], R=[ptok], W=[ytok])
                    for ti, (t0, n) in enumerate([(0, 256), (256, 256), (512, 256), (768, 256), (1024, 64)]):
                        rstd_fm(y[:, :, t0:t0 + n], n, [ytok], sqt, rs)
                        for kc in range(KC):
                            tmp, ttok = tmps.get()
                            tt(tmp[:, 0:n], y[:, kc, t0:t0 + n], rs[0][:, 0:n], ALU.mult, R=[ytok, rs[1]], W=[ttok])
                            if t0 < TP:
                                stt(xT[:, kc, ph, t0:t0 + n], tmp[:, 0:n], G2[:, kc, 0:1], xT[:, kc, ph, t0:t0 + n], ALU.mult, ALU.add,
                                    R=[ttok, mod_tok], W=[xtok[ph]])
                            else:
                                sc = seqcols(ph)
                                v3 = tmp[:, 0:n].rearrange("p (s j) -> p s j", j=LS)
                                tt(v3, v3, G2[:, kc, sc].unsqueeze(2).broadcast_to([128, SPP, LS]), ALU.mult, R=[ttok, mod_tok], W=[ttok])
                                x3 = xT[:, kc, ph, t0:t0 + n].rearrange("p (s j) -> p s j", j=LS)
                                tt(x3, x3, v3, ALU.add, R=[ttok], W=[xtok[ph]])
                    K.barrier()

        def main_program():
            for l in range(depth):
                with contextlib.ExitStack() as SL:
                    stage(1)
                    adaln(l, SL)
                    if l % 2 == 1:
                        jj_ = l // 2
                        if jj_ == 0:
                            memset(lbT[:, :, 0:1], 0.0, W=[lb_tok])
                            memset(lbT[:, :, 1:2], 1.0, W=[lb_tok])
                        else:
                            hl = prm[:, l, P_HLB:P_HLB + 16].rearrange("p (h t) -> p h t", t=2)
                            tt(lbT[:, :, 0:1], hl[:, :, 1:2], hl[:, :, 0:1], ALU.subtract, R=[prm_tok], W=[lb_tok])
                            act(lbT[:, :, 0:1], lbT[:, :, 0:1], AF.Sigmoid, R=[lb_tok], W=[lb_tok])
                            ts(lbT[:, :, 1:2], lbT[:, :, 0:1], -1.0, 1.0, ALU.mult, ALU.add, R=[lb_tok], W=[lb_tok])
                    K.barrier()
                for ph in range(2):
                    with contextlib.ExitStack() as SA:
                        hT = sb(SA, "m_hT", [128, KC, TPH], BF16)
                        htoks = [Tok() for _ in range(3)]
                        oT = sb(SA, "m_oT", [128, KC, TPH], BF16)
                        otoks = [Tok() for _ in range(3)]
                        with contextlib.ExitStack() as SP:
                            stage(2)
                            prenorm(ph, modA[0], modA[1], hT, htoks, SP)
                            K.barrier()
                        with contextlib.ExitStack() as SM:
                            if l % 2 == 0:
                                gdn_phase(l, ph, hT, htoks, oT, otoks, SM)
                            else:
                                hgrn_phase(l, ph, hT, htoks, oT, otoks, SM)
                            K.barrier()
                        with contextlib.ExitStack() as SO:
                            stage(5)
                            Wout = gdn_w_out[l // 2] if l % 2 == 0 else hgrn_w_out[l // 2]
                            outproj_postnorm(l, ph, Wout, oT, otoks, modA[2], SO)
                            K.barrier()
                    stage(6)
                    ffn_phase(l, ph)
            for ph in range(2):
                K.dma("sp", yT_d.rearrange("p (k h t) -> p k h t", k=KC, h=2)[:, :, ph, :], xT[:, :, ph, :], R=[xtok[ph]], semtok=xtok[ph])

        try:
            main_program()
        except StopBuild:
            K.barrier()
        K.final_wait("sp")

        with nc.Block() as block:
            K.replay(block)
    return nc


def _consts():
    c = np.zeros((128, NCST), np.float32)
    c[:, C_IDENT:C_IDENT + 128] = np.eye(128, dtype=np.float32)
    i = np.arange(128)[:, None]
    jx = np.arange(128)[None, :]
    BIG = 1.0e4
    valid = (i // 64 == jx // 64) & (i > jx)
    c[:, C_PENP:C_PENP + 128] = np.where(valid, 0.0, BIG)
    i8 = np.arange(64)[:, None]
    j8 = np.arange(64)[None, :]
    valid = (i8 // 8 == j8 // 8) & (i8 > j8)
    c[0:64, C_PENS:C_PENS + 64] = np.where(valid, 0.0, BIG)
    for ncb, off, bs in ((2, C_CM2, 128), (4, C_CM4, 128), (8, C_CM8, 64)):
        C = bs // ncb
        m = np.zeros((ncb, bs), np.float32)
        for cc in range(ncb):
            m[cc, cc * C:(cc + 1) * C] = 1.0
        c[:, off:off + ncb * bs] = m.reshape(1, -1)
    for ncb, off, bs in ((2, C_RM2, 128), (4, C_RM4, 128), (8, C_RM8, 64)):
        C = bs // ncb
        m = np.zeros((bs, ncb), np.float32)
        for cc in range(ncb):
            m[cc * C:(cc + 1) * C, cc] = 1.0
        c[0:bs, off:off + ncb] = m
    c[:, C_MIP:C_MIP + 128] = ((i // 32 == jx // 32) & (i <= jx)).astype(np.float32)
    c[0:64, C_MIS:C_MIS + 64] = ((i8 // 8 == j8 // 8) & (i8 <= j8)).astype(np.float32)
    e = np.zeros((8, 8, 128), np.float32)
    for h in range(8):
        e[h, h, :] = 1.0
    c[0:8, C_ESEL:C_ESEL + 1024] = e.reshape(8, 1024)
    return c


def _fm(v):
    sh = v.shape
    k = sh[-1] // 128
    v = v.reshape(sh[:-1] + (k, 128))
    return np.moveaxis(np.moveaxis(v, -1, 0), -1, 1)


_NC_CACHE = {}


def kernel(x_prompt, x_sample, state_gdn, state_gdn_conv, state_hgrn, state_ffn_conv, c_prompt, c_sample,
           ada_w, ada_b, norm_pre_mix, norm_post_mix, norm_pre_ffn, norm_post_ffn,
           gdn_w_in, gdn_conv_w, gdn_conv_b, gdn_a_log, gdn_dt_bias, gdn_norm, gdn_w_out,
           hgrn_lb, hgrn_w_in, hgrn_norm, hgrn_w_out,
           ffn_w_gu, ffn_conv_w, ffn_conv_b, ffn_w_down, _depth=DEPTH, _stage=None):
    f32 = np.float32
    A = lambda a: np.ascontiguousarray(np.asarray(a, dtype=f32))
    x_prompt, x_sample = A(x_prompt), A(x_sample)
    state_gdn, state_gdn_conv, state_hgrn, state_ffn_conv = A(state_gdn), A(state_gdn_conv), A(state_hgrn), A(state_ffn_conv)
    c_prompt, c_sample = A(c_prompt), A(c_sample)
    prm = np.zeros((128, 4, NPRM), f32)
    nrow = np.zeros((128, 4, 128), f32)
    ada_b_, gcw, gcb = A(ada_b), A(gdn_conv_w), A(gdn_conv_b)
    fcw, fcb = A(ffn_conv_w), A(ffn_conv_b)
    hlb = A(hgrn_lb)
    for l in range(4):
        prm[:, l, P_ADAB:P_ADAB + 48] = ada_b_[l].reshape(48, 128).T
        prm[:, l, P_NPRE_MIX:P_NPRE_MIX + 8] = A(norm_pre_mix)[l].reshape(8, 128).T
        prm[:, l, P_NPOST_MIX:P_NPOST_MIX + 8] = A(norm_post_mix)[l].reshape(8, 128).T
        prm[:, l, P_NPRE_FFN:P_NPRE_FFN + 8] = A(norm_pre_ffn)[l].reshape(8, 128).T
        prm[:, l, P_NPOST_FFN:P_NPOST_FFN + 8] = A(norm_post_ffn)[l].reshape(8, 128).T
        j = l // 2
        if l % 2 == 0:
            prm[:, l, P_GCW:P_GCW + 96] = gcw[j].reshape(4, 24, 128).transpose(2, 1, 0).reshape(128, 96)
            prm[:, l, P_GCB:P_GCB + 24] = gcb[j].reshape(24, 128).T
            prm[0:8, l, P_ALOG] = A(gdn_a_log)[j]
            prm[0:8, l, P_DTB] = A(gdn_dt_bias)[j]
            nrow[:, l, :] = A(gdn_norm)[j][None, :]
        else:
            nrow[:, l, :] = A(hgrn_norm)[j][None, :]
        prm[:, l, P_FCW:P_FCW + 66] = fcw[l].reshape(3, 22, 128).transpose(2, 1, 0).reshape(128, 66)
        prm[:, l, P_FCB:P_FCB + 22] = fcb[l].reshape(22, 128).T
        prm[:, l, P_HLB:P_HLB + 16] = hlb.reshape(2, 8, 128).transpose(2, 1, 0).reshape(128, 16)
    cst = _consts()
    shared = dict(prm=prm.reshape(128, -1), nrow=nrow.reshape(128, -1), cst=cst,
                  ada_w=A(ada_w), gdn_w_in=A(gdn_w_in), gdn_w_out=A(gdn_w_out), hgrn_w_in=A(hgrn_w_in),
                  hgrn_w_out=A(hgrn_w_out), ffn_w_gu=A(ffn_w_gu), ffn_w_down=A(ffn_w_down))
    in_maps = []
    for c in range(NCORE):
        xt = np.zeros((128, KC, 2, TPH), f32)
        xp = _fm(x_prompt[c])
        xs = _fm(x_sample[16 * c:16 * c + 16])
        for ph in range(2):
            xt[:, :, ph, 0:TP] = xp[:, :, TP * ph:TP * ph + TP]
            xt[:, :, ph, TP:] = xs[:, :, 8 * ph:8 * ph + 8, :].reshape(128, KC, TS)
        cc = np.concatenate([c_prompt[c:c + 1], c_sample[16 * c:16 * c + 16]], 0)
        cT = _fm(cc)
        gc = _fm(state_gdn_conv[:, 16 * c:16 * c + 16])
        gci = np.zeros((2, 2, 128, 24, 8, 3), f32)
        fc = _fm(state_ffn_conv[:, 16 * c:16 * c + 16])
        fci = np.zeros((4, 2, 128, NFF, 8, 2), f32)
        for ph in range(2):
            gci[:, ph] = gc[:, :, :, 8 * ph:8 * ph + 8, :].transpose(2, 0, 1, 3, 4)
            fci[:, ph] = fc[:, :, :, 8 * ph:8 * ph + 8, :].transpose(2, 0, 1, 3, 4)
        m = dict(shared)
        m.update(xT=xt.reshape(128, -1), cT=np.ascontiguousarray(cT).reshape(128, -1),
                 sgdn=np.ascontiguousarray(state_gdn[:, 16 * c:16 * c + 16]),
                 shgrn=np.ascontiguousarray(state_hgrn[:, 16 * c:16 * c + 16]),
                 gci=gci.reshape(2, 2, 128, -1), fci=fci.reshape(4, 2, 128, -1))
        in_maps.append(m)
    ck = (_depth, _stage)
    if ck not in _NC_CACHE:
        _STAGE_LIMIT[0] = _stage
        _STAGE_LIMIT[1] = False
        _NC_CACHE[ck] = build_nc(_depth)
        _STAGE_LIMIT[0] = None
        _STAGE_LIMIT[1] = False
    nc = _NC_CACHE[ck]
    res = run_bass_kernel_spmd(nc, in_maps, core_ids=list(range(NCORE)))
    R = res.results
    y_prompt = np.zeros((8, 2048, D), f32)
    y_sample = np.zeros((128, 8, D), f32)
    p_gdn = np.zeros((2, 8, H, 128, 128), f32)
    p_hgrn = np.zeros((2, 8, H, 128, 128), f32)
    s_gdn = np.zeros((2, 128, H, 128, 128), f32)
    s_hgrn = np.zeros((2, 128, H, 128, 128), f32)
    p_gconv = np.zeros((2, 8, 3, 3072), f32)
    s_gconv = np.zeros((2, 128, 3, 3072), f32)
    p_fconv = np.zeros((4, 8, 2, DFF), f32)
    s_fconv = np.zeros((4, 128, 2, DFF), f32)
    for c in range(NCORE):
        r = R[c]
        yt = r["yT"].reshape(128, KC, 2, TPH)
        for ph in range(2):
            y_prompt[c, TP * ph:TP * ph + TP] = yt[:, :, ph, 0:TP].transpose(2, 1, 0).reshape(TP, D)
            ys = yt[:, :, ph, TP:].reshape(128, KC, 8, 8)
            y_sample[16 * c + 8 * ph:16 * c + 8 * ph + 8] = ys.transpose(2, 3, 1, 0).reshape(8, 8, D)
        p_gdn[:, c] = r["pgdn"]
        p_hgrn[:, c] = r["phgrn"]
        s_gdn[:, 16 * c:16 * c + 16] = r["sgdn_o"]
        s_hgrn[:, 16 * c:16 * c + 16] = r["shgrn_o"]
        g = r["gco"].reshape(2, 2, 128, 24, 9, 3)
        f = r["fco"].reshape(4, 2, 128, NFF, 9, 2)
        for ph in range(2):
            gs = g[:, ph, :, :, 0:8, :].transpose(0, 3, 4, 2, 1).reshape(2, 8, 3, 3072)
            s_gconv[:, 16 * c + 8 * ph:16 * c + 8 * ph + 8] = gs
            fs = f[:, ph, :, :, 0:8, :].transpose(0, 3, 4, 2, 1).reshape(4, 8, 2, DFF)
            s_fconv[:, 16 * c + 8 * ph:16 * c + 8 * ph + 8] = fs
        p_gconv[:, c] = g[:, 1, :, :, 8, :].transpose(0, 3, 2, 1).reshape(2, 3, 3072)
        p_fconv[:, c] = f[:, 1, :, :, 8, :].transpose(0, 3, 2, 1).reshape(4, 2, DFF)
    return (y_prompt, y_sample, p_gdn, p_gconv, p_hgrn, p_fconv, s_gdn, s_gconv, s_hgrn, s_fconv)
```

```python
import contextlib
import numpy as np
import concourse.bass as bass
import concourse.mybir as mybir
from concourse.bass_utils import run_bass_kernel_spmd

F32 = mybir.dt.float32
BF16 = mybir.dt.bfloat16
AF = mybir.ActivationFunctionType
ALU = mybir.AluOpType

NCORE = 8
D = 1024
KC = 8
H = 8
DFF = 2816
NFF = 22
DEPTH = 4
TP = 1024
SPP = 8
LS = 8
TS = SPP * LS
TPH = TP + TS
TILES = [(0, 512), (512, 512), (1024, 64)]
EPS = 1e-6
NPRM = 48 + 32 + 96 + 24 + 66 + 22 + 16 + 2

P_ADAB = 0
P_NPRE_MIX = 48
P_NPOST_MIX = 56
P_NPRE_FFN = 64
P_NPOST_FFN = 72
P_GCW = 80
P_GCB = 176
P_FCW = 200
P_FCB = 266
P_HLB = 288
P_ALOG = 304
P_DTB = 305

C_IDENT = 0
C_PENP = 128
C_PENS = 256
C_CM2 = 320
C_CM4 = 576
C_CM8 = 1088
C_RM2 = 1600
C_RM4 = 1602
C_RM8 = 1606
C_MIP = 1614
C_MIS = 1742
C_ESEL = 1806
NCST = 1806 + 1024


class Tok:
    __slots__ = ("w", "r", "dsem", "dval", "excl")

    def __init__(self, excl=False):
        self.w = {}
        self.r = {}
        self.dsem = None
        self.dval = 0
        self.excl = excl


class Sched:
    ENGS = ("pe", "act", "dve", "pool", "sp")

    def __init__(self, nc, es):
        self.nc = nc
        self.es = es
        self.q = {k: [] for k in self.ENGS}
        self.n = {k: 0 for k in self.ENGS}
        self.seen = {k: {} for k in self.ENGS}
        self.sem = {}
        for k in ("pe", "act", "dve", "pool"):
            self.sem[k] = es.enter_context(nc.semaphore("s_" + k))
        self.dma_latest = {}
        self.nd = 0

    def _waits(self, en, R, W):
        need = {}

        def add(ev):
            key, sem, val = ev
            if key not in need or need[key][1] < val:
                need[key] = (sem, val)

        for t in R:
            for ev in t.w.values():
                add(ev)
            if t.excl:
                for ev in t.r.values():
                    add(ev)
        for t in W:
            for ev in t.w.values():
                add(ev)
            for ev in t.r.values():
                add(ev)
        out = []
        seen = self.seen[en]
        for key, (sem, val) in need.items():
            if key == en and (en == "pe" or _NO_SAME_ENGINE_WAIT[0]):
                continue
            if seen.get(key, 0) >= val:
                continue
            seen[key] = val
            out.append((sem, val))
        return out

    def op(self, en, fn, R=(), W=()):
        if _STAGE_LIMIT[1]:
            return
        waits = self._waits(en, R, W)
        self.n[en] += 1
        sem = self.sem[en]
        self.q[en].append((waits, fn, sem, 1))
        ev = (en, sem, self.n[en])
        for t in W:
            t.w[en] = ev
        for t in R:
            t.r[en] = ev

    def dma(self, qn, out, in_, R=(), W=(), semtok=None):
        if _STAGE_LIMIT[1]:
            return
        waits = self._waits(qn, R, W)
        t = semtok
        if t.dsem is None:
            t.dsem = {}
        if qn not in t.dsem:
            self.nd += 1
            t.dsem[qn] = [self.es.enter_context(self.nc.semaphore("d%d" % self.nd)), 0]
        ent = t.dsem[qn]
        ent[1] += 16
        dsem, dval = ent[0], ent[1]
        key = "d%d_%s" % (id(t), qn)
        self.q[qn].append((waits, (lambda e: e.dma_start(out=out, in_=in_)), dsem, 16))
        ev = (key, dsem, dval)
        for x in W:
            x.w[key] = ev
        for x in R:
            x.r[key] = ev
        self.dma_latest[key] = (dsem, dval)

    def barrier(self):
        for en in self.ENGS:
            waits = []
            seen = self.seen[en]
            for k in ("pe", "act", "dve", "pool"):
                if k == en and en == "pe":
                    continue
                v = self.n[k]
                if v > 0 and seen.get(k, 0) < v:
                    seen[k] = v
                    waits.append((self.sem[k], v))
            for key, (sem, val) in self.dma_latest.items():
                if seen.get(key, 0) < val:
                    seen[key] = val
                    waits.append((sem, val))
            if waits:
                self.q[en].append((waits, None, None, 0))

    def nsems(self):
        return self.nd + 4

    def final_wait(self, en="sp"):
        waits = []
        for key, (sem, val) in self.dma_latest.items():
            waits.append((sem, val))
        for k in ("pe", "act", "dve", "pool"):
            if self.n[k] > 0:
                waits.append((self.sem[k], self.n[k]))
        self.q[en].append((waits, None, None, 0))

    def replay(self, block):
        q = self.q

        def run(e, lst):
            for waits, fn, sem, inc in lst:
                if fn is None or not _INLINE_WAIT[0] or not waits:
                    for s, v in waits:
                        e.wait_ge(s, v)
                    if fn is not None:
                        fn(e).then_inc(sem, inc)
                else:
                    for s, v in waits[:-1]:
                        e.wait_ge(s, v)
                    s, v = waits[-1]
                    fn(e)._wait_ge(s, v).then_inc(sem, inc)

        @block.tensor
        def _(e):
            run(e, q["pe"])

        @block.scalar
        def _(e):
            run(e, q["act"])

        @block.vector
        def _(e):
            run(e, q["dve"])

        @block.gpsimd
        def _(e):
            run(e, q["pool"])

        @block.sync
        def _(e):
            run(e, q["sp"])


class StopBuild(Exception):
    pass


_STAGE_LIMIT = [None, False]
_NO_SAME_ENGINE_WAIT = [False]
_INLINE_WAIT = [True]


def stage(n):
    if _STAGE_LIMIT[0] is not None and n > _STAGE_LIMIT[0]:
        _STAGE_LIMIT[1] = True


class Ring:
    def __init__(self, items):
        self.items = items
        self.i = 0

    def get(self):
        r = self.items[self.i]
        self.i = (self.i + 1) % len(self.items)
        return r


def build_nc(depth=DEPTH):
    nc = bass.Bass("TRN2", target_bir_lowering=False)

    def din(name, shape):
        return nc.dram_tensor(name, list(shape), F32, kind="ExternalInput").ap()

    def dout(name, shape):
        return nc.dram_tensor(name, list(shape), F32, kind="ExternalOutput").ap()

    xT_d = din("xT", [128, KC * 2 * TPH])
    cT_d = din("cT", [128, KC * 17])
    sgdn_d = din("sgdn", [2, 16, H, 128, 128])
    shgrn_d = din("shgrn", [2, 16, H, 128, 128])
    gci_d = din("gci", [2, 2, 128, 24 * 8 * 3])
    fci_d = din("fci", [4, 2, 128, NFF * 8 * 2])
    prm_d = din("prm", [128, 4 * NPRM])
    nrow_d = din("nrow", [128, 4 * 128])
    cst_d = din("cst", [128, NCST])
    ada_w = din("ada_w", [4, D, 6 * D])
    gdn_w_in = din("gdn_w_in", [2, D, 4112])
    gdn_w_out = din("gdn_w_out", [2, D, D])
    hgrn_w_in = din("hgrn_w_in", [2, D, 4096])
    hgrn_w_out = din("hgrn_w_out", [2, D, D])
    ffn_w_gu = din("ffn_w_gu", [4, D, 2 * DFF])
    ffn_w_down = din("ffn_w_down", [4, DFF, D])

    yT_d = dout("yT", [128, KC * 2 * TPH])
    pgdn_d = dout("pgdn", [2, H, 128, 128])
    sgdn_o = dout("sgdn_o", [2, 16, H, 128, 128])
    phgrn_d = dout("phgrn", [2, H, 128, 128])
    shgrn_o = dout("shgrn_o", [2, 16, H, 128, 128])
    gco_d = dout("gco", [2, 2, 128, 24 * 9 * 3])
    fco_d = dout("fco", [4, 2, 128, NFF * 9 * 2])

    with contextlib.ExitStack() as es:
        K = Sched(nc, es)

        cnt = [0]

        def sb(stack, name, shape, dt):
            cnt[0] += 1
            return stack.enter_context(nc.sbuf_tensor("sb%d_%s" % (cnt[0], name), list(shape), dt))

        xT = sb(es, "xT", [128, KC, 2, TPH], F32)
        xtok = [Tok(), Tok()]
        cst = sb(es, "cst", [128, NCST], F32)
        cst_tok = Tok()
        prm = sb(es, "prm", [128, 4, NPRM], F32)
        prm_tok = Tok()
        nrow = sb(es, "nrow", [128, 4, 128], F32)
        nrow_tok = Tok()
        ident_bf = sb(es, "ident_bf", [128, 128], BF16)
        ones_bf = sb(es, "ones_bf", [128, 128], BF16)
        cb = sb(es, "cb", [128, 8], F32)
        misc_tok = Tok()
        csT = sb(es, "csT", [128, KC, 17], BF16)
        cs_tok = Tok()
        modA = [sb(es, "modA%d" % i, [128, KC, 17], F32) for i in range(6)]
        mod_tok = Tok()
        Sf_all = sb(es, "Sf_all", [128, H, 128], F32)
        Sf_tok = [Tok() for _ in range(H)]
        gcar = sb(es, "gcar", [128, 24, 3], F32)
        gcar_tok = Tok()
        fcar = sb(es, "fcar", [128, NFF, 2], F32)
        fcar_tok = Tok()
        lbT = sb(es, "lbT", [128, H, 2], F32)
        lb_tok = Tok()
        wun = [(sb(es, "wun%d" % i, [128, KC, 128], BF16), Tok()) for i in range(5)]
        wring = Ring(wun)
        wres_toks = [Tok() for _ in range(KC)]
        Ss_ptok, gci_ptok, gco_ptok, fci_ptok, fco_ptok = Tok(), Tok(), Tok(), Tok(), Tok()
        wdn_ptoks = [Tok(), Tok()]

        psF = [es.enter_context(nc.psum_tensor("psF%d" % i, [128, 512], F32)) for i in range(7)]
        psB = es.enter_context(nc.psum_tensor("psB", [128, 1024], BF16))
        bkt = [Tok(excl=True) for _ in range(8)]
        pbig = Ring([(psF[0], bkt[0]), (psF[1], bkt[1])])
        psmall = Ring([(psF[2 + i % 3][:, (i // 3) * 128:(i // 3) * 128 + 128], bkt[2 + i % 3]) for i in range(12)])
        pbf = Ring([(psB[:, i * 128:(i + 1) * 128], bkt[7]) for i in range(8)])
        o_ps, o_ptok = psF[5], bkt[5]
        u_ps, u_ptok = psF[6], bkt[6]
        pwide = Ring([(psF[i], bkt[i]) for i in range(7)])

        def mm(out, lhsT, rhs, start=True, stop=True, R=(), W=()):
            K.op("pe", lambda e: e.matmul(out, lhsT, rhs, start=start, stop=stop), R, W)

        def tr(out, in_, idn, R=(), W=()):
            K.op("pe", lambda e: e.transpose(out, in_, idn), R, W)

        def act(out, in_, func, bias=None, scale=None, accum=None, R=(), W=()):
            kw = {}
            if bias is not None:
                kw["bias"] = bias
            if scale is not None:
                kw["scale"] = scale
            if accum is not None:
                kw["accum_out"] = accum
            K.op("act", lambda e: e.activation(out=out, in_=in_, func=func, **kw), R, W)

        def tt(out, a, b, op, R=(), W=(), en="dve"):
            K.op(en, lambda e: e.tensor_tensor(out=out, in0=a, in1=b, op=op), R, W)

        def ts(out, a, s1, s2, op0, op1=None, R=(), W=(), en="dve"):
            if op1 is None:
                K.op(en, lambda e: e.tensor_scalar(out=out, in0=a, scalar1=s1, scalar2=None, op0=op0), R, W)
            else:
                K.op(en, lambda e: e.tensor_scalar(out=out, in0=a, scalar1=s1, scalar2=s2, op0=op0, op1=op1), R, W)

        def stt(out, a, sc, b, op0, op1, R=(), W=(), en="dve"):
            K.op(en, lambda e: e.scalar_tensor_tensor(out=out, in0=a, scalar=sc, in1=b, op0=op0, op1=op1), R, W)

        def cp(out, in_, R=(), W=(), en="dve"):
            K.op(en, lambda e: e.tensor_copy(out=out, in_=in_), R, W)

        def recip(out, in_, R=(), W=()):
            K.op("dve", lambda e: e.reciprocal(out=out, in_=in_), R, W)

        def memset(ap, val, W=(), en="dve"):
            K.op(en, lambda e: e.memset(ap, val), (), W)

        def wload(W2d, c0, ncols=128):
            t, tok = wring.get()
            K.dma("pool", t[:, :, 0:ncols], W2d[:, c0:c0 + ncols].rearrange("(k p) n -> p k n", p=128), W=[tok], semtok=tok)
            return t, tok

        K.dma("sp", cst[:], cst_d[:, :], W=[cst_tok], semtok=cst_tok)
        K.dma("sp", prm[:].rearrange("p l n -> p (l n)"), prm_d[:, :], W=[prm_tok], semtok=prm_tok)
        K.dma("sp", nrow[:].rearrange("p l n -> p (l n)"), nrow_d[:, :], W=[nrow_tok], semtok=nrow_tok)
        for ph in range(2):
            K.dma("sp", xT[:, :, ph, :], xT_d.rearrange("p (k h t) -> p k h t", k=KC, h=2)[:, :, ph, :], W=[xtok[ph]], semtok=xtok[ph])
        ident = cst[:, C_IDENT:C_IDENT + 128]
        cp(ident_bf[:], ident, R=[cst_tok], W=[misc_tok])
        memset(ones_bf[:], 1.0, W=[misc_tok])
        memset(cb[:, 0:1], 1024.0 * EPS, W=[misc_tok])
        memset(cb[:, 1:2], EPS, W=[misc_tok])
        memset(cb[:, 2:3], 128.0 * EPS, W=[misc_tok])
        memset(cb[:, 3:4], 1.0, W=[misc_tok])
        memset(cb[:, 4:5], 0.0, W=[misc_tok])
        ts(nrow[:], nrow[:], float(np.sqrt(128.0)), None, ALU.mult, R=[nrow_tok], W=[nrow_tok])
        for l in range(4):
            ts(prm[:, l, P_NPRE_MIX:P_NPRE_MIX + 32], prm[:, l, P_NPRE_MIX:P_NPRE_MIX + 32], 32.0, None, ALU.mult, R=[prm_tok], W=[prm_tok])
        with contextlib.ExitStack() as s0:
            cTt = sb(s0, "cTt", [128, KC, 17], F32)
            ctok = Tok()
            K.dma("sp", cTt[:].rearrange("p k q -> p (k q)"), cT_d[:, :], W=[ctok], semtok=ctok)
            act(csT[:], cTt[:], AF.Silu, R=[ctok], W=[cs_tok])
            K.barrier()

        def adaln(l, S):
            mod = sb(S, "mod", [128, 48, 17], F32)
            mtok = Tok()
            slots = [pbig.get(), pbig.get()]
            for cc in range(48):
                wt, wtok = wload(ada_w[l], cc * 128)
                ps, ptok = slots[cc // 24]
                col = (cc % 24) * 17
                for kc in range(KC):
                    mm(ps[:, col:col + 17], wt[:, kc, :], csT[:, kc, :], start=(kc == 0), stop=(kc == KC - 1),
                       R=[wtok, cs_tok], W=[ptok])
            for bk in range(2):
                ps, ptok = slots[bk]
                tt(mod[:, 24 * bk:24 * bk + 24, :], ps[:, 0:408].rearrange("p (c q) -> p c q", q=17),
                   prm[:, l, P_ADAB + 24 * bk:P_ADAB + 24 * bk + 24].unsqueeze(2).broadcast_to([128, 24, 17]),
                   ALU.add, R=[ptok, prm_tok], W=[mtok])
            def nb(off):
                return prm[:, l, off:off + 8].unsqueeze(2).broadcast_to([128, 8, 17])
            stt(modA[0][:], mod[:, 8:16, :], 1.0, nb(P_NPRE_MIX), ALU.add, ALU.mult, R=[mtok, prm_tok], W=[mod_tok])
            cp(modA[1][:], mod[:, 0:8, :], R=[mtok], W=[mod_tok])
            stt(modA[2][:], mod[:, 16:24, :], 1.0, nb(P_NPOST_MIX), ALU.add, ALU.mult, R=[mtok, prm_tok], W=[mod_tok])
            stt(modA[3][:], mod[:, 32:40, :], 1.0, nb(P_NPRE_FFN), ALU.add, ALU.mult, R=[mtok, prm_tok], W=[mod_tok])
            cp(modA[4][:], mod[:, 24:32, :], R=[mtok], W=[mod_tok])
            stt(modA[5][:], mod[:, 40:48, :], 1.0, nb(P_NPOST_FFN), ALU.add, ALU.mult, R=[mtok, prm_tok], W=[mod_tok])

        def seqcols(ph):
            return slice(1 + SPP * ph, 1 + SPP * ph + SPP)

        def rstd_fm(src3, n, R, sqt, rs):
            sq, sqtok = sqt
            rst, rstok = rs
            act(sq[:, :, 0:n], src3, AF.Square, R=R, W=[sqtok])
            ps, ptok = pbig.get()
            for kc in range(KC):
                mm(ps[:, 0:n], ones_bf[:], sq[:, kc, 0:n], start=(kc == 0), stop=(kc == KC - 1), R=[sqtok, misc_tok], W=[ptok])
            act(rst[:, 0:n], ps[:, 0:n], AF.Sqrt, bias=cb[:, 0:1], scale=1.0, R=[ptok, misc_tok], W=[rstok])
            recip(rst[:, 0:n], rst[:, 0:n], R=[rstok], W=[rstok])

        def prenorm(ph, A, B, hT, htoks, S):
            sqt = (sb(S, "pn_sq", [128, KC, 512], BF16), Tok())
            rs = (sb(S, "pn_rs", [128, 512], F32), Tok())
            tmps = Ring([(sb(S, "pn_t%d" % i, [128, 512], F32), Tok()) for i in range(3)])
            for ti, (t0, n) in enumerate(TILES):
                rstd_fm(xT[:, :, ph, t0:t0 + n], n, [xtok[ph]], sqt, rs)
                for kc in range(KC):
                    tmp, ttok = tmps.get()
                    tt(tmp[:, 0:n], xT[:, kc, ph, t0:t0 + n], rs[0][:, 0:n], ALU.mult, R=[xtok[ph], rs[1]], W=[ttok])
                    if ti < 2:
                        act(hT[:, kc, t0:t0 + n], tmp[:, 0:n], AF.Identity, bias=B[:, kc, 0:1], scale=A[:, kc, 0:1],
                            R=[ttok, mod_tok], W=[htoks[ti]])
                    else:
                        sc = seqcols(ph)
                        tt(tmp[:, 0:n].rearrange("p (s j) -> p s j", j=LS), tmp[:, 0:n].rearrange("p (s j) -> p s j", j=LS),
                           A[:, kc, sc].unsqueeze(2).broadcast_to([128, SPP, LS]), ALU.mult, R=[ttok, mod_tok], W=[ttok])
                        tt(hT[:, kc, t0:t0 + n].rearrange("p (s j) -> p s j", j=LS), tmp[:, 0:n].rearrange("p (s j) -> p s j", j=LS),
                           B[:, kc, sc].unsqueeze(2).broadcast_to([128, SPP, LS]), ALU.add, R=[ttok, mod_tok], W=[htoks[ti]])

        def proj_tile(wt, wtok, hT, htoks, ti, M=128, ring=None):
            t0, n = TILES[ti]
            ps, ptok = (ring or pbig).get()
            for kc in range(KC):
                mm(ps[0:M, 0:n], wt[:, kc, 0:M], hT[:, kc, t0:t0 + n], start=(kc == 0), stop=(kc == KC - 1),
                   R=[wtok, htoks[ti]], W=[ptok])
            return ps, ptok, n

        def outproj_postnorm(l, ph, Wout2d, oT, otoks, G, S):
            y = sb(S, "op_y", [128, KC, 512], F32)
            ytok = Tok()
            sqt = (sb(S, "op_sq", [128, KC, 512], BF16), Tok())
            rs = (sb(S, "op_rs", [128, 512], F32), Tok())
            tmps = Ring([(sb(S, "op_t%d" % i, [128, 512], F32), Tok()) for i in range(2)])
            wres = sb(S, "op_w", [128, KC, KC, 128], BF16)
            wrtok = wres_toks
            for oc in range(KC):
                K.dma("pool", wres[:, oc, :, :], Wout2d[:, oc * 128:(oc + 1) * 128].rearrange("(k p) n -> p k n", p=128), W=[wrtok[oc]], semtok=wrtok[oc])
            for ti, (t0, n) in enumerate(TILES):
                for oc in range(KC):
                    ps, ptok = pwide.get()
                    for kc in range(KC):
                        mm(ps[:, 0:n], wres[:, oc, kc, :], oT[:, kc, t0:t0 + n], start=(kc == 0), stop=(kc == KC - 1),
                           R=[wrtok[oc], otoks[ti]], W=[ptok])
                    cp(y[:, oc, 0:n], ps[:, 0:n], R=[ptok], W=[ytok])
                resid_update(ph, t0, n, y, ytok, G, sqt, rs, tmps)

        def resid_update(ph, t0, n, y, ytok, G, sqt, rs, tmps):
            rstd_fm(y[:, :, 0:n], n, [ytok], sqt, rs)
            for kc in range(KC):
                tmp, ttok = tmps.get()
                tt(tmp[:, 0:n], y[:, kc, 0:n], rs[0][:, 0:n], ALU.mult, R=[ytok, rs[1]], W=[ttok])
                if t0 < TP:
                    stt(xT[:, kc, ph, t0:t0 + n], tmp[:, 0:n], G[:, kc, 0:1], xT[:, kc, ph, t0:t0 + n], ALU.mult, ALU.add,
                        R=[ttok, mod_tok], W=[xtok[ph]])
                else:
                    sc = seqcols(ph)
                    v3 = tmp[:, 0:n].rearrange("p (s j) -> p s j", j=LS)
                    tt(v3, v3, G[:, kc, sc].unsqueeze(2).broadcast_to([128, SPP, LS]), ALU.mult, R=[ttok, mod_tok], W=[ttok])
                    x3 = xT[:, kc, ph, t0:t0 + n].rearrange("p (s j) -> p s j", j=LS)
                    tt(x3, x3, v3, ALU.add, R=[ttok], W=[xtok[ph]])

        def blocks():
            return [(b * 128, 128, "P") for b in range(8)] + [(TP, 64, "S")]

        def o_post(l, o_rows, BS, h, c0, oT, otok_t, sgT, sgtok, tmpf, tmpb):
            o_ps_, o_ptok_ = o_rows if o_rows is not None else (o_ps, o_ptok)
            junk, jtok = tmpf.get()
            ss, sstok = tmpf.get()
            act(junk[0:BS, :], o_ps_[0:BS, 0:128], AF.Square, accum=ss[0:BS, 0:1], R=[o_ptok_], W=[jtok, sstok])
            act(ss[0:BS, 0:1], ss[0:BS, 0:1], AF.Sqrt, bias=cb[0:BS, 2:3], scale=1.0, R=[sstok, misc_tok], W=[sstok])
            recip(ss[0:BS, 0:1], ss[0:BS, 0:1], R=[sstok], W=[sstok])
            on, ontok = tmpb.get()
            stt(on[0:BS, :], o_ps_[0:BS, 0:128], ss[0:BS, 0:1], nrow[0:BS, l, :], ALU.mult, ALU.mult,
                R=[o_ptok_, sstok, nrow_tok], W=[ontok])
            pb, pbtok = pbf.get()
            tr(pb[:, 0:BS], on[0:BS, :], ident_bf[0:BS, 0:BS], R=[ontok, misc_tok], W=[pbtok])
            tt(oT[:, h, c0:c0 + BS], pb[:, 0:BS], sgT[:, c0:c0 + BS], ALU.mult, R=[pbtok, sgtok], W=[otok_t])

        def gdn_phase(l, ph, hT, htoks, oT, otoks, S):
            j = l // 2
            Win = gdn_w_in[j]
            stage(3)
            G = 4
            dG = sb(S, "g_dG", [8, TPH], F32)
            dtok_ = Tok()
            dtk = sb(S, "g_dtk", [128, 9, 3, 8], F32)
            ex = sb(S, "g_ex", [128, 9, 4, 8], F32)
            glb = sb(S, "g_glb", [128, H, 24], F32)
            dk_tok = Tok()
            gci = sb(S, "g_gci", [128, 24, 8, 3], F32)
            gci_tok = gci_ptok
            gco = sb(S, "g_gco", [128, 24, 9, 3], F32)
            gco_tok = gco_ptok
            K.dma("sp", gci[:].rearrange("p a s r -> p (a s r)"), gci_d[j, ph], W=[gci_tok], semtok=gci_tok)
            memset(gco[:], 0.0, W=[gco_tok])
            with contextlib.ExitStack() as SD:
                dB = sb(SD, "g_dB", [8, TPH], F32)
                dL = sb(SD, "g_dL", [8, TPH], F32)
                dg = sb(SD, "g_dg", [8, TPH], F32)
                rmask = dL
                nega = sb(SD, "g_nega", [8, 1], F32)
                memset(rmask[:], 1.0, W=[dtok_])
                memset(rmask[:, 0:TP].rearrange("p (c t) -> p c t", t=64)[:, :, 0:1], 0.0, W=[dtok_])
                memset(rmask[:, TP:TPH].rearrange("p (c t) -> p c t", t=8)[:, :, 0:1], 0.0, W=[dtok_])
                act(nega[:], prm[0:8, l, P_ALOG:P_ALOG + 1], AF.Exp, R=[prm_tok], W=[dtok_])
                ts(nega[:], nega[:], -1.0, None, ALU.mult, R=[dtok_], W=[dtok_])
                wb, wbtok = wload(Win, 4096, 8)
                for ti in range(3):
                    ps, ptok, n = proj_tile(wb, wbtok, hT, htoks, ti, M=8)
                    t0 = TILES[ti][0]
                    act(dB[:, t0:t0 + n], ps[0:8, 0:n], AF.Sigmoid, R=[ptok], W=[dtok_])
                wa, watok = wload(Win, 4104, 8)
                for ti in range(3):
                    ps, ptok, n = proj_tile(wa, watok, hT, htoks, ti, M=8)
                    t0 = TILES[ti][0]
                    act(dg[:, t0:t0 + n], ps[0:8, 0:n], AF.Exp, bias=prm[0:8, l, P_DTB:P_DTB + 1], scale=1.0, R=[ptok, prm_tok], W=[dtok_])
                act(dg[:], dg[:], AF.Ln, bias=cb[0:8, 3:4], scale=1.0, R=[dtok_, misc_tok], W=[dtok_])
                ts(dg[:], dg[:], nega[:, 0:1], None, ALU.mult, R=[dtok_], W=[dtok_])
                K.op("dve", lambda e: e.tensor_tensor_scan(out=dG[:], data0=rmask[:], data1=dg[:], initial=0.0, op0=ALU.mult, op1=ALU.add),
                     [dtok_], [dtok_])
                gp = dG[:, 0:TP].rearrange("p (c t) -> p c t", t=64)
                tt(dL[:, 0:TP].rearrange("p (c t) -> p c t", t=64), gp[:, :, 63:64].broadcast_to([8, 16, 64]), gp, ALU.subtract, R=[dtok_], W=[dtok_])
                gs = dG[:, TP:TPH].rearrange("p (c t) -> p c t", t=8)
                tt(dL[:, TP:TPH].rearrange("p (c t) -> p c t", t=8), gs[:, :, 7:8].broadcast_to([8, 8, 8]), gs, ALU.subtract, R=[dtok_], W=[dtok_])
                ps, ptok = pbig.get()
                for bi, (c0, BS, kind) in enumerate(blocks()):
                    for qi, src in enumerate((dB, dG, dL)):
                        col = (bi * 3 + qi) * 8
                        tr(ps[0:BS, col:col + 8], src[:, c0:c0 + BS], cst[0:8, C_IDENT:C_IDENT + 8], R=[dtok_, cst_tok], W=[ptok])
                memset(dtk[:], 0.0, W=[dk_tok])
                cp(dtk[:, 0:8].rearrange("p b q h -> p (b q h)"), ps[:, 0:192], R=[ptok], W=[dk_tok])
                cp(dtk[0:64, 8].rearrange("p q h -> p (q h)"), ps[0:64, 192:216], R=[ptok], W=[dk_tok])
                act(ex[:, :, 0:2, :], dtk[:, :, 1:3, :], AF.Exp, R=[dk_tok], W=[dk_tok])
                tt(ex[:, :, 2, :], dtk[:, :, 0, :], ex[:, :, 0, :], ALU.mult, R=[dk_tok], W=[dk_tok])
                ts(ex[:, :, 3, :], dtk[:, :, 0, :], -1.0, None, ALU.mult, R=[dk_tok], W=[dk_tok])
                ps, ptok = pbig.get()
                for h in range(H):
                    esel_h = cst[0:8, C_ESEL + h * 128:C_ESEL + (h + 1) * 128]
                    mm(ps[:, h * 24:h * 24 + 16], esel_h, dG[:, 0:TP].rearrange("p (c t) -> p c t", t=64)[:, :, 63], R=[dtok_, cst_tok], W=[ptok])
                    mm(ps[:, h * 24 + 16:h * 24 + 24], esel_h, dG[:, TP:TPH].rearrange("p (c t) -> p c t", t=8)[:, :, 7], R=[dtok_, cst_tok], W=[ptok])
                act(glb[:].rearrange("p h c -> p (h c)"), ps[:, 0:192], AF.Exp, R=[ptok], W=[dk_tok])
                K.barrier()

            with contextlib.ExitStack() as S1:
                qT = sb(S1, "g_qT", [128, TPH], BF16)
                kT = sb(S1, "g_kT", [128, TPH], BF16)
                vT = sb(S1, "g_vT", [128, TPH], BF16)
                sgT = sb(S1, "g_sgT", [128, TPH], BF16)
                q_tok, k_tok, v_tok, sg_tok = Tok(), Tok(), Tok(), Tok()
                Sbf = sb(S1, "g_Sbf", [128, 128], BF16)
                Sbf_tok = Tok()
                Ss = sb(S1, "g_Ss", [128, SPP, 128], F32)
                Ss_tok = Ss_ptok
                Ssb = sb(S1, "g_Ssb", [128, SPP, 128], BF16)
                Ssb_tok = Tok()
                Sn, Sn_tok = Ss, Ss_tok
                u_sb = sb(S1, "g_usb", [128, 128], BF16)
                usb_tok = Tok()
                memset(u_sb[:], 0.0, W=[usb_tok])
                psblk = Ring([(psF[i % 5][:, (i // 5) * 128:(i // 5) * 128 + 128], bkt[i % 5]) for i in range(20)])

                for h in range(H):
                    stage(4 + 0.1 * h)
                    with contextlib.ExitStack() as SPJ:
                        pre_r = Ring([(sb(SPJ, "g_pre%d" % i, [128, 3 + TP + SPP * 11], F32), Tok()) for i in range(2)])
                        cv_r = Ring([(sb(SPJ, "g_cv%d" % i, [128, TPH], F32), Tok()) for i in range(2)])
                        sq_r = Ring([(sb(SPJ, "g_sqb%d" % i, [128, 512], BF16), Tok()) for i in range(2)])
                        ri_r = Ring([(sb(SPJ, "g_rinv%d" % i, [128, 512], F32), Tok()) for i in range(2)])
                        for role in range(4):
                            wt, wtok = wload(Win, role * 1024 + h * 128)
                            if role < 3:
                                pre, pre_tok = pre_r.get()
                                cv, cv_tok = cv_r.get()
                                pv = pre[:, 3 + TP:3 + TP + SPP * 11].rearrange("p (s r) -> p s r", r=11)
                                ch = role * 8 + h
                                if ph == 0:
                                    memset(pre[:, 0:3], 0.0, W=[pre_tok])
                                else:
                                    cp(pre[:, 0:3], gcar[:, ch, :], R=[gcar_tok], W=[pre_tok])
                                cp(pv[:, :, 0:3], gci[:, ch, :, :], R=[gci_tok], W=[pre_tok])
                            for ti in range(3):
                                ps, ptok, n = proj_tile(wt, wtok, hT, htoks, ti, ring=pwide)
                                t0 = TILES[ti][0]
                                if role == 3:
                                    act(sgT[:, t0:t0 + n], ps[:, 0:n], AF.Silu, R=[ptok], W=[sg_tok])
                                elif ti < 2:
                                    act(pre[:, 3 + t0:3 + t0 + n], ps[:, 0:n], AF.Copy, R=[ptok], W=[pre_tok])
                                else:
                                    act(pv[:, :, 3:11], ps[:, 0:n].rearrange("p (s r) -> p s r", r=LS), AF.Copy, R=[ptok], W=[pre_tok])
                            if role == 3:
                                continue
                            if ph == 0:
                                cp(gcar[:, ch, :], pre[:, TP:TP + 3], R=[pre_tok], W=[gcar_tok])
                            else:
                                cp(gco[:, ch, 8, :], pre[:, TP:TP + 3], R=[pre_tok], W=[gco_tok])
                            cp(gco[:, ch, 0:8, :], pv[:, :, 8:11], R=[pre_tok], W=[gco_tok])
                            wc = prm[:, l, P_GCW + ch * 4:P_GCW + ch * 4 + 4]
                            bc = prm[:, l, P_GCB + ch:P_GCB + ch + 1]
                            cvs = cv[:, TP:TPH].rearrange("p (s r) -> p s r", r=LS)
                            ts(cv[:, 0:TP], pre[:, 0:TP], wc[:, 0:1], bc, ALU.mult, ALU.add, R=[pre_tok, prm_tok], W=[cv_tok])
                            ts(cvs, pv[:, :, 0:8], wc[:, 0:1], bc, ALU.mult, ALU.add, R=[pre_tok, prm_tok], W=[cv_tok])
                            for tap in range(1, 4):
                                stt(cv[:, 0:TP], pre[:, tap:tap + TP], wc[:, tap:tap + 1], cv[:, 0:TP], ALU.mult, ALU.add,
                                    R=[pre_tok, prm_tok], W=[cv_tok])
                                stt(cvs, pv[:, :, tap:tap + 8], wc[:, tap:tap + 1], cvs, ALU.mult, ALU.add, R=[pre_tok, prm_tok], W=[cv_tok])
                            if role == 2:
                                act(vT[:], cv[:], AF.Silu, R=[cv_tok], W=[v_tok])
                                continue
                            act(cv[:], cv[:], AF.Silu, R=[cv_tok], W=[cv_tok])
                            for ti, (t0, n) in enumerate(TILES):
                                sqb, sqb_tok = sq_r.get()
                                rinv, rinv_tok = ri_r.get()
                                act(sqb[:, 0:n], cv[:, t0:t0 + n], AF.Square, R=[cv_tok], W=[sqb_tok])
                                ps, ptok = pwide.get()
                                mm(ps[:, 0:n], ones_bf[:], sqb[:, 0:n], R=[sqb_tok, misc_tok], W=[ptok])
                                act(rinv[:, 0:n], ps[:, 0:n], AF.Sqrt, bias=cb[:, 1:2], scale=1.0, R=[ptok, misc_tok], W=[rinv_tok])
                                recip(rinv[:, 0:n], rinv[:, 0:n], R=[rinv_tok], W=[rinv_tok])
                                if role == 0:
                                    stt(qT[:, t0:t0 + n], cv[:, t0:t0 + n], float(128.0 ** -0.5), rinv[:, 0:n], ALU.mult, ALU.mult,
                                        R=[cv_tok, rinv_tok], W=[q_tok])
                                else:
                                    tt(kT[:, t0:t0 + n], cv[:, t0:t0 + n], rinv[:, 0:n], ALU.mult, R=[cv_tok, rinv_tok], W=[k_tok])
                        K.barrier()

                    stage(4.05 + 0.1 * h)
                    K.dma("pool", Ss[:], sgdn_d[j, SPP * ph:SPP * ph + SPP, h].rearrange("s k v -> k s v"), W=[Ss_tok], semtok=Ss_tok)
                    act(Ssb[:], Ss[:], AF.Copy, R=[Ss_tok], W=[Ssb_tok])
                    if ph == 0:
                        memset(Sf_all[:, h, :], 0.0, W=[Sf_tok[h]])
                    cp(Sbf[:], Sf_all[:, h, :], R=[Sf_tok[h]], W=[Sbf_tok])

                    with contextlib.ExitStack() as SBK:
                        slots = []
                        for k in range(G):
                            sl = {}
                            sl["X"] = [(sb(SBK, "g_x%d_%d" % (k, i), [128, 128], F32), Tok()) for i in range(6)]
                            sl["Y"] = [(sb(SBK, "g_y%d_%d" % (k, i), [128, 128], BF16), Tok()) for i in range(4)]
                            sl["kdm"] = (sb(SBK, "g_kdm%d" % k, [128, 8], F32), Tok())
                            npad = 8 if k == 0 else 2
                            for nm in ("Kpad", "Wpad", "Qpad", "tw"):
                                sl[nm] = (sb(SBK, "g_%s%d" % (nm, k), [128, npad, 128], BF16), Tok())
                            slots.append(sl)
                        opf = Ring([(sb(SBK, "g_of%d" % i, [128, 128], F32), Tok()) for i in range(3)])
                        opb = Ring([(sb(SBK, "g_ob%d" % i, [128, 128], BF16), Tok()) for i in range(2)])

                        def pre_state(bi, c0, BS, kind, sl):
                            if kind == "P":
                                ncb, C, nlev = 2, 64, 5
                                pen = cst[:, C_PENP:C_PENP + 128]
                                cmask = cst[:, C_CM2:C_CM2 + 256].rearrange("p (c t) -> p c t", c=2)
                                rmk = cst[:, C_RM2:C_RM2 + 2]
                            else:
                                ncb, C, nlev = 8, 8, 2
                                pen = cst[0:64, C_PENS:C_PENS + 64]
                                cmask = cst[:, C_CM8:C_CM8 + 512].rearrange("p (c t) -> p c t", c=8)
                                rmk = cst[0:64, C_RM8:C_RM8 + 8]
                            cols = slice(c0, c0 + BS)
                            X, Y = sl["X"], sl["Y"]
                            Kpad, kp_tok = sl["Kpad"]
                            Wpad, wp_tok = sl["Wpad"]
                            Qpad, qp_tok = sl["Qpad"]
                            tw, twtok = sl["tw"]
                            Gcol = dtk[0:BS, bi, 1, h:h + 1]
                            beta_c = dtk[0:BS, bi, 0, h:h + 1]
                            eGL_c = ex[0:BS, bi, 1, h:h + 1]
                            bG_c = ex[0:BS, bi, 2, h:h + 1]
                            nbeta_c = ex[0:BS, bi, 3, h:h + 1]
                            esel_h = cst[0:8, C_ESEL + h * 128:C_ESEL + (h + 1) * 128]
                            gps, gptok = psblk.get()
                            mm(gps[:, 0:BS], esel_h, dG[:, cols], R=[dtok_, cst_tok], W=[gptok])
                            yield
                            r_, rtok = X[0]
                            stt(r_[0:BS, 0:BS], gps[0:BS, 0:BS], Gcol, pen, ALU.subtract, ALU.max, R=[gptok, dk_tok, cst_tok], W=[rtok])
                            eGr, egtok = X[2]
                            act(eGr[:, 0:BS], gps[:, 0:BS], AF.Exp, R=[gptok], W=[egtok])
                            yield
                            Ds, dstok = X[1]
                            act(Ds[0:BS, 0:BS], r_[0:BS, 0:BS], AF.Exp, scale=-1.0, R=[rtok], W=[dstok])
                            tt(tw[:, 0:ncb, 0:BS], cmask[:, :, 0:BS], eGr[:, 0:BS].unsqueeze(1).broadcast_to([128, ncb, BS]), ALU.mult,
                               R=[cst_tok, egtok], W=[twtok])
                            yield
                            tt(Qpad[:, 0:ncb, 0:BS], tw[:, 0:ncb, 0:BS], qT[:, cols].unsqueeze(1).broadcast_to([128, ncb, BS]), ALU.mult,
                               R=[twtok, q_tok], W=[qp_tok])
                            dps, dptok = psblk.get()
                            tr(dps[0:BS, 0:BS], Ds[0:BS, 0:BS], ident[0:BS, 0:BS], R=[dstok, cst_tok], W=[dptok])
                            kkps, kktok = psblk.get()
                            mm(kkps[0:BS, 0:BS], kT[:, cols], kT[:, cols], R=[k_tok], W=[kktok])
                            qkps, qktok = psblk.get()
                            mm(qkps[0:BS, 0:BS], kT[:, cols], qT[:, cols], R=[k_tok, q_tok], W=[qktok])
                            yield
                            DmT, dmtok = X[0]
                            tt(DmT[0:BS, 0:BS], dps[0:BS, 0:BS], ident[0:BS, 0:BS], ALU.add, R=[dptok, cst_tok], W=[dmtok])
                            B, btok = X[2]
                            stt(B[0:BS, 0:BS], kkps[0:BS, 0:BS], nbeta_c, Ds[0:BS, 0:BS], ALU.mult, ALU.mult, R=[kktok, dk_tok, dstok], W=[btok])
                            yield
                            attnT, attok = Y[0]
                            tt(attnT[0:BS, 0:BS], qkps[0:BS, 0:BS], DmT[0:BS, 0:BS], ALU.mult, R=[qktok, dmtok], W=[attok])
                            btps, btptok = psblk.get()
                            tr(btps[0:BS, 0:BS], B[0:BS, 0:BS], ident[0:BS, 0:BS], R=[btok, cst_tok], W=[btptok])
                            yield
                            Bt, bttok = X[3]
                            act(Bt[0:BS, 0:BS], btps[0:BS, 0:BS], AF.Copy, R=[btptok], W=[bttok])
                            TT, tttok = X[4]
                            tt(TT[0:BS, 0:BS], btps[0:BS, 0:BS], ident[0:BS, 0:BS], ALU.add, R=[btptok, cst_tok], W=[tttok])
                            yield
                            cur = (X[2], X[3])
                            nxt = (X[5], X[1])
                            for lev in range(nlev):
                                last = (lev == nlev - 1)
                                (B, btok), (Bt, bttok) = cur
                                (Bn, bntok), (Btn, btntok) = nxt
                                p2, p2tok = psblk.get()
                                mm(p2[0:BS, 0:BS], Bt[0:BS, 0:BS], B[0:BS, 0:BS], R=[bttok, btok], W=[p2tok])
                                if not last:
                                    p1, p1tok = psblk.get()
                                    mm(p1[0:BS, 0:BS], B[0:BS, 0:BS], Bt[0:BS, 0:BS], R=[bttok, btok], W=[p1tok])
                                yield
                                cp(Bn[0:BS, 0:BS], p2[0:BS, 0:BS], R=[p2tok], W=[bntok])
                                if not last:
                                    act(Btn[0:BS, 0:BS], p1[0:BS, 0:BS], AF.Copy, R=[p1tok], W=[btntok])
                                yield
                                p3, p3tok = psblk.get()
                                mm(p3[0:BS, 0:BS], Bn[0:BS, 0:BS], TT[0:BS, 0:BS], R=[bntok, tttok], W=[p3tok])
                                yield
                                tt(TT[0:BS, 0:BS], p3[0:BS, 0:BS], TT[0:BS, 0:BS], ALU.add, R=[p3tok, tttok], W=[tttok])
                                yield
                                cur, nxt = nxt, cur
                            TTb, ttbtok = Y[1]
                            act(TTb[0:BS, 0:BS], TT[0:BS, 0:BS], AF.Copy, R=[tttok], W=[ttbtok])
                            pk, pktok = pbf.get()
                            tr(pk[0:BS, :], kT[:, cols], ident_bf[:], R=[k_tok, misc_tok], W=[pktok])
                            pvv, pvtok = pbf.get()
                            tr(pvv[0:BS, :], vT[:, cols], ident_bf[:], R=[v_tok, misc_tok], W=[pvtok])
                            yield
                            vb, vbtok = Y[2]
                            act(vb[0:BS, :], pvv[0:BS, :], AF.Copy, scale=beta_c, R=[pvtok, dk_tok], W=[vbtok])
                            kbg, kbgtok = Y[3]
                            act(kbg[0:BS, :], pk[0:BS, :], AF.Copy, scale=bG_c, R=[pktok, dk_tok], W=[kbgtok])
                            kdm, kdmtok = sl["kdm"]
                            ts(kdm[0:BS, 0:ncb], rmk, eGL_c, None, ALU.mult, R=[cst_tok, dk_tok], W=[kdmtok])
                            tt(Kpad[0:BS, 0:ncb, :], pk[0:BS, :].unsqueeze(1).broadcast_to([BS, ncb, 128]),
                               kdm[0:BS, 0:ncb].unsqueeze(2).broadcast_to([BS, ncb, 128]), ALU.mult, R=[pktok, kdmtok], W=[kp_tok])
                            yield
                            wps, wptok = psblk.get()
                            mm(wps[:, 0:BS], kbg[0:BS, :], TTb[0:BS, 0:BS], R=[kbgtok, ttbtok], W=[wptok])
                            yield
                            stt(Wpad[:, 0:ncb, 0:BS], wps[:, 0:BS].unsqueeze(1).broadcast_to([128, ncb, BS]), -1.0, cmask[:, :, 0:BS],
                                ALU.mult, ALU.mult, R=[wptok, cst_tok], W=[wp_tok])

                        def state_part(bi, c0, BS, kind, sl):
                            if kind == "P":
                                ncb, C = 2, 64
                            else:
                                ncb, C = 8, 8
                            Y = sl["Y"]
                            attnT, attok = Y[0]
                            TTb, ttbtok = Y[1]
                            vb, vbtok = Y[2]
                            Kpad, kp_tok = sl["Kpad"]
                            Wpad, wp_tok = sl["Wpad"]
                            Qpad, qp_tok = sl["Qpad"]
                            if kind == "P":
                                for c in range(ncb):
                                    rows = slice(c * C, (c + 1) * C)
                                    mm(u_ps[rows, 0:128], TTb[rows, rows], vb[rows, :], start=True, stop=False, R=[ttbtok, vbtok], W=[u_ptok])
                                    mm(u_ps[rows, 0:128], Wpad[:, c, rows], Sbf[:], start=False, stop=True, R=[wp_tok, Sbf_tok], W=[u_ptok])
                                    act(u_sb[rows, :], u_ps[rows, 0:128], AF.Copy, R=[u_ptok], W=[usb_tok])
                                    mm(o_ps[rows, 0:128], Qpad[:, c, rows], Sbf[:], start=True, stop=False, R=[qp_tok, Sbf_tok], W=[o_ptok])
                                    mm(o_ps[rows, 0:128], attnT[rows, rows], u_sb[rows, :], start=False, stop=True, R=[attok, usb_tok], W=[o_ptok])
                                    sps, sptok = psblk.get()
                                    mm(sps[:, :], Kpad[0:BS, c, :], u_sb[0:BS, :], R=[kp_tok, usb_tok], W=[sptok])
                                    stt(Sf_all[:, h, :], Sf_all[:, h, :], glb[:, h, (bi * 2 + c):(bi * 2 + c) + 1], sps[:, :], ALU.mult, ALU.add,
                                        R=[sptok, dk_tok], W=[Sf_tok[h]])
                                    act(Sbf[:], Sf_all[:, h, :], AF.Copy, R=[Sf_tok[h]], W=[Sbf_tok])
                            else:
                                mm(u_ps[0:BS, 0:128], TTb[0:BS, 0:BS], vb[0:BS, :], start=True, stop=False, R=[ttbtok, vbtok], W=[u_ptok])
                                for c in range(ncb):
                                    mm(u_ps[0:BS, 0:128], Wpad[:, c, 0:BS], Ssb[:, c, :], start=False, stop=(c == ncb - 1), R=[wp_tok, Ssb_tok], W=[u_ptok])
                                act(u_sb[0:BS, :], u_ps[0:BS, 0:128], AF.Copy, R=[u_ptok], W=[usb_tok])
                                for c in range(ncb):
                                    mm(o_ps[0:BS, 0:128], Qpad[:, c, 0:BS], Ssb[:, c, :], start=(c == 0), stop=False, R=[qp_tok, Ssb_tok], W=[o_ptok])
                                mm(o_ps[0:BS, 0:128], attnT[0:BS, 0:BS], u_sb[0:BS, :], start=False, stop=True, R=[attok, usb_tok], W=[o_ptok])
                                sl2 = [pbig.get(), pbig.get()]
                                for c in range(ncb):
                                    ps, ptok = sl2[c // 4]
                                    mm(ps[:, (c % 4) * 128:(c % 4) * 128 + 128], Kpad[0:BS, c, :], u_sb[0:BS, :], R=[kp_tok, usb_tok], W=[ptok])
                                tt(Sn[:], Ss[:], glb[:, h, 16:24].unsqueeze(2).broadcast_to([128, 8, 128]), ALU.mult, R=[Ss_tok, dk_tok], W=[Sn_tok])
                                for hb in range(2):
                                    ps, ptok = sl2[hb]
                                    tt(Sn[:, hb * 4:hb * 4 + 4, :], Sn[:, hb * 4:hb * 4 + 4, :], ps[:, :].rearrange("p (s v) -> p s v", v=128), ALU.add,
                                       R=[ptok], W=[Sn_tok])
                                K.dma("sp", sgdn_o[j, SPP * ph:SPP * ph + SPP, h].rearrange("s k v -> k s v"), Sn[:], R=[Sn_tok], semtok=Sn_tok)
                            o_post(l, None, BS, h, c0, oT, otoks[min(c0 // 512, 2)], sgT, sg_tok, opf, opb)

                        blks = blocks()
                        groups = [[0, 1, 2, 3], [4, 5, 6, 7], [8]]
                        for grp in groups:
                            stage(4.06 + 0.1 * h + 0.001 * grp[0])
                            gens = [pre_state(bi, blks[bi][0], blks[bi][1], blks[bi][2], slots[k]) for k, bi in enumerate(grp)]
                            alive = gens
                            while alive:
                                nx = []
                                for g in alive:
                                    try:
                                        next(g)
                                        nx.append(g)
                                    except StopIteration:
                                        pass
                                alive = nx
                            for k, bi in enumerate(grp):
                                state_part(bi, blks[bi][0], blks[bi][1], blks[bi][2], slots[k])
                        if ph == 1:
                            K.dma("sp", pgdn_d[j, h], Sf_all[:, h, :], R=[Sf_tok[h]], semtok=Sf_tok[h])
                        K.barrier()
                K.dma("sp", gco_d[j, ph], gco[:].rearrange("p a s r -> p (a s r)"), R=[gco_tok], semtok=gco_tok)
                K.barrier()

        def hgrn_phase(l, ph, hT, htoks, oT, otoks, S):
            j = l // 2
            Win = hgrn_w_in[j]
            with contextlib.ExitStack() as S1:
                rmask = sb(S1, "h_rm", [128, TPH], F32)
                rm_tok = Tok()
                eG = sb(S1, "h_eG", [128, TPH], F32)
                eG_tok = Tok()
                qgT = sb(S1, "h_qgT", [128, TPH], BF16)
                kiT = sb(S1, "h_kiT", [128, TPH], BF16)
                kdT = sb(S1, "h_kdT", [128, TPH], BF16)
                vT = sb(S1, "h_vT", [128, TPH], BF16)
                sgT = sb(S1, "h_sgT", [128, TPH], BF16)
                qg_tok, ki_tok, kd_tok, v_tok, sg_tok = Tok(), Tok(), Tok(), Tok(), Tok()
                Sbf = sb(S1, "h_Sbf", [128, 128], BF16)
                Sbf_tok = Tok()
                Ss = sb(S1, "h_Ss", [128, SPP, 128], F32)
                Ss_tok = Ss_ptok
                Ssb = sb(S1, "h_Ssb", [128, SPP, 128], BF16)
                Ssb_tok = Tok()
                Sn, Sn_tok = Ss, Ss_tok
                memset(rmask[:], 1.0, W=[rm_tok])
                memset(rmask[:, 0:TP].rearrange("p (c t) -> p c t", t=32)[:, :, 0:1], 0.0, W=[rm_tok])
                memset(rmask[:, TP:TPH].rearrange("p (c t) -> p c t", t=8)[:, :, 0:1], 0.0, W=[rm_tok])
                psblk = Ring([(psF[i % 5][:, (i // 5) * 128:(i // 5) * 128 + 128], bkt[i % 5]) for i in range(20)])
                obank = Ring([(psF[5], bkt[5]), (psF[6], bkt[6])])
                for h in range(H):
                    stage(7 + 0.1 * h)
                    lb = lbT[:, h, 0:1]
                    oml = lbT[:, h, 1:2]
                    with contextlib.ExitStack() as SPL:
                        fa = sb(SPL, "h_fa", [128, TPH], F32)
                        fa_tok = Tok()
                        fk = sb(SPL, "h_fk", [128, TPH], F32)
                        fk_tok = Tok()
                        fG = sb(SPL, "h_fG", [128, TPH], F32)
                        fG_tok = Tok()
                        fe = sb(SPL, "h_fe", [128, TPH], F32)
                        fe_tok = Tok()
                        sq_ = sb(SPL, "h_sq", [128, TPH], F32)
                        sq_tok = Tok()
                        for role in range(4):
                            wt, wtok = wload(Win, role * 1024 + h * 128)
                            for ti in range(3):
                                ps, ptok, n = proj_tile(wt, wtok, hT, htoks, ti, ring=pwide)
                                t0 = TILES[ti][0]
                                if role == 0:
                                    act(sq_[:, t0:t0 + n], ps[:, 0:n], AF.Silu, R=[ptok], W=[sq_tok])
                                elif role == 1:
                                    act(fa[:, t0:t0 + n], ps[:, 0:n], AF.Sigmoid, R=[ptok], W=[fa_tok])
                                elif role == 2:
                                    act(vT[:, t0:t0 + n], ps[:, 0:n], AF.Copy, R=[ptok], W=[v_tok])
                                else:
                                    act(sgT[:, t0:t0 + n], ps[:, 0:n], AF.Silu, R=[ptok], W=[sg_tok])
                        ts(fa[:], fa[:], oml, lb, ALU.mult, ALU.add, R=[fa_tok, lb_tok], W=[fa_tok])
                        ts(fk[:], fa[:], -1.0, 1.0, ALU.mult, ALU.add, R=[fa_tok], W=[fk_tok])
                        act(fa[:], fa[:], AF.Ln, R=[fa_tok], W=[fa_tok])
                        K.op("dve", lambda e: e.tensor_tensor_scan(out=fG[:], data0=rmask[:], data1=fa[:], initial=0.0, op0=ALU.mult, op1=ALU.add),
                             [fa_tok, rm_tok], [fG_tok])
                        act(eG[:], fG[:], AF.Exp, R=[fG_tok], W=[eG_tok])
                        tt(qgT[:], sq_[:], eG[:], ALU.mult, R=[sq_tok, eG_tok], W=[qg_tok])
                        act(fe[:], fG[:], AF.Exp, scale=-1.0, R=[fG_tok], W=[fe_tok])
                        tt(kiT[:], fk[:], fe[:], ALU.mult, R=[fk_tok, fe_tok], W=[ki_tok])
                        gp = fG[:, 0:TP].rearrange("p (c t) -> p c t", t=32)
                        tt(fa[:, 0:TP].rearrange("p (c t) -> p c t", t=32), gp[:, :, 31:32].broadcast_to([128, 32, 32]), gp, ALU.subtract,
                           R=[fG_tok], W=[fa_tok])
                        gs = fG[:, TP:TPH].rearrange("p (c t) -> p c t", t=8)
                        tt(fa[:, TP:TPH].rearrange("p (c t) -> p c t", t=8), gs[:, :, 7:8].broadcast_to([128, 8, 8]), gs, ALU.subtract,
                           R=[fG_tok], W=[fa_tok])
                        act(fe[:], fa[:], AF.Exp, R=[fa_tok], W=[fe_tok])
                        tt(kdT[:], fk[:], fe[:], ALU.mult, R=[fk_tok, fe_tok], W=[kd_tok])
                        K.barrier()

                    stage(7.05 + 0.1 * h)
                    K.dma("pool", Ss[:], shgrn_d[j, SPP * ph:SPP * ph + SPP, h].rearrange("s k v -> k s v"), W=[Ss_tok], semtok=Ss_tok)
                    act(Ssb[:], Ss[:], AF.Copy, R=[Ss_tok], W=[Ssb_tok])
                    if ph == 0:
                        memset(Sf_all[:, h, :], 0.0, W=[Sf_tok[h]])
                    cp(Sbf[:], Sf_all[:, h, :], R=[Sf_tok[h]], W=[Sbf_tok])

                    with contextlib.ExitStack() as SBK:
                        blks = blocks()
                        bufs = []
                        for bi, (c0, BS, kind) in enumerate(blks):
                            npad = 4 if kind == "P" else 8
                            bufs.append(dict(
                                attnT=(sb(SBK, "h_at%d" % bi, [128, 128], BF16), Tok()),
                                vtk=(sb(SBK, "h_vt%d" % bi, [128, 128], BF16), Tok()),
                                Kpad=(sb(SBK, "h_kp%d" % bi, [128, npad, 128], BF16), Tok()),
                                Qpad=(sb(SBK, "h_qp%d" % bi, [128, npad, 128], BF16), Tok())))
                        opf = Ring([(sb(SBK, "h_of%d" % i, [128, 128], F32), Tok()) for i in range(4)])
                        opb = Ring([(sb(SBK, "h_ob%d" % i, [128, 128], BF16), Tok()) for i in range(2)])

                        def consts_for(kind):
                            if kind == "P":
                                return (4, 32, cst[:, C_MIP:C_MIP + 128],
                                        cst[:, C_CM4:C_CM4 + 512].rearrange("p (c t) -> p c t", c=4), cst[:, C_RM4:C_RM4 + 4])
                            return (8, 8, cst[0:64, C_MIS:C_MIS + 64],
                                    cst[:, C_CM8:C_CM8 + 512].rearrange("p (c t) -> p c t", c=8), cst[0:64, C_RM8:C_RM8 + 8])

                        for bi, (c0, BS, kind) in enumerate(blks):
                            ncb, C, mi, cmask, rmk = consts_for(kind)
                            bf = bufs[bi]
                            cols = slice(c0, c0 + BS)
                            aps, aptok = psblk.get()
                            mm(aps[0:BS, 0:BS], kiT[:, cols], qgT[:, cols], R=[ki_tok, qg_tok], W=[aptok])
                            attnT, attok = bf["attnT"]
                            tt(attnT[0:BS, 0:BS], aps[0:BS, 0:BS], mi, ALU.mult, R=[aptok, cst_tok], W=[attok])
                            pvv, pvtok = pbf.get()
                            tr(pvv[0:BS, :], vT[:, cols], ident_bf[:], R=[v_tok, misc_tok], W=[pvtok])
                            vtk, vtktok = bf["vtk"]
                            act(vtk[0:BS, :], pvv[0:BS, :], AF.Copy, R=[pvtok], W=[vtktok])
                            pk, pktok = pbf.get()
                            tr(pk[0:BS, :], kdT[:, cols], ident_bf[:], R=[kd_tok, misc_tok], W=[pktok])
                            Kpad, kp_tok = bf["Kpad"]
                            tt(Kpad[0:BS, 0:ncb, :], pk[0:BS, :].unsqueeze(1).broadcast_to([BS, ncb, 128]),
                               rmk.unsqueeze(2).broadcast_to([BS, ncb, 128]), ALU.mult, R=[pktok, cst_tok], W=[kp_tok])
                            Qpad, qp_tok = bf["Qpad"]
                            tt(Qpad[:, 0:ncb, 0:BS], cmask[:, :, 0:BS], qgT[:, cols].unsqueeze(1).broadcast_to([128, ncb, BS]), ALU.mult,
                               R=[cst_tok, qg_tok], W=[qp_tok])

                        pending = None
                        for bi, (c0, BS, kind) in enumerate(blks):
                            ncb, C, mi, cmask, rmk = consts_for(kind)
                            bf = bufs[bi]
                            attnT, attok = bf["attnT"]
                            vtk, vtktok = bf["vtk"]
                            Kpad, kp_tok = bf["Kpad"]
                            Qpad, qp_tok = bf["Qpad"]
                            ops_, optok = obank.get()
                            if kind == "P":
                                for c in range(ncb):
                                    mm(ops_[0:BS, 0:128], Qpad[:, c, 0:BS], Sbf[:], start=(c == 0), stop=False, R=[qp_tok, Sbf_tok], W=[optok])
                                    sps, sptok = psblk.get()
                                    mm(sps[:, :], Kpad[0:BS, c, :], vtk[0:BS, :], R=[kp_tok, vtktok], W=[sptok])
                                    ce = c0 + (c + 1) * C - 1
                                    stt(Sbf[:], Sf_all[:, h, :], eG[:, ce:ce + 1], sps[:, :], ALU.mult, ALU.add,
                                        R=[sptok, eG_tok, Sf_tok[h]], W=[Sbf_tok])
                                    stt(Sf_all[:, h, :], Sf_all[:, h, :], eG[:, ce:ce + 1], sps[:, :], ALU.mult, ALU.add,
                                        R=[sptok, eG_tok], W=[Sf_tok[h]])
                                mm(ops_[0:BS, 0:128], attnT[0:BS, 0:BS], vtk[0:BS, :], start=False, stop=True, R=[attok, vtktok], W=[optok])
                            else:
                                for c in range(ncb):
                                    mm(ops_[0:BS, 0:128], Qpad[:, c, 0:BS], Ssb[:, c, :], start=(c == 0), stop=False, R=[qp_tok, Ssb_tok], W=[optok])
                                mm(ops_[0:BS, 0:128], attnT[0:BS, 0:BS], vtk[0:BS, :], start=False, stop=True, R=[attok, vtktok], W=[optok])
                                sl2 = [pbig.get(), pbig.get()]
                                for c in range(ncb):
                                    ps, ptok = sl2[c // 4]
                                    mm(ps[:, (c % 4) * 128:(c % 4) * 128 + 128], Kpad[0:BS, c, :], vtk[0:BS, :], R=[kp_tok, vtktok], W=[ptok])
                                ege = eG[:, TP:TPH].rearrange("p (s t) -> p s t", t=8)[:, :, 7:8]
                                tt(Sn[:], Ss[:], ege.broadcast_to([128, 8, 128]), ALU.mult, R=[Ss_tok, eG_tok], W=[Sn_tok])
                                for hb in range(2):
                                    ps, ptok = sl2[hb]
                                    tt(Sn[:, hb * 4:hb * 4 + 4, :], Sn[:, hb * 4:hb * 4 + 4, :], ps[:, :].rearrange("p (s v) -> p s v", v=128), ALU.add,
                                       R=[ptok], W=[Sn_tok])
                                K.dma("sp", shgrn_o[j, SPP * ph:SPP * ph + SPP, h].rearrange("s k v -> k s v"), Sn[:], R=[Sn_tok], semtok=Sn_tok)
                            if pending is not None:
                                o_post(*pending)
                            pending = (l, (ops_, optok), BS, h, c0, oT, otoks[min(c0 // 512, 2)], sgT, sg_tok, opf, opb)
                        o_post(*pending)
                        if ph == 1:
                            K.dma("sp", phgrn_d[j, h], Sf_all[:, h, :], R=[Sf_tok[h]], semtok=Sf_tok[h])
                        K.barrier()
                K.barrier()

        def ffn_phase(l, ph):
            A2, B2, G2 = modA[3], modA[4], modA[5]
            with contextlib.ExitStack() as SF:
                aT = sb(SF, "f_aT", [128, NFF, TPH], BF16)
                a_toks = [Tok() for _ in range(3)]
                fci = sb(SF, "f_fci", [128, NFF, 8, 2], F32)
                fci_tok = fci_ptok
                fco = sb(SF, "f_fco", [128, NFF, 9, 2], F32)
                fco_tok = fco_ptok
                K.dma("sp", fci[:].rearrange("p a s r -> p (a s r)"), fci_d[l, ph], W=[fci_tok], semtok=fci_tok)
                memset(fco[:], 0.0, W=[fco_tok])
                with contextlib.ExitStack() as S1:
                    hT = sb(S1, "f_hT", [128, KC, TPH], BF16)
                    htoks = [Tok() for _ in range(3)]
                    with contextlib.ExitStack() as SPN:
                        prenorm(ph, A2, B2, hT, htoks, SPN)
                        K.barrier()
                    gpre_r = Ring([(sb(S1, "f_gpre%d" % i, [128, 2 + TP + SPP * 10], F32), Tok()) for i in range(2)])
                    up_r = Ring([(sb(S1, "f_up%d" % i, [128, TPH], F32), Tok()) for i in range(2)])
                    gc_r = Ring([(sb(S1, "f_gc%d" % i, [128, TPH], F32), Tok()) for i in range(2)])
                    for jj in range(NFF):
                        gpre, gp_tok = gpre_r.get()
                        up, up_tok = up_r.get()
                        gc, gc_tok = gc_r.get()
                        pv = gpre[:, 2 + TP:2 + TP + SPP * 10].rearrange("p (s r) -> p s r", r=10)
                        wg, wgtok = wload(ffn_w_gu[l], jj * 128)
                        wu, wutok = wload(ffn_w_gu[l], DFF + jj * 128)
                        if ph == 0:
                            memset(gpre[:, 0:2], 0.0, W=[gp_tok])
                        else:
                            cp(gpre[:, 0:2], fcar[:, jj, :], R=[fcar_tok], W=[gp_tok])
                        cp(pv[:, :, 0:2], fci[:, jj, :, :], R=[fci_tok], W=[gp_tok])
                        for ti in range(3):
                            t0 = TILES[ti][0]
                            ps, ptok, n = proj_tile(wg, wgtok, hT, htoks, ti, ring=pwide)
                            if ti < 2:
                                act(gpre[:, 2 + t0:2 + t0 + n], ps[:, 0:n], AF.Copy, R=[ptok], W=[gp_tok])
                            else:
                                act(pv[:, :, 2:10], ps[:, 0:n].rearrange("p (s r) -> p s r", r=LS), AF.Copy, R=[ptok], W=[gp_tok])
                            ps, ptok, n = proj_tile(wu, wutok, hT, htoks, ti, ring=pwide)
                            act(up[:, t0:t0 + n], ps[:, 0:n], AF.Copy, R=[ptok], W=[up_tok])
                        if ph == 0:
                            cp(fcar[:, jj, :], gpre[:, TP:TP + 2], R=[gp_tok], W=[fcar_tok])
                        else:
                            cp(fco[:, jj, 8, :], gpre[:, TP:TP + 2], R=[gp_tok], W=[fco_tok])
                        cp(fco[:, jj, 0:8, :], pv[:, :, 8:10], R=[gp_tok], W=[fco_tok])
                        wc = prm[:, l, P_FCW + jj * 3:P_FCW + jj * 3 + 3]
                        bc = prm[:, l, P_FCB + jj:P_FCB + jj + 1]
                        gcs = gc[:, TP:TPH].rearrange("p (s r) -> p s r", r=LS)
                        ts(gc[:, 0:TP], gpre[:, 0:TP], wc[:, 0:1], bc, ALU.mult, ALU.add, R=[gp_tok, prm_tok], W=[gc_tok])
                        ts(gcs, pv[:, :, 0:8], wc[:, 0:1], bc, ALU.mult, ALU.add, R=[gp_tok, prm_tok], W=[gc_tok])
                        for tap in range(1, 3):
                            stt(gc[:, 0:TP], gpre[:, tap:tap + TP], wc[:, tap:tap + 1], gc[:, 0:TP], ALU.mult, ALU.add,
                                R=[gp_tok, prm_tok], W=[gc_tok])
                            stt(gcs, pv[:, :, tap:tap + 8], wc[:, tap:tap + 1], gcs, ALU.mult, ALU.add, R=[gp_tok, prm_tok], W=[gc_tok])
                        act(gc[:], gc[:], AF.Silu, R=[gc_tok], W=[gc_tok])
                        for ti, (t0, n) in enumerate(TILES):
                            tt(aT[:, jj, t0:t0 + n], gc[:, t0:t0 + n], up[:, t0:t0 + n], ALU.mult, R=[gc_tok, up_tok], W=[a_toks[ti]])
                    K.dma("sp", fco_d[l, ph], fco[:].rearrange("p a s r -> p (a s r)"), R=[fco_tok], semtok=fco_tok)
                    K.barrier()
                with contextlib.ExitStack() as S2:
                    y = sb(S2, "f_y", [128, KC, TPH], F32)
                    ytok = Tok()
                    wdn = Ring([(sb(S2, "f_wd%d" % i, [128, NFF, 128], BF16), wdn_ptoks[i]) for i in range(2)])
                    sqt = (sb(S2, "f_sq", [128, KC, 256], BF16), Tok())
                    rs = (sb(S2, "f_rs", [128, 256], F32), Tok())
                    tmps = Ring([(sb(S2, "f_t%d" % i, [128, 256], F32), Tok()) for i in range(2)])
                    for oc in range(KC):
                        wt, wtok = wdn.get()
                        K.dma("pool", wt[:], ffn_w_down[l][:, oc * 128:(oc + 1) * 128].rearrange("(j p) n -> p j n", p=128), W=[wtok], semtok=wtok)
                        for ti, (t0, n) in enumerate(TILES):
                            ps, ptok = pwide.get()
                            for jj in range(NFF):
                                mm(ps[:, 0:n], wt[:, jj, :], aT[:, jj, t0:t0 + n], start=(jj == 0), stop=(jj == NFF - 1),
                                   R=[wtok, a_toks[ti]], W=[ptok])
                            cp(y[:, oc, t0:t0 + n], ps[:, 0:n], R=[ptok], W=[ytok])
                    for ti, (t0, n) in enumerate([(0, 256), (256, 256), (512, 256), (768, 256), (1024, 64)]):
                        rstd_fm(y[:, :, t0:t0 + n], n, [ytok], sqt, rs)
                        for kc in range(KC):
                            tmp, ttok = tmps.get()
                            tt(tmp[:, 0:n], y[:, kc, t0:t0 + n], rs[0][:, 0:n], ALU.mult, R=[ytok, rs[1]], W=[ttok])
                            if t0 < TP:
                                stt(xT[:, kc, ph, t0:t0 + n], tmp[:, 0:n], G2[:, kc, 0:1], xT[:, kc, ph, t0:t0 + n], ALU.mult, ALU.add,
                                    R=[ttok, mod_tok], W=[xtok[ph]])
                            else:
                                sc = seqcols(ph)
                                v3 = tmp[:, 0:n].rearrange("p (s j) -> p s j", j=LS)
                                tt(v3, v3, G2[:, kc, sc].unsqueeze(2).broadcast_to([128, SPP, LS]), ALU.mult, R=[ttok, mod_tok], W=[ttok])
                                x3 = xT[:, kc, ph, t0:t0 + n].rearrange("p (s j) -> p s j", j=LS)
                                tt(x3, x3, v3, ALU.add, R=[ttok], W=[xtok[ph]])
                    K.barrier()

        def main_program():
            for l in range(depth):
                with contextlib.ExitStack() as SL:
                    stage(1)
                    adaln(l, SL)
                    if l % 2 == 1:
                        jj_ = l // 2
                        if jj_ == 0:
                            memset(lbT[:, :, 0:1], 0.0, W=[lb_tok])
                            memset(lbT[:, :, 1:2], 1.0, W=[lb_tok])
                        else:
                            hl = prm[:, l, P_HLB:P_HLB + 16].rearrange("p (h t) -> p h t", t=2)
                            tt(lbT[:, :, 0:1], hl[:, :, 1:2], hl[:, :, 0:1], ALU.subtract, R=[prm_tok], W=[lb_tok])
                            act(lbT[:, :, 0:1], lbT[:, :, 0:1], AF.Sigmoid, R=[lb_tok], W=[lb_tok])
                            ts(lbT[:, :, 1:2], lbT[:, :, 0:1], -1.0, 1.0, ALU.mult, ALU.add, R=[lb_tok], W=[lb_tok])
                    K.barrier()
                for ph in range(2):
                    with contextlib.ExitStack() as SA:
                        hT = sb(SA, "m_hT", [128, KC, TPH], BF16)
                        htoks = [Tok() for _ in range(3)]
                        oT = sb(SA, "m_oT", [128, KC, TPH], BF16)
                        otoks = [Tok() for _ in range(3)]
                        with contextlib.ExitStack() as SP:
                            stage(2)
                            prenorm(ph, modA[0], modA[1], hT, htoks, SP)
                            K.barrier()
                        with contextlib.ExitStack() as SM:
                            if l % 2 == 0:
                                gdn_phase(l, ph, hT, htoks, oT, otoks, SM)
                            else:
                                hgrn_phase(l, ph, hT, htoks, oT, otoks, SM)
                            K.barrier()
                        with contextlib.ExitStack() as SO:
                            stage(5)
                            Wout = gdn_w_out[l // 2] if l % 2 == 0 else hgrn_w_out[l // 2]
                            outproj_postnorm(l, ph, Wout, oT, otoks, modA[2], SO)
                            K.barrier()
                    stage(6)
                    ffn_phase(l, ph)
            for ph in range(2):
                K.dma("sp", yT_d.rearrange("p (k h t) -> p k h t", k=KC, h=2)[:, :, ph, :], xT[:, :, ph, :], R=[xtok[ph]], semtok=xtok[ph])

        try:
            main_program()
        except StopBuild:
            K.barrier()
        K.final_wait("sp")

        with nc.Block() as block:
            K.replay(block)
    return nc


def _consts():
    c = np.zeros((128, NCST), np.float32)
    c[:, C_IDENT:C_IDENT + 128] = np.eye(128, dtype=np.float32)
    i = np.arange(128)[:, None]
    jx = np.arange(128)[None, :]
    BIG = 1.0e4
    valid = (i // 64 == jx // 64) & (i > jx)
    c[:, C_PENP:C_PENP + 128] = np.where(valid, 0.0, BIG)
    i8 = np.arange(64)[:, None]
    j8 = np.arange(64)[None, :]
    valid = (i8 // 8 == j8 // 8) & (i8 > j8)
    c[0:64, C_PENS:C_PENS + 64] = np.where(valid, 0.0, BIG)
    for ncb, off, bs in ((2, C_CM2, 128), (4, C_CM4, 128), (8, C_CM8, 64)):
        C = bs // ncb
        m = np.zeros((ncb, bs), np.float32)
        for cc in range(ncb):
            m[cc, cc * C:(cc + 1) * C] = 1.0
        c[:, off:off + ncb * bs] = m.reshape(1, -1)
    for ncb, off, bs in ((2, C_RM2, 128), (4, C_RM4, 128), (8, C_RM8, 64)):
        C = bs // ncb
        m = np.zeros((bs, ncb), np.float32)
        for cc in range(ncb):
            m[cc * C:(cc + 1) * C, cc] = 1.0
        c[0:bs, off:off + ncb] = m
    c[:, C_MIP:C_MIP + 128] = ((i // 32 == jx // 32) & (i <= jx)).astype(np.float32)
    c[0:64, C_MIS:C_MIS + 64] = ((i8 // 8 == j8 // 8) & (i8 <= j8)).astype(np.float32)
    e = np.zeros((8, 8, 128), np.float32)
    for h in range(8):
        e[h, h, :] = 1.0
    c[0:8, C_ESEL:C_ESEL + 1024] = e.reshape(8, 1024)
    return c


def _fm(v):
    sh = v.shape
    k = sh[-1] // 128
    v = v.reshape(sh[:-1] + (k, 128))
    return np.moveaxis(np.moveaxis(v, -1, 0), -1, 1)


_NC_CACHE = {}


def kernel(x_prompt, x_sample, state_gdn, state_gdn_conv, state_hgrn, state_ffn_conv, c_prompt, c_sample,
           ada_w, ada_b, norm_pre_mix, norm_post_mix, norm_pre_ffn, norm_post_ffn,
           gdn_w_in, gdn_conv_w, gdn_conv_b, gdn_a_log, gdn_dt_bias, gdn_norm, gdn_w_out,
           hgrn_lb, hgrn_w_in, hgrn_norm, hgrn_w_out,
           ffn_w_gu, ffn_conv_w, ffn_conv_b, ffn_w_down, _depth=DEPTH, _stage=None):
    f32 = np.float32
    A = lambda a: np.ascontiguousarray(np.asarray(a, dtype=f32))
    x_prompt, x_sample = A(x_prompt), A(x_sample)
    state_gdn, state_gdn_conv, state_hgrn, state_ffn_conv = A(state_gdn), A(state_gdn_conv), A(state_hgrn), A(state_ffn_conv)
    c_prompt, c_sample = A(c_prompt), A(c_sample)
    prm = np.zeros((128, 4, NPRM), f32)
    nrow = np.zeros((128, 4, 128), f32)
    ada_b_, gcw, gcb = A(ada_b), A(gdn_conv_w), A(gdn_conv_b)
    fcw, fcb = A(ffn_conv_w), A(ffn_conv_b)
    hlb = A(hgrn_lb)
    for l in range(4):
        prm[:, l, P_ADAB:P_ADAB + 48] = ada_b_[l].reshape(48, 128).T
        prm[:, l, P_NPRE_MIX:P_NPRE_MIX + 8] = A(norm_pre_mix)[l].reshape(8, 128).T
        prm[:, l, P_NPOST_MIX:P_NPOST_MIX + 8] = A(norm_post_mix)[l].reshape(8, 128).T
        prm[:, l, P_NPRE_FFN:P_NPRE_FFN + 8] = A(norm_pre_ffn)[l].reshape(8, 128).T
        prm[:, l, P_NPOST_FFN:P_NPOST_FFN + 8] = A(norm_post_ffn)[l].reshape(8, 128).T
        j = l // 2
        if l % 2 == 0:
            prm[:, l, P_GCW:P_GCW + 96] = gcw[j].reshape(4, 24, 128).transpose(2, 1, 0).reshape(128, 96)
            prm[:, l, P_GCB:P_GCB + 24] = gcb[j].reshape(24, 128).T
            prm[0:8, l, P_ALOG] = A(gdn_a_log)[j]
            prm[0:8, l, P_DTB] = A(gdn_dt_bias)[j]
            nrow[:, l, :] = A(gdn_norm)[j][None, :]
        else:
            nrow[:, l, :] = A(hgrn_norm)[j][None, :]
        prm[:, l, P_FCW:P_FCW + 66] = fcw[l].reshape(3, 22, 128).transpose(2, 1, 0).reshape(128, 66)
        prm[:, l, P_FCB:P_FCB + 22] = fcb[l].reshape(22, 128).T
        prm[:, l, P_HLB:P_HLB + 16] = hlb.reshape(2, 8, 128).transpose(2, 1, 0).reshape(128, 16)
    cst = _consts()
    shared = dict(prm=prm.reshape(128, -1), nrow=nrow.reshape(128, -1), cst=cst,
                  ada_w=A(ada_w), gdn_w_in=A(gdn_w_in), gdn_w_out=A(gdn_w_out), hgrn_w_in=A(hgrn_w_in),
                  hgrn_w_out=A(hgrn_w_out), ffn_w_gu=A(ffn_w_gu), ffn_w_down=A(ffn_w_down))
    in_maps = []
    for c in range(NCORE):
        xt = np.zeros((128, KC, 2, TPH), f32)
        xp = _fm(x_prompt[c])
        xs = _fm(x_sample[16 * c:16 * c + 16])
        for ph in range(2):
            xt[:, :, ph, 0:TP] = xp[:, :, TP * ph:TP * ph + TP]
            xt[:, :, ph, TP:] = xs[:, :, 8 * ph:8 * ph + 8, :].reshape(128, KC, TS)
        cc = np.concatenate([c_prompt[c:c + 1], c_sample[16 * c:16 * c + 16]], 0)
        cT = _fm(cc)
        gc = _fm(state_gdn_conv[:, 16 * c:16 * c + 16])
        gci = np.zeros((2, 2, 128, 24, 8, 3), f32)
        fc = _fm(state_ffn_conv[:, 16 * c:16 * c + 16])
        fci = np.zeros((4, 2, 128, NFF, 8, 2), f32)
        for ph in range(2):
            gci[:, ph] = gc[:, :, :, 8 * ph:8 * ph + 8, :].transpose(2, 0, 1, 3, 4)
            fci[:, ph] = fc[:, :, :, 8 * ph:8 * ph + 8, :].transpose(2, 0, 1, 3, 4)
        m = dict(shared)
        m.update(xT=xt.reshape(128, -1), cT=np.ascontiguousarray(cT).reshape(128, -1),
                 sgdn=np.ascontiguousarray(state_gdn[:, 16 * c:16 * c + 16]),
                 shgrn=np.ascontiguousarray(state_hgrn[:, 16 * c:16 * c + 16]),
                 gci=gci.reshape(2, 2, 128, -1), fci=fci.reshape(4, 2, 128, -1))
        in_maps.append(m)
    ck = (_depth, _stage)
    if ck not in _NC_CACHE:
        _STAGE_LIMIT[0] = _stage
        _STAGE_LIMIT[1] = False
        _NC_CACHE[ck] = build_nc(_depth)
        _STAGE_LIMIT[0] = None
        _STAGE_LIMIT[1] = False
    nc = _NC_CACHE[ck]
    res = run_bass_kernel_spmd(nc, in_maps, core_ids=list(range(NCORE)))
    R = res.results
    y_prompt = np.zeros((8, 2048, D), f32)
    y_sample = np.zeros((128, 8, D), f32)
    p_gdn = np.zeros((2, 8, H, 128, 128), f32)
    p_hgrn = np.zeros((2, 8, H, 128, 128), f32)
    s_gdn = np.zeros((2, 128, H, 128, 128), f32)
    s_hgrn = np.zeros((2, 128, H, 128, 128), f32)
    p_gconv = np.zeros((2, 8, 3, 3072), f32)
    s_gconv = np.zeros((2, 128, 3, 3072), f32)
    p_fconv = np.zeros((4, 8, 2, DFF), f32)
    s_fconv = np.zeros((4, 128, 2, DFF), f32)
    for c in range(NCORE):
        r = R[c]
        yt = r["yT"].reshape(128, KC, 2, TPH)
        for ph in range(2):
            y_prompt[c, TP * ph:TP * ph + TP] = yt[:, :, ph, 0:TP].transpose(2, 1, 0).reshape(TP, D)
            ys = yt[:, :, ph, TP:].reshape(128, KC, 8, 8)
            y_sample[16 * c + 8 * ph:16 * c + 8 * ph + 8] = ys.transpose(2, 3, 1, 0).reshape(8, 8, D)
        p_gdn[:, c] = r["pgdn"]
        p_hgrn[:, c] = r["phgrn"]
        s_gdn[:, 16 * c:16 * c + 16] = r["sgdn_o"]
        s_hgrn[:, 16 * c:16 * c + 16] = r["shgrn_o"]
        g = r["gco"].reshape(2, 2, 128, 24, 9, 3)
        f = r["fco"].reshape(4, 2, 128, NFF, 9, 2)
        for ph in range(2):
            gs = g[:, ph, :, :, 0:8, :].transpose(0, 3, 4, 2, 1).reshape(2, 8, 3, 3072)
            s_gconv[:, 16 * c + 8 * ph:16 * c + 8 * ph + 8] = gs
            fs = f[:, ph, :, :, 0:8, :].transpose(0, 3, 4, 2, 1).reshape(4, 8, 2, DFF)
            s_fconv[:, 16 * c + 8 * ph:16 * c + 8 * ph + 8] = fs
        p_gconv[:, c] = g[:, 1, :, :, 8, :].transpose(0, 3, 2, 1).reshape(2, 3, 3072)
        p_fconv[:, c] = f[:, 1, :, :, 8, :].transpose(0, 3, 2, 1).reshape(4, 2, DFF)
    return (y_prompt, y_sample, p_gdn, p_gconv, p_hgrn, p_fconv, s_gdn, s_gconv, s_hgrn, s_fconv)
```

```python
import contextlib
import numpy as np
import concourse.bass as bass
import concourse.mybir as mybir
from concourse.bass_utils import run_bass_kernel_spmd

F32 = mybir.dt.float32
BF16 = mybir.dt.bfloat16
AF = mybir.ActivationFunctionType
ALU = mybir.AluOpType

NCORE = 8
D = 1024
KC = 8
H = 8
DFF = 2816
NFF = 22
DEPTH = 4
TP = 1024
SPP = 8
LS = 8
TS = SPP * LS
TPH = TP + TS
TILES = [(0, 512), (512, 512), (1024, 64)]
EPS = 1e-6
NPRM = 48 + 32 + 96 + 24 + 66 + 22 + 16 + 2

P_ADAB = 0
P_NPRE_MIX = 48
P_NPOST_MIX = 56
P_NPRE_FFN = 64
P_NPOST_FFN = 72
P_GCW = 80
P_GCB = 176
P_FCW = 200
P_FCB = 266
P_HLB = 288
P_ALOG = 304
P_DTB = 305

C_IDENT = 0
C_PENP = 128
C_PENS = 256
C_CM2 = 320
C_CM4 = 576
C_CM8 = 1088
C_RM2 = 1600
C_RM4 = 1602
C_RM8 = 1606
C_MIP = 1614
C_MIS = 1742
C_ESEL = 1806
NCST = 1806 + 1024


class Tok:
    __slots__ = ("w", "r", "dsem", "dval", "excl")

    def __init__(self, excl=False):
        self.w = {}
        self.r = {}
        self.dsem = None
        self.dval = 0
        self.excl = excl


class Sched:
    ENGS = ("pe", "act", "dve", "pool", "sp")

    def __init__(self, nc, es):
        self.nc = nc
        self.es = es
        self.q = {k: [] for k in self.ENGS}
        self.n = {k: 0 for k in self.ENGS}
        self.seen = {k: {} for k in self.ENGS}
        self.sem = {}
        for k in ("pe", "act", "dve", "pool"):
            self.sem[k] = es.enter_context(nc.semaphore("s_" + k))
        self.dma_latest = {}
        self.nd = 0

    def _waits(self, en, R, W):
        need = {}

        def add(ev):
            key, sem, val = ev
            if key not in need or need[key][1] < val:
                need[key] = (sem, val)

        for t in R:
            for ev in t.w.values():
                add(ev)
            if t.excl:
                for ev in t.r.values():
                    add(ev)
        for t in W:
            for ev in t.w.values():
                add(ev)
            for ev in t.r.values():
                add(ev)
        out = []
        seen = self.seen[en]
        for key, (sem, val) in need.items():
            if key == en and (en == "pe" or _NO_SAME_ENGINE_WAIT[0]):
                continue
            if seen.get(key, 0) >= val:
                continue
            seen[key] = val
            out.append((sem, val))
        return out

    def op(self, en, fn, R=(), W=()):
        if _STAGE_LIMIT[1]:
            return
        waits = self._waits(en, R, W)
        self.n[en] += 1
        sem = self.sem[en]
        self.q[en].append((waits, fn, sem, 1))
        ev = (en, sem, self.n[en])
        for t in W:
            t.w[en] = ev
        for t in R:
            t.r[en] = ev

    def dma(self, qn, out, in_, R=(), W=(), semtok=None):
        if _STAGE_LIMIT[1]:
            return
        waits = self._waits(qn, R, W)
        t = semtok
        if t.dsem is None:
            t.dsem = {}
        if qn not in t.dsem:
            self.nd += 1
            t.dsem[qn] = [self.es.enter_context(self.nc.semaphore("d%d" % self.nd)), 0]
        ent = t.dsem[qn]
        ent[1] += 16
        dsem, dval = ent[0], ent[1]
        key = "d%d_%s" % (id(t), qn)
        self.q[qn].append((waits, (lambda e: e.dma_start(out=out, in_=in_)), dsem, 16))
        ev = (key, dsem, dval)
        for x in W:
            x.w[key] = ev
        for x in R:
            x.r[key] = ev
        self.dma_latest[key] = (dsem, dval)

    def barrier(self):
        for en in self.ENGS:
            waits = []
            seen = self.seen[en]
            for k in ("pe", "act", "dve", "pool"):
                if k == en and en == "pe":
                    continue
                v = self.n[k]
                if v > 0 and seen.get(k, 0) < v:
                    seen[k] = v
                    waits.append((self.sem[k], v))
            for key, (sem, val) in self.dma_latest.items():
                if seen.get(key, 0) < val:
                    seen[key] = val
                    waits.append((sem, val))
            if waits:
                self.q[en].append((waits, None, None, 0))

    def nsems(self):
        return self.nd + 4

    def final_wait(self, en="sp"):
        waits = []
        for key, (sem, val) in self.dma_latest.items():
            waits.append((sem, val))
        for k in ("pe", "act", "dve", "pool"):
            if self.n[k] > 0:
                waits.append((self.sem[k], self.n[k]))
        self.q[en].append((waits, None, None, 0))

    def replay(self, block):
        q = self.q

        def run(e, lst):
            for waits, fn, sem, inc in lst:
                if fn is None or not _INLINE_WAIT[0] or not waits:
                    for s, v in waits:
                        e.wait_ge(s, v)
                    if fn is not None:
                        fn(e).then_inc(sem, inc)
                else:
                    for s, v in waits[:-1]:
                        e.wait_ge(s, v)
                    s, v = waits[-1]
                    fn(e)._wait_ge(s, v).then_inc(sem, inc)

        @block.tensor
        def _(e):
            run(e, q["pe"])

        @block.scalar
        def _(e):
            run(e, q["act"])

        @block.vector
        def _(e):
            run(e, q["dve"])

        @block.gpsimd
        def _(e):
            run(e, q["pool"])

        @block.sync
        def _(e):
            run(e, q["sp"])


class StopBuild(Exception):
    pass


_STAGE_LIMIT = [None, False]
_NO_SAME_ENGINE_WAIT = [False]
_INLINE_WAIT = [True]


def stage(n):
    if _STAGE_LIMIT[0] is not None and n > _STAGE_LIMIT[0]:
        _STAGE_LIMIT[1] = True


class Ring:
    def __init__(self, items):
        self.items = items
        self.i = 0

    def get(self):
        r = self.items[self.i]
        self.i = (self.i + 1) % len(self.items)
        return r


def build_nc(depth=DEPTH):
    nc = bass.Bass("TRN2", target_bir_lowering=False)

    def din(name, shape):
        return nc.dram_tensor(name, list(shape), F32, kind="ExternalInput").ap()

    def dout(name, shape):
        return nc.dram_tensor(name, list(shape), F32, kind="ExternalOutput").ap()

    xT_d = din("xT", [128, KC * 2 * TPH])
    cT_d = din("cT", [128, KC * 17])
    sgdn_d = din("sgdn", [2, 16, H, 128, 128])
    shgrn_d = din("shgrn", [2, 16, H, 128, 128])
    gci_d = din("gci", [2, 2, 128, 24 * 8 * 3])
    fci_d = din("fci", [4, 2, 128, NFF * 8 * 2])
    prm_d = din("prm", [128, 4 * NPRM])
    nrow_d = din("nrow", [128, 4 * 128])
    cst_d = din("cst", [128, NCST])
    ada_w = din("ada_w", [4, D, 6 * D])
    gdn_w_in = din("gdn_w_in", [2, D, 4112])
    gdn_w_out = din("gdn_w_out", [2, D, D])
    hgrn_w_in = din("hgrn_w_in", [2, D, 4096])
    hgrn_w_out = din("hgrn_w_out", [2, D, D])
    ffn_w_gu = din("ffn_w_gu", [4, D, 2 * DFF])
    ffn_w_down = din("ffn_w_down", [4, DFF, D])

    yT_d = dout("yT", [128, KC * 2 * TPH])
    pgdn_d = dout("pgdn", [2, H, 128, 128])
    sgdn_o = dout("sgdn_o", [2, 16, H, 128, 128])
    phgrn_d = dout("phgrn", [2, H, 128, 128])
    shgrn_o = dout("shgrn_o", [2, 16, H, 128, 128])
    gco_d = dout("gco", [2, 2, 128, 24 * 9 * 3])
    fco_d = dout("fco", [4, 2, 128, NFF * 9 * 2])

    with contextlib.ExitStack() as es:
        K = Sched(nc, es)

        cnt = [0]

        def sb(stack, name, shape, dt):
            cnt[0] += 1
            return stack.enter_context(nc.sbuf_tensor("sb%d_%s" % (cnt[0], name), list(shape), dt))

        xT = sb(es, "xT", [128, KC, 2, TPH], F32)
        xtok = [Tok(), Tok()]
        cst = sb(es, "cst", [128, NCST], F32)
        cst_tok = Tok()
        prm = sb(es, "prm", [128, 4, NPRM], F32)
        prm_tok = Tok()
        nrow = sb(es, "nrow", [128, 4, 128], F32)
        nrow_tok = Tok()
        ident_bf = sb(es, "ident_bf", [128, 128], BF16)
        ones_bf = sb(es, "ones_bf", [128, 128], BF16)
        cb = sb(es, "cb", [128, 8], F32)
        misc_tok = Tok()
        csT = sb(es, "csT", [128, KC, 17], BF16)
        cs_tok = Tok()
        modA = [sb(es, "modA%d" % i, [128, KC, 17], F32) for i in range(6)]
        mod_tok = Tok()
        Sf_all = sb(es, "Sf_all", [128, H, 128], F32)
        Sf_tok = [Tok() for _ in range(H)]
        gcar = sb(es, "gcar", [128, 24, 3], F32)
        gcar_tok = Tok()
        fcar = sb(es, "fcar", [128, NFF, 2], F32)
        fcar_tok = Tok()
        lbT = sb(es, "lbT", [128, H, 2], F32)
        lb_tok = Tok()
        wun = [(sb(es, "wun%d" % i, [128, KC, 128], BF16), Tok()) for i in range(5)]
        wring = Ring(wun)
        wres_toks = [Tok() for _ in range(KC)]
        Ss_ptok, gci_ptok, gco_ptok, fci_ptok, fco_ptok = Tok(), Tok(), Tok(), Tok(), Tok()
        wdn_ptoks = [Tok(), Tok()]

        psF = [es.enter_context(nc.psum_tensor("psF%d" % i, [128, 512], F32)) for i in range(7)]
        psB = es.enter_context(nc.psum_tensor("psB", [128, 1024], BF16))
        bkt = [Tok(excl=True) for _ in range(8)]
        pbig = Ring([(psF[0], bkt[0]), (psF[1], bkt[1])])
        psmall = Ring([(psF[2 + i % 3][:, (i // 3) * 128:(i // 3) * 128 + 128], bkt[2 + i % 3]) for i in range(12)])
        pbf = Ring([(psB[:, i * 128:(i + 1) * 128], bkt[7]) for i in range(8)])
        o_ps, o_ptok = psF[5], bkt[5]
        u_ps, u_ptok = psF[6], bkt[6]
        pwide = Ring([(psF[i], bkt[i]) for i in range(7)])

        def mm(out, lhsT, rhs, start=True, stop=True, R=(), W=()):
            K.op("pe", lambda e: e.matmul(out, lhsT, rhs, start=start, stop=stop), R, W)

        def tr(out, in_, idn, R=(), W=()):
            K.op("pe", lambda e: e.transpose(out, in_, idn), R, W)

        def act(out, in_, func, bias=None, scale=None, accum=None, R=(), W=()):
            kw = {}
            if bias is not None:
                kw["bias"] = bias
            if scale is not None:
                kw["scale"] = scale
            if accum is not None:
                kw["accum_out"] = accum
            K.op("act", lambda e: e.activation(out=out, in_=in_, func=func, **kw), R, W)

        def tt(out, a, b, op, R=(), W=(), en="dve"):
            K.op(en, lambda e: e.tensor_tensor(out=out, in0=a, in1=b, op=op), R, W)

        def ts(out, a, s1, s2, op0, op1=None, R=(), W=(), en="dve"):
            if op1 is None:
                K.op(en, lambda e: e.tensor_scalar(out=out, in0=a, scalar1=s1, scalar2=None, op0=op0), R, W)
            else:
                K.op(en, lambda e: e.tensor_scalar(out=out, in0=a, scalar1=s1, scalar2=s2, op0=op0, op1=op1), R, W)

        def stt(out, a, sc, b, op0, op1, R=(), W=(), en="dve"):
            K.op(en, lambda e: e.scalar_tensor_tensor(out=out, in0=a, scalar=sc, in1=b, op0=op0, op1=op1), R, W)

        def cp(out, in_, R=(), W=(), en="dve"):
            K.op(en, lambda e: e.tensor_copy(out=out, in_=in_), R, W)

        def recip(out, in_, R=(), W=()):
            K.op("dve", lambda e: e.reciprocal(out=out, in_=in_), R, W)

        def memset(ap, val, W=(), en="dve"):
            K.op(en, lambda e: e.memset(ap, val), (), W)

        def wload(W2d, c0, ncols=128):
            t, tok = wring.get()
            K.dma("pool", t[:, :, 0:ncols], W2d[:, c0:c0 + ncols].rearrange("(k p) n -> p k n", p=128), W=[tok], semtok=tok)
            return t, tok

        K.dma("sp", cst[:], cst_d[:, :], W=[cst_tok], semtok=cst_tok)
        K.dma("sp", prm[:].rearrange("p l n -> p (l n)"), prm_d[:, :], W=[prm_tok], semtok=prm_tok)
        K.dma("sp", nrow[:].rearrange("p l n -> p (l n)"), nrow_d[:, :], W=[nrow_tok], semtok=nrow_tok)
        for ph in range(2):
            K.dma("sp", xT[:, :, ph, :], xT_d.rearrange("p (k h t) -> p k h t", k=KC, h=2)[:, :, ph, :], W=[xtok[ph]], semtok=xtok[ph])
        ident = cst[:, C_IDENT:C_IDENT + 128]
        cp(ident_bf[:], ident, R=[cst_tok], W=[misc_tok])
        memset(ones_bf[:], 1.0, W=[misc_tok])
        memset(cb[:, 0:1], 1024.0 * EPS, W=[misc_tok])
        memset(cb[:, 1:2], EPS, W=[misc_tok])
        memset(cb[:, 2:3], 128.0 * EPS, W=[misc_tok])
        memset(cb[:, 3:4], 1.0, W=[misc_tok])
        memset(cb[:, 4:5], 0.0, W=[misc_tok])
        ts(nrow[:], nrow[:], float(np.sqrt(128.0)), None, ALU.mult, R=[nrow_tok], W=[nrow_tok])
        for l in range(4):
            ts(prm[:, l, P_NPRE_MIX:P_NPRE_MIX + 32], prm[:, l, P_NPRE_MIX:P_NPRE_MIX + 32], 32.0, None, ALU.mult, R=[prm_tok], W=[prm_tok])
        with contextlib.ExitStack() as s0:
            cTt = sb(s0, "cTt", [128, KC, 17], F32)
            ctok = Tok()
            K.dma("sp", cTt[:].rearrange("p k q -> p (k q)"), cT_d[:, :], W=[ctok], semtok=ctok)
            act(csT[:], cTt[:], AF.Silu, R=[ctok], W=[cs_tok])
            K.barrier()

        def adaln(l, S):
            mod = sb(S, "mod", [128, 48, 17], F32)
            mtok = Tok()
            slots = [pbig.get(), pbig.get()]
            for cc in range(48):
                wt, wtok = wload(ada_w[l], cc * 128)
                ps, ptok = slots[cc // 24]
                col = (cc % 24) * 17
                for kc in range(KC):
                    mm(ps[:, col:col + 17], wt[:, kc, :], csT[:, kc, :], start=(kc == 0), stop=(kc == KC - 1),
                       R=[wtok, cs_tok], W=[ptok])
            for bk in range(2):
                ps, ptok = slots[bk]
                tt(mod[:, 24 * bk:24 * bk + 24, :], ps[:, 0:408].rearrange("p (c q) -> p c q", q=17),
                   prm[:, l, P_ADAB + 24 * bk:P_ADAB + 24 * bk + 24].unsqueeze(2).broadcast_to([128, 24, 17]),
                   ALU.add, R=[ptok, prm_tok], W=[mtok])
            def nb(off):
                return prm[:, l, off:off + 8].unsqueeze(2).broadcast_to([128, 8, 17])
            stt(modA[0][:], mod[:, 8:16, :], 1.0, nb(P_NPRE_MIX), ALU.add, ALU.mult, R=[mtok, prm_tok], W=[mod_tok])
            cp(modA[1][:], mod[:, 0:8, :], R=[mtok], W=[mod_tok])
            stt(modA[2][:], mod[:, 16:24, :], 1.0, nb(P_NPOST_MIX), ALU.add, ALU.mult, R=[mtok, prm_tok], W=[mod_tok])
            stt(modA[3][:], mod[:, 32:40, :], 1.0, nb(P_NPRE_FFN), ALU.add, ALU.mult, R=[mtok, prm_tok], W=[mod_tok])
            cp(modA[4][:], mod[:, 24:32, :], R=[mtok], W=[mod_tok])
            stt(modA[5][:], mod[:, 40:48, :], 1.0, nb(P_NPOST_FFN), ALU.add, ALU.mult, R=[mtok, prm_tok], W=[mod_tok])

        def seqcols(ph):
            return slice(1 + SPP * ph, 1 + SPP * ph + SPP)

        def rstd_fm(src3, n, R, sqt, rs):
            sq, sqtok = sqt
            rst, rstok = rs
            act(sq[:, :, 0:n], src3, AF.Square, R=R, W=[sqtok])
            ps, ptok = pbig.get()
            for kc in range(KC):
                mm(ps[:, 0:n], ones_bf[:], sq[:, kc, 0:n], start=(kc == 0), stop=(kc == KC - 1), R=[sqtok, misc_tok], W=[ptok])
            act(rst[:, 0:n], ps[:, 0:n], AF.Sqrt, bias=cb[:, 0:1], scale=1.0, R=[ptok, misc_tok], W=[rstok])
            recip(rst[:, 0:n], rst[:, 0:n], R=[rstok], W=[rstok])

        def prenorm(ph, A, B, hT, htoks, S):
            sqt = (sb(S, "pn_sq", [128, KC, 512], BF16), Tok())
            rs = (sb(S, "pn_rs", [128, 512], F32), Tok())
            tmps = Ring([(sb(S, "pn_t%d" % i, [128, 512], F32), Tok()) for i in range(3)])
            for ti, (t0, n) in enumerate(TILES):
                rstd_fm(xT[:, :, ph, t0:t0 + n], n, [xtok[ph]], sqt, rs)
                for kc in range(KC):
                    tmp, ttok = tmps.get()
                    tt(tmp[:, 0:n], xT[:, kc, ph, t0:t0 + n], rs[0][:, 0:n], ALU.mult, R=[xtok[ph], rs[1]], W=[ttok])
                    if ti < 2:
                        act(hT[:, kc, t0:t0 + n], tmp[:, 0:n], AF.Identity, bias=B[:, kc, 0:1], scale=A[:, kc, 0:1],
                            R=[ttok, mod_tok], W=[htoks[ti]])
                    else:
                        sc = seqcols(ph)
                        tt(tmp[:, 0:n].rearrange("p (s j) -> p s j", j=LS), tmp[:, 0:n].rearrange("p (s j) -> p s j", j=LS),
                           A[:, kc, sc].unsqueeze(2).broadcast_to([128, SPP, LS]), ALU.mult, R=[ttok, mod_tok], W=[ttok])
                        tt(hT[:, kc, t0:t0 + n].rearrange("p (s j) -> p s j", j=LS), tmp[:, 0:n].rearrange("p (s j) -> p s j", j=LS),
                           B[:, kc, sc].unsqueeze(2).broadcast_to([128, SPP, LS]), ALU.add, R=[ttok, mod_tok], W=[htoks[ti]])

        def proj_tile(wt, wtok, hT, htoks, ti, M=128, ring=None):
            t0, n = TILES[ti]
            ps, ptok = (ring or pbig).get()
            for kc in range(KC):
                mm(ps[0:M, 0:n], wt[:, kc, 0:M], hT[:, kc, t0:t0 + n], start=(kc == 0), stop=(kc == KC - 1),
                   R=[wtok, htoks[ti]], W=[ptok])
            return ps, ptok, n

        def outproj_postnorm(l, ph, Wout2d, oT, otoks, G, S):
            y = sb(S, "op_y", [128, KC, 512], F32)
            ytok = Tok()
            sqt = (sb(S, "op_sq", [128, KC, 512], BF16), Tok())
            rs = (sb(S, "op_rs", [128, 512], F32), Tok())
            tmps = Ring([(sb(S, "op_t%d" % i, [128, 512], F32), Tok()) for i in range(2)])
            wres = sb(S, "op_w", [128, KC, KC, 128], BF16)
            wrtok = wres_toks
            for oc in range(KC):
                K.dma("pool", wres[:, oc, :, :], Wout2d[:, oc * 128:(oc + 1) * 128].rearrange("(k p) n -> p k n", p=128), W=[wrtok[oc]], semtok=wrtok[oc])
            for ti, (t0, n) in enumerate(TILES):
                for oc in range(KC):
                    ps, ptok = pwide.get()
                    for kc in range(KC):
                        mm(ps[:, 0:n], wres[:, oc, kc, :], oT[:, kc, t0:t0 + n], start=(kc == 0), stop=(kc == KC - 1),
                           R=[wrtok[oc], otoks[ti]], W=[ptok])
                    cp(y[:, oc, 0:n], ps[:, 0:n], R=[ptok], W=[ytok])
                resid_update(ph, t0, n, y, ytok, G, sqt, rs, tmps)

        def resid_update(ph, t0, n, y, ytok, G, sqt, rs, tmps):
            rstd_fm(y[:, :, 0:n], n, [ytok], sqt, rs)
            for kc in range(KC):
                tmp, ttok = tmps.get()
                tt(tmp[:, 0:n], y[:, kc, 0:n], rs[0][:, 0:n], ALU.mult, R=[ytok, rs[1]], W=[ttok])
                if t0 < TP:
                    stt(xT[:, kc, ph, t0:t0 + n], tmp[:, 0:n], G[:, kc, 0:1], xT[:, kc, ph, t0:t0 + n], ALU.mult, ALU.add,
                        R=[ttok, mod_tok], W=[xtok[ph]])
                else:
                    sc = seqcols(ph)
                    v3 = tmp[:, 0:n].rearrange("p (s j) -> p s j", j=LS)
                    tt(v3, v3, G[:, kc, sc].unsqueeze(2).broadcast_to([128, SPP, LS]), ALU.mult, R=[ttok, mod_tok], W=[ttok])
                    x3 = xT[:, kc, ph, t0:t0 + n].rearrange("p (s j) -> p s j", j=LS)
                    tt(x3, x3, v3, ALU.add, R=[ttok], W=[xtok[ph]])

        def blocks():
            return [(b * 128, 128, "P") for b in range(8)] + [(TP, 64, "S")]

        def o_post(l, o_rows, BS, h, c0, oT, otok_t, sgT, sgtok, tmpf, tmpb):
            o_ps_, o_ptok_ = o_rows if o_rows is not None else (o_ps, o_ptok)
            junk, jtok = tmpf.get()
            ss, sstok = tmpf.get()
            act(junk[0:BS, :], o_ps_[0:BS, 0:128], AF.Square, accum=ss[0:BS, 0:1], R=[o_ptok_], W=[jtok, sstok])
            act(ss[0:BS, 0:1], ss[0:BS, 0:1], AF.Ln, bias=cb[0:BS, 2:3], scale=1.0, R=[sstok, misc_tok], W=[sstok])
            act(ss[0:BS, 0:1], ss[0:BS, 0:1], AF.Exp, scale=-0.5, R=[sstok], W=[sstok])
            on, ontok = tmpb.get()
            stt(on[0:BS, :], o_ps_[0:BS, 0:128], ss[0:BS, 0:1], nrow[0:BS, l, :], ALU.mult, ALU.mult,
                R=[o_ptok_, sstok, nrow_tok], W=[ontok])
            pb, pbtok = pbf.get()
            tr(pb[:, 0:BS], on[0:BS, :], ident_bf[0:BS, 0:BS], R=[ontok, misc_tok], W=[pbtok])
            tt(oT[:, h, c0:c0 + BS], pb[:, 0:BS], sgT[:, c0:c0 + BS], ALU.mult, R=[pbtok, sgtok], W=[otok_t])

        def gdn_phase(l, ph, hT, htoks, oT, otoks, S):
            j = l // 2
            Win = gdn_w_in[j]
            stage(3)
            G = 4
            dG = sb(S, "g_dG", [8, TPH], F32)
            dtok_ = Tok()
            dtk = sb(S, "g_dtk", [128, 9, 3, 8], F32)
            ex = sb(S, "g_ex", [128, 9, 4, 8], F32)
            glb = sb(S, "g_glb", [128, H, 24], F32)
            dk_tok = Tok()
            gci = sb(S, "g_gci", [128, 24, 8, 3], F32)
            gci_tok = gci_ptok
            gco = sb(S, "g_gco", [128, 24, 9, 3], F32)
            gco_tok = gco_ptok
            K.dma("sp", gci[:].rearrange("p a s r -> p (a s r)"), gci_d[j, ph], W=[gci_tok], semtok=gci_tok)
            memset(gco[:], 0.0, W=[gco_tok])
            with contextlib.ExitStack() as SD:
                dB = sb(SD, "g_dB", [8, TPH], F32)
                dL = sb(SD, "g_dL", [8, TPH], F32)
                dg = sb(SD, "g_dg", [8, TPH], F32)
                rmask = dL
                nega = sb(SD, "g_nega", [8, 1], F32)
                memset(rmask[:], 1.0, W=[dtok_])
                memset(rmask[:, 0:TP].rearrange("p (c t) -> p c t", t=64)[:, :, 0:1], 0.0, W=[dtok_])
                memset(rmask[:, TP:TPH].rearrange("p (c t) -> p c t", t=8)[:, :, 0:1], 0.0, W=[dtok_])
                act(nega[:], prm[0:8, l, P_ALOG:P_ALOG + 1], AF.Exp, R=[prm_tok], W=[dtok_])
                ts(nega[:], nega[:], -1.0, None, ALU.mult, R=[dtok_], W=[dtok_])
                wb, wbtok = wload(Win, 4096, 8)
                for ti in range(3):
                    ps, ptok, n = proj_tile(wb, wbtok, hT, htoks, ti, M=8)
                    t0 = TILES[ti][0]
                    act(dB[:, t0:t0 + n], ps[0:8, 0:n], AF.Sigmoid, R=[ptok], W=[dtok_])
                wa, watok = wload(Win, 4104, 8)
                for ti in range(3):
                    ps, ptok, n = proj_tile(wa, watok, hT, htoks, ti, M=8)
                    t0 = TILES[ti][0]
                    act(dg[:, t0:t0 + n], ps[0:8, 0:n], AF.Exp, bias=prm[0:8, l, P_DTB:P_DTB + 1], scale=1.0, R=[ptok, prm_tok], W=[dtok_])
                act(dg[:], dg[:], AF.Ln, bias=cb[0:8, 3:4], scale=1.0, R=[dtok_, misc_tok], W=[dtok_])
                ts(dg[:], dg[:], nega[:, 0:1], None, ALU.mult, R=[dtok_], W=[dtok_])
                K.op("dve", lambda e: e.tensor_tensor_scan(out=dG[:], data0=rmask[:], data1=dg[:], initial=0.0, op0=ALU.mult, op1=ALU.add),
                     [dtok_], [dtok_])
                gp = dG[:, 0:TP].rearrange("p (c t) -> p c t", t=64)
                tt(dL[:, 0:TP].rearrange("p (c t) -> p c t", t=64), gp[:, :, 63:64].broadcast_to([8, 16, 64]), gp, ALU.subtract, R=[dtok_], W=[dtok_])
                gs = dG[:, TP:TPH].rearrange("p (c t) -> p c t", t=8)
                tt(dL[:, TP:TPH].rearrange("p (c t) -> p c t", t=8), gs[:, :, 7:8].broadcast_to([8, 8, 8]), gs, ALU.subtract, R=[dtok_], W=[dtok_])
                ps, ptok = pbig.get()
                for bi, (c0, BS, kind) in enumerate(blocks()):
                    for qi, src in enumerate((dB, dG, dL)):
                        col = (bi * 3 + qi) * 8
                        tr(ps[0:BS, col:col + 8], src[:, c0:c0 + BS], cst[0:8, C_IDENT:C_IDENT + 8], R=[dtok_, cst_tok], W=[ptok])
                memset(dtk[:], 0.0, W=[dk_tok])
                cp(dtk[:, 0:8].rearrange("p b q h -> p (b q h)"), ps[:, 0:192], R=[ptok], W=[dk_tok])
                cp(dtk[0:64, 8].rearrange("p q h -> p (q h)"), ps[0:64, 192:216], R=[ptok], W=[dk_tok])
                act(ex[:, :, 0:2, :], dtk[:, :, 1:3, :], AF.Exp, R=[dk_tok], W=[dk_tok])
                tt(ex[:, :, 2, :], dtk[:, :, 0, :], ex[:, :, 0, :], ALU.mult, R=[dk_tok], W=[dk_tok])
                ts(ex[:, :, 3, :], dtk[:, :, 0, :], -1.0, None, ALU.mult, R=[dk_tok], W=[dk_tok])
                ps, ptok = pbig.get()
                for h in range(H):
                    esel_h = cst[0:8, C_ESEL + h * 128:C_ESEL + (h + 1) * 128]
                    mm(ps[:, h * 24:h * 24 + 16], esel_h, dG[:, 0:TP].rearrange("p (c t) -> p c t", t=64)[:, :, 63], R=[dtok_, cst_tok], W=[ptok])
                    mm(ps[:, h * 24 + 16:h * 24 + 24], esel_h, dG[:, TP:TPH].rearrange("p (c t) -> p c t", t=8)[:, :, 7], R=[dtok_, cst_tok], W=[ptok])
                act(glb[:].rearrange("p h c -> p (h c)"), ps[:, 0:192], AF.Exp, R=[ptok], W=[dk_tok])
                K.barrier()

            with contextlib.ExitStack() as S1:
                qT = sb(S1, "g_qT", [128, TPH], BF16)
                kT = sb(S1, "g_kT", [128, TPH], BF16)
                vT = sb(S1, "g_vT", [128, TPH], BF16)
                sgT = sb(S1, "g_sgT", [128, TPH], BF16)
                q_tok, k_tok, v_tok, sg_tok = Tok(), Tok(), Tok(), Tok()
                Sbf = sb(S1, "g_Sbf", [128, 128], BF16)
                Sbf_tok = Tok()
                Ss = sb(S1, "g_Ss", [128, SPP, 128], F32)
                Ss_tok = Ss_ptok
                Ssb = sb(S1, "g_Ssb", [128, SPP, 128], BF16)
                Ssb_tok = Tok()
                Sn, Sn_tok = Ss, Ss_tok
                u_sb = sb(S1, "g_usb", [128, 128], BF16)
                usb_tok = Tok()
                memset(u_sb[:], 0.0, W=[usb_tok])
                psblk = Ring([(psF[i % 4][:, (i // 4) * 128:(i // 4) * 128 + 128], bkt[i % 4]) for i in range(16)])
                obank = Ring([(psF[5], bkt[5]), (psF[4], bkt[4])])

                for h in range(H):
                    stage(4 + 0.1 * h)
                    with contextlib.ExitStack() as SPJ:
                        pre_r = Ring([(sb(SPJ, "g_pre%d" % i, [128, 3 + TP + SPP * 11], F32), Tok()) for i in range(2)])
                        cv_r = Ring([(sb(SPJ, "g_cv%d" % i, [128, TPH], F32), Tok()) for i in range(2)])
                        sq_r = Ring([(sb(SPJ, "g_sqb%d" % i, [128, 512], BF16), Tok()) for i in range(2)])
                        ri_r = Ring([(sb(SPJ, "g_rinv%d" % i, [128, 512], F32), Tok()) for i in range(2)])
                        for role in range(4):
                            wt, wtok = wload(Win, role * 1024 + h * 128)
                            if role < 3:
                                pre, pre_tok = pre_r.get()
                                cv, cv_tok = cv_r.get()
                                pv = pre[:, 3 + TP:3 + TP + SPP * 11].rearrange("p (s r) -> p s r", r=11)
                                ch = role * 8 + h
                                if ph == 0:
                                    memset(pre[:, 0:3], 0.0, W=[pre_tok])
                                else:
                                    cp(pre[:, 0:3], gcar[:, ch, :], R=[gcar_tok], W=[pre_tok])
                                cp(pv[:, :, 0:3], gci[:, ch, :, :], R=[gci_tok], W=[pre_tok])
                            for ti in range(3):
                                ps, ptok, n = proj_tile(wt, wtok, hT, htoks, ti, ring=pwide)
                                t0 = TILES[ti][0]
                                if role == 3:
                                    act(sgT[:, t0:t0 + n], ps[:, 0:n], AF.Silu, R=[ptok], W=[sg_tok])
                                elif ti < 2:
                                    act(pre[:, 3 + t0:3 + t0 + n], ps[:, 0:n], AF.Copy, R=[ptok], W=[pre_tok])
                                else:
                                    act(pv[:, :, 3:11], ps[:, 0:n].rearrange("p (s r) -> p s r", r=LS), AF.Copy, R=[ptok], W=[pre_tok])
                            if role == 3:
                                continue
                            if ph == 0:
                                cp(gcar[:, ch, :], pre[:, TP:TP + 3], R=[pre_tok], W=[gcar_tok])
                            else:
                                cp(gco[:, ch, 8, :], pre[:, TP:TP + 3], R=[pre_tok], W=[gco_tok])
                            cp(gco[:, ch, 0:8, :], pv[:, :, 8:11], R=[pre_tok], W=[gco_tok])
                            wc = prm[:, l, P_GCW + ch * 4:P_GCW + ch * 4 + 4]
                            bc = prm[:, l, P_GCB + ch:P_GCB + ch + 1]
                            cvs = cv[:, TP:TPH].rearrange("p (s r) -> p s r", r=LS)
                            ts(cv[:, 0:TP], pre[:, 0:TP], wc[:, 0:1], bc, ALU.mult, ALU.add, R=[pre_tok, prm_tok], W=[cv_tok])
                            ts(cvs, pv[:, :, 0:8], wc[:, 0:1], bc, ALU.mult, ALU.add, R=[pre_tok, prm_tok], W=[cv_tok])
                            for tap in range(1, 4):
                                stt(cv[:, 0:TP], pre[:, tap:tap + TP], wc[:, tap:tap + 1], cv[:, 0:TP], ALU.mult, ALU.add,
                                    R=[pre_tok, prm_tok], W=[cv_tok])
                                stt(cvs, pv[:, :, tap:tap + 8], wc[:, tap:tap + 1], cvs, ALU.mult, ALU.add, R=[pre_tok, prm_tok], W=[cv_tok])
                            if role == 2:
                                act(vT[:], cv[:], AF.Silu, R=[cv_tok], W=[v_tok])
                                continue
                            act(cv[:], cv[:], AF.Silu, R=[cv_tok], W=[cv_tok])
                            for ti, (t0, n) in enumerate(TILES):
                                sqb, sqb_tok = sq_r.get()
                                rinv, rinv_tok = ri_r.get()
                                act(sqb[:, 0:n], cv[:, t0:t0 + n], AF.Square, R=[cv_tok], W=[sqb_tok])
                                ps, ptok = pwide.get()
                                mm(ps[:, 0:n], ones_bf[:], sqb[:, 0:n], R=[sqb_tok, misc_tok], W=[ptok])
                                act(rinv[:, 0:n], ps[:, 0:n], AF.Sqrt, bias=cb[:, 1:2], scale=1.0, R=[ptok, misc_tok], W=[rinv_tok])
                                recip(rinv[:, 0:n], rinv[:, 0:n], R=[rinv_tok], W=[rinv_tok])
                                if role == 0:
                                    stt(qT[:, t0:t0 + n], cv[:, t0:t0 + n], float(128.0 ** -0.5), rinv[:, 0:n], ALU.mult, ALU.mult,
                                        R=[cv_tok, rinv_tok], W=[q_tok])
                                else:
                                    tt(kT[:, t0:t0 + n], cv[:, t0:t0 + n], rinv[:, 0:n], ALU.mult, R=[cv_tok, rinv_tok], W=[k_tok])
                        K.barrier()

                    stage(4.05 + 0.1 * h)
                    K.dma("pool", Ss[:], sgdn_d[j, SPP * ph:SPP * ph + SPP, h].rearrange("s k v -> k s v"), W=[Ss_tok], semtok=Ss_tok)
                    act(Ssb[:], Ss[:], AF.Copy, R=[Ss_tok], W=[Ssb_tok])
                    if ph == 0:
                        memset(Sf_all[:, h, :], 0.0, W=[Sf_tok[h]])
                    cp(Sbf[:], Sf_all[:, h, :], R=[Sf_tok[h]], W=[Sbf_tok])

                    with contextlib.ExitStack() as SBK:
                        slots = []
                        for k in range(G):
                            sl = {}
                            sl["X"] = [(sb(SBK, "g_x%d_%d" % (k, i), [128, 128], F32), Tok()) for i in range(6)]
                            sl["kbg"] = (sb(SBK, "g_kbg%d" % k, [128, 128], BF16), Tok())
                            sl["HS"] = [[(sb(SBK, "g_hs%d_%d_%d" % (k, par, i), [128, 128], BF16), Tok()) for i in range(6)] for par in range(2)]
                            slots.append(sl)
                        spad = {}
                        for nm in ("Kpad", "Wpad", "Qpad", "tw"):
                            spad[nm] = (sb(SBK, "g_s%s" % nm, [128, 8, 128], BF16), Tok())
                        spad["kdm"] = (sb(SBK, "g_skdm", [128, 8], F32), Tok())
                        opf = Ring([(sb(SBK, "g_of%d" % i, [128, 128], F32), Tok()) for i in range(4)])
                        opb = Ring([(sb(SBK, "g_ob%d" % i, [128, 128], BF16), Tok()) for i in range(2)])

                        def pre_state(bi, c0, BS, kind, sl, par):
                            if kind == "P":
                                ncb, C, nlev = 2, 64, 5
                                pen = cst[:, C_PENP:C_PENP + 128]
                            else:
                                ncb, C, nlev = 8, 8, 2
                                pen = cst[0:64, C_PENS:C_PENS + 64]
                                cmask = cst[:, C_CM8:C_CM8 + 512].rearrange("p (c t) -> p c t", c=8)
                                rmk = cst[0:64, C_RM8:C_RM8 + 8]
                            cols = slice(c0, c0 + BS)
                            X = sl["X"]
                            HS = sl["HS"][par]
                            Gcol = dtk[0:BS, bi, 1, h:h + 1]
                            beta_c = dtk[0:BS, bi, 0, h:h + 1]
                            eGL_c = ex[0:BS, bi, 1, h:h + 1]
                            bG_c = ex[0:BS, bi, 2, h:h + 1]
                            nbeta_c = ex[0:BS, bi, 3, h:h + 1]
                            esel_h = cst[0:8, C_ESEL + h * 128:C_ESEL + (h + 1) * 128]
                            gps, gptok = psblk.get()
                            mm(gps[:, 0:BS], esel_h, dG[:, cols], R=[dtok_, cst_tok], W=[gptok])
                            yield
                            r_, rtok = X[0]
                            stt(r_[0:BS, 0:BS], gps[0:BS, 0:BS], Gcol, pen, ALU.subtract, ALU.max, R=[gptok, dk_tok, cst_tok], W=[rtok])
                            eGr, egtok = X[2]
                            act(eGr[:, 0:BS], gps[:, 0:BS], AF.Exp, R=[gptok], W=[egtok])
                            yield
                            Ds, dstok = X[1]
                            act(Ds[0:BS, 0:BS], r_[0:BS, 0:BS], AF.Exp, scale=-1.0, R=[rtok], W=[dstok])
                            if kind == "P":
                                qg, qgtok = HS[3]
                                tt(qg[:, 0:BS], qT[:, cols], eGr[:, 0:BS], ALU.mult, R=[q_tok, egtok], W=[qgtok])
                            else:
                                tw, twtok = spad["tw"]
                                Qpad, qp_tok = spad["Qpad"]
                                tt(tw[:, 0:ncb, 0:BS], cmask[:, :, 0:BS], eGr[:, 0:BS].unsqueeze(1).broadcast_to([128, ncb, BS]), ALU.mult,
                                   R=[cst_tok, egtok], W=[twtok])
                                tt(Qpad[:, 0:ncb, 0:BS], tw[:, 0:ncb, 0:BS], qT[:, cols].unsqueeze(1).broadcast_to([128, ncb, BS]), ALU.mult,
                                   R=[twtok, q_tok], W=[qp_tok])
                            yield
                            dps, dptok = psblk.get()
                            tr(dps[0:BS, 0:BS], Ds[0:BS, 0:BS], ident[0:BS, 0:BS], R=[dstok, cst_tok], W=[dptok])
                            kkps, kktok = psblk.get()
                            mm(kkps[0:BS, 0:BS], kT[:, cols], kT[:, cols], R=[k_tok], W=[kktok])
                            qkps, qktok = psblk.get()
                            mm(qkps[0:BS, 0:BS], kT[:, cols], qT[:, cols], R=[k_tok, q_tok], W=[qktok])
                            yield
                            DmT, dmtok = X[0]
                            tt(DmT[0:BS, 0:BS], dps[0:BS, 0:BS], ident[0:BS, 0:BS], ALU.add, R=[dptok, cst_tok], W=[dmtok])
                            B, btok = X[2]
                            stt(B[0:BS, 0:BS], kkps[0:BS, 0:BS], nbeta_c, Ds[0:BS, 0:BS], ALU.mult, ALU.mult, R=[kktok, dk_tok, dstok], W=[btok])
                            yield
                            attnT, attok = HS[0]
                            tt(attnT[0:BS, 0:BS], qkps[0:BS, 0:BS], DmT[0:BS, 0:BS], ALU.mult, R=[qktok, dmtok], W=[attok])
                            btps, btptok = psblk.get()
                            tr(btps[0:BS, 0:BS], B[0:BS, 0:BS], ident[0:BS, 0:BS], R=[btok, cst_tok], W=[btptok])
                            yield
                            Bt, bttok = X[3]
                            act(Bt[0:BS, 0:BS], btps[0:BS, 0:BS], AF.Copy, R=[btptok], W=[bttok])
                            TT, tttok = X[4]
                            tt(TT[0:BS, 0:BS], btps[0:BS, 0:BS], ident[0:BS, 0:BS], ALU.add, R=[btptok, cst_tok], W=[tttok])
                            yield
                            cur = (X[2], X[3])
                            nxt = (X[5], X[1])
                            for lev in range(nlev):
                                last = (lev == nlev - 1)
                                (B, btok), (Bt, bttok) = cur
                                (Bn, bntok), (Btn, btntok) = nxt
                                p2, p2tok = psblk.get()
                                mm(p2[0:BS, 0:BS], Bt[0:BS, 0:BS], B[0:BS, 0:BS], R=[bttok, btok], W=[p2tok])
                                if not last:
                                    p1, p1tok = psblk.get()
                                    mm(p1[0:BS, 0:BS], B[0:BS, 0:BS], Bt[0:BS, 0:BS], R=[bttok, btok], W=[p1tok])
                                yield
                                cp(Bn[0:BS, 0:BS], p2[0:BS, 0:BS], R=[p2tok], W=[bntok])
                                if not last:
                                    act(Btn[0:BS, 0:BS], p1[0:BS, 0:BS], AF.Copy, R=[p1tok], W=[btntok])
                                yield
                                p3, p3tok = psblk.get()
                                mm(p3[0:BS, 0:BS], Bn[0:BS, 0:BS], TT[0:BS, 0:BS], R=[bntok, tttok], W=[p3tok])
                                yield
                                tt(TT[0:BS, 0:BS], p3[0:BS, 0:BS], TT[0:BS, 0:BS], ALU.add, R=[p3tok, tttok], W=[tttok])
                                yield
                                cur, nxt = nxt, cur
                            TTb, ttbtok = HS[1]
                            act(TTb[0:BS, 0:BS], TT[0:BS, 0:BS], AF.Copy, R=[tttok], W=[ttbtok])
                            pk, pktok = pbf.get()
                            tr(pk[0:BS, :], kT[:, cols], ident_bf[:], R=[k_tok, misc_tok], W=[pktok])
                            pvv, pvtok = pbf.get()
                            tr(pvv[0:BS, :], vT[:, cols], ident_bf[:], R=[v_tok, misc_tok], W=[pvtok])
                            yield
                            vb, vbtok = HS[2]
                            act(vb[0:BS, :], pvv[0:BS, :], AF.Copy, scale=beta_c, R=[pvtok, dk_tok], W=[vbtok])
                            kbg, kbgtok = sl["kbg"]
                            act(kbg[0:BS, :], pk[0:BS, :], AF.Copy, scale=bG_c, R=[pktok, dk_tok], W=[kbgtok])
                            if kind == "P":
                                kd, kdtok = HS[5]
                                act(kd[0:BS, :], pk[0:BS, :], AF.Copy, scale=eGL_c, R=[pktok, dk_tok], W=[kdtok])
                            else:
                                kdm, kdmtok = spad["kdm"]
                                Kpad, kp_tok = spad["Kpad"]
                                ts(kdm[0:BS, 0:ncb], rmk, eGL_c, None, ALU.mult, R=[cst_tok, dk_tok], W=[kdmtok])
                                tt(Kpad[0:BS, 0:ncb, :], pk[0:BS, :].unsqueeze(1).broadcast_to([BS, ncb, 128]),
                                   kdm[0:BS, 0:ncb].unsqueeze(2).broadcast_to([BS, ncb, 128]), ALU.mult, R=[pktok, kdmtok], W=[kp_tok])
                            yield
                            wps, wptok = psblk.get()
                            mm(wps[:, 0:BS], kbg[0:BS, :], TTb[0:BS, 0:BS], R=[kbgtok, ttbtok], W=[wptok])
                            yield
                            if kind == "P":
                                Wneg, wntok = HS[4]
                                act(Wneg[:, 0:BS], wps[:, 0:BS], AF.Copy, scale=-1.0, R=[wptok], W=[wntok])
                            else:
                                Wpad, wp_tok = spad["Wpad"]
                                stt(Wpad[:, 0:ncb, 0:BS], wps[:, 0:BS].unsqueeze(1).broadcast_to([128, ncb, BS]), -1.0, cmask[:, :, 0:BS],
                                    ALU.mult, ALU.mult, R=[wptok, cst_tok], W=[wp_tok])

                        pend = [None]

                        def flush_post():
                            if pend[0] is not None:
                                o_post(*pend[0])
                                pend[0] = None

                        def state_group(grp, par):
                            for k, bi in enumerate(grp):
                                c0, BS, kind = blks[bi]
                                sl = slots[k]
                                HS = sl["HS"][par]
                                attnT, attok = HS[0]
                                TTb, ttbtok = HS[1]
                                vb, vbtok = HS[2]
                                ops_, optok = obank.get()
                                if kind == "P":
                                    qg, qgtok = HS[3]
                                    Wneg, wntok = HS[4]
                                    kd, kdtok = HS[5]
                                    for c in range(2):
                                        rows = slice(c * 64, (c + 1) * 64)
                                        mm(u_ps[rows, 0:128], TTb[rows, rows], vb[rows, :], start=True, stop=False, R=[ttbtok, vbtok], W=[u_ptok])
                                        mm(u_ps[rows, 0:128], Wneg[:, rows], Sbf[:], start=False, stop=True, R=[wntok, Sbf_tok], W=[u_ptok])
                                        act(u_sb[rows, :], u_ps[rows, 0:128], AF.Copy, R=[u_ptok], W=[usb_tok])
                                        mm(ops_[rows, 0:128], qg[:, rows], Sbf[:], start=True, stop=False, R=[qgtok, Sbf_tok], W=[optok])
                                        mm(ops_[rows, 0:128], attnT[rows, rows], u_sb[rows, :], start=False, stop=True, R=[attok, usb_tok], W=[optok])
                                        sps, sptok = psblk.get()
                                        mm(sps[:, :], kd[rows, :], u_sb[rows, :], R=[kdtok, usb_tok], W=[sptok])
                                        gl = glb[:, h, (bi * 2 + c):(bi * 2 + c) + 1]
                                        stt(Sbf[:], Sf_all[:, h, :], gl, sps[:, :], ALU.mult, ALU.add, R=[sptok, dk_tok, Sf_tok[h]], W=[Sbf_tok])
                                        stt(Sf_all[:, h, :], Sf_all[:, h, :], gl, sps[:, :], ALU.mult, ALU.add, R=[sptok, dk_tok], W=[Sf_tok[h]])
                                        yield
                                else:
                                    ncb = 8
                                    Kpad, kp_tok = spad["Kpad"]
                                    Wpad, wp_tok = spad["Wpad"]
                                    Qpad, qp_tok = spad["Qpad"]
                                    mm(u_ps[0:BS, 0:128], TTb[0:BS, 0:BS], vb[0:BS, :], start=True, stop=False, R=[ttbtok, vbtok], W=[u_ptok])
                                    for c in range(ncb):
                                        mm(u_ps[0:BS, 0:128], Wpad[:, c, 0:BS], Ssb[:, c, :], start=False, stop=(c == ncb - 1), R=[wp_tok, Ssb_tok], W=[u_ptok])
                                    act(u_sb[0:BS, :], u_ps[0:BS, 0:128], AF.Copy, R=[u_ptok], W=[usb_tok])
                                    for c in range(ncb):
                                        mm(ops_[0:BS, 0:128], Qpad[:, c, 0:BS], Ssb[:, c, :], start=(c == 0), stop=False, R=[qp_tok, Ssb_tok], W=[optok])
                                    mm(ops_[0:BS, 0:128], attnT[0:BS, 0:BS], u_sb[0:BS, :], start=False, stop=True, R=[attok, usb_tok], W=[optok])
                                    sl2 = [pbig.get(), pbig.get()]
                                    for c in range(ncb):
                                        ps, ptok = sl2[c // 4]
                                        mm(ps[:, (c % 4) * 128:(c % 4) * 128 + 128], Kpad[0:BS, c, :], u_sb[0:BS, :], R=[kp_tok, usb_tok], W=[ptok])
                                    tt(Sn[:], Ss[:], glb[:, h, 16:24].unsqueeze(2).broadcast_to([128, 8, 128]), ALU.mult, R=[Ss_tok, dk_tok], W=[Sn_tok])
                                    for hb in range(2):
                                        ps, ptok = sl2[hb]
                                        tt(Sn[:, hb * 4:hb * 4 + 4, :], Sn[:, hb * 4:hb * 4 + 4, :], ps[:, :].rearrange("p (s v) -> p s v", v=128), ALU.add,
                                           R=[ptok], W=[Sn_tok])
                                    K.dma("sp", sgdn_o[j, SPP * ph:SPP * ph + SPP, h].rearrange("s k v -> k s v"), Sn[:], R=[Sn_tok], semtok=Sn_tok)
                                flush_post()
                                pend[0] = (l, (ops_, optok), BS, h, c0, oT, otoks[min(c0 // 512, 2)], sgT, sg_tok, opf, opb)
                                yield

                        def run_rr(gens):
                            alive = list(gens)
                            while alive:
                                nx = []
                                for g in alive:
                                    try:
                                        next(g)
                                        nx.append(g)
                                    except StopIteration:
                                        pass
                                alive = nx

                        blks = blocks()
                        groups = [[0, 1, 2, 3], [4, 5, 6, 7], [8]]
                        stage(4.06 + 0.1 * h)
                        run_rr([pre_state(bi, blks[bi][0], blks[bi][1], blks[bi][2], slots[k], 0) for k, bi in enumerate(groups[0])])
                        run_rr([state_group(groups[0], 0)] +
                               [pre_state(bi, blks[bi][0], blks[bi][1], blks[bi][2], slots[k], 1) for k, bi in enumerate(groups[1])])
                        run_rr([state_group(groups[1], 1)] +
                               [pre_state(8, blks[8][0], blks[8][1], blks[8][2], slots[0], 0)])
                        run_rr([state_group(groups[2], 0)])
                        flush_post()
                        if ph == 1:
                            K.dma("sp", pgdn_d[j, h], Sf_all[:, h, :], R=[Sf_tok[h]], semtok=Sf_tok[h])
                        K.barrier()
                K.dma("sp", gco_d[j, ph], gco[:].rearrange("p a s r -> p (a s r)"), R=[gco_tok], semtok=gco_tok)
                K.barrier()

        def hgrn_phase(l, ph, hT, htoks, oT, otoks, S):
            j = l // 2
            Win = hgrn_w_in[j]
            with contextlib.ExitStack() as S1:
                rmask = sb(S1, "h_rm", [128, TPH], F32)
                rm_tok = Tok()
                eG = sb(S1, "h_eG", [128, TPH], F32)
                eG_tok = Tok()
                qgT = sb(S1, "h_qgT", [128, TPH], BF16)
                kiT = sb(S1, "h_kiT", [128, TPH], BF16)
                kdT = sb(S1, "h_kdT", [128, TPH], BF16)
                vT = sb(S1, "h_vT", [128, TPH], BF16)
                sgT = sb(S1, "h_sgT", [128, TPH], BF16)
                qg_tok, ki_tok, kd_tok, v_tok, sg_tok = Tok(), Tok(), Tok(), Tok(), Tok()
                Sbf = sb(S1, "h_Sbf", [128, 128], BF16)
                Sbf_tok = Tok()
                Ss = sb(S1, "h_Ss", [128, SPP, 128], F32)
                Ss_tok = Ss_ptok
                Ssb = sb(S1, "h_Ssb", [128, SPP, 128], BF16)
                Ssb_tok = Tok()
                Sn, Sn_tok = Ss, Ss_tok
                memset(rmask[:], 1.0, W=[rm_tok])
                memset(rmask[:, 0:TP].rearrange("p (c t) -> p c t", t=32)[:, :, 0:1], 0.0, W=[rm_tok])
                memset(rmask[:, TP:TPH].rearrange("p (c t) -> p c t", t=8)[:, :, 0:1], 0.0, W=[rm_tok])
                psblk = Ring([(psF[i % 5][:, (i // 5) * 128:(i // 5) * 128 + 128], bkt[i % 5]) for i in range(20)])
                obank = Ring([(psF[5], bkt[5]), (psF[6], bkt[6])])
                for h in range(H):
                    stage(7 + 0.1 * h)
                    lb = lbT[:, h, 0:1]
                    oml = lbT[:, h, 1:2]
                    with contextlib.ExitStack() as SPL:
                        fa = sb(SPL, "h_fa", [128, TPH], F32)
                        fa_tok = Tok()
                        fk = sb(SPL, "h_fk", [128, TPH], F32)
                        fk_tok = Tok()
                        fG = sb(SPL, "h_fG", [128, TPH], F32)
                        fG_tok = Tok()
                        fe = sb(SPL, "h_fe", [128, TPH], F32)
                        fe_tok = Tok()
                        sq_ = sb(SPL, "h_sq", [128, TPH], F32)
                        sq_tok = Tok()
                        for role in range(4):
                            wt, wtok = wload(Win, role * 1024 + h * 128)
                            for ti in range(3):
                                ps, ptok, n = proj_tile(wt, wtok, hT, htoks, ti, ring=pwide)
                                t0 = TILES[ti][0]
                                if role == 0:
                                    act(sq_[:, t0:t0 + n], ps[:, 0:n], AF.Silu, R=[ptok], W=[sq_tok])
                                elif role == 1:
                                    act(fa[:, t0:t0 + n], ps[:, 0:n], AF.Sigmoid, R=[ptok], W=[fa_tok])
                                elif role == 2:
                                    act(vT[:, t0:t0 + n], ps[:, 0:n], AF.Copy, R=[ptok], W=[v_tok])
                                else:
                                    act(sgT[:, t0:t0 + n], ps[:, 0:n], AF.Silu, R=[ptok], W=[sg_tok])
                        ts(fa[:], fa[:], oml, lb, ALU.mult, ALU.add, R=[fa_tok, lb_tok], W=[fa_tok])
                        ts(fk[:], fa[:], -1.0, 1.0, ALU.mult, ALU.add, R=[fa_tok], W=[fk_tok])
                        act(fa[:], fa[:], AF.Ln, R=[fa_tok], W=[fa_tok])
                        K.op("dve", lambda e: e.tensor_tensor_scan(out=fG[:], data0=rmask[:], data1=fa[:], initial=0.0, op0=ALU.mult, op1=ALU.add),
                             [fa_tok, rm_tok], [fG_tok])
                        act(eG[:], fG[:], AF.Exp, R=[fG_tok], W=[eG_tok])
                        tt(qgT[:], sq_[:], eG[:], ALU.mult, R=[sq_tok, eG_tok], W=[qg_tok])
                        act(fe[:], fG[:], AF.Exp, scale=-1.0, R=[fG_tok], W=[fe_tok])
                        tt(kiT[:], fk[:], fe[:], ALU.mult, R=[fk_tok, fe_tok], W=[ki_tok])
                        gp = fG[:, 0:TP].rearrange("p (c t) -> p c t", t=32)
                        tt(fa[:, 0:TP].rearrange("p (c t) -> p c t", t=32), gp[:, :, 31:32].broadcast_to([128, 32, 32]), gp, ALU.subtract,
                           R=[fG_tok], W=[fa_tok])
                        gs = fG[:, TP:TPH].rearrange("p (c t) -> p c t", t=8)
                        tt(fa[:, TP:TPH].rearrange("p (c t) -> p c t", t=8), gs[:, :, 7:8].broadcast_to([128, 8, 8]), gs, ALU.subtract,
                           R=[fG_tok], W=[fa_tok])
                        act(fe[:], fa[:], AF.Exp, R=[fa_tok], W=[fe_tok])
                        tt(kdT[:], fk[:], fe[:], ALU.mult, R=[fk_tok, fe_tok], W=[kd_tok])
                        K.barrier()

                    stage(7.05 + 0.1 * h)
                    K.dma("pool", Ss[:], shgrn_d[j, SPP * ph:SPP * ph + SPP, h].rearrange("s k v -> k s v"), W=[Ss_tok], semtok=Ss_tok)
                    act(Ssb[:], Ss[:], AF.Copy, R=[Ss_tok], W=[Ssb_tok])
                    if ph == 0:
                        memset(Sf_all[:, h, :], 0.0, W=[Sf_tok[h]])
                    cp(Sbf[:], Sf_all[:, h, :], R=[Sf_tok[h]], W=[Sbf_tok])

                    with contextlib.ExitStack() as SBK:
                        blks = blocks()
                        bufs = []
                        for bi, (c0, BS, kind) in enumerate(blks):
                            npad = 4 if kind == "P" else 8
                            bufs.append(dict(
                                attnT=(sb(SBK, "h_at%d" % bi, [128, 128], BF16), Tok()),
                                vtk=(sb(SBK, "h_vt%d" % bi, [128, 128], BF16), Tok()),
                                Kpad=(sb(SBK, "h_kp%d" % bi, [128, npad, 128], BF16), Tok()),
                                Qpad=(sb(SBK, "h_qp%d" % bi, [128, npad, 128], BF16), Tok())))
                        opf = Ring([(sb(SBK, "h_of%d" % i, [128, 128], F32), Tok()) for i in range(4)])
                        opb = Ring([(sb(SBK, "h_ob%d" % i, [128, 128], BF16), Tok()) for i in range(2)])

                        def consts_for(kind):
                            if kind == "P":
                                return (4, 32, cst[:, C_MIP:C_MIP + 128],
                                        cst[:, C_CM4:C_CM4 + 512].rearrange("p (c t) -> p c t", c=4), cst[:, C_RM4:C_RM4 + 4])
                            return (8, 8, cst[0:64, C_MIS:C_MIS + 64],
                                    cst[:, C_CM8:C_CM8 + 512].rearrange("p (c t) -> p c t", c=8), cst[0:64, C_RM8:C_RM8 + 8])

                        for bi, (c0, BS, kind) in enumerate(blks):
                            ncb, C, mi, cmask, rmk = consts_for(kind)
                            bf = bufs[bi]
                            cols = slice(c0, c0 + BS)
                            aps, aptok = psblk.get()
                            mm(aps[0:BS, 0:BS], kiT[:, cols], qgT[:, cols], R=[ki_tok, qg_tok], W=[aptok])
                            attnT, attok = bf["attnT"]
                            tt(attnT[0:BS, 0:BS], aps[0:BS, 0:BS], mi, ALU.mult, R=[aptok, cst_tok], W=[attok])
                            pvv, pvtok = pbf.get()
                            tr(pvv[0:BS, :], vT[:, cols], ident_bf[:], R=[v_tok, misc_tok], W=[pvtok])
                            vtk, vtktok = bf["vtk"]
                            act(vtk[0:BS, :], pvv[0:BS, :], AF.Copy, R=[pvtok], W=[vtktok])
                            pk, pktok = pbf.get()
                            tr(pk[0:BS, :], kdT[:, cols], ident_bf[:], R=[kd_tok, misc_tok], W=[pktok])
                            Kpad, kp_tok = bf["Kpad"]
                            tt(Kpad[0:BS, 0:ncb, :], pk[0:BS, :].unsqueeze(1).broadcast_to([BS, ncb, 128]),
                               rmk.unsqueeze(2).broadcast_to([BS, ncb, 128]), ALU.mult, R=[pktok, cst_tok], W=[kp_tok])
                            Qpad, qp_tok = bf["Qpad"]
                            tt(Qpad[:, 0:ncb, 0:BS], cmask[:, :, 0:BS], qgT[:, cols].unsqueeze(1).broadcast_to([128, ncb, BS]), ALU.mult,
                               R=[cst_tok, qg_tok], W=[qp_tok])

                        pending = None
                        for bi, (c0, BS, kind) in enumerate(blks):
                            ncb, C, mi, cmask, rmk = consts_for(kind)
                            bf = bufs[bi]
                            attnT, attok = bf["attnT"]
                            vtk, vtktok = bf["vtk"]
                            Kpad, kp_tok = bf["Kpad"]
                            Qpad, qp_tok = bf["Qpad"]
                            ops_, optok = obank.get()
                            if kind == "P":
                                for c in range(ncb):
                                    mm(ops_[0:BS, 0:128], Qpad[:, c, 0:BS], Sbf[:], start=(c == 0), stop=False, R=[qp_tok, Sbf_tok], W=[optok])
                                    sps, sptok = psblk.get()
                                    mm(sps[:, :], Kpad[0:BS, c, :], vtk[0:BS, :], R=[kp_tok, vtktok], W=[sptok])
                                    ce = c0 + (c + 1) * C - 1
                                    stt(Sbf[:], Sf_all[:, h, :], eG[:, ce:ce + 1], sps[:, :], ALU.mult, ALU.add,
                                        R=[sptok, eG_tok, Sf_tok[h]], W=[Sbf_tok])
                                    stt(Sf_all[:, h, :], Sf_all[:, h, :], eG[:, ce:ce + 1], sps[:, :], ALU.mult, ALU.add,
                                        R=[sptok, eG_tok], W=[Sf_tok[h]])
                                mm(ops_[0:BS, 0:128], attnT[0:BS, 0:BS], vtk[0:BS, :], start=False, stop=True, R=[attok, vtktok], W=[optok])
                            else:
                                for c in range(ncb):
                                    mm(ops_[0:BS, 0:128], Qpad[:, c, 0:BS], Ssb[:, c, :], start=(c == 0), stop=False, R=[qp_tok, Ssb_tok], W=[optok])
                                mm(ops_[0:BS, 0:128], attnT[0:BS, 0:BS], vtk[0:BS, :], start=False, stop=True, R=[attok, vtktok], W=[optok])
                                sl2 = [pbig.get(), pbig.get()]
                                for c in range(ncb):
                                    ps, ptok = sl2[c // 4]
                                    mm(ps[:, (c % 4) * 128:(c % 4) * 128 + 128], Kpad[0:BS, c, :], vtk[0:BS, :], R=[kp_tok, vtktok], W=[ptok])
                                ege = eG[:, TP:TPH].rearrange("p (s t) -> p s t", t=8)[:, :, 7:8]
                                tt(Sn[:], Ss[:], ege.broadcast_to([128, 8, 128]), ALU.mult, R=[Ss_tok, eG_tok], W=[Sn_tok])
                                for hb in range(2):
                                    ps, ptok = sl2[hb]
                                    tt(Sn[:, hb * 4:hb * 4 + 4, :], Sn[:, hb * 4:hb * 4 + 4, :], ps[:, :].rearrange("p (s v) -> p s v", v=128), ALU.add,
                                       R=[ptok], W=[Sn_tok])
                                K.dma("sp", shgrn_o[j, SPP * ph:SPP * ph + SPP, h].rearrange("s k v -> k s v"), Sn[:], R=[Sn_tok], semtok=Sn_tok)
                            if pending is not None:
                                o_post(*pending)
                            pending = (l, (ops_, optok), BS, h, c0, oT, otoks[min(c0 // 512, 2)], sgT, sg_tok, opf, opb)
                        o_post(*pending)
                        if ph == 1:
                            K.dma("sp", phgrn_d[j, h], Sf_all[:, h, :], R=[Sf_tok[h]], semtok=Sf_tok[h])
                        K.barrier()
                K.barrier()

        def ffn_phase(l, ph):
            A2, B2, G2 = modA[3], modA[4], modA[5]
            with contextlib.ExitStack() as SF:
                aT = sb(SF, "f_aT", [128, NFF, TPH], BF16)
                a_toks = [Tok() for _ in range(3)]
                fci = sb(SF, "f_fci", [128, NFF, 8, 2], F32)
                fci_tok = fci_ptok
                fco = sb(SF, "f_fco", [128, NFF, 9, 2], F32)
                fco_tok = fco_ptok
                K.dma("sp", fci[:].rearrange("p a s r -> p (a s r)"), fci_d[l, ph], W=[fci_tok], semtok=fci_tok)
                memset(fco[:], 0.0, W=[fco_tok])
                with contextlib.ExitStack() as S1:
                    hT = sb(S1, "f_hT", [128, KC, TPH], BF16)
                    htoks = [Tok() for _ in range(3)]
                    with contextlib.ExitStack() as SPN:
                        prenorm(ph, A2, B2, hT, htoks, SPN)
                        K.barrier()
                    gpre_r = Ring([(sb(S1, "f_gpre%d" % i, [128, 2 + TP + SPP * 10], F32), Tok()) for i in range(2)])
                    up_r = Ring([(sb(S1, "f_up%d" % i, [128, TPH], F32), Tok()) for i in range(2)])
                    gc_r = Ring([(sb(S1, "f_gc%d" % i, [128, TPH], F32), Tok()) for i in range(2)])
                    for jj in range(NFF):
                        gpre, gp_tok = gpre_r.get()
                        up, up_tok = up_r.get()
                        gc, gc_tok = gc_r.get()
                        pv = gpre[:, 2 + TP:2 + TP + SPP * 10].rearrange("p (s r) -> p s r", r=10)
                        wg, wgtok = wload(ffn_w_gu[l], jj * 128)
                        wu, wutok = wload(ffn_w_gu[l], DFF + jj * 128)
                        if ph == 0:
                            memset(gpre[:, 0:2], 0.0, W=[gp_tok])
                        else:
                            cp(gpre[:, 0:2], fcar[:, jj, :], R=[fcar_tok], W=[gp_tok])
                        cp(pv[:, :, 0:2], fci[:, jj, :, :], R=[fci_tok], W=[gp_tok])
                        for ti in range(3):
                            t0 = TILES[ti][0]
                            ps, ptok, n = proj_tile(wg, wgtok, hT, htoks, ti, ring=pwide)
                            if ti < 2:
                                act(gpre[:, 2 + t0:2 + t0 + n], ps[:, 0:n], AF.Copy, R=[ptok], W=[gp_tok])
                            else:
                                act(pv[:, :, 2:10], ps[:, 0:n].rearrange("p (s r) -> p s r", r=LS), AF.Copy, R=[ptok], W=[gp_tok])
                            ps, ptok, n = proj_tile(wu, wutok, hT, htoks, ti, ring=pwide)
                            act(up[:, t0:t0 + n], ps[:, 0:n], AF.Copy, R=[ptok], W=[up_tok])
                        if ph == 0:
                            cp(fcar[:, jj, :], gpre[:, TP:TP + 2], R=[gp_tok], W=[fcar_tok])
                        else:
                            cp(fco[:, jj, 8, :], gpre[:, TP:TP + 2], R=[gp_tok], W=[fco_tok])
                        cp(fco[:, jj, 0:8, :], pv[:, :, 8:10], R=[gp_tok], W=[fco_tok])
                        wc = prm[:, l, P_FCW + jj * 3:P_FCW + jj * 3 + 3]
                        bc = prm[:, l, P_FCB + jj:P_FCB + jj + 1]
                        gcs = gc[:, TP:TPH].rearrange("p (s r) -> p s r", r=LS)
                        ts(gc[:, 0:TP], gpre[:, 0:TP], wc[:, 0:1], bc, ALU.mult, ALU.add, R=[gp_tok, prm_tok], W=[gc_tok])
                        ts(gcs, pv[:, :, 0:8], wc[:, 0:1], bc, ALU.mult, ALU.add, R=[gp_tok, prm_tok], W=[gc_tok])
                        for tap in range(1, 3):
                            stt(gc[:, 0:TP], gpre[:, tap:tap + TP], wc[:, tap:tap + 1], gc[:, 0:TP], ALU.mult, ALU.add,
                                R=[gp_tok, prm_tok], W=[gc_tok])
                            stt(gcs, pv[:, :, tap:tap + 8], wc[:, tap:tap + 1], gcs, ALU.mult, ALU.add, R=[gp_tok, prm_tok], W=[gc_tok])
                        act(gc[:], gc[:], AF.Silu, R=[gc_tok], W=[gc_tok])
                        for ti, (t0, n) in enumerate(TILES):
                            tt(aT[:, jj, t0:t0 + n], gc[:, t0:t0 + n], up[:, t0:t0 + n], ALU.mult, R=[gc_tok, up_tok], W=[a_toks[ti]])
                    K.dma("sp", fco_d[l, ph], fco[:].rearrange("p a s r -> p (a s r)"), R=[fco_tok], semtok=fco_tok)
                    K.barrier()
                with contextlib.ExitStack() as S2:
                    y = sb(S2, "f_y", [128, KC, TPH], F32)
                    ytok = Tok()
                    wdn = Ring([(sb(S2, "f_wd%d" % i, [128, NFF, 128], BF16), wdn_ptoks[i]) for i in range(2)])
                    sqt = (sb(S2, "f_sq", [128, KC, 256], BF16), Tok())
                    rs = (sb(S2, "f_rs", [128, 256], F32), Tok())
                    tmps = Ring([(sb(S2, "f_t%d" % i, [128, 256], F32), Tok()) for i in range(2)])
                    for oc in range(KC):
                        wt, wtok = wdn.get()
                        K.dma("pool", wt[:], ffn_w_down[l][:, oc * 128:(oc + 1) * 128].rearrange("(j p) n -> p j n", p=128), W=[wtok], semtok=wtok)
                        for ti, (t0, n) in enumerate(TILES):
                            ps, ptok = pwide.get()
                            for jj in range(NFF):
                                mm(ps[:, 0:n], wt[:, jj, :], aT[:, jj, t0:t0 + n], start=(jj == 0), stop=(jj == NFF - 1),
                                   R=[wtok, a_toks[ti]], W=[ptok])
                            cp(y[:, oc, t0:t0 + n], ps[:, 0:n], R=[ptok], W=[ytok])
                    for ti, (t0, n) in enumerate([(0, 256), (256, 256), (512, 256), (768, 256), (1024, 64)]):
                        rstd_fm(y[:, :, t0:t0 + n], n, [ytok], sqt, rs)
                        for kc in range(KC):
                            tmp, ttok = tmps.get()
                            tt(tmp[:, 0:n], y[:, kc, t0:t0 + n], rs[0][:, 0:n], ALU.mult, R=[ytok, rs[1]], W=[ttok])
                            if t0 < TP:
                                stt(xT[:, kc, ph, t0:t0 + n], tmp[:, 0:n], G2[:, kc, 0:1], xT[:, kc, ph, t0:t0 + n], ALU.mult, ALU.add,
                                    R=[ttok, mod_tok], W=[xtok[ph]])
                            else:
                                sc = seqcols(ph)
                                v3 = tmp[:, 0:n].rearrange("p (s j) -> p s j", j=LS)
                                tt(v3, v3, G2[:, kc, sc].unsqueeze(2).broadcast_to([128, SPP, LS]), ALU.mult, R=[ttok, mod_tok], W=[ttok])
                                x3 = xT[:, kc, ph, t0:t0 + n].rearrange("p (s j) -> p s j", j=LS)
                                tt(x3, x3, v3, ALU.add, R=[ttok], W=[xtok[ph]])
                    K.barrier()

        def main_program():
            for l in range(depth):
                with contextlib.ExitStack() as SL:
                    stage(1)
                    adaln(l, SL)
                    if l % 2 == 1:
                        jj_ = l // 2
                        if jj_ == 0:
                            memset(lbT[:, :, 0:1], 0.0, W=[lb_tok])
                            memset(lbT[:, :, 1:2], 1.0, W=[lb_tok])
                        else:
                            hl = prm[:, l, P_HLB:P_HLB + 16].rearrange("p (h t) -> p h t", t=2)
                            tt(lbT[:, :, 0:1], hl[:, :, 1:2], hl[:, :, 0:1], ALU.subtract, R=[prm_tok], W=[lb_tok])
                            act(lbT[:, :, 0:1], lbT[:, :, 0:1], AF.Sigmoid, R=[lb_tok], W=[lb_tok])
                            ts(lbT[:, :, 1:2], lbT[:, :, 0:1], -1.0, 1.0, ALU.mult, ALU.add, R=[lb_tok], W=[lb_tok])
                    K.barrier()
                for ph in range(2):
                    with contextlib.ExitStack() as SA:
                        hT = sb(SA, "m_hT", [128, KC, TPH], BF16)
                        htoks = [Tok() for _ in range(3)]
                        oT = sb(SA, "m_oT", [128, KC, TPH], BF16)
                        otoks = [Tok() for _ in range(3)]
                        with contextlib.ExitStack() as SP:
                            stage(2)
                            prenorm(ph, modA[0], modA[1], hT, htoks, SP)
                            K.barrier()
                        with contextlib.ExitStack() as SM:
                            if l % 2 == 0:
                                gdn_phase(l, ph, hT, htoks, oT, otoks, SM)
                            else:
                                hgrn_phase(l, ph, hT, htoks, oT, otoks, SM)
                            K.barrier()
                        with contextlib.ExitStack() as SO:
                            stage(5)
                            Wout = gdn_w_out[l // 2] if l % 2 == 0 else hgrn_w_out[l // 2]
                            outproj_postnorm(l, ph, Wout, oT, otoks, modA[2], SO)
                            K.barrier()
                    stage(6)
                    ffn_phase(l, ph)
            for ph in range(2):
                K.dma("sp", yT_d.rearrange("p (k h t) -> p k h t", k=KC, h=2)[:, :, ph, :], xT[:, :, ph, :], R=[xtok[ph]], semtok=xtok[ph])

        try:
            main_program()
        except StopBuild:
            K.barrier()
        K.final_wait("sp")

        with nc.Block() as block:
            K.replay(block)
    return nc


def _consts():
    c = np.zeros((128, NCST), np.float32)
    c[:, C_IDENT:C_IDENT + 128] = np.eye(128, dtype=np.float32)
    i = np.arange(128)[:, None]
    jx = np.arange(128)[None, :]
    BIG = 1.0e4
    valid = (i // 64 == jx // 64) & (i > jx)
    c[:, C_PENP:C_PENP + 128] = np.where(valid, 0.0, BIG)
    i8 = np.arange(64)[:, None]
    j8 = np.arange(64)[None, :]
    valid = (i8 // 8 == j8 // 8) & (i8 > j8)
    c[0:64, C_PENS:C_PENS + 64] = np.where(valid, 0.0, BIG)
    for ncb, off, bs in ((2, C_CM2, 128), (4, C_CM4, 128), (8, C_CM8, 64)):
        C = bs // ncb
        m = np.zeros((ncb, bs), np.float32)
        for cc in range(ncb):
            m[cc, cc * C:(cc + 1) * C] = 1.0
        c[:, off:off + ncb * bs] = m.reshape(1, -1)
    for ncb, off, bs in ((2, C_RM2, 128), (4, C_RM4, 128), (8, C_RM8, 64)):
        C = bs // ncb
        m = np.zeros((bs, ncb), np.float32)
        for cc in range(ncb):
            m[cc * C:(cc + 1) * C, cc] = 1.0
        c[0:bs, off:off + ncb] = m
    c[:, C_MIP:C_MIP + 128] = ((i // 32 == jx // 32) & (i <= jx)).astype(np.float32)
    c[0:64, C_MIS:C_MIS + 64] = ((i8 // 8 == j8 // 8) & (i8 <= j8)).astype(np.float32)
    e = np.zeros((8, 8, 128), np.float32)
    for h in range(8):
        e[h, h, :] = 1.0
    c[0:8, C_ESEL:C_ESEL + 1024] = e.reshape(8, 1024)
    return c


def _fm(v):
    sh = v.shape
    k = sh[-1] // 128
    v = v.reshape(sh[:-1] + (k, 128))
    return np.moveaxis(np.moveaxis(v, -1, 0), -1, 1)


_NC_CACHE = {}


def kernel(x_prompt, x_sample, state_gdn, state_gdn_conv, state_hgrn, state_ffn_conv, c_prompt, c_sample,
           ada_w, ada_b, norm_pre_mix, norm_post_mix, norm_pre_ffn, norm_post_ffn,
           gdn_w_in, gdn_conv_w, gdn_conv_b, gdn_a_log, gdn_dt_bias, gdn_norm, gdn_w_out,
           hgrn_lb, hgrn_w_in, hgrn_norm, hgrn_w_out,
           ffn_w_gu, ffn_conv_w, ffn_conv_b, ffn_w_down, _depth=DEPTH, _stage=None):
    f32 = np.float32
    A = lambda a: np.ascontiguousarray(np.asarray(a, dtype=f32))
    x_prompt, x_sample = A(x_prompt), A(x_sample)
    state_gdn, state_gdn_conv, state_hgrn, state_ffn_conv = A(state_gdn), A(state_gdn_conv), A(state_hgrn), A(state_ffn_conv)
    c_prompt, c_sample = A(c_prompt), A(c_sample)
    prm = np.zeros((128, 4, NPRM), f32)
    nrow = np.zeros((128, 4, 128), f32)
    ada_b_, gcw, gcb = A(ada_b), A(gdn_conv_w), A(gdn_conv_b)
    fcw, fcb = A(ffn_conv_w), A(ffn_conv_b)
    hlb = A(hgrn_lb)
    for l in range(4):
        prm[:, l, P_ADAB:P_ADAB + 48] = ada_b_[l].reshape(48, 128).T
        prm[:, l, P_NPRE_MIX:P_NPRE_MIX + 8] = A(norm_pre_mix)[l].reshape(8, 128).T
        prm[:, l, P_NPOST_MIX:P_NPOST_MIX + 8] = A(norm_post_mix)[l].reshape(8, 128).T
        prm[:, l, P_NPRE_FFN:P_NPRE_FFN + 8] = A(norm_pre_ffn)[l].reshape(8, 128).T
        prm[:, l, P_NPOST_FFN:P_NPOST_FFN + 8] = A(norm_post_ffn)[l].reshape(8, 128).T
        j = l // 2
        if l % 2 == 0:
            prm[:, l, P_GCW:P_GCW + 96] = gcw[j].reshape(4, 24, 128).transpose(2, 1, 0).reshape(128, 96)
            prm[:, l, P_GCB:P_GCB + 24] = gcb[j].reshape(24, 128).T
            prm[0:8, l, P_ALOG] = A(gdn_a_log)[j]
            prm[0:8, l, P_DTB] = A(gdn_dt_bias)[j]
            nrow[:, l, :] = A(gdn_norm)[j][None, :]
        else:
            nrow[:, l, :] = A(hgrn_norm)[j][None, :]
        prm[:, l, P_FCW:P_FCW + 66] = fcw[l].reshape(3, 22, 128).transpose(2, 1, 0).reshape(128, 66)
        prm[:, l, P_FCB:P_FCB + 22] = fcb[l].reshape(22, 128).T
        prm[:, l, P_HLB:P_HLB + 16] = hlb.reshape(2, 8, 128).transpose(2, 1, 0).reshape(128, 16)
    cst = _consts()
    shared = dict(prm=prm.reshape(128, -1), nrow=nrow.reshape(128, -1), cst=cst,
                  ada_w=A(ada_w), gdn_w_in=A(gdn_w_in), gdn_w_out=A(gdn_w_out), hgrn_w_in=A(hgrn_w_in),
                  hgrn_w_out=A(hgrn_w_out), ffn_w_gu=A(ffn_w_gu), ffn_w_down=A(ffn_w_down))
    in_maps = []
    for c in range(NCORE):
        xt = np.zeros((128, KC, 2, TPH), f32)
        xp = _fm(x_prompt[c])
        xs = _fm(x_sample[16 * c:16 * c + 16])
        for ph in range(2):
            xt[:, :, ph, 0:TP] = xp[:, :, TP * ph:TP * ph + TP]
            xt[:, :, ph, TP:] = xs[:, :, 8 * ph:8 * ph + 8, :].reshape(128, KC, TS)
        cc = np.concatenate([c_prompt[c:c + 1], c_sample[16 * c:16 * c + 16]], 0)
        cT = _fm(cc)
        gc = _fm(state_gdn_conv[:, 16 * c:16 * c + 16])
        gci = np.zeros((2, 2, 128, 24, 8, 3), f32)
        fc = _fm(state_ffn_conv[:, 16 * c:16 * c + 16])
        fci = np.zeros((4, 2, 128, NFF, 8, 2), f32)
        for ph in range(2):
            gci[:, ph] = gc[:, :, :, 8 * ph:8 * ph + 8, :].transpose(2, 0, 1, 3, 4)
            fci[:, ph] = fc[:, :, :, 8 * ph:8 * ph + 8, :].transpose(2, 0, 1, 3, 4)
        m = dict(shared)
        m.update(xT=xt.reshape(128, -1), cT=np.ascontiguousarray(cT).reshape(128, -1),
                 sgdn=np.ascontiguousarray(state_gdn[:, 16 * c:16 * c + 16]),
                 shgrn=np.ascontiguousarray(state_hgrn[:, 16 * c:16 * c + 16]),
                 gci=gci.reshape(2, 2, 128, -1), fci=fci.reshape(4, 2, 128, -1))
        in_maps.append(m)
    ck = (_depth, _stage)
    if ck not in _NC_CACHE:
        _STAGE_LIMIT[0] = _stage
        _STAGE_LIMIT[1] = False
        _NC_CACHE[ck] = build_nc(_depth)
        _STAGE_LIMIT[0] = None
        _STAGE_LIMIT[1] = False
    nc = _NC_CACHE[ck]
    res = run_bass_kernel_spmd(nc, in_maps, core_ids=list(range(NCORE)))
    R = res.results
    y_prompt = np.zeros((8, 2048, D), f32)
    y_sample = np.zeros((128, 8, D), f32)
    p_gdn = np.zeros((2, 8, H, 128, 128), f32)
    p_hgrn = np.zeros((2, 8, H, 128, 128), f32)
    s_gdn = np.zeros((2, 128, H, 128, 128), f32)
    s_hgrn = np.zeros((2, 128, H, 128, 128), f32)
    p_gconv = np.zeros((2, 8, 3, 3072), f32)
    s_gconv = np.zeros((2, 128, 3, 3072), f32)
    p_fconv = np.zeros((4, 8, 2, DFF), f32)
    s_fconv = np.zeros((4, 128, 2, DFF), f32)
    for c in range(NCORE):
        r = R[c]
        yt = r["yT"].reshape(128, KC, 2, TPH)
        for ph in range(2):
            y_prompt[c, TP * ph:TP * ph + TP] = yt[:, :, ph, 0:TP].transpose(2, 1, 0).reshape(TP, D)
            ys = yt[:, :, ph, TP:].reshape(128, KC, 8, 8)
            y_sample[16 * c + 8 * ph:16 * c + 8 * ph + 8] = ys.transpose(2, 3, 1, 0).reshape(8, 8, D)
        p_gdn[:, c] = r["pgdn"]
        p_hgrn[:, c] = r["phgrn"]
        s_gdn[:, 16 * c:16 * c + 16] = r["sgdn_o"]
        s_hgrn[:, 16 * c:16 * c + 16] = r["shgrn_o"]
        g = r["gco"].reshape(2, 2, 128, 24, 9, 3)
        f = r["fco"].reshape(4, 2, 128, NFF, 9, 2)
        for ph in range(2):
            gs = g[:, ph, :, :, 0:8, :].transpose(0, 3, 4, 2, 1).reshape(2, 8, 3, 3072)
            s_gconv[:, 16 * c + 8 * ph:16 * c + 8 * ph + 8] = gs
            fs = f[:, ph, :, :, 0:8, :].transpose(0, 3, 4, 2, 1).reshape(4, 8, 2, DFF)
            s_fconv[:, 16 * c + 8 * ph:16 * c + 8 * ph + 8] = fs
        p_gconv[:, c] = g[:, 1, :, :, 8, :].transpose(0, 3, 2, 1).reshape(2, 3, 3072)
        p_fconv[:, c] = f[:, 1, :, :, 8, :].transpose(0, 3, 2, 1).reshape(4, 2, DFF)
    return (y_prompt, y_sample, p_gdn, p_gconv, p_hgrn, p_fconv, s_gdn, s_gconv, s_hgrn, s_fconv)
```

```python
import contextlib
import numpy as np
import concourse.bass as bass
import concourse.mybir as mybir
from concourse.bass_utils import run_bass_kernel_spmd

F32 = mybir.dt.float32
BF16 = mybir.dt.bfloat16
AF = mybir.ActivationFunctionType
ALU = mybir.AluOpType

NCORE = 8
D = 1024
KC = 8
H = 8
DFF = 2816
NFF = 22
DEPTH = 4
TP = 1024
SPP = 8
LS = 8
TS = SPP * LS
TPH = TP + TS
TILES = [(0, 512), (512, 512), (1024, 64)]
EPS = 1e-6
NPRM = 48 + 32 + 96 + 24 + 66 + 22 + 16 + 2

P_ADAB = 0
P_NPRE_MIX = 48
P_NPOST_MIX = 56
P_NPRE_FFN = 64
P_NPOST_FFN = 72
P_GCW = 80
P_GCB = 176
P_FCW = 200
P_FCB = 266
P_HLB = 288
P_ALOG = 304
P_DTB = 305

C_IDENT = 0
C_PENP = 128
C_PENS = 256
C_CM2 = 320
C_CM4 = 576
C_CM8 = 1088
C_RM2 = 1600
C_RM4 = 1602
C_RM8 = 1606
C_MIP = 1614
C_MIS = 1742
C_ESEL = 1806
NCST = 1806 + 1024


class Tok:
    __slots__ = ("w", "r", "dsem", "dval", "excl")

    def __init__(self, excl=False):
        self.w = {}
        self.r = {}
        self.dsem = None
        self.dval = 0
        self.excl = excl


class Sched:
    ENGS = ("pe", "act", "dve", "pool", "sp")

    def __init__(self, nc, es):
        self.nc = nc
        self.es = es
        self.q = {k: [] for k in self.ENGS}
        self.n = {k: 0 for k in self.ENGS}
        self.seen = {k: {} for k in self.ENGS}
        self.sem = {}
        for k in ("pe", "act", "dve", "pool"):
            self.sem[k] = es.enter_context(nc.semaphore("s_" + k))
        self.dma_latest = {}
        self.nd = 0

    def _waits(self, en, R, W):
        need = {}

        def add(ev):
            key, sem, val = ev
            if key not in need or need[key][1] < val:
                need[key] = (sem, val)

        for t in R:
            for ev in t.w.values():
                add(ev)
            if t.excl:
                for ev in t.r.values():
                    add(ev)
        for t in W:
            for ev in t.w.values():
                add(ev)
            for ev in t.r.values():
                add(ev)
        out = []
        seen = self.seen[en]
        for key, (sem, val) in need.items():
            if key == en and (en == "pe" or _NO_SAME_ENGINE_WAIT[0]):
                continue
            if seen.get(key, 0) >= val:
                continue
            seen[key] = val
            out.append((sem, val))
        return out

    def op(self, en, fn, R=(), W=()):
        if _STAGE_LIMIT[1]:
            return
        waits = self._waits(en, R, W)
        self.n[en] += 1
        sem = self.sem[en]
        self.q[en].append((waits, fn, sem, 1))
        ev = (en, sem, self.n[en])
        for t in W:
            t.w[en] = ev
        for t in R:
            t.r[en] = ev

    def dma(self, qn, out, in_, R=(), W=(), semtok=None):
        if _STAGE_LIMIT[1]:
            return
        waits = self._waits(qn, R, W)
        t = semtok
        if t.dsem is None:
            t.dsem = {}
        if qn not in t.dsem:
            self.nd += 1
            t.dsem[qn] = [self.es.enter_context(self.nc.semaphore("d%d" % self.nd)), 0]
        ent = t.dsem[qn]
        ent[1] += 16
        dsem, dval = ent[0], ent[1]
        key = "d%d_%s" % (id(t), qn)
        self.q[qn].append((waits, (lambda e: e.dma_start(out=out, in_=in_)), dsem, 16))
        ev = (key, dsem, dval)
        for x in W:
            x.w[key] = ev
        for x in R:
            x.r[key] = ev
        self.dma_latest[key] = (dsem, dval)

    def barrier(self):
        for en in self.ENGS:
            waits = []
            seen = self.seen[en]
            for k in ("pe", "act", "dve", "pool"):
                if k == en and en == "pe":
                    continue
                v = self.n[k]
                if v > 0 and seen.get(k, 0) < v:
                    seen[k] = v
                    waits.append((self.sem[k], v))
            for key, (sem, val) in self.dma_latest.items():
                if seen.get(key, 0) < val:
                    seen[key] = val
                    waits.append((sem, val))
            if waits:
                self.q[en].append((waits, None, None, 0))

    def nsems(self):
        return self.nd + 4

    def final_wait(self, en="sp"):
        waits = []
        for key, (sem, val) in self.dma_latest.items():
            waits.append((sem, val))
        for k in ("pe", "act", "dve", "pool"):
            if self.n[k] > 0:
                waits.append((self.sem[k], self.n[k]))
        self.q[en].append((waits, None, None, 0))

    def replay(self, block):
        q = self.q

        def run(e, lst):
            for waits, fn, sem, inc in lst:
                if fn is None or not _INLINE_WAIT[0] or not waits:
                    for s, v in waits:
                        e.wait_ge(s, v)
                    if fn is not None:
                        fn(e).then_inc(sem, inc)
                else:
                    for s, v in waits[:-1]:
                        e.wait_ge(s, v)
                    s, v = waits[-1]
                    fn(e)._wait_ge(s, v).then_inc(sem, inc)

        @block.tensor
        def _(e):
            run(e, q["pe"])

        @block.scalar
        def _(e):
            run(e, q["act"])

        @block.vector
        def _(e):
            run(e, q["dve"])

        @block.gpsimd
        def _(e):
            run(e, q["pool"])

        @block.sync
        def _(e):
            run(e, q["sp"])


class StopBuild(Exception):
    pass


_STAGE_LIMIT = [None, False]
_NO_SAME_ENGINE_WAIT = [False]
_INLINE_WAIT = [True]


def stage(n):
    if _STAGE_LIMIT[0] is not None and n > _STAGE_LIMIT[0]:
        _STAGE_LIMIT[1] = True


class Ring:
    def __init__(self, items):
        self.items = items
        self.i = 0

    def get(self):
        r = self.items[self.i]
        self.i = (self.i + 1) % len(self.items)
        return r


def build_nc(depth=DEPTH):
    nc = bass.Bass("TRN2", target_bir_lowering=False)

    def din(name, shape):
        return nc.dram_tensor(name, list(shape), F32, kind="ExternalInput").ap()

    def dout(name, shape):
        return nc.dram_tensor(name, list(shape), F32, kind="ExternalOutput").ap()

    xT_d = din("xT", [128, KC * 2 * TPH])
    cT_d = din("cT", [128, KC * 17])
    sgdn_d = din("sgdn", [2, 16, H, 128, 128])
    shgrn_d = din("shgrn", [2, 16, H, 128, 128])
    gci_d = din("gci", [2, 2, 128, 24 * 8 * 3])
    fci_d = din("fci", [4, 2, 128, NFF * 8 * 2])
    prm_d = din("prm", [128, 4 * NPRM])
    nrow_d = din("nrow", [128, 4 * 128])
    cst_d = din("cst", [128, NCST])
    ada_w = din("ada_w", [4, D, 6 * D])
    gdn_w_in = din("gdn_w_in", [2, D, 4112])
    gdn_w_out = din("gdn_w_out", [2, D, D])
    hgrn_w_in = din("hgrn_w_in", [2, D, 4096])
    hgrn_w_out = din("hgrn_w_out", [2, D, D])
    ffn_w_gu = din("ffn_w_gu", [4, D, 2 * DFF])
    ffn_w_down = din("ffn_w_down", [4, DFF, D])

    yT_d = dout("yT", [128, KC * 2 * TPH])
    pgdn_d = dout("pgdn", [2, H, 128, 128])
    sgdn_o = dout("sgdn_o", [2, 16, H, 128, 128])
    phgrn_d = dout("phgrn", [2, H, 128, 128])
    shgrn_o = dout("shgrn_o", [2, 16, H, 128, 128])
    gco_d = dout("gco", [2, 2, 128, 24 * 9 * 3])
    fco_d = dout("fco", [4, 2, 128, NFF * 9 * 2])

    with contextlib.ExitStack() as es:
        K = Sched(nc, es)

        cnt = [0]

        def sb(stack, name, shape, dt):
            cnt[0] += 1
            return stack.enter_context(nc.sbuf_tensor("sb%d_%s" % (cnt[0], name), list(shape), dt))

        xT = sb(es, "xT", [128, KC, 2, TPH], F32)
        xtok = [Tok(), Tok()]
        cst = sb(es, "cst", [128, NCST], F32)
        cst_tok = Tok()
        prm = sb(es, "prm", [128, 4, NPRM], F32)
        prm_tok = Tok()
        nrow = sb(es, "nrow", [128, 4, 128], F32)
        nrow_tok = Tok()
        ident_bf = sb(es, "ident_bf", [128, 128], BF16)
        ones_bf = sb(es, "ones_bf", [128, 128], BF16)
        cb = sb(es, "cb", [128, 8], F32)
        misc_tok = Tok()
        csT = sb(es, "csT", [128, KC, 17], BF16)
        cs_tok = Tok()
        _ms = [sb(es, "modA%d" % i, [128, KC, 17], F32) for i in range(6)]
        modsets = [_ms, _ms]
        mod_tok = Tok()
        Sf_all = sb(es, "Sf_all", [128, H, 128], F32)
        Sf_tok = [Tok() for _ in range(H)]
        gcar = sb(es, "gcar", [128, 24, 3], F32)
        gcar_tok = Tok()
        fcar = sb(es, "fcar", [128, NFF, 2], F32)
        fcar_tok = Tok()
        lbT = sb(es, "lbT", [128, H, 2], F32)
        lb_tok = Tok()
        wun = [(sb(es, "wun%d" % i, [128, KC, 128], BF16), Tok()) for i in range(5)]
        wring = Ring(wun)
        wres_toks = [Tok() for _ in range(KC)]
        Ss_ptok, gci_ptok, gco_ptok, fci_ptok, fco_ptok = Tok(), Tok(), Tok(), Tok(), Tok()
        wdn_ptoks = [Tok(), Tok()]

        psF = [es.enter_context(nc.psum_tensor("psF%d" % i, [128, 512], F32)) for i in range(7)]
        psB = es.enter_context(nc.psum_tensor("psB", [128, 1024], BF16))
        bkt = [Tok(excl=True) for _ in range(8)]
        pbig = Ring([(psF[0], bkt[0]), (psF[1], bkt[1])])
        psmall = Ring([(psF[2 + i % 3][:, (i // 3) * 128:(i // 3) * 128 + 128], bkt[2 + i % 3]) for i in range(12)])
        pbf = Ring([(psB[:, i * 128:(i + 1) * 128], bkt[7]) for i in range(8)])
        o_ps, o_ptok = psF[5], bkt[5]
        u_ps, u_ptok = psF[6], bkt[6]
        pwide = Ring([(psF[i], bkt[i]) for i in range(7)])

        def mm(out, lhsT, rhs, start=True, stop=True, R=(), W=()):
            K.op("pe", lambda e: e.matmul(out, lhsT, rhs, start=start, stop=stop), R, W)

        def tr(out, in_, idn, R=(), W=()):
            K.op("pe", lambda e: e.transpose(out, in_, idn), R, W)

        def act(out, in_, func, bias=None, scale=None, accum=None, R=(), W=()):
            kw = {}
            if bias is not None:
                kw["bias"] = bias
            if scale is not None:
                kw["scale"] = scale
            if accum is not None:
                kw["accum_out"] = accum
            K.op("act", lambda e: e.activation(out=out, in_=in_, func=func, **kw), R, W)

        def tt(out, a, b, op, R=(), W=(), en="dve"):
            K.op(en, lambda e: e.tensor_tensor(out=out, in0=a, in1=b, op=op), R, W)

        def ts(out, a, s1, s2, op0, op1=None, R=(), W=(), en="dve"):
            if op1 is None:
                K.op(en, lambda e: e.tensor_scalar(out=out, in0=a, scalar1=s1, scalar2=None, op0=op0), R, W)
            else:
                K.op(en, lambda e: e.tensor_scalar(out=out, in0=a, scalar1=s1, scalar2=s2, op0=op0, op1=op1), R, W)

        def stt(out, a, sc, b, op0, op1, R=(), W=(), en="dve"):
            K.op(en, lambda e: e.scalar_tensor_tensor(out=out, in0=a, scalar=sc, in1=b, op0=op0, op1=op1), R, W)

        def cp(out, in_, R=(), W=(), en="dve"):
            K.op(en, lambda e: e.tensor_copy(out=out, in_=in_), R, W)

        def recip(out, in_, R=(), W=()):
            K.op("dve", lambda e: e.reciprocal(out=out, in_=in_), R, W)

        def memset(ap, val, W=(), en="dve"):
            K.op(en, lambda e: e.memset(ap, val), (), W)

        def wload(W2d, c0, ncols=128):
            t, tok = wring.get()
            K.dma("pool", t[:, :, 0:ncols], W2d[:, c0:c0 + ncols].rearrange("(k p) n -> p k n", p=128), W=[tok], semtok=tok)
            return t, tok

        K.dma("sp", cst[:], cst_d[:, :], W=[cst_tok], semtok=cst_tok)
        K.dma("sp", prm[:].rearrange("p l n -> p (l n)"), prm_d[:, :], W=[prm_tok], semtok=prm_tok)
        K.dma("sp", nrow[:].rearrange("p l n -> p (l n)"), nrow_d[:, :], W=[nrow_tok], semtok=nrow_tok)
        for ph in range(2):
            K.dma("sp", xT[:, :, ph, :], xT_d.rearrange("p (k h t) -> p k h t", k=KC, h=2)[:, :, ph, :], W=[xtok[ph]], semtok=xtok[ph])
        ident = cst[:, C_IDENT:C_IDENT + 128]
        cp(ident_bf[:], ident, R=[cst_tok], W=[misc_tok])
        memset(ones_bf[:], 1.0, W=[misc_tok])
        memset(cb[:, 0:1], 1024.0 * EPS, W=[misc_tok])
        memset(cb[:, 1:2], EPS, W=[misc_tok])
        memset(cb[:, 2:3], 128.0 * EPS, W=[misc_tok])
        memset(cb[:, 3:4], 1.0, W=[misc_tok])
        memset(cb[:, 4:5], 0.0, W=[misc_tok])
        ts(nrow[:], nrow[:], float(np.sqrt(128.0)), None, ALU.mult, R=[nrow_tok], W=[nrow_tok])
        for l in range(4):
            ts(prm[:, l, P_NPRE_MIX:P_NPRE_MIX + 32], prm[:, l, P_NPRE_MIX:P_NPRE_MIX + 32], 32.0, None, ALU.mult, R=[prm_tok], W=[prm_tok])
        with contextlib.ExitStack() as s0:
            cTt = sb(s0, "cTt", [128, KC, 17], F32)
            ctok = Tok()
            K.dma("sp", cTt[:].rearrange("p k q -> p (k q)"), cT_d[:, :], W=[ctok], semtok=ctok)
            act(csT[:], cTt[:], AF.Silu, R=[ctok], W=[cs_tok])
            K.barrier()

        def adaln_gen(l, S):
            modA = modsets[l % 2]
            mod = sb(S, "modraw", [128, 48, 17], F32)
            mtok = Tok()
            slots = [(psF[5], bkt[5]), (psF[6], bkt[6])]
            for cc in range(48):
                wt, wtok = wload(ada_w[l], cc * 128)
                ps, ptok = slots[cc // 24]
                col = (cc % 24) * 17
                for kc in range(KC):
                    mm(ps[:, col:col + 17], wt[:, kc, :], csT[:, kc, :], start=(kc == 0), stop=(kc == KC - 1),
                       R=[wtok, cs_tok], W=[ptok])
                yield
            for bk in range(2):
                ps, ptok = slots[bk]
                tt(mod[:, 24 * bk:24 * bk + 24, :], ps[:, 0:408].rearrange("p (c q) -> p c q", q=17),
                   prm[:, l, P_ADAB + 24 * bk:P_ADAB + 24 * bk + 24].unsqueeze(2).broadcast_to([128, 24, 17]),
                   ALU.add, R=[ptok, prm_tok], W=[mtok])
            def nb(off):
                return prm[:, l, off:off + 8].unsqueeze(2).broadcast_to([128, 8, 17])
            stt(modA[0][:], mod[:, 8:16, :], 1.0, nb(P_NPRE_MIX), ALU.add, ALU.mult, R=[mtok, prm_tok], W=[mod_tok])
            cp(modA[1][:], mod[:, 0:8, :], R=[mtok], W=[mod_tok])
            stt(modA[2][:], mod[:, 16:24, :], 1.0, nb(P_NPOST_MIX), ALU.add, ALU.mult, R=[mtok, prm_tok], W=[mod_tok])
            stt(modA[3][:], mod[:, 32:40, :], 1.0, nb(P_NPRE_FFN), ALU.add, ALU.mult, R=[mtok, prm_tok], W=[mod_tok])
            cp(modA[4][:], mod[:, 24:32, :], R=[mtok], W=[mod_tok])
            stt(modA[5][:], mod[:, 40:48, :], 1.0, nb(P_NPOST_FFN), ALU.add, ALU.mult, R=[mtok, prm_tok], W=[mod_tok])

        def seqcols(ph):
            return slice(1 + SPP * ph, 1 + SPP * ph + SPP)

        def rstd_fm(src3, n, R, sqt, rs):
            sq, sqtok = sqt
            rst, rstok = rs
            act(sq[:, :, 0:n], src3, AF.Square, R=R, W=[sqtok])
            ps, ptok = pbig.get()
            for kc in range(KC):
                mm(ps[:, 0:n], ones_bf[:], sq[:, kc, 0:n], start=(kc == 0), stop=(kc == KC - 1), R=[sqtok, misc_tok], W=[ptok])
            act(rst[:, 0:n], ps[:, 0:n], AF.Ln, bias=cb[:, 0:1], scale=1.0, R=[ptok, misc_tok], W=[rstok])
            act(rst[:, 0:n], rst[:, 0:n], AF.Exp, scale=-0.5, R=[rstok], W=[rstok])

        def prenorm(ph, A, B, hT, htoks, S):
            sqt_r = Ring([(sb(S, "pn_sq%d" % i, [128, KC, 512], BF16), Tok()) for i in range(2)])
            rs_r = Ring([(sb(S, "pn_rs%d" % i, [128, 512], F32), Tok()) for i in range(2)])
            tmps = Ring([(sb(S, "pn_t%d" % i, [128, 512], F32), Tok()) for i in range(3)])
            for ti, (t0, n) in enumerate(TILES):
                sqt = sqt_r.get()
                rs = rs_r.get()
                rstd_fm(xT[:, :, ph, t0:t0 + n], n, [xtok[ph]], sqt, rs)
                for kc in range(KC):
                    tmp, ttok = tmps.get()
                    tt(tmp[:, 0:n], xT[:, kc, ph, t0:t0 + n], rs[0][:, 0:n], ALU.mult, R=[xtok[ph], rs[1]], W=[ttok])
                    if ti < 2:
                        act(hT[:, kc, t0:t0 + n], tmp[:, 0:n], AF.Identity, bias=B[:, kc, 0:1], scale=A[:, kc, 0:1],
                            R=[ttok, mod_tok], W=[htoks[ti]])
                    else:
                        sc = seqcols(ph)
                        tt(tmp[:, 0:n].rearrange("p (s j) -> p s j", j=LS), tmp[:, 0:n].rearrange("p (s j) -> p s j", j=LS),
                           A[:, kc, sc].unsqueeze(2).broadcast_to([128, SPP, LS]), ALU.mult, R=[ttok, mod_tok], W=[ttok])
                        tt(hT[:, kc, t0:t0 + n].rearrange("p (s j) -> p s j", j=LS), tmp[:, 0:n].rearrange("p (s j) -> p s j", j=LS),
                           B[:, kc, sc].unsqueeze(2).broadcast_to([128, SPP, LS]), ALU.add, R=[ttok, mod_tok], W=[htoks[ti]])

        def proj_tile(wt, wtok, hT, htoks, ti, M=128, ring=None):
            t0, n = TILES[ti]
            ps, ptok = (ring or pbig).get()
            for kc in range(KC):
                mm(ps[0:M, 0:n], wt[:, kc, 0:M], hT[:, kc, t0:t0 + n], start=(kc == 0), stop=(kc == KC - 1),
                   R=[wtok, htoks[ti]], W=[ptok])
            return ps, ptok, n

        def proj_all(wt, wtok, hT, htoks, ring=None, M=128):
            sl3 = [(ring or pwide).get() for _ in range(3)]
            for kc in range(KC):
                for ti, (t0, n) in enumerate(TILES):
                    ps, ptok = sl3[ti]
                    mm(ps[0:M, 0:n], wt[:, kc, 0:M], hT[:, kc, t0:t0 + n], start=(kc == 0), stop=(kc == KC - 1),
                       R=[wtok, htoks[ti]], W=[ptok])
            return [(sl3[ti][0], sl3[ti][1], TILES[ti][1]) for ti in range(3)]

        def outproj_postnorm(l, ph, Wout2d, oT, otoks, G, S):
            y = sb(S, "op_y", [128, KC, 512], F32)
            ytok = Tok()
            sqt = (sb(S, "op_sq", [128, KC, 512], BF16), Tok())
            rs = (sb(S, "op_rs", [128, 512], F32), Tok())
            tmps = Ring([(sb(S, "op_t%d" % i, [128, 512], F32), Tok()) for i in range(2)])
            wres = sb(S, "op_w", [128, KC, KC, 128], BF16)
            wrtok = wres_toks
            for oc in range(KC):
                K.dma("pool", wres[:, oc, :, :], Wout2d[:, oc * 128:(oc + 1) * 128].rearrange("(k p) n -> p k n", p=128), W=[wrtok[oc]], semtok=wrtok[oc])
            for ti, (t0, n) in enumerate(TILES):
                for oc in range(KC):
                    ps, ptok = pwide.get()
                    for kc in range(KC):
                        mm(ps[:, 0:n], wres[:, oc, kc, :], oT[:, kc, t0:t0 + n], start=(kc == 0), stop=(kc == KC - 1),
                           R=[wrtok[oc], otoks[ti]], W=[ptok])
                    cp(y[:, oc, 0:n], ps[:, 0:n], R=[ptok], W=[ytok])
                resid_update(ph, t0, n, y, ytok, G, sqt, rs, tmps)

        def resid_update(ph, t0, n, y, ytok, G, sqt, rs, tmps):
            rstd_fm(y[:, :, 0:n], n, [ytok], sqt, rs)
            for kc in range(KC):
                tmp, ttok = tmps.get()
                tt(tmp[:, 0:n], y[:, kc, 0:n], rs[0][:, 0:n], ALU.mult, R=[ytok, rs[1]], W=[ttok])
                if t0 < TP:
                    stt(xT[:, kc, ph, t0:t0 + n], tmp[:, 0:n], G[:, kc, 0:1], xT[:, kc, ph, t0:t0 + n], ALU.mult, ALU.add,
                        R=[ttok, mod_tok], W=[xtok[ph]])
                else:
                    sc = seqcols(ph)
                    v3 = tmp[:, 0:n].rearrange("p (s j) -> p s j", j=LS)
                    tt(v3, v3, G[:, kc, sc].unsqueeze(2).broadcast_to([128, SPP, LS]), ALU.mult, R=[ttok, mod_tok], W=[ttok])
                    x3 = xT[:, kc, ph, t0:t0 + n].rearrange("p (s j) -> p s j", j=LS)
                    tt(x3, x3, v3, ALU.add, R=[ttok], W=[xtok[ph]])

        def blocks():
            return [(b * 128, 128, "P") for b in range(8)] + [(TP, 64, "S")]

        def o_post(l, o_rows, BS, h, c0, oT, otok_t, sgT, sgtok, tmpf, tmpb):
            o_ps_, o_ptok_ = o_rows if o_rows is not None else (o_ps, o_ptok)
            junk, jtok = tmpf.get()
            ss, sstok = tmpf.get()
            act(junk[0:BS, :], o_ps_[0:BS, 0:128], AF.Square, accum=ss[0:BS, 0:1], R=[o_ptok_], W=[jtok, sstok])
            act(ss[0:BS, 0:1], ss[0:BS, 0:1], AF.Ln, bias=cb[0:BS, 2:3], scale=1.0, R=[sstok, misc_tok], W=[sstok])
            act(ss[0:BS, 0:1], ss[0:BS, 0:1], AF.Exp, scale=-0.5, R=[sstok], W=[sstok])
            on, ontok = tmpb.get()
            stt(on[0:BS, :], o_ps_[0:BS, 0:128], ss[0:BS, 0:1], nrow[0:BS, l, :], ALU.mult, ALU.mult,
                R=[o_ptok_, sstok, nrow_tok], W=[ontok])
            pb, pbtok = pbf.get()
            tr(pb[:, 0:BS], on[0:BS, :], ident_bf[0:BS, 0:BS], R=[ontok, misc_tok], W=[pbtok])
            tt(oT[:, h, c0:c0 + BS], pb[:, 0:BS], sgT[:, c0:c0 + BS], ALU.mult, R=[pbtok, sgtok], W=[otok_t])

        def gdn_phase(l, ph, hT, htoks, oT, otoks, S):
            j = l // 2
            Win = gdn_w_in[j]
            stage(3)
            G = 4
            dG = sb(S, "g_dG", [8, TPH], F32)
            dtok_ = Tok()
            dtk = sb(S, "g_dtk", [128, 9, 3, 8], F32)
            ex = sb(S, "g_ex", [128, 9, 4, 8], F32)
            glb = sb(S, "g_glb", [128, H, 24], F32)
            dk_tok = Tok()
            gci = sb(S, "g_gci", [128, 24, 8, 3], F32)
            gci_tok = gci_ptok
            gco = sb(S, "g_gco", [128, 24, 9, 3], F32)
            gco_tok = gco_ptok
            K.dma("sp", gci[:].rearrange("p a s r -> p (a s r)"), gci_d[j, ph], W=[gci_tok], semtok=gci_tok)
            memset(gco[:], 0.0, W=[gco_tok])
            with contextlib.ExitStack() as SD:
                dB = sb(SD, "g_dB", [8, TPH], F32)
                dL = sb(SD, "g_dL", [8, TPH], F32)
                dg = sb(SD, "g_dg", [8, TPH], F32)
                rmask = dL
                nega = sb(SD, "g_nega", [8, 1], F32)
                memset(rmask[:], 1.0, W=[dtok_])
                memset(rmask[:, 0:TP].rearrange("p (c t) -> p c t", t=64)[:, :, 0:1], 0.0, W=[dtok_])
                memset(rmask[:, TP:TPH].rearrange("p (c t) -> p c t", t=8)[:, :, 0:1], 0.0, W=[dtok_])
                act(nega[:], prm[0:8, l, P_ALOG:P_ALOG + 1], AF.Exp, R=[prm_tok], W=[dtok_])
                ts(nega[:], nega[:], -1.0, None, ALU.mult, R=[dtok_], W=[dtok_])
                wb, wbtok = wload(Win, 4096, 8)
                for ti in range(3):
                    ps, ptok, n = proj_tile(wb, wbtok, hT, htoks, ti, M=8)
                    t0 = TILES[ti][0]
                    act(dB[:, t0:t0 + n], ps[0:8, 0:n], AF.Sigmoid, R=[ptok], W=[dtok_])
                wa, watok = wload(Win, 4104, 8)
                for ti in range(3):
                    ps, ptok, n = proj_tile(wa, watok, hT, htoks, ti, M=8)
                    t0 = TILES[ti][0]
                    act(dg[:, t0:t0 + n], ps[0:8, 0:n], AF.Exp, bias=prm[0:8, l, P_DTB:P_DTB + 1], scale=1.0, R=[ptok, prm_tok], W=[dtok_])
                act(dg[:], dg[:], AF.Ln, bias=cb[0:8, 3:4], scale=1.0, R=[dtok_, misc_tok], W=[dtok_])
                ts(dg[:], dg[:], nega[:, 0:1], None, ALU.mult, R=[dtok_], W=[dtok_])
                K.op("dve", lambda e, dG=dG, rmask=rmask, dg=dg: e.tensor_tensor_scan(out=dG[:], data0=rmask[:], data1=dg[:], initial=0.0, op0=ALU.mult, op1=ALU.add),
                     [dtok_], [dtok_])
                gp = dG[:, 0:TP].rearrange("p (c t) -> p c t", t=64)
                tt(dL[:, 0:TP].rearrange("p (c t) -> p c t", t=64), gp[:, :, 63:64].broadcast_to([8, 16, 64]), gp, ALU.subtract, R=[dtok_], W=[dtok_])
                gs = dG[:, TP:TPH].rearrange("p (c t) -> p c t", t=8)
                tt(dL[:, TP:TPH].rearrange("p (c t) -> p c t", t=8), gs[:, :, 7:8].broadcast_to([8, 8, 8]), gs, ALU.subtract, R=[dtok_], W=[dtok_])
                ps, ptok = pbig.get()
                for bi, (c0, BS, kind) in enumerate(blocks()):
                    for qi, src in enumerate((dB, dG, dL)):
                        col = (bi * 3 + qi) * 8
                        tr(ps[0:BS, col:col + 8], src[:, c0:c0 + BS], cst[0:8, C_IDENT:C_IDENT + 8], R=[dtok_, cst_tok], W=[ptok])
                memset(dtk[:], 0.0, W=[dk_tok])
                cp(dtk[:, 0:8].rearrange("p b q h -> p (b q h)"), ps[:, 0:192], R=[ptok], W=[dk_tok])
                cp(dtk[0:64, 8].rearrange("p q h -> p (q h)"), ps[0:64, 192:216], R=[ptok], W=[dk_tok])
                act(ex[:, :, 0:2, :], dtk[:, :, 1:3, :], AF.Exp, R=[dk_tok], W=[dk_tok])
                tt(ex[:, :, 2, :], dtk[:, :, 0, :], ex[:, :, 0, :], ALU.mult, R=[dk_tok], W=[dk_tok])
                ts(ex[:, :, 3, :], dtk[:, :, 0, :], -1.0, None, ALU.mult, R=[dk_tok], W=[dk_tok])
                ps, ptok = pbig.get()
                for h in range(H):
                    esel_h = cst[0:8, C_ESEL + h * 128:C_ESEL + (h + 1) * 128]
                    mm(ps[:, h * 24:h * 24 + 16], esel_h, dG[:, 0:TP].rearrange("p (c t) -> p c t", t=64)[:, :, 63], R=[dtok_, cst_tok], W=[ptok])
                    mm(ps[:, h * 24 + 16:h * 24 + 24], esel_h, dG[:, TP:TPH].rearrange("p (c t) -> p c t", t=8)[:, :, 7], R=[dtok_, cst_tok], W=[ptok])
                act(glb[:].rearrange("p h c -> p (h c)"), ps[:, 0:192], AF.Exp, R=[ptok], W=[dk_tok])
                K.barrier()

            with contextlib.ExitStack() as S1:
                qT = sb(S1, "g_qT", [128, TPH], BF16)
                kT = sb(S1, "g_kT", [128, TPH], BF16)
                vT = sb(S1, "g_vT", [128, TPH], BF16)
                sgT = sb(S1, "g_sgT", [128, TPH], BF16)
                q_tok, k_tok, v_tok, sg_tok = Tok(), Tok(), Tok(), Tok()
                Sbf = sb(S1, "g_Sbf", [128, 128], BF16)
                Sbf_tok = Tok()
                Ss = sb(S1, "g_Ss", [128, SPP, 128], F32)
                Ss_tok = Ss_ptok
                Ssb = sb(S1, "g_Ssb", [128, SPP, 128], BF16)
                Ssb_tok = Tok()
                Sn, Sn_tok = Ss, Ss_tok
                u_sb = sb(S1, "g_usb", [128, 128], BF16)
                usb_tok = Tok()
                memset(u_sb[:], 0.0, W=[usb_tok])
                psblk = Ring([(psF[i % 4][:, (i // 4) * 128:(i // 4) * 128 + 128], bkt[i % 4]) for i in range(16)])
                obank = Ring([(psF[5], bkt[5]), (psF[4], bkt[4])])

                for h in range(H):
                    stage(4 + 0.1 * h)
                    with contextlib.ExitStack() as SPJ:
                        pre_r = Ring([(sb(SPJ, "g_pre%d" % i, [128, 3 + TP + SPP * 11], F32), Tok()) for i in range(2)])
                        cv_r = Ring([(sb(SPJ, "g_cv%d" % i, [128, TPH], F32), Tok()) for i in range(2)])
                        sq_r = Ring([(sb(SPJ, "g_sqb%d" % i, [128, 512], BF16), Tok()) for i in range(2)])
                        ri_r = Ring([(sb(SPJ, "g_rinv%d" % i, [128, 512], F32), Tok()) for i in range(2)])
                        for role in range(4):
                            wt, wtok = wload(Win, role * 1024 + h * 128)
                            if role < 3:
                                pre, pre_tok = pre_r.get()
                                cv, cv_tok = cv_r.get()
                                pv = pre[:, 3 + TP:3 + TP + SPP * 11].rearrange("p (s r) -> p s r", r=11)
                                ch = role * 8 + h
                                if ph == 0:
                                    memset(pre[:, 0:3], 0.0, W=[pre_tok])
                                else:
                                    cp(pre[:, 0:3], gcar[:, ch, :], R=[gcar_tok], W=[pre_tok])
                                cp(pv[:, :, 0:3], gci[:, ch, :, :], R=[gci_tok], W=[pre_tok])
                            pall = proj_all(wt, wtok, hT, htoks)
                            for ti in range(3):
                                ps, ptok, n = pall[ti]
                                t0 = TILES[ti][0]
                                if role == 3:
                                    act(sgT[:, t0:t0 + n], ps[:, 0:n], AF.Silu, R=[ptok], W=[sg_tok])
                                elif ti < 2:
                                    act(pre[:, 3 + t0:3 + t0 + n], ps[:, 0:n], AF.Copy, R=[ptok], W=[pre_tok])
                                else:
                                    act(pv[:, :, 3:11], ps[:, 0:n].rearrange("p (s r) -> p s r", r=LS), AF.Copy, R=[ptok], W=[pre_tok])
                            if role == 3:
                                continue
                            if ph == 0:
                                cp(gcar[:, ch, :], pre[:, TP:TP + 3], R=[pre_tok], W=[gcar_tok])
                            else:
                                cp(gco[:, ch, 8, :], pre[:, TP:TP + 3], R=[pre_tok], W=[gco_tok])
                            cp(gco[:, ch, 0:8, :], pv[:, :, 8:11], R=[pre_tok], W=[gco_tok])
                            wc = prm[:, l, P_GCW + ch * 4:P_GCW + ch * 4 + 4]
                            bc = prm[:, l, P_GCB + ch:P_GCB + ch + 1]
                            cvs = cv[:, TP:TPH].rearrange("p (s r) -> p s r", r=LS)
                            ts(cv[:, 0:TP], pre[:, 0:TP], wc[:, 0:1], bc, ALU.mult, ALU.add, R=[pre_tok, prm_tok], W=[cv_tok])
                            ts(cvs, pv[:, :, 0:8], wc[:, 0:1], bc, ALU.mult, ALU.add, R=[pre_tok, prm_tok], W=[cv_tok])
                            for tap in range(1, 4):
                                stt(cv[:, 0:TP], pre[:, tap:tap + TP], wc[:, tap:tap + 1], cv[:, 0:TP], ALU.mult, ALU.add,
                                    R=[pre_tok, prm_tok], W=[cv_tok])
                                stt(cvs, pv[:, :, tap:tap + 8], wc[:, tap:tap + 1], cvs, ALU.mult, ALU.add, R=[pre_tok, prm_tok], W=[cv_tok])
                            if role == 2:
                                act(vT[:], cv[:], AF.Silu, R=[cv_tok], W=[v_tok])
                                continue
                            act(cv[:], cv[:], AF.Silu, R=[cv_tok], W=[cv_tok])
                            for ti, (t0, n) in enumerate(TILES):
                                sqb, sqb_tok = sq_r.get()
                                rinv, rinv_tok = ri_r.get()
                                act(sqb[:, 0:n], cv[:, t0:t0 + n], AF.Square, R=[cv_tok], W=[sqb_tok])
                                ps, ptok = pwide.get()
                                mm(ps[:, 0:n], ones_bf[:], sqb[:, 0:n], R=[sqb_tok, misc_tok], W=[ptok])
                                act(rinv[:, 0:n], ps[:, 0:n], AF.Ln, bias=cb[:, 1:2], scale=1.0, R=[ptok, misc_tok], W=[rinv_tok])
                                act(rinv[:, 0:n], rinv[:, 0:n], AF.Exp, scale=-0.5, R=[rinv_tok], W=[rinv_tok])
                                if role == 0:
                                    stt(qT[:, t0:t0 + n], cv[:, t0:t0 + n], float(128.0 ** -0.5), rinv[:, 0:n], ALU.mult, ALU.mult,
                                        R=[cv_tok, rinv_tok], W=[q_tok])
                                else:
                                    tt(kT[:, t0:t0 + n], cv[:, t0:t0 + n], rinv[:, 0:n], ALU.mult, R=[cv_tok, rinv_tok], W=[k_tok])
                        K.barrier()

                    stage(4.05 + 0.1 * h)
                    K.dma("pool", Ss[:], sgdn_d[j, SPP * ph:SPP * ph + SPP, h].rearrange("s k v -> k s v"), W=[Ss_tok], semtok=Ss_tok)
                    act(Ssb[:], Ss[:], AF.Copy, R=[Ss_tok], W=[Ssb_tok])
                    if ph == 0:
                        memset(Sf_all[:, h, :], 0.0, W=[Sf_tok[h]])
                    cp(Sbf[:], Sf_all[:, h, :], R=[Sf_tok[h]], W=[Sbf_tok])

                    with contextlib.ExitStack() as SBK:
                        slots = []
                        for k in range(G):
                            sl = {}
                            sl["X"] = [(sb(SBK, "g_x%d_%d" % (k, i), [128, 128], F32), Tok()) for i in range(6)]
                            sl["kbg"] = (sb(SBK, "g_kbg%d" % k, [128, 128], BF16), Tok())
                            sl["HS"] = [[(sb(SBK, "g_hs%d_%d_%d" % (k, par, i), [128, 128], BF16), Tok()) for i in range(6)] for par in range(2)]
                            slots.append(sl)
                        spad = {}
                        for nm in ("Kpad", "Wpad", "Qpad", "tw"):
                            spad[nm] = (sb(SBK, "g_s%s" % nm, [128, 8, 128], BF16), Tok())
                        spad["kdm"] = (sb(SBK, "g_skdm", [128, 8], F32), Tok())
                        opf = Ring([(sb(SBK, "g_of%d" % i, [128, 128], F32), Tok()) for i in range(4)])
                        opb = Ring([(sb(SBK, "g_ob%d" % i, [128, 128], BF16), Tok()) for i in range(2)])

                        def pre_state(bi, c0, BS, kind, sl, par):
                            if kind == "P":
                                ncb, C, nlev = 2, 64, 5
                                pen = cst[:, C_PENP:C_PENP + 128]
                            else:
                                ncb, C, nlev = 8, 8, 2
                                pen = cst[0:64, C_PENS:C_PENS + 64]
                                cmask = cst[:, C_CM8:C_CM8 + 512].rearrange("p (c t) -> p c t", c=8)
                                rmk = cst[0:64, C_RM8:C_RM8 + 8]
                            cols = slice(c0, c0 + BS)
                            X = sl["X"]
                            HS = sl["HS"][par]
                            Gcol = dtk[0:BS, bi, 1, h:h + 1]
                            beta_c = dtk[0:BS, bi, 0, h:h + 1]
                            eGL_c = ex[0:BS, bi, 1, h:h + 1]
                            bG_c = ex[0:BS, bi, 2, h:h + 1]
                            nbeta_c = ex[0:BS, bi, 3, h:h + 1]
                            esel_h = cst[0:8, C_ESEL + h * 128:C_ESEL + (h + 1) * 128]
                            gps, gptok = psblk.get()
                            mm(gps[:, 0:BS], esel_h, dG[:, cols], R=[dtok_, cst_tok], W=[gptok])
                            yield
                            r_, rtok = X[0]
                            stt(r_[0:BS, 0:BS], gps[0:BS, 0:BS], Gcol, pen, ALU.subtract, ALU.max, R=[gptok, dk_tok, cst_tok], W=[rtok])
                            eGr, egtok = X[2]
                            act(eGr[:, 0:BS], gps[:, 0:BS], AF.Exp, R=[gptok], W=[egtok])
                            yield
                            Ds, dstok = X[1]
                            act(Ds[0:BS, 0:BS], r_[0:BS, 0:BS], AF.Exp, scale=-1.0, R=[rtok], W=[dstok])
                            if kind == "P":
                                qg, qgtok = HS[3]
                                tt(qg[:, 0:BS], qT[:, cols], eGr[:, 0:BS], ALU.mult, R=[q_tok, egtok], W=[qgtok])
                            else:
                                tw, twtok = spad["tw"]
                                Qpad, qp_tok = spad["Qpad"]
                                tt(tw[:, 0:ncb, 0:BS], cmask[:, :, 0:BS], eGr[:, 0:BS].unsqueeze(1).broadcast_to([128, ncb, BS]), ALU.mult,
                                   R=[cst_tok, egtok], W=[twtok])
                                tt(Qpad[:, 0:ncb, 0:BS], tw[:, 0:ncb, 0:BS], qT[:, cols].unsqueeze(1).broadcast_to([128, ncb, BS]), ALU.mult,
                                   R=[twtok, q_tok], W=[qp_tok])
                            yield
                            dps, dptok = psblk.get()
                            tr(dps[0:BS, 0:BS], Ds[0:BS, 0:BS], ident[0:BS, 0:BS], R=[dstok, cst_tok], W=[dptok])
                            kkps, kktok = psblk.get()
                            mm(kkps[0:BS, 0:BS], kT[:, cols], kT[:, cols], R=[k_tok], W=[kktok])
                            qkps, qktok = psblk.get()
                            mm(qkps[0:BS, 0:BS], kT[:, cols], qT[:, cols], R=[k_tok, q_tok], W=[qktok])
                            yield
                            DmT, dmtok = X[0]
                            tt(DmT[0:BS, 0:BS], dps[0:BS, 0:BS], ident[0:BS, 0:BS], ALU.add, R=[dptok, cst_tok], W=[dmtok])
                            B, btok = X[2]
                            stt(B[0:BS, 0:BS], kkps[0:BS, 0:BS], nbeta_c, Ds[0:BS, 0:BS], ALU.mult, ALU.mult, R=[kktok, dk_tok, dstok], W=[btok])
                            yield
                            attnT, attok = HS[0]
                            tt(attnT[0:BS, 0:BS], qkps[0:BS, 0:BS], DmT[0:BS, 0:BS], ALU.mult, R=[qktok, dmtok], W=[attok])
                            btps, btptok = psblk.get()
                            tr(btps[0:BS, 0:BS], B[0:BS, 0:BS], ident[0:BS, 0:BS], R=[btok, cst_tok], W=[btptok])
                            yield
                            Bt, bttok = X[3]
                            act(Bt[0:BS, 0:BS], btps[0:BS, 0:BS], AF.Copy, R=[btptok], W=[bttok])
                            TT, tttok = X[4]
                            tt(TT[0:BS, 0:BS], btps[0:BS, 0:BS], ident[0:BS, 0:BS], ALU.add, R=[btptok, cst_tok], W=[tttok])
                            yield
                            cur = (X[2], X[3])
                            nxt = (X[5], X[1])
                            for lev in range(nlev):
                                last = (lev == nlev - 1)
                                (B, btok), (Bt, bttok) = cur
                                (Bn, bntok), (Btn, btntok) = nxt
                                p2, p2tok = psblk.get()
                                mm(p2[0:BS, 0:BS], Bt[0:BS, 0:BS], B[0:BS, 0:BS], R=[bttok, btok], W=[p2tok])
                                if not last:
                                    p1, p1tok = psblk.get()
                                    mm(p1[0:BS, 0:BS], B[0:BS, 0:BS], Bt[0:BS, 0:BS], R=[bttok, btok], W=[p1tok])
                                yield
                                cp(Bn[0:BS, 0:BS], p2[0:BS, 0:BS], R=[p2tok], W=[bntok])
                                if not last:
                                    act(Btn[0:BS, 0:BS], p1[0:BS, 0:BS], AF.Copy, R=[p1tok], W=[btntok])
                                yield
                                p3, p3tok = psblk.get()
                                mm(p3[0:BS, 0:BS], Bn[0:BS, 0:BS], TT[0:BS, 0:BS], R=[bntok, tttok], W=[p3tok])
                                yield
                                tt(TT[0:BS, 0:BS], p3[0:BS, 0:BS], TT[0:BS, 0:BS], ALU.add, R=[p3tok, tttok], W=[tttok])
                                yield
                                cur, nxt = nxt, cur
                            TTb, ttbtok = HS[1]
                            act(TTb[0:BS, 0:BS], TT[0:BS, 0:BS], AF.Copy, R=[tttok], W=[ttbtok])
                            pk, pktok = pbf.get()
                            tr(pk[0:BS, :], kT[:, cols], ident_bf[:], R=[k_tok, misc_tok], W=[pktok])
                            pvv, pvtok = pbf.get()
                            tr(pvv[0:BS, :], vT[:, cols], ident_bf[:], R=[v_tok, misc_tok], W=[pvtok])
                            yield
                            vb, vbtok = HS[2]
                            act(vb[0:BS, :], pvv[0:BS, :], AF.Copy, scale=beta_c, R=[pvtok, dk_tok], W=[vbtok])
                            kbg, kbgtok = sl["kbg"]
                            act(kbg[0:BS, :], pk[0:BS, :], AF.Copy, scale=bG_c, R=[pktok, dk_tok], W=[kbgtok])
                            if kind == "P":
                                kd, kdtok = HS[5]
                                act(kd[0:BS, :], pk[0:BS, :], AF.Copy, scale=eGL_c, R=[pktok, dk_tok], W=[kdtok])
                            else:
                                kdm, kdmtok = spad["kdm"]
                                Kpad, kp_tok = spad["Kpad"]
                                ts(kdm[0:BS, 0:ncb], rmk, eGL_c, None, ALU.mult, R=[cst_tok, dk_tok], W=[kdmtok])
                                tt(Kpad[0:BS, 0:ncb, :], pk[0:BS, :].unsqueeze(1).broadcast_to([BS, ncb, 128]),
                                   kdm[0:BS, 0:ncb].unsqueeze(2).broadcast_to([BS, ncb, 128]), ALU.mult, R=[pktok, kdmtok], W=[kp_tok])
                            yield
                            wps, wptok = psblk.get()
                            mm(wps[:, 0:BS], kbg[0:BS, :], TTb[0:BS, 0:BS], R=[kbgtok, ttbtok], W=[wptok])
                            yield
                            if kind == "P":
                                Wneg, wntok = HS[4]
                                act(Wneg[:, 0:BS], wps[:, 0:BS], AF.Copy, scale=-1.0, R=[wptok], W=[wntok])
                            else:
                                Wpad, wp_tok = spad["Wpad"]
                                stt(Wpad[:, 0:ncb, 0:BS], wps[:, 0:BS].unsqueeze(1).broadcast_to([128, ncb, BS]), -1.0, cmask[:, :, 0:BS],
                                    ALU.mult, ALU.mult, R=[wptok, cst_tok], W=[wp_tok])

                        pend = [None]

                        def flush_post():
                            if pend[0] is not None:
                                o_post(*pend[0])
                                pend[0] = None

                        def state_group(grp, par):
                            for k, bi in enumerate(grp):
                                c0, BS, kind = blks[bi]
                                sl = slots[k]
                                HS = sl["HS"][par]
                                attnT, attok = HS[0]
                                TTb, ttbtok = HS[1]
                                vb, vbtok = HS[2]
                                ops_, optok = obank.get()
                                if kind == "P":
                                    qg, qgtok = HS[3]
                                    Wneg, wntok = HS[4]
                                    kd, kdtok = HS[5]
                                    for c in range(2):
                                        rows = slice(c * 64, (c + 1) * 64)
                                        mm(u_ps[rows, 0:128], TTb[rows, rows], vb[rows, :], start=True, stop=False, R=[ttbtok, vbtok], W=[u_ptok])
                                        mm(u_ps[rows, 0:128], Wneg[:, rows], Sbf[:], start=False, stop=True, R=[wntok, Sbf_tok], W=[u_ptok])
                                        act(u_sb[rows, :], u_ps[rows, 0:128], AF.Copy, R=[u_ptok], W=[usb_tok])
                                        mm(ops_[rows, 0:128], qg[:, rows], Sbf[:], start=True, stop=False, R=[qgtok, Sbf_tok], W=[optok])
                                        mm(ops_[rows, 0:128], attnT[rows, rows], u_sb[rows, :], start=False, stop=True, R=[attok, usb_tok], W=[optok])
                                        sps, sptok = psblk.get()
                                        mm(sps[:, :], kd[rows, :], u_sb[rows, :], R=[kdtok, usb_tok], W=[sptok])
                                        gl = glb[:, h, (bi * 2 + c):(bi * 2 + c) + 1]
                                        stt(Sbf[:], Sf_all[:, h, :], gl, sps[:, :], ALU.mult, ALU.add, R=[sptok, dk_tok, Sf_tok[h]], W=[Sbf_tok])
                                        stt(Sf_all[:, h, :], Sf_all[:, h, :], gl, sps[:, :], ALU.mult, ALU.add, R=[sptok, dk_tok], W=[Sf_tok[h]])
                                        yield
                                else:
                                    ncb = 8
                                    Kpad, kp_tok = spad["Kpad"]
                                    Wpad, wp_tok = spad["Wpad"]
                                    Qpad, qp_tok = spad["Qpad"]
                                    mm(u_ps[0:BS, 0:128], TTb[0:BS, 0:BS], vb[0:BS, :], start=True, stop=False, R=[ttbtok, vbtok], W=[u_ptok])
                                    for c in range(ncb):
                                        mm(u_ps[0:BS, 0:128], Wpad[:, c, 0:BS], Ssb[:, c, :], start=False, stop=(c == ncb - 1), R=[wp_tok, Ssb_tok], W=[u_ptok])
                                    act(u_sb[0:BS, :], u_ps[0:BS, 0:128], AF.Copy, R=[u_ptok], W=[usb_tok])
                                    for c in range(ncb):
                                        mm(ops_[0:BS, 0:128], Qpad[:, c, 0:BS], Ssb[:, c, :], start=(c == 0), stop=False, R=[qp_tok, Ssb_tok], W=[optok])
                                    mm(ops_[0:BS, 0:128], attnT[0:BS, 0:BS], u_sb[0:BS, :], start=False, stop=True, R=[attok, usb_tok], W=[optok])
                                    sl2 = [pbig.get(), pbig.get()]
                                    for c in range(ncb):
                                        ps, ptok = sl2[c // 4]
                                        mm(ps[:, (c % 4) * 128:(c % 4) * 128 + 128], Kpad[0:BS, c, :], u_sb[0:BS, :], R=[kp_tok, usb_tok], W=[ptok])
                                    tt(Sn[:], Ss[:], glb[:, h, 16:24].unsqueeze(2).broadcast_to([128, 8, 128]), ALU.mult, R=[Ss_tok, dk_tok], W=[Sn_tok])
                                    for hb in range(2):
                                        ps, ptok = sl2[hb]
                                        tt(Sn[:, hb * 4:hb * 4 + 4, :], Sn[:, hb * 4:hb * 4 + 4, :], ps[:, :].rearrange("p (s v) -> p s v", v=128), ALU.add,
                                           R=[ptok], W=[Sn_tok])
                                    K.dma("sp", sgdn_o[j, SPP * ph:SPP * ph + SPP, h].rearrange("s k v -> k s v"), Sn[:], R=[Sn_tok], semtok=Sn_tok)
                                flush_post()
                                pend[0] = (l, (ops_, optok), BS, h, c0, oT, otoks[min(c0 // 512, 2)], sgT, sg_tok, opf, opb)
                                yield

                        def run_rr(gens):
                            alive = list(gens)
                            while alive:
                                nx = []
                                for g in alive:
                                    try:
                                        next(g)
                                        nx.append(g)
                                    except StopIteration:
                                        pass
                                alive = nx

                        blks = blocks()
                        groups = [[0, 1, 2, 3], [4, 5, 6, 7], [8]]
                        stage(4.06 + 0.1 * h)
                        run_rr([pre_state(bi, blks[bi][0], blks[bi][1], blks[bi][2], slots[k], 0) for k, bi in enumerate(groups[0])])
                        run_rr([state_group(groups[0], 0)] +
                               [pre_state(bi, blks[bi][0], blks[bi][1], blks[bi][2], slots[k], 1) for k, bi in enumerate(groups[1])])
                        run_rr([state_group(groups[1], 1)] +
                               [pre_state(8, blks[8][0], blks[8][1], blks[8][2], slots[0], 0)])
                        run_rr([state_group(groups[2], 0)])
                        flush_post()
                        if ph == 1:
                            K.dma("sp", pgdn_d[j, h], Sf_all[:, h, :], R=[Sf_tok[h]], semtok=Sf_tok[h])
                        K.barrier()
                K.dma("sp", gco_d[j, ph], gco[:].rearrange("p a s r -> p (a s r)"), R=[gco_tok], semtok=gco_tok)
                K.barrier()

        def hgrn_phase(l, ph, hT, htoks, oT, otoks, S):
            j = l // 2
            Win = hgrn_w_in[j]
            with contextlib.ExitStack() as S1:
                rmask = sb(S1, "h_rm", [128, TPH], F32)
                rm_tok = Tok()
                eG = sb(S1, "h_eG", [128, TPH], F32)
                eG_tok = Tok()
                qgT = sb(S1, "h_qgT", [128, TPH], BF16)
                kiT = sb(S1, "h_kiT", [128, TPH], BF16)
                kdT = sb(S1, "h_kdT", [128, TPH], BF16)
                vT = sb(S1, "h_vT", [128, TPH], BF16)
                sgT = sb(S1, "h_sgT", [128, TPH], BF16)
                qg_tok, ki_tok, kd_tok, v_tok, sg_tok = Tok(), Tok(), Tok(), Tok(), Tok()
                Sbf = sb(S1, "h_Sbf", [128, 128], BF16)
                Sbf_tok = Tok()
                Ss = sb(S1, "h_Ss", [128, SPP, 128], F32)
                Ss_tok = Ss_ptok
                Ssb = sb(S1, "h_Ssb", [128, SPP, 128], BF16)
                Ssb_tok = Tok()
                Sn, Sn_tok = Ss, Ss_tok
                memset(rmask[:], 1.0, W=[rm_tok])
                memset(rmask[:, 0:TP].rearrange("p (c t) -> p c t", t=32)[:, :, 0:1], 0.0, W=[rm_tok])
                memset(rmask[:, TP:TPH].rearrange("p (c t) -> p c t", t=8)[:, :, 0:1], 0.0, W=[rm_tok])
                psblk = Ring([(psF[i % 5][:, (i // 5) * 128:(i // 5) * 128 + 128], bkt[i % 5]) for i in range(20)])
                obank = Ring([(psF[5], bkt[5]), (psF[6], bkt[6])])
                for h in range(H):
                    stage(7 + 0.1 * h)
                    lb = lbT[:, h, 0:1]
                    oml = lbT[:, h, 1:2]
                    with contextlib.ExitStack() as SPL:
                        fa = sb(SPL, "h_fa", [128, TPH], F32)
                        fa_tok = Tok()
                        fk = sb(SPL, "h_fk", [128, TPH], F32)
                        fk_tok = Tok()
                        fG = sb(SPL, "h_fG", [128, TPH], F32)
                        fG_tok = Tok()
                        fe = sb(SPL, "h_fe", [128, TPH], F32)
                        fe_tok = Tok()
                        sq_ = sb(SPL, "h_sq", [128, TPH], F32)
                        sq_tok = Tok()
                        for role in range(4):
                            wt, wtok = wload(Win, role * 1024 + h * 128)
                            pall = proj_all(wt, wtok, hT, htoks)
                            for ti in range(3):
                                ps, ptok, n = pall[ti]
                                t0 = TILES[ti][0]
                                if role == 0:
                                    act(sq_[:, t0:t0 + n], ps[:, 0:n], AF.Silu, R=[ptok], W=[sq_tok])
                                elif role == 1:
                                    act(fa[:, t0:t0 + n], ps[:, 0:n], AF.Sigmoid, R=[ptok], W=[fa_tok])
                                elif role == 2:
                                    act(vT[:, t0:t0 + n], ps[:, 0:n], AF.Copy, R=[ptok], W=[v_tok])
                                else:
                                    act(sgT[:, t0:t0 + n], ps[:, 0:n], AF.Silu, R=[ptok], W=[sg_tok])
                        ts(fa[:], fa[:], oml, lb, ALU.mult, ALU.add, R=[fa_tok, lb_tok], W=[fa_tok])
                        ts(fk[:], fa[:], -1.0, 1.0, ALU.mult, ALU.add, R=[fa_tok], W=[fk_tok])
                        act(fa[:], fa[:], AF.Ln, R=[fa_tok], W=[fa_tok])
                        K.op("dve", lambda e, fG=fG, fa=fa: e.tensor_tensor_scan(out=fG[:], data0=rmask[:], data1=fa[:], initial=0.0, op0=ALU.mult, op1=ALU.add),
                             [fa_tok, rm_tok], [fG_tok])
                        act(eG[:], fG[:], AF.Exp, R=[fG_tok], W=[eG_tok])
                        tt(qgT[:], sq_[:], eG[:], ALU.mult, R=[sq_tok, eG_tok], W=[qg_tok])
                        act(fe[:], fG[:], AF.Exp, scale=-1.0, R=[fG_tok], W=[fe_tok])
                        tt(kiT[:], fk[:], fe[:], ALU.mult, R=[fk_tok, fe_tok], W=[ki_tok])
                        gp = fG[:, 0:TP].rearrange("p (c t) -> p c t", t=32)
                        tt(fa[:, 0:TP].rearrange("p (c t) -> p c t", t=32), gp[:, :, 31:32].broadcast_to([128, 32, 32]), gp, ALU.subtract,
                           R=[fG_tok], W=[fa_tok])
                        gs = fG[:, TP:TPH].rearrange("p (c t) -> p c t", t=8)
                        tt(fa[:, TP:TPH].rearrange("p (c t) -> p c t", t=8), gs[:, :, 7:8].broadcast_to([128, 8, 8]), gs, ALU.subtract,
                           R=[fG_tok], W=[fa_tok])
                        act(fe[:], fa[:], AF.Exp, R=[fa_tok], W=[fe_tok])
                        tt(kdT[:], fk[:], fe[:], ALU.mult, R=[fk_tok, fe_tok], W=[kd_tok])
                        K.barrier()

                    stage(7.05 + 0.1 * h)
                    K.dma("pool", Ss[:], shgrn_d[j, SPP * ph:SPP * ph + SPP, h].rearrange("s k v -> k s v"), W=[Ss_tok], semtok=Ss_tok)
                    act(Ssb[:], Ss[:], AF.Copy, R=[Ss_tok], W=[Ssb_tok])
                    if ph == 0:
                        memset(Sf_all[:, h, :], 0.0, W=[Sf_tok[h]])
                    cp(Sbf[:], Sf_all[:, h, :], R=[Sf_tok[h]], W=[Sbf_tok])

                    with contextlib.ExitStack() as SBK:
                        blks = blocks()
                        bufs = []
                        for bi, (c0, BS, kind) in enumerate(blks):
                            npad = 4 if kind == "P" else 8
                            bufs.append(dict(
                                attnT=(sb(SBK, "h_at%d" % bi, [128, 128], BF16), Tok()),
                                vtk=(sb(SBK, "h_vt%d" % bi, [128, 128], BF16), Tok()),
                                Kpad=(sb(SBK, "h_kp%d" % bi, [128, npad, 128], BF16), Tok()),
                                Qpad=(sb(SBK, "h_qp%d" % bi, [128, npad, 128], BF16), Tok())))
                        opf = Ring([(sb(SBK, "h_of%d" % i, [128, 128], F32), Tok()) for i in range(4)])
                        opb = Ring([(sb(SBK, "h_ob%d" % i, [128, 128], BF16), Tok()) for i in range(2)])

                        def consts_for(kind):
                            if kind == "P":
                                return (4, 32, cst[:, C_MIP:C_MIP + 128],
                                        cst[:, C_CM4:C_CM4 + 512].rearrange("p (c t) -> p c t", c=4), cst[:, C_RM4:C_RM4 + 4])
                            return (8, 8, cst[0:64, C_MIS:C_MIS + 64],
                                    cst[:, C_CM8:C_CM8 + 512].rearrange("p (c t) -> p c t", c=8), cst[0:64, C_RM8:C_RM8 + 8])

                        for bi, (c0, BS, kind) in enumerate(blks):
                            ncb, C, mi, cmask, rmk = consts_for(kind)
                            bf = bufs[bi]
                            cols = slice(c0, c0 + BS)
                            aps, aptok = psblk.get()
                            mm(aps[0:BS, 0:BS], kiT[:, cols], qgT[:, cols], R=[ki_tok, qg_tok], W=[aptok])
                            attnT, attok = bf["attnT"]
                            tt(attnT[0:BS, 0:BS], aps[0:BS, 0:BS], mi, ALU.mult, R=[aptok, cst_tok], W=[attok])
                            pvv, pvtok = pbf.get()
                            tr(pvv[0:BS, :], vT[:, cols], ident_bf[:], R=[v_tok, misc_tok], W=[pvtok])
                            vtk, vtktok = bf["vtk"]
                            act(vtk[0:BS, :], pvv[0:BS, :], AF.Copy, R=[pvtok], W=[vtktok])
                            pk, pktok = pbf.get()
                            tr(pk[0:BS, :], kdT[:, cols], ident_bf[:], R=[kd_tok, misc_tok], W=[pktok])
                            Kpad, kp_tok = bf["Kpad"]
                            tt(Kpad[0:BS, 0:ncb, :], pk[0:BS, :].unsqueeze(1).broadcast_to([BS, ncb, 128]),
                               rmk.unsqueeze(2).broadcast_to([BS, ncb, 128]), ALU.mult, R=[pktok, cst_tok], W=[kp_tok])
                            Qpad, qp_tok = bf["Qpad"]
                            tt(Qpad[:, 0:ncb, 0:BS], cmask[:, :, 0:BS], qgT[:, cols].unsqueeze(1).broadcast_to([128, ncb, BS]), ALU.mult,
                               R=[cst_tok, qg_tok], W=[qp_tok])

                        pending = None
                        for bi, (c0, BS, kind) in enumerate(blks):
                            ncb, C, mi, cmask, rmk = consts_for(kind)
                            bf = bufs[bi]
                            attnT, attok = bf["attnT"]
                            vtk, vtktok = bf["vtk"]
                            Kpad, kp_tok = bf["Kpad"]
                            Qpad, qp_tok = bf["Qpad"]
                            ops_, optok = obank.get()
                            if kind == "P":
                                for c in range(ncb):
                                    mm(ops_[0:BS, 0:128], Qpad[:, c, 0:BS], Sbf[:], start=(c == 0), stop=False, R=[qp_tok, Sbf_tok], W=[optok])
                                    sps, sptok = psblk.get()
                                    mm(sps[:, :], Kpad[0:BS, c, :], vtk[0:BS, :], R=[kp_tok, vtktok], W=[sptok])
                                    ce = c0 + (c + 1) * C - 1
                                    stt(Sbf[:], Sf_all[:, h, :], eG[:, ce:ce + 1], sps[:, :], ALU.mult, ALU.add,
                                        R=[sptok, eG_tok, Sf_tok[h]], W=[Sbf_tok])
                                    stt(Sf_all[:, h, :], Sf_all[:, h, :], eG[:, ce:ce + 1], sps[:, :], ALU.mult, ALU.add,
                                        R=[sptok, eG_tok], W=[Sf_tok[h]])
                                mm(ops_[0:BS, 0:128], attnT[0:BS, 0:BS], vtk[0:BS, :], start=False, stop=True, R=[attok, vtktok], W=[optok])
                            else:
                                for c in range(ncb):
                                    mm(ops_[0:BS, 0:128], Qpad[:, c, 0:BS], Ssb[:, c, :], start=(c == 0), stop=False, R=[qp_tok, Ssb_tok], W=[optok])
                                mm(ops_[0:BS, 0:128], attnT[0:BS, 0:BS], vtk[0:BS, :], start=False, stop=True, R=[attok, vtktok], W=[optok])
                                sl2 = [pbig.get(), pbig.get()]
                                for c in range(ncb):
                                    ps, ptok = sl2[c // 4]
                                    mm(ps[:, (c % 4) * 128:(c % 4) * 128 + 128], Kpad[0:BS, c, :], vtk[0:BS, :], R=[kp_tok, vtktok], W=[ptok])
                                ege = eG[:, TP:TPH].rearrange("p (s t) -> p s t", t=8)[:, :, 7:8]
                                tt(Sn[:], Ss[:], ege.broadcast_to([128, 8, 128]), ALU.mult, R=[Ss_tok, eG_tok], W=[Sn_tok])
                                for hb in range(2):
                                    ps, ptok = sl2[hb]
                                    tt(Sn[:, hb * 4:hb * 4 + 4, :], Sn[:, hb * 4:hb * 4 + 4, :], ps[:, :].rearrange("p (s v) -> p s v", v=128), ALU.add,
                                       R=[ptok], W=[Sn_tok])
                                K.dma("sp", shgrn_o[j, SPP * ph:SPP * ph + SPP, h].rearrange("s k v -> k s v"), Sn[:], R=[Sn_tok], semtok=Sn_tok)
                            if pending is not None:
                                o_post(*pending)
                            pending = (l, (ops_, optok), BS, h, c0, oT, otoks[min(c0 // 512, 2)], sgT, sg_tok, opf, opb)
                        o_post(*pending)
                        if ph == 1:
                            K.dma("sp", phgrn_d[j, h], Sf_all[:, h, :], R=[Sf_tok[h]], semtok=Sf_tok[h])
                        K.barrier()
                K.barrier()

        pw5 = Ring([(psF[i], bkt[i]) for i in range(5)])

        def ffn_phase(l, ph, want_ada=False):
            agen = None
            modA = modsets[l % 2]
            A2, B2, G2 = modA[3], modA[4], modA[5]
            with contextlib.ExitStack() as SF:
                aT = sb(SF, "f_aT", [128, NFF, TPH], BF16)
                a_toks = [Tok() for _ in range(3)]
                fci = sb(SF, "f_fci", [128, NFF, 8, 2], F32)
                fci_tok = fci_ptok
                fco = sb(SF, "f_fco", [128, NFF, 9, 2], F32)
                fco_tok = fco_ptok
                K.dma("sp", fci[:].rearrange("p a s r -> p (a s r)"), fci_d[l, ph], W=[fci_tok], semtok=fci_tok)
                memset(fco[:], 0.0, W=[fco_tok])
                with contextlib.ExitStack() as S1:
                    hT = sb(S1, "f_hT", [128, KC, TPH], BF16)
                    htoks = [Tok() for _ in range(3)]
                    with contextlib.ExitStack() as SPN:
                        prenorm(ph, A2, B2, hT, htoks, SPN)
                        K.barrier()
                    gpre_r = Ring([(sb(S1, "f_gpre%d" % i, [128, 2 + TP + SPP * 10], F32), Tok()) for i in range(2)])
                    up_r = Ring([(sb(S1, "f_up%d" % i, [128, TPH], F32), Tok()) for i in range(2)])
                    gc_r = Ring([(sb(S1, "f_gc%d" % i, [128, TPH], F32), Tok()) for i in range(2)])
                    if want_ada:
                        agen = adaln_gen(l + 1, S1)
                    for jj in range(NFF):
                        gpre, gp_tok = gpre_r.get()
                        up, up_tok = up_r.get()
                        gc, gc_tok = gc_r.get()
                        pv = gpre[:, 2 + TP:2 + TP + SPP * 10].rearrange("p (s r) -> p s r", r=10)
                        wg, wgtok = wload(ffn_w_gu[l], jj * 128)
                        wu, wutok = wload(ffn_w_gu[l], DFF + jj * 128)
                        if ph == 0:
                            memset(gpre[:, 0:2], 0.0, W=[gp_tok])
                        else:
                            cp(gpre[:, 0:2], fcar[:, jj, :], R=[fcar_tok], W=[gp_tok])
                        cp(pv[:, :, 0:2], fci[:, jj, :, :], R=[fci_tok], W=[gp_tok])
                        pg = proj_all(wg, wgtok, hT, htoks)
                        for ti in range(3):
                            t0 = TILES[ti][0]
                            ps, ptok, n = pg[ti]
                            if ti < 2:
                                act(gpre[:, 2 + t0:2 + t0 + n], ps[:, 0:n], AF.Copy, R=[ptok], W=[gp_tok])
                            else:
                                act(pv[:, :, 2:10], ps[:, 0:n].rearrange("p (s r) -> p s r", r=LS), AF.Copy, R=[ptok], W=[gp_tok])
                        pu = proj_all(wu, wutok, hT, htoks)
                        for ti in range(3):
                            t0 = TILES[ti][0]
                            ps, ptok, n = pu[ti]
                            act(up[:, t0:t0 + n], ps[:, 0:n], AF.Copy, R=[ptok], W=[up_tok])
                        if ph == 0:
                            cp(fcar[:, jj, :], gpre[:, TP:TP + 2], R=[gp_tok], W=[fcar_tok])
                        else:
                            cp(fco[:, jj, 8, :], gpre[:, TP:TP + 2], R=[gp_tok], W=[fco_tok])
                        cp(fco[:, jj, 0:8, :], pv[:, :, 8:10], R=[gp_tok], W=[fco_tok])
                        wc = prm[:, l, P_FCW + jj * 3:P_FCW + jj * 3 + 3]
                        bc = prm[:, l, P_FCB + jj:P_FCB + jj + 1]
                        gcs = gc[:, TP:TPH].rearrange("p (s r) -> p s r", r=LS)
                        ts(gc[:, 0:TP], gpre[:, 0:TP], wc[:, 0:1], bc, ALU.mult, ALU.add, R=[gp_tok, prm_tok], W=[gc_tok])
                        ts(gcs, pv[:, :, 0:8], wc[:, 0:1], bc, ALU.mult, ALU.add, R=[gp_tok, prm_tok], W=[gc_tok])
                        for tap in range(1, 3):
                            stt(gc[:, 0:TP], gpre[:, tap:tap + TP], wc[:, tap:tap + 1], gc[:, 0:TP], ALU.mult, ALU.add,
                                R=[gp_tok, prm_tok], W=[gc_tok])
                            stt(gcs, pv[:, :, tap:tap + 8], wc[:, tap:tap + 1], gcs, ALU.mult, ALU.add, R=[gp_tok, prm_tok], W=[gc_tok])
                        act(gc[:], gc[:], AF.Silu, R=[gc_tok], W=[gc_tok])
                        for ti, (t0, n) in enumerate(TILES):
                            tt(aT[:, jj, t0:t0 + n], gc[:, t0:t0 + n], up[:, t0:t0 + n], ALU.mult, R=[gc_tok, up_tok], W=[a_toks[ti]])
                        if agen is not None:
                            for _ in range(3):
                                next(agen, None)
                    if agen is not None:
                        for _ in agen:
                            pass
                    K.dma("sp", fco_d[l, ph], fco[:].rearrange("p a s r -> p (a s r)"), R=[fco_tok], semtok=fco_tok)
                    K.barrier()
                with contextlib.ExitStack() as S2:
                    y = sb(S2, "f_y", [128, KC, TPH], F32)
                    ytok = Tok()
                    wdn = Ring([(sb(S2, "f_wd%d" % i, [128, NFF, 128], BF16), wdn_ptoks[i]) for i in range(2)])
                    sqt = (sb(S2, "f_sq", [128, KC, 256], BF16), Tok())
                    rs = (sb(S2, "f_rs", [128, 256], F32), Tok())
                    tmps = Ring([(sb(S2, "f_t%d" % i, [128, 256], F32), Tok()) for i in range(2)])
                    for oc in range(KC):
                        wt, wtok = wdn.get()
                        K.dma("pool", wt[:], ffn_w_down[l][:, oc * 128:(oc + 1) * 128].rearrange("(j p) n -> p j n", p=128), W=[wtok], semtok=wtok)
                        sl3 = [pwide.get() for _ in range(3)]
                        for jj in range(NFF):
                            for ti, (t0, n) in enumerate(TILES):
                                ps, ptok = sl3[ti]
                                mm(ps[:, 0:n], wt[:, jj, :], aT[:, jj, t0:t0 + n], start=(jj == 0), stop=(jj == NFF - 1),
                                   R=[wtok, a_toks[ti]], W=[ptok])
                        for ti, (t0, n) in enumerate(TILES):
                            ps, ptok = sl3[ti]
                            cp(y[:, oc, t0:t0 + n], ps[:, 0:n], R=[ptok], W=[ytok])
                    for ti, (t0, n) in enumerate([(0, 256), (256, 256), (512, 256), (768, 256), (1024, 64)]):
                        rstd_fm(y[:, :, t0:t0 + n], n, [ytok], sqt, rs)
                        for kc in range(KC):
                            tmp, ttok = tmps.get()
                            tt(tmp[:, 0:n], y[:, kc, t0:t0 + n], rs[0][:, 0:n], ALU.mult, R=[ytok, rs[1]], W=[ttok])
                            if t0 < TP:
                                stt(xT[:, kc, ph, t0:t0 + n], tmp[:, 0:n], G2[:, kc, 0:1], xT[:, kc, ph, t0:t0 + n], ALU.mult, ALU.add,
                                    R=[ttok, mod_tok], W=[xtok[ph]])
                            else:
                                sc = seqcols(ph)
                                v3 = tmp[:, 0:n].rearrange("p (s j) -> p s j", j=LS)
                                tt(v3, v3, G2[:, kc, sc].unsqueeze(2).broadcast_to([128, SPP, LS]), ALU.mult, R=[ttok, mod_tok], W=[ttok])
                                x3 = xT[:, kc, ph, t0:t0 + n].rearrange("p (s j) -> p s j", j=LS)
                                tt(x3, x3, v3, ALU.add, R=[ttok], W=[xtok[ph]])
                    K.barrier()

        def main_program():
            for l in range(depth):
                modA = modsets[l % 2]
                with contextlib.ExitStack() as SL:
                    stage(1)
                    for _ in adaln_gen(l, SL):
                        pass
                    if l % 2 == 1:
                        jj_ = l // 2
                        if jj_ == 0:
                            memset(lbT[:, :, 0:1], 0.0, W=[lb_tok])
                            memset(lbT[:, :, 1:2], 1.0, W=[lb_tok])
                        else:
                            hl = prm[:, l, P_HLB:P_HLB + 16].rearrange("p (h t) -> p h t", t=2)
                            tt(lbT[:, :, 0:1], hl[:, :, 1:2], hl[:, :, 0:1], ALU.subtract, R=[prm_tok], W=[lb_tok])
                            act(lbT[:, :, 0:1], lbT[:, :, 0:1], AF.Sigmoid, R=[lb_tok], W=[lb_tok])
                            ts(lbT[:, :, 1:2], lbT[:, :, 0:1], -1.0, 1.0, ALU.mult, ALU.add, R=[lb_tok], W=[lb_tok])
                    K.barrier()
                for ph in range(2):
                    with contextlib.ExitStack() as SA:
                        hT = sb(SA, "m_hT", [128, KC, TPH], BF16)
                        htoks = [Tok() for _ in range(3)]
                        oT = sb(SA, "m_oT", [128, KC, TPH], BF16)
                        otoks = [Tok() for _ in range(3)]
                        with contextlib.ExitStack() as SP:
                            stage(2)
                            prenorm(ph, modA[0], modA[1], hT, htoks, SP)
                            K.barrier()
                        with contextlib.ExitStack() as SM:
                            if l % 2 == 0:
                                gdn_phase(l, ph, hT, htoks, oT, otoks, SM)
                            else:
                                hgrn_phase(l, ph, hT, htoks, oT, otoks, SM)
                            K.barrier()
                        with contextlib.ExitStack() as SO:
                            stage(5)
                            Wout = gdn_w_out[l // 2] if l % 2 == 0 else hgrn_w_out[l // 2]
                            outproj_postnorm(l, ph, Wout, oT, otoks, modA[2], SO)
                            K.barrier()
                    stage(6)
                    ffn_phase(l, ph, False)
            for ph in range(2):
                K.dma("sp", yT_d.rearrange("p (k h t) -> p k h t", k=KC, h=2)[:, :, ph, :], xT[:, :, ph, :], R=[xtok[ph]], semtok=xtok[ph])

        try:
            main_program()
        except StopBuild:
            K.barrier()
        K.final_wait("sp")

        with nc.Block() as block:
            K.replay(block)
    return nc


def _consts():
    c = np.zeros((128, NCST), np.float32)
    c[:, C_IDENT:C_IDENT + 128] = np.eye(128, dtype=np.float32)
    i = np.arange(128)[:, None]
    jx = np.arange(128)[None, :]
    BIG = 1.0e4
    valid = (i // 64 == jx // 64) & (i > jx)
    c[:, C_PENP:C_PENP + 128] = np.where(valid, 0.0, BIG)
    i8 = np.arange(64)[:, None]
    j8 = np.arange(64)[None, :]
    valid = (i8 // 8 == j8 // 8) & (i8 > j8)
    c[0:64, C_PENS:C_PENS + 64] = np.where(valid, 0.0, BIG)
    for ncb, off, bs in ((2, C_CM2, 128), (4, C_CM4, 128), (8, C_CM8, 64)):
        C = bs // ncb
        m = np.zeros((ncb, bs), np.float32)
        for cc in range(ncb):
            m[cc, cc * C:(cc + 1) * C] = 1.0
        c[:, off:off + ncb * bs] = m.reshape(1, -1)
    for ncb, off, bs in ((2, C_RM2, 128), (4, C_RM4, 128), (8, C_RM8, 64)):
        C = bs // ncb
        m = np.zeros((bs, ncb), np.float32)
        for cc in range(ncb):
            m[cc * C:(cc + 1) * C, cc] = 1.0
        c[0:bs, off:off + ncb] = m
    c[:, C_MIP:C_MIP + 128] = ((i // 32 == jx // 32) & (i <= jx)).astype(np.float32)
    c[0:64, C_MIS:C_MIS + 64] = ((i8 // 8 == j8 // 8) & (i8 <= j8)).astype(np.float32)
    e = np.zeros((8, 8, 128), np.float32)
    for h in range(8):
        e[h, h, :] = 1.0
    c[0:8, C_ESEL:C_ESEL + 1024] = e.reshape(8, 1024)
    return c


def _fm(v):
    sh = v.shape
    k = sh[-1] // 128
    v = v.reshape(sh[:-1] + (k, 128))
    return np.moveaxis(np.moveaxis(v, -1, 0), -1, 1)


_NC_CACHE = {}


def kernel(x_prompt, x_sample, state_gdn, state_gdn_conv, state_hgrn, state_ffn_conv, c_prompt, c_sample,
           ada_w, ada_b, norm_pre_mix, norm_post_mix, norm_pre_ffn, norm_post_ffn,
           gdn_w_in, gdn_conv_w, gdn_conv_b, gdn_a_log, gdn_dt_bias, gdn_norm, gdn_w_out,
           hgrn_lb, hgrn_w_in, hgrn_norm, hgrn_w_out,
           ffn_w_gu, ffn_conv_w, ffn_conv_b, ffn_w_down, _depth=DEPTH, _stage=None):
    f32 = np.float32
    A = lambda a: np.ascontiguousarray(np.asarray(a, dtype=f32))
    x_prompt, x_sample = A(x_prompt), A(x_sample)
    state_gdn, state_gdn_conv, state_hgrn, state_ffn_conv = A(state_gdn), A(state_gdn_conv), A(state_hgrn), A(state_ffn_conv)
    c_prompt, c_sample = A(c_prompt), A(c_sample)
    prm = np.zeros((128, 4, NPRM), f32)
    nrow = np.zeros((128, 4, 128), f32)
    ada_b_, gcw, gcb = A(ada_b), A(gdn_conv_w), A(gdn_conv_b)
    fcw, fcb = A(ffn_conv_w), A(ffn_conv_b)
    hlb = A(hgrn_lb)
    for l in range(4):
        prm[:, l, P_ADAB:P_ADAB + 48] = ada_b_[l].reshape(48, 128).T
        prm[:, l, P_NPRE_MIX:P_NPRE_MIX + 8] = A(norm_pre_mix)[l].reshape(8, 128).T
        prm[:, l, P_NPOST_MIX:P_NPOST_MIX + 8] = A(norm_post_mix)[l].reshape(8, 128).T
        prm[:, l, P_NPRE_FFN:P_NPRE_FFN + 8] = A(norm_pre_ffn)[l].reshape(8, 128).T
        prm[:, l, P_NPOST_FFN:P_NPOST_FFN + 8] = A(norm_post_ffn)[l].reshape(8, 128).T
        j = l // 2
        if l % 2 == 0:
            prm[:, l, P_GCW:P_GCW + 96] = gcw[j].reshape(4, 24, 128).transpose(2, 1, 0).reshape(128, 96)
            prm[:, l, P_GCB:P_GCB + 24] = gcb[j].reshape(24, 128).T
            prm[0:8, l, P_ALOG] = A(gdn_a_log)[j]
            prm[0:8, l, P_DTB] = A(gdn_dt_bias)[j]
            nrow[:, l, :] = A(gdn_norm)[j][None, :]
        else:
            nrow[:, l, :] = A(hgrn_norm)[j][None, :]
        prm[:, l, P_FCW:P_FCW + 66] = fcw[l].reshape(3, 22, 128).transpose(2, 1, 0).reshape(128, 66)
        prm[:, l, P_FCB:P_FCB + 22] = fcb[l].reshape(22, 128).T
        prm[:, l, P_HLB:P_HLB + 16] = hlb.reshape(2, 8, 128).transpose(2, 1, 0).reshape(128, 16)
    cst = _consts()
    shared = dict(prm=prm.reshape(128, -1), nrow=nrow.reshape(128, -1), cst=cst,
                  ada_w=A(ada_w), gdn_w_in=A(gdn_w_in), gdn_w_out=A(gdn_w_out), hgrn_w_in=A(hgrn_w_in),
                  hgrn_w_out=A(hgrn_w_out), ffn_w_gu=A(ffn_w_gu), ffn_w_down=A(ffn_w_down))
    in_maps = []
    for c in range(NCORE):
        xt = np.zeros((128, KC, 2, TPH), f32)
        xp = _fm(x_prompt[c])
        xs = _fm(x_sample[16 * c:16 * c + 16])
        for ph in range(2):
            xt[:, :, ph, 0:TP] = xp[:, :, TP * ph:TP * ph + TP]
            xt[:, :, ph, TP:] = xs[:, :, 8 * ph:8 * ph + 8, :].reshape(128, KC, TS)
        cc = np.concatenate([c_prompt[c:c + 1], c_sample[16 * c:16 * c + 16]], 0)
        cT = _fm(cc)
        gc = _fm(state_gdn_conv[:, 16 * c:16 * c + 16])
        gci = np.zeros((2, 2, 128, 24, 8, 3), f32)
        fc = _fm(state_ffn_conv[:, 16 * c:16 * c + 16])
        fci = np.zeros((4, 2, 128, NFF, 8, 2), f32)
        for ph in range(2):
            gci[:, ph] = gc[:, :, :, 8 * ph:8 * ph + 8, :].transpose(2, 0, 1, 3, 4)
            fci[:, ph] = fc[:, :, :, 8 * ph:8 * ph + 8, :].transpose(2, 0, 1, 3, 4)
        m = dict(shared)
        m.update(xT=xt.reshape(128, -1), cT=np.ascontiguousarray(cT).reshape(128, -1),
                 sgdn=np.ascontiguousarray(state_gdn[:, 16 * c:16 * c + 16]),
                 shgrn=np.ascontiguousarray(state_hgrn[:, 16 * c:16 * c + 16]),
                 gci=gci.reshape(2, 2, 128, -1), fci=fci.reshape(4, 2, 128, -1))
        in_maps.append(m)
    ck = (_depth, _stage)
    if ck not in _NC_CACHE:
        _STAGE_LIMIT[0] = _stage
        _STAGE_LIMIT[1] = False
        _NC_CACHE[ck] = build_nc(_depth)
        _STAGE_LIMIT[0] = None
        _STAGE_LIMIT[1] = False
    nc = _NC_CACHE[ck]
    res = run_bass_kernel_spmd(nc, in_maps, core_ids=list(range(NCORE)))
    R = res.results
    y_prompt = np.zeros((8, 2048, D), f32)
    y_sample = np.zeros((128, 8, D), f32)
    p_gdn = np.zeros((2, 8, H, 128, 128), f32)
    p_hgrn = np.zeros((2, 8, H, 128, 128), f32)
    s_gdn = np.zeros((2, 128, H, 128, 128), f32)
    s_hgrn = np.zeros((2, 128, H, 128, 128), f32)
    p_gconv = np.zeros((2, 8, 3, 3072), f32)
    s_gconv = np.zeros((2, 128, 3, 3072), f32)
    p_fconv = np.zeros((4, 8, 2, DFF), f32)
    s_fconv = np.zeros((4, 128, 2, DFF), f32)
    for c in range(NCORE):
        r = R[c]
        yt = r["yT"].reshape(128, KC, 2, TPH)
        for ph in range(2):
            y_prompt[c, TP * ph:TP * ph + TP] = yt[:, :, ph, 0:TP].transpose(2, 1, 0).reshape(TP, D)
            ys = yt[:, :, ph, TP:].reshape(128, KC, 8, 8)
            y_sample[16 * c + 8 * ph:16 * c + 8 * ph + 8] = ys.transpose(2, 3, 1, 0).reshape(8, 8, D)
        p_gdn[:, c] = r["pgdn"]
        p_hgrn[:, c] = r["phgrn"]
        s_gdn[:, 16 * c:16 * c + 16] = r["sgdn_o"]
        s_hgrn[:, 16 * c:16 * c + 16] = r["shgrn_o"]
        g = r["gco"].reshape(2, 2, 128, 24, 9, 3)
        f = r["fco"].reshape(4, 2, 128, NFF, 9, 2)
        for ph in range(2):
            gs = g[:, ph, :, :, 0:8, :].transpose(0, 3, 4, 2, 1).reshape(2, 8, 3, 3072)
            s_gconv[:, 16 * c + 8 * ph:16 * c + 8 * ph + 8] = gs
            fs = f[:, ph, :, :, 0:8, :].transpose(0, 3, 4, 2, 1).reshape(4, 8, 2, DFF)
            s_fconv[:, 16 * c + 8 * ph:16 * c + 8 * ph + 8] = fs
        p_gconv[:, c] = g[:, 1, :, :, 8, :].transpose(0, 3, 2, 1).reshape(2, 3, 3072)
        p_fconv[:, c] = f[:, 1, :, :, 8, :].transpose(0, 3, 2, 1).reshape(4, 2, DFF)
    return (y_prompt, y_sample, p_gdn, p_gconv, p_hgrn, p_fconv, s_gdn, s_gconv, s_hgrn, s_fconv)
```

```python
import contextlib
import numpy as np
import concourse.bass as bass
import concourse.mybir as mybir
from concourse.bass_utils import run_bass_kernel_spmd

F32 = mybir.dt.float32
BF16 = mybir.dt.bfloat16
AF = mybir.ActivationFunctionType
ALU = mybir.AluOpType

NCORE = 8
D = 1024
KC = 8
H = 8
DFF = 2816
NFF = 22
DEPTH = 4
TP = 1024
SPP = 8
LS = 8
TS = SPP * LS
TPH = TP + TS
TILES = [(0, 512), (512, 512), (1024, 64)]
EPS = 1e-6
NPRM = 48 + 32 + 96 + 24 + 66 + 22 + 16 + 2

P_ADAB = 0
P_NPRE_MIX = 48
P_NPOST_MIX = 56
P_NPRE_FFN = 64
P_NPOST_FFN = 72
P_GCW = 80
P_GCB = 176
P_FCW = 200
P_FCB = 266
P_HLB = 288
P_ALOG = 304
P_DTB = 305

C_IDENT = 0
C_PENP = 128
C_PENS = 256
C_CM2 = 320
C_CM4 = 576
C_CM8 = 1088
C_RM2 = 1600
C_RM4 = 1602
C_RM8 = 1606
C_MIP = 1614
C_MIS = 1742
C_ESEL = 1806
NCST = 1806 + 1024


_CARRY = [{}]


class Tok:
    __slots__ = ("w", "r", "dsem", "dval", "excl")

    def __init__(self, excl=False):
        self.w = {}
        self.r = dict(_CARRY[0])
        self.dsem = None
        self.dval = 0
        self.excl = excl


class Sched:
    ENGS = ("pe", "act", "dve", "pool", "sp")

    def __init__(self, nc, es):
        self.nc = nc
        self.es = es
        self.q = {k: [] for k in self.ENGS}
        self.n = {k: 0 for k in self.ENGS}
        self.seen = {k: {} for k in self.ENGS}
        self.sem = {}
        for k in ("pe", "act", "dve", "pool"):
            self.sem[k] = es.enter_context(nc.semaphore("s_" + k))
        self.dma_latest = {}
        self.nd = 0
        self.carry = {}

    def _waits(self, en, R, W):
        need = {}

        def add(ev):
            key, sem, val = ev
            if key not in need or need[key][1] < val:
                need[key] = (sem, val)

        for t in R:
            for ev in t.w.values():
                add(ev)
            if t.excl:
                for ev in t.r.values():
                    add(ev)
        for t in W:
            for ev in t.w.values():
                add(ev)
            for ev in t.r.values():
                add(ev)
        out = []
        seen = self.seen[en]
        for key, (sem, val) in need.items():
            if key == en and (en == "pe" or _NO_SAME_ENGINE_WAIT[0]):
                continue
            if seen.get(key, 0) >= val:
                continue
            seen[key] = val
            out.append((sem, val))
        return out

    def op(self, en, fn, R=(), W=()):
        if _STAGE_LIMIT[1]:
            return
        waits = self._waits(en, R, W)
        self.n[en] += 1
        sem = self.sem[en]
        self.q[en].append((waits, fn, sem, 1))
        ev = (en, sem, self.n[en])
        for t in W:
            t.w[en] = ev
        for t in R:
            t.r[en] = ev

    def dma(self, qn, out, in_, R=(), W=(), semtok=None):
        if _STAGE_LIMIT[1]:
            return
        waits = self._waits(qn, R, W)
        t = semtok
        if t.dsem is None:
            t.dsem = {}
        if qn not in t.dsem:
            self.nd += 1
            t.dsem[qn] = [self.es.enter_context(self.nc.semaphore("d%d" % self.nd)), 0]
        ent = t.dsem[qn]
        ent[1] += 16
        dsem, dval = ent[0], ent[1]
        key = "d%d_%s" % (id(t), qn)
        self.q[qn].append((waits, (lambda e: e.dma_start(out=out, in_=in_)), dsem, 16))
        ev = (key, dsem, dval)
        for x in W:
            x.w[key] = ev
        for x in R:
            x.r[key] = ev
        self.dma_latest[key] = (dsem, dval)

    def barrier(self, head=False):
        if head and _SKIP_HEAD_BARRIERS[0]:
            return
        for en in self.ENGS:
            waits = []
            seen = self.seen[en]
            for k in ("pe", "act", "dve", "pool"):
                if k == en and en == "pe":
                    continue
                v = self.n[k]
                if v > 0 and seen.get(k, 0) < v:
                    seen[k] = v
                    waits.append((self.sem[k], v))
            for key, (sem, val) in self.dma_latest.items():
                if seen.get(key, 0) < val:
                    seen[key] = val
                    waits.append((sem, val))
            if waits:
                self.q[en].append((waits, None, None, 0))

    def snapshot(self):
        snap = {}
        for k in ("pe", "act", "dve", "pool"):
            if self.n[k] > 0:
                snap[k] = (k, self.sem[k], self.n[k])
        for key, (sem, val) in self.dma_latest.items():
            snap[key] = (key, sem, val)
        self.carry = snap
        _CARRY[0] = snap

    def nsems(self):
        return self.nd + 4

    def final_wait(self, en="sp"):
        waits = []
        for key, (sem, val) in self.dma_latest.items():
            waits.append((sem, val))
        for k in ("pe", "act", "dve", "pool"):
            if self.n[k] > 0:
                waits.append((self.sem[k], self.n[k]))
        self.q[en].append((waits, None, None, 0))

    def replay(self, block):
        q = self.q

        def run(e, lst):
            for waits, fn, sem, inc in lst:
                if fn is None or not _INLINE_WAIT[0] or not waits:
                    for s, v in waits:
                        e.wait_ge(s, v)
                    if fn is not None:
                        fn(e).then_inc(sem, inc)
                else:
                    for s, v in waits[:-1]:
                        e.wait_ge(s, v)
                    s, v = waits[-1]
                    fn(e)._wait_ge(s, v).then_inc(sem, inc)

        @block.tensor
        def _(e):
            run(e, q["pe"])

        @block.scalar
        def _(e):
            run(e, q["act"])

        @block.vector
        def _(e):
            run(e, q["dve"])

        @block.gpsimd
        def _(e):
            run(e, q["pool"])

        @block.sync
        def _(e):
            run(e, q["sp"])


class StopBuild(Exception):
    pass


_STAGE_LIMIT = [None, False]
_NO_SAME_ENGINE_WAIT = [False]
_INLINE_WAIT = [True]
_FP32R = [False]
_SKIP_HEAD_BARRIERS = [False]


def stage(n):
    if _STAGE_LIMIT[0] is not None and n > _STAGE_LIMIT[0]:
        _STAGE_LIMIT[1] = True


class Ring:
    def __init__(self, items):
        self.items = items
        self.i = 0

    def get(self):
        r = self.items[self.i]
        self.i = (self.i + 1) % len(self.items)
        return r


def build_nc(depth=DEPTH):
    _CARRY[0] = {}
    nc = bass.Bass("TRN2", target_bir_lowering=False)

    def din(name, shape):
        return nc.dram_tensor(name, list(shape), F32, kind="ExternalInput").ap()

    def dout(name, shape):
        return nc.dram_tensor(name, list(shape), F32, kind="ExternalOutput").ap()

    xT_d = din("xT", [128, KC * 2 * TPH])
    cT_d = din("cT", [128, KC * 17])
    sgdn_d = din("sgdn", [2, 16, H, 128, 128])
    shgrn_d = din("shgrn", [2, 16, H, 128, 128])
    gci_d = din("gci", [2, 2, 128, 24 * 8 * 3])
    fci_d = din("fci", [4, 2, 128, NFF * 8 * 2])
    prm_d = din("prm", [128, 4 * NPRM])
    nrow_d = din("nrow", [128, 4 * 128])
    cst_d = din("cst", [128, NCST])
    ada_w = din("ada_w", [4, D, 6 * D])
    gdn_w_in = din("gdn_w_in", [2, D, 4112])
    gdn_w_out = din("gdn_w_out", [2, D, D])
    hgrn_w_in = din("hgrn_w_in", [2, D, 4096])
    hgrn_w_out = din("hgrn_w_out", [2, D, D])
    ffn_w_gu = din("ffn_w_gu", [4, D, 2 * DFF])
    ffn_w_down = din("ffn_w_down", [4, DFF, D])

    yT_d = dout("yT", [128, KC * 2 * TPH])
    pgdn_d = dout("pgdn", [2, H, 128, 128])
    sgdn_o = dout("sgdn_o", [2, 16, H, 128, 128])
    phgrn_d = dout("phgrn", [2, H, 128, 128])
    shgrn_o = dout("shgrn_o", [2, 16, H, 128, 128])
    gco_d = dout("gco", [2, 2, 128, 24 * 9 * 3])
    fco_d = dout("fco", [4, 2, 128, NFF * 9 * 2])

    with contextlib.ExitStack() as es:
        K = Sched(nc, es)

        cnt = [0]

        def T():
            t = Tok()
            t.r.update(K.carry)
            return t

        def retok(t):
            t.r.update(_CARRY[0])
            return t

        def sb(stack, name, shape, dt):
            cnt[0] += 1
            return stack.enter_context(nc.sbuf_tensor("sb%d_%s" % (cnt[0], name), list(shape), dt))

        xT = sb(es, "xT", [128, KC, 2, TPH], F32)
        xtok = [Tok(), Tok()]
        cst = sb(es, "cst", [128, NCST], F32)
        cst_tok = Tok()
        prm = sb(es, "prm", [128, 4, NPRM], F32)
        prm_tok = Tok()
        nrow = sb(es, "nrow", [128, 4, 128], F32)
        nrow_tok = Tok()
        ident_bf = sb(es, "ident_bf", [128, 128], BF16)
        ones_bf = sb(es, "ones_bf", [128, 128], BF16)
        cb = sb(es, "cb", [128, 8], F32)
        misc_tok = Tok()
        csT = sb(es, "csT", [128, KC, 17], BF16)
        cs_tok = Tok()
        _ms = [sb(es, "modA%d" % i, [128, KC, 17], F32) for i in range(6)]
        modsets = [_ms, _ms]
        mod_tok = Tok()
        Sf_all = sb(es, "Sf_all", [128, H, 128], F32)
        Sf_tok = [Tok() for _ in range(H)]
        gcar = sb(es, "gcar", [128, 24, 3], F32)
        gcar_tok = Tok()
        fcar = sb(es, "fcar", [128, NFF, 2], F32)
        fcar_tok = Tok()
        lbT = sb(es, "lbT", [128, H, 2], F32)
        lb_tok = Tok()
        wun = [(sb(es, "wun%d" % i, [128, KC, 128], BF16), Tok()) for i in range(5)]
        wring = Ring(wun)
        wres_toks = [Tok() for _ in range(KC)]
        Ss_ptok, gci_ptok, gco_ptok, fci_ptok, fco_ptok = Tok(), Tok(), Tok(), Tok(), Tok()
        wdn_ptoks = [Tok(), Tok()]

        psF = [es.enter_context(nc.psum_tensor("psF%d" % i, [128, 512], F32)) for i in range(7)]
        psB = es.enter_context(nc.psum_tensor("psB", [128, 1024], BF16))
        bkt = [Tok(excl=True) for _ in range(8)]
        pbig = Ring([(psF[0], bkt[0]), (psF[1], bkt[1])])
        psmall = Ring([(psF[2 + i % 3][:, (i // 3) * 128:(i // 3) * 128 + 128], bkt[2 + i % 3]) for i in range(12)])
        pbf = Ring([(psB[:, i * 128:(i + 1) * 128], bkt[7]) for i in range(8)])
        o_ps, o_ptok = psF[5], bkt[5]
        u_ps, u_ptok = psF[6], bkt[6]
        pwide = Ring([(psF[i], bkt[i]) for i in range(7)])

        def mm(out, lhsT, rhs, start=True, stop=True, R=(), W=()):
            K.op("pe", lambda e: e.matmul(out, lhsT, rhs, start=start, stop=stop), R, W)

        def mmr(out, lhsT, rhs, R=(), W=()):
            if _FP32R[0] and lhsT.shape[-1] == 128:
                F32R = mybir.dt.float32r
                K.op("pe", lambda e: e.matmul(out, lhsT.bitcast(F32R), rhs.bitcast(F32R), start=True, stop=True), R, W)
            else:
                K.op("pe", lambda e: e.matmul(out, lhsT, rhs, start=True, stop=True), R, W)

        def tr(out, in_, idn, R=(), W=()):
            K.op("pe", lambda e: e.transpose(out, in_, idn), R, W)

        def act(out, in_, func, bias=None, scale=None, accum=None, R=(), W=()):
            kw = {}
            if bias is not None:
                kw["bias"] = bias
            if scale is not None:
                kw["scale"] = scale
            if accum is not None:
                kw["accum_out"] = accum
            K.op("act", lambda e: e.activation(out=out, in_=in_, func=func, **kw), R, W)

        def tt(out, a, b, op, R=(), W=(), en="dve"):
            K.op(en, lambda e: e.tensor_tensor(out=out, in0=a, in1=b, op=op), R, W)

        def ts(out, a, s1, s2, op0, op1=None, R=(), W=(), en="dve"):
            if op1 is None:
                K.op(en, lambda e: e.tensor_scalar(out=out, in0=a, scalar1=s1, scalar2=None, op0=op0), R, W)
            else:
                K.op(en, lambda e: e.tensor_scalar(out=out, in0=a, scalar1=s1, scalar2=s2, op0=op0, op1=op1), R, W)

        def stt(out, a, sc, b, op0, op1, R=(), W=(), en="dve"):
            K.op(en, lambda e: e.scalar_tensor_tensor(out=out, in0=a, scalar=sc, in1=b, op0=op0, op1=op1), R, W)

        def cp(out, in_, R=(), W=(), en="dve"):
            K.op(en, lambda e: e.tensor_copy(out=out, in_=in_), R, W)

        def recip(out, in_, R=(), W=()):
            K.op("dve", lambda e: e.reciprocal(out=out, in_=in_), R, W)

        def memset(ap, val, W=(), en="dve"):
            K.op(en, lambda e: e.memset(ap, val), (), W)

        def wload(W2d, c0, ncols=128):
            t, tok = wring.get()
            K.dma("pool", t[:, :, 0:ncols], W2d[:, c0:c0 + ncols].rearrange("(k p) n -> p k n", p=128), W=[tok], semtok=tok)
            return t, tok

        K.dma("sp", cst[:], cst_d[:, :], W=[cst_tok], semtok=cst_tok)
        K.dma("sp", prm[:].rearrange("p l n -> p (l n)"), prm_d[:, :], W=[prm_tok], semtok=prm_tok)
        K.dma("sp", nrow[:].rearrange("p l n -> p (l n)"), nrow_d[:, :], W=[nrow_tok], semtok=nrow_tok)
        for ph in range(2):
            K.dma("sp", xT[:, :, ph, :], xT_d.rearrange("p (k h t) -> p k h t", k=KC, h=2)[:, :, ph, :], W=[xtok[ph]], semtok=xtok[ph])
        ident = cst[:, C_IDENT:C_IDENT + 128]
        cp(ident_bf[:], ident, R=[cst_tok], W=[misc_tok])
        memset(ones_bf[:], 1.0, W=[misc_tok])
        memset(cb[:, 0:1], 1024.0 * EPS, W=[misc_tok])
        memset(cb[:, 1:2], EPS, W=[misc_tok])
        memset(cb[:, 2:3], 128.0 * EPS, W=[misc_tok])
        memset(cb[:, 3:4], 1.0, W=[misc_tok])
        memset(cb[:, 4:5], 0.0, W=[misc_tok])
        ts(nrow[:], nrow[:], float(np.sqrt(128.0)), None, ALU.mult, R=[nrow_tok], W=[nrow_tok])
        for l in range(4):
            ts(prm[:, l, P_NPRE_MIX:P_NPRE_MIX + 32], prm[:, l, P_NPRE_MIX:P_NPRE_MIX + 32], 32.0, None, ALU.mult, R=[prm_tok], W=[prm_tok])
        with contextlib.ExitStack() as s0:
            cTt = sb(s0, "cTt", [128, KC, 17], F32)
            ctok = Tok()
            K.dma("sp", cTt[:].rearrange("p k q -> p (k q)"), cT_d[:, :], W=[ctok], semtok=ctok)
            act(csT[:], cTt[:], AF.Silu, R=[ctok], W=[cs_tok])
            K.snapshot()

        def adaln_gen(l, S):
            modA = modsets[l % 2]
            mod = sb(S, "modraw", [128, 48, 17], F32)
            mtok = Tok()
            slots = [(psF[5], bkt[5]), (psF[6], bkt[6])]
            for cc in range(48):
                wt, wtok = wload(ada_w[l], cc * 128)
                ps, ptok = slots[cc // 24]
                col = (cc % 24) * 17
                for kc in range(KC):
                    mm(ps[:, col:col + 17], wt[:, kc, :], csT[:, kc, :], start=(kc == 0), stop=(kc == KC - 1),
                       R=[wtok, cs_tok], W=[ptok])
                yield
            for bk in range(2):
                ps, ptok = slots[bk]
                tt(mod[:, 24 * bk:24 * bk + 24, :], ps[:, 0:408].rearrange("p (c q) -> p c q", q=17),
                   prm[:, l, P_ADAB + 24 * bk:P_ADAB + 24 * bk + 24].unsqueeze(2).broadcast_to([128, 24, 17]),
                   ALU.add, R=[ptok, prm_tok], W=[mtok])
            def nb(off):
                return prm[:, l, off:off + 8].unsqueeze(2).broadcast_to([128, 8, 17])
            stt(modA[0][:], mod[:, 8:16, :], 1.0, nb(P_NPRE_MIX), ALU.add, ALU.mult, R=[mtok, prm_tok], W=[mod_tok])
            cp(modA[1][:], mod[:, 0:8, :], R=[mtok], W=[mod_tok])
            stt(modA[2][:], mod[:, 16:24, :], 1.0, nb(P_NPOST_MIX), ALU.add, ALU.mult, R=[mtok, prm_tok], W=[mod_tok])
            stt(modA[3][:], mod[:, 32:40, :], 1.0, nb(P_NPRE_FFN), ALU.add, ALU.mult, R=[mtok, prm_tok], W=[mod_tok])
            cp(modA[4][:], mod[:, 24:32, :], R=[mtok], W=[mod_tok])
            stt(modA[5][:], mod[:, 40:48, :], 1.0, nb(P_NPOST_FFN), ALU.add, ALU.mult, R=[mtok, prm_tok], W=[mod_tok])

        def seqcols(ph):
            return slice(1 + SPP * ph, 1 + SPP * ph + SPP)

        def rstd_fm(src3, n, R, sqt, rs):
            sq, sqtok = sqt
            rst, rstok = rs
            act(sq[:, :, 0:n], src3, AF.Square, R=R, W=[sqtok])
            ps, ptok = pbig.get()
            for kc in range(KC):
                mm(ps[:, 0:n], ones_bf[:], sq[:, kc, 0:n], start=(kc == 0), stop=(kc == KC - 1), R=[sqtok, misc_tok], W=[ptok])
            act(rst[:, 0:n], ps[:, 0:n], AF.Ln, bias=cb[:, 0:1], scale=1.0, R=[ptok, misc_tok], W=[rstok])
            act(rst[:, 0:n], rst[:, 0:n], AF.Exp, scale=-0.5, R=[rstok], W=[rstok])

        def prenorm(ph, A, B, hT, htoks, S):
            sqt_r = Ring([(sb(S, "pn_sq%d" % i, [128, KC, 512], BF16), Tok()) for i in range(2)])
            rs_r = Ring([(sb(S, "pn_rs%d" % i, [128, 512], F32), Tok()) for i in range(2)])
            tmps = Ring([(sb(S, "pn_t%d" % i, [128, 512], F32), Tok()) for i in range(3)])
            for ti, (t0, n) in enumerate(TILES):
                sqt = sqt_r.get()
                rs = rs_r.get()
                rstd_fm(xT[:, :, ph, t0:t0 + n], n, [xtok[ph]], sqt, rs)
                for kc in range(KC):
                    tmp, ttok = tmps.get()
                    tt(tmp[:, 0:n], xT[:, kc, ph, t0:t0 + n], rs[0][:, 0:n], ALU.mult, R=[xtok[ph], rs[1]], W=[ttok])
                    if ti < 2:
                        act(hT[:, kc, t0:t0 + n], tmp[:, 0:n], AF.Identity, bias=B[:, kc, 0:1], scale=A[:, kc, 0:1],
                            R=[ttok, mod_tok], W=[htoks[ti]])
                    else:
                        sc = seqcols(ph)
                        tt(tmp[:, 0:n].rearrange("p (s j) -> p s j", j=LS), tmp[:, 0:n].rearrange("p (s j) -> p s j", j=LS),
                           A[:, kc, sc].unsqueeze(2).broadcast_to([128, SPP, LS]), ALU.mult, R=[ttok, mod_tok], W=[ttok])
                        tt(hT[:, kc, t0:t0 + n].rearrange("p (s j) -> p s j", j=LS), tmp[:, 0:n].rearrange("p (s j) -> p s j", j=LS),
                           B[:, kc, sc].unsqueeze(2).broadcast_to([128, SPP, LS]), ALU.add, R=[ttok, mod_tok], W=[htoks[ti]])

        def proj_tile(wt, wtok, hT, htoks, ti, M=128, ring=None):
            t0, n = TILES[ti]
            ps, ptok = (ring or pbig).get()
            for kc in range(KC):
                mm(ps[0:M, 0:n], wt[:, kc, 0:M], hT[:, kc, t0:t0 + n], start=(kc == 0), stop=(kc == KC - 1),
                   R=[wtok, htoks[ti]], W=[ptok])
            return ps, ptok, n

        def proj_all(wt, wtok, hT, htoks, ring=None, M=128):
            sl3 = [(ring or pwide).get() for _ in range(3)]
            for kc in range(KC):
                for ti, (t0, n) in enumerate(TILES):
                    ps, ptok = sl3[ti]
                    mm(ps[0:M, 0:n], wt[:, kc, 0:M], hT[:, kc, t0:t0 + n], start=(kc == 0), stop=(kc == KC - 1),
                       R=[wtok, htoks[ti]], W=[ptok])
            return [(sl3[ti][0], sl3[ti][1], TILES[ti][1]) for ti in range(3)]

        def outproj_postnorm(l, ph, Wout2d, oT, otoks, G, S):
            y = sb(S, "op_y", [128, KC, 512], F32)
            ytok = Tok()
            sqt = (sb(S, "op_sq", [128, KC, 512], BF16), Tok())
            rs = (sb(S, "op_rs", [128, 512], F32), Tok())
            tmps = Ring([(sb(S, "op_t%d" % i, [128, 512], F32), Tok()) for i in range(2)])
            wres = sb(S, "op_w", [128, KC, KC, 128], BF16)
            wrtok = [retok(t) for t in wres_toks]
            for oc in range(KC):
                K.dma("pool", wres[:, oc, :, :], Wout2d[:, oc * 128:(oc + 1) * 128].rearrange("(k p) n -> p k n", p=128), W=[wrtok[oc]], semtok=wrtok[oc])
            for ti, (t0, n) in enumerate(TILES):
                for oc in range(KC):
                    ps, ptok = pwide.get()
                    for kc in range(KC):
                        mm(ps[:, 0:n], wres[:, oc, kc, :], oT[:, kc, t0:t0 + n], start=(kc == 0), stop=(kc == KC - 1),
                           R=[wrtok[oc], otoks[ti]], W=[ptok])
                    cp(y[:, oc, 0:n], ps[:, 0:n], R=[ptok], W=[ytok])
                resid_update(ph, t0, n, y, ytok, G, sqt, rs, tmps)

        def resid_update(ph, t0, n, y, ytok, G, sqt, rs, tmps):
            rstd_fm(y[:, :, 0:n], n, [ytok], sqt, rs)
            for kc in range(KC):
                tmp, ttok = tmps.get()
                tt(tmp[:, 0:n], y[:, kc, 0:n], rs[0][:, 0:n], ALU.mult, R=[ytok, rs[1]], W=[ttok])
                if t0 < TP:
                    stt(xT[:, kc, ph, t0:t0 + n], tmp[:, 0:n], G[:, kc, 0:1], xT[:, kc, ph, t0:t0 + n], ALU.mult, ALU.add,
                        R=[ttok, mod_tok], W=[xtok[ph]])
                else:
                    sc = seqcols(ph)
                    v3 = tmp[:, 0:n].rearrange("p (s j) -> p s j", j=LS)
                    tt(v3, v3, G[:, kc, sc].unsqueeze(2).broadcast_to([128, SPP, LS]), ALU.mult, R=[ttok, mod_tok], W=[ttok])
                    x3 = xT[:, kc, ph, t0:t0 + n].rearrange("p (s j) -> p s j", j=LS)
                    tt(x3, x3, v3, ALU.add, R=[ttok], W=[xtok[ph]])

        def blocks():
            return [(b * 128, 128, "P") for b in range(8)] + [(TP, 64, "S")]

        def o_post(l, o_rows, BS, h, c0, oT, otok_t, sgT, sgtok, tmpf, tmpb):
            o_ps_, o_ptok_ = o_rows if o_rows is not None else (o_ps, o_ptok)
            junk, jtok = tmpf.get()
            ss, sstok = tmpf.get()
            act(junk[0:BS, :], o_ps_[0:BS, 0:128], AF.Square, accum=ss[0:BS, 0:1], R=[o_ptok_], W=[jtok, sstok])
            act(ss[0:BS, 0:1], ss[0:BS, 0:1], AF.Ln, bias=cb[0:BS, 2:3], scale=1.0, R=[sstok, misc_tok], W=[sstok])
            act(ss[0:BS, 0:1], ss[0:BS, 0:1], AF.Exp, scale=-0.5, R=[sstok], W=[sstok])
            on, ontok = tmpb.get()
            stt(on[0:BS, :], o_ps_[0:BS, 0:128], ss[0:BS, 0:1], nrow[0:BS, l, :], ALU.mult, ALU.mult,
                R=[o_ptok_, sstok, nrow_tok], W=[ontok])
            pb, pbtok = pbf.get()
            tr(pb[:, 0:BS], on[0:BS, :], ident_bf[0:BS, 0:BS], R=[ontok, misc_tok], W=[pbtok])
            tt(oT[:, h, c0:c0 + BS], pb[:, 0:BS], sgT[:, c0:c0 + BS], ALU.mult, R=[pbtok, sgtok], W=[otok_t])

        def gdn_phase(l, ph, hT, htoks, oT, otoks, S):
            j = l // 2
            Win = gdn_w_in[j]
            stage(3)
            G = 4
            dG = sb(S, "g_dG", [8, TPH], F32)
            dtok_ = Tok()
            dtk = sb(S, "g_dtk", [128, 9, 3, 8], F32)
            ex = sb(S, "g_ex", [128, 9, 4, 8], F32)
            glb = sb(S, "g_glb", [128, H, 24], F32)
            dk_tok = Tok()
            gci = sb(S, "g_gci", [128, 24, 8, 3], F32)
            gci_tok = retok(gci_ptok)
            gco = sb(S, "g_gco", [128, 24, 9, 3], F32)
            gco_tok = retok(gco_ptok)
            K.dma("sp", gci[:].rearrange("p a s r -> p (a s r)"), gci_d[j, ph], W=[gci_tok], semtok=gci_tok)
            memset(gco[:], 0.0, W=[gco_tok])
            with contextlib.ExitStack() as SD:
                dB = sb(SD, "g_dB", [8, TPH], F32)
                dL = sb(SD, "g_dL", [8, TPH], F32)
                dg = sb(SD, "g_dg", [8, TPH], F32)
                rmask = dL
                nega = sb(SD, "g_nega", [8, 1], F32)
                memset(rmask[:], 1.0, W=[dtok_])
                memset(rmask[:, 0:TP].rearrange("p (c t) -> p c t", t=64)[:, :, 0:1], 0.0, W=[dtok_])
                memset(rmask[:, TP:TPH].rearrange("p (c t) -> p c t", t=8)[:, :, 0:1], 0.0, W=[dtok_])
                act(nega[:], prm[0:8, l, P_ALOG:P_ALOG + 1], AF.Exp, R=[prm_tok], W=[dtok_])
                ts(nega[:], nega[:], -1.0, None, ALU.mult, R=[dtok_], W=[dtok_])
                wb, wbtok = wload(Win, 4096, 8)
                for ti in range(3):
                    ps, ptok, n = proj_tile(wb, wbtok, hT, htoks, ti, M=8)
                    t0 = TILES[ti][0]
                    act(dB[:, t0:t0 + n], ps[0:8, 0:n], AF.Sigmoid, R=[ptok], W=[dtok_])
                wa, watok = wload(Win, 4104, 8)
                for ti in range(3):
                    ps, ptok, n = proj_tile(wa, watok, hT, htoks, ti, M=8)
                    t0 = TILES[ti][0]
                    act(dg[:, t0:t0 + n], ps[0:8, 0:n], AF.Exp, bias=prm[0:8, l, P_DTB:P_DTB + 1], scale=1.0, R=[ptok, prm_tok], W=[dtok_])
                act(dg[:], dg[:], AF.Ln, bias=cb[0:8, 3:4], scale=1.0, R=[dtok_, misc_tok], W=[dtok_])
                ts(dg[:], dg[:], nega[:, 0:1], None, ALU.mult, R=[dtok_], W=[dtok_])
                K.op("dve", lambda e, dG=dG, rmask=rmask, dg=dg: e.tensor_tensor_scan(out=dG[:], data0=rmask[:], data1=dg[:], initial=0.0, op0=ALU.mult, op1=ALU.add),
                     [dtok_], [dtok_])
                gp = dG[:, 0:TP].rearrange("p (c t) -> p c t", t=64)
                tt(dL[:, 0:TP].rearrange("p (c t) -> p c t", t=64), gp[:, :, 63:64].broadcast_to([8, 16, 64]), gp, ALU.subtract, R=[dtok_], W=[dtok_])
                gs = dG[:, TP:TPH].rearrange("p (c t) -> p c t", t=8)
                tt(dL[:, TP:TPH].rearrange("p (c t) -> p c t", t=8), gs[:, :, 7:8].broadcast_to([8, 8, 8]), gs, ALU.subtract, R=[dtok_], W=[dtok_])
                ps, ptok = pbig.get()
                for bi, (c0, BS, kind) in enumerate(blocks()):
                    for qi, src in enumerate((dB, dG, dL)):
                        col = (bi * 3 + qi) * 8
                        tr(ps[0:BS, col:col + 8], src[:, c0:c0 + BS], cst[0:8, C_IDENT:C_IDENT + 8], R=[dtok_, cst_tok], W=[ptok])
                memset(dtk[:], 0.0, W=[dk_tok])
                cp(dtk[:, 0:8].rearrange("p b q h -> p (b q h)"), ps[:, 0:192], R=[ptok], W=[dk_tok])
                cp(dtk[0:64, 8].rearrange("p q h -> p (q h)"), ps[0:64, 192:216], R=[ptok], W=[dk_tok])
                act(ex[:, :, 0:2, :], dtk[:, :, 1:3, :], AF.Exp, R=[dk_tok], W=[dk_tok])
                tt(ex[:, :, 2, :], dtk[:, :, 0, :], ex[:, :, 0, :], ALU.mult, R=[dk_tok], W=[dk_tok])
                ts(ex[:, :, 3, :], dtk[:, :, 0, :], -1.0, None, ALU.mult, R=[dk_tok], W=[dk_tok])
                ps, ptok = pbig.get()
                for h in range(H):
                    esel_h = cst[0:8, C_ESEL + h * 128:C_ESEL + (h + 1) * 128]
                    mm(ps[:, h * 24:h * 24 + 16], esel_h, dG[:, 0:TP].rearrange("p (c t) -> p c t", t=64)[:, :, 63], R=[dtok_, cst_tok], W=[ptok])
                    mm(ps[:, h * 24 + 16:h * 24 + 24], esel_h, dG[:, TP:TPH].rearrange("p (c t) -> p c t", t=8)[:, :, 7], R=[dtok_, cst_tok], W=[ptok])
                act(glb[:].rearrange("p h c -> p (h c)"), ps[:, 0:192], AF.Exp, R=[ptok], W=[dk_tok])
                K.snapshot()

            with contextlib.ExitStack() as S1:
                qT = sb(S1, "g_qT", [128, TPH], BF16)
                kT = sb(S1, "g_kT", [128, TPH], BF16)
                vT = sb(S1, "g_vT", [128, TPH], BF16)
                sgT = sb(S1, "g_sgT", [128, TPH], BF16)
                q_tok, k_tok, v_tok, sg_tok = Tok(), Tok(), Tok(), Tok()
                Sbf = sb(S1, "g_Sbf", [128, 128], BF16)
                Sbf_tok = Tok()
                Ss = sb(S1, "g_Ss", [128, SPP, 128], F32)
                Ss_tok = retok(Ss_ptok)
                Ssb = sb(S1, "g_Ssb", [128, SPP, 128], BF16)
                Ssb_tok = Tok()
                Sn, Sn_tok = Ss, Ss_tok
                u_sb = sb(S1, "g_usb", [128, 128], BF16)
                usb_tok = Tok()
                memset(u_sb[:], 0.0, W=[usb_tok])
                psblk = Ring([(psF[i % 4][:, (i // 4) * 128:(i // 4) * 128 + 128], bkt[i % 4]) for i in range(16)])
                obank = Ring([(psF[5], bkt[5]), (psF[4], bkt[4])])

                for h in range(H):
                    stage(4 + 0.1 * h)
                    with contextlib.ExitStack() as SPJ:
                        pre_r = Ring([(sb(SPJ, "g_pre%d" % i, [128, 3 + TP + SPP * 11], F32), T()) for i in range(2)])
                        cv_r = Ring([(sb(SPJ, "g_cv%d" % i, [128, TPH], F32), T()) for i in range(2)])
                        sq_r = Ring([(sb(SPJ, "g_sqb%d" % i, [128, 512], BF16), T()) for i in range(2)])
                        ri_r = Ring([(sb(SPJ, "g_rinv%d" % i, [128, 512], F32), T()) for i in range(2)])
                        for role in range(4):
                            wt, wtok = wload(Win, role * 1024 + h * 128)
                            if role < 3:
                                pre, pre_tok = pre_r.get()
                                cv, cv_tok = cv_r.get()
                                pv = pre[:, 3 + TP:3 + TP + SPP * 11].rearrange("p (s r) -> p s r", r=11)
                                ch = role * 8 + h
                                if ph == 0:
                                    memset(pre[:, 0:3], 0.0, W=[pre_tok])
                                else:
                                    cp(pre[:, 0:3], gcar[:, ch, :], R=[gcar_tok], W=[pre_tok])
                                cp(pv[:, :, 0:3], gci[:, ch, :, :], R=[gci_tok], W=[pre_tok])
                            pall = proj_all(wt, wtok, hT, htoks)
                            for ti in range(3):
                                ps, ptok, n = pall[ti]
                                t0 = TILES[ti][0]
                                if role == 3:
                                    act(sgT[:, t0:t0 + n], ps[:, 0:n], AF.Silu, R=[ptok], W=[sg_tok])
                                elif ti < 2:
                                    act(pre[:, 3 + t0:3 + t0 + n], ps[:, 0:n], AF.Copy, R=[ptok], W=[pre_tok])
                                else:
                                    act(pv[:, :, 3:11], ps[:, 0:n].rearrange("p (s r) -> p s r", r=LS), AF.Copy, R=[ptok], W=[pre_tok])
                            if role == 3:
                                continue
                            if ph == 0:
                                cp(gcar[:, ch, :], pre[:, TP:TP + 3], R=[pre_tok], W=[gcar_tok])
                            else:
                                cp(gco[:, ch, 8, :], pre[:, TP:TP + 3], R=[pre_tok], W=[gco_tok])
                            cp(gco[:, ch, 0:8, :], pv[:, :, 8:11], R=[pre_tok], W=[gco_tok])
                            wc = prm[:, l, P_GCW + ch * 4:P_GCW + ch * 4 + 4]
                            bc = prm[:, l, P_GCB + ch:P_GCB + ch + 1]
                            cvs = cv[:, TP:TPH].rearrange("p (s r) -> p s r", r=LS)
                            ts(cv[:, 0:TP], pre[:, 0:TP], wc[:, 0:1], bc, ALU.mult, ALU.add, R=[pre_tok, prm_tok], W=[cv_tok])
                            ts(cvs, pv[:, :, 0:8], wc[:, 0:1], bc, ALU.mult, ALU.add, R=[pre_tok, prm_tok], W=[cv_tok])
                            for tap in range(1, 4):
                                stt(cv[:, 0:TP], pre[:, tap:tap + TP], wc[:, tap:tap + 1], cv[:, 0:TP], ALU.mult, ALU.add,
                                    R=[pre_tok, prm_tok], W=[cv_tok])
                                stt(cvs, pv[:, :, tap:tap + 8], wc[:, tap:tap + 1], cvs, ALU.mult, ALU.add, R=[pre_tok, prm_tok], W=[cv_tok])
                            if role == 2:
                                act(vT[:], cv[:], AF.Silu, R=[cv_tok], W=[v_tok])
                                continue
                            act(cv[:], cv[:], AF.Silu, R=[cv_tok], W=[cv_tok])
                            for ti, (t0, n) in enumerate(TILES):
                                sqb, sqb_tok = sq_r.get()
                                rinv, rinv_tok = ri_r.get()
                                act(sqb[:, 0:n], cv[:, t0:t0 + n], AF.Square, R=[cv_tok], W=[sqb_tok])
                                ps, ptok = pwide.get()
                                mm(ps[:, 0:n], ones_bf[:], sqb[:, 0:n], R=[sqb_tok, misc_tok], W=[ptok])
                                act(rinv[:, 0:n], ps[:, 0:n], AF.Ln, bias=cb[:, 1:2], scale=1.0, R=[ptok, misc_tok], W=[rinv_tok])
                                act(rinv[:, 0:n], rinv[:, 0:n], AF.Exp, scale=-0.5, R=[rinv_tok], W=[rinv_tok])
                                if role == 0:
                                    stt(qT[:, t0:t0 + n], cv[:, t0:t0 + n], float(128.0 ** -0.5), rinv[:, 0:n], ALU.mult, ALU.mult,
                                        R=[cv_tok, rinv_tok], W=[q_tok])
                                else:
                                    tt(kT[:, t0:t0 + n], cv[:, t0:t0 + n], rinv[:, 0:n], ALU.mult, R=[cv_tok, rinv_tok], W=[k_tok])
                        K.snapshot()

                    stage(4.05 + 0.1 * h)
                    K.dma("pool", Ss[:], sgdn_d[j, SPP * ph:SPP * ph + SPP, h].rearrange("s k v -> k s v"), W=[Ss_tok], semtok=Ss_tok)
                    act(Ssb[:], Ss[:], AF.Copy, R=[Ss_tok], W=[Ssb_tok])
                    if ph == 0:
                        memset(Sf_all[:, h, :], 0.0, W=[Sf_tok[h]])
                    cp(Sbf[:], Sf_all[:, h, :], R=[Sf_tok[h]], W=[Sbf_tok])

                    with contextlib.ExitStack() as SBK:
                        slots = []
                        for k in range(G):
                            sl = {}
                            sl["X"] = [(sb(SBK, "g_x%d_%d" % (k, i), [128, 128], F32), T()) for i in range(6)]
                            sl["kbg"] = (sb(SBK, "g_kbg%d" % k, [128, 128], BF16), T())
                            sl["HS"] = [[(sb(SBK, "g_hs%d_%d_%d" % (k, par, i), [128, 128], BF16), T()) for i in range(6)] for par in range(2)]
                            slots.append(sl)
                        spad = {}
                        for nm in ("Kpad", "Wpad", "Qpad", "tw"):
                            spad[nm] = (sb(SBK, "g_s%s" % nm, [128, 8, 128], BF16), T())
                        spad["kdm"] = (sb(SBK, "g_skdm", [128, 8], F32), T())
                        opf = Ring([(sb(SBK, "g_of%d" % i, [128, 128], F32), T()) for i in range(4)])
                        opb = Ring([(sb(SBK, "g_ob%d" % i, [128, 128], BF16), T()) for i in range(2)])

                        def pre_state(bi, c0, BS, kind, sl, par):
                            if kind == "P":
                                ncb, C, nlev = 2, 64, 5
                                pen = cst[:, C_PENP:C_PENP + 128]
                            else:
                                ncb, C, nlev = 8, 8, 2
                                pen = cst[0:64, C_PENS:C_PENS + 64]
                                cmask = cst[:, C_CM8:C_CM8 + 512].rearrange("p (c t) -> p c t", c=8)
                                rmk = cst[0:64, C_RM8:C_RM8 + 8]
                            cols = slice(c0, c0 + BS)
                            X = sl["X"]
                            HS = sl["HS"][par]
                            Gcol = dtk[0:BS, bi, 1, h:h + 1]
                            beta_c = dtk[0:BS, bi, 0, h:h + 1]
                            eGL_c = ex[0:BS, bi, 1, h:h + 1]
                            bG_c = ex[0:BS, bi, 2, h:h + 1]
                            nbeta_c = ex[0:BS, bi, 3, h:h + 1]
                            esel_h = cst[0:8, C_ESEL + h * 128:C_ESEL + (h + 1) * 128]
                            gps, gptok = psblk.get()
                            mm(gps[:, 0:BS], esel_h, dG[:, cols], R=[dtok_, cst_tok], W=[gptok])
                            yield
                            r_, rtok = X[0]
                            stt(r_[0:BS, 0:BS], gps[0:BS, 0:BS], Gcol, pen, ALU.subtract, ALU.max, R=[gptok, dk_tok, cst_tok], W=[rtok])
                            eGr, egtok = X[2]
                            act(eGr[:, 0:BS], gps[:, 0:BS], AF.Exp, R=[gptok], W=[egtok])
                            yield
                            Ds, dstok = X[1]
                            act(Ds[0:BS, 0:BS], r_[0:BS, 0:BS], AF.Exp, scale=-1.0, R=[rtok], W=[dstok])
                            if kind == "P":
                                qg, qgtok = HS[3]
                                tt(qg[:, 0:BS], qT[:, cols], eGr[:, 0:BS], ALU.mult, R=[q_tok, egtok], W=[qgtok])
                            else:
                                tw, twtok = spad["tw"]
                                Qpad, qp_tok = spad["Qpad"]
                                tt(tw[:, 0:ncb, 0:BS], cmask[:, :, 0:BS], eGr[:, 0:BS].unsqueeze(1).broadcast_to([128, ncb, BS]), ALU.mult,
                                   R=[cst_tok, egtok], W=[twtok])
                                tt(Qpad[:, 0:ncb, 0:BS], tw[:, 0:ncb, 0:BS], qT[:, cols].unsqueeze(1).broadcast_to([128, ncb, BS]), ALU.mult,
                                   R=[twtok, q_tok], W=[qp_tok])
                            yield
                            dps, dptok = psblk.get()
                            tr(dps[0:BS, 0:BS], Ds[0:BS, 0:BS], ident[0:BS, 0:BS], R=[dstok, cst_tok], W=[dptok])
                            kkps, kktok = psblk.get()
                            mm(kkps[0:BS, 0:BS], kT[:, cols], kT[:, cols], R=[k_tok], W=[kktok])
                            qkps, qktok = psblk.get()
                            mm(qkps[0:BS, 0:BS], kT[:, cols], qT[:, cols], R=[k_tok, q_tok], W=[qktok])
                            yield
                            DmT, dmtok = X[0]
                            tt(DmT[0:BS, 0:BS], dps[0:BS, 0:BS], ident[0:BS, 0:BS], ALU.add, R=[dptok, cst_tok], W=[dmtok])
                            B, btok = X[2]
                            stt(B[0:BS, 0:BS], kkps[0:BS, 0:BS], nbeta_c, Ds[0:BS, 0:BS], ALU.mult, ALU.mult, R=[kktok, dk_tok, dstok], W=[btok])
                            yield
                            attnT, attok = HS[0]
                            tt(attnT[0:BS, 0:BS], qkps[0:BS, 0:BS], DmT[0:BS, 0:BS], ALU.mult, R=[qktok, dmtok], W=[attok])
                            btps, btptok = psblk.get()
                            tr(btps[0:BS, 0:BS], B[0:BS, 0:BS], ident[0:BS, 0:BS], R=[btok, cst_tok], W=[btptok])
                            yield
                            Bt, bttok = X[3]
                            act(Bt[0:BS, 0:BS], btps[0:BS, 0:BS], AF.Copy, R=[btptok], W=[bttok])
                            TT, tttok = X[4]
                            tt(TT[0:BS, 0:BS], btps[0:BS, 0:BS], ident[0:BS, 0:BS], ALU.add, R=[btptok, cst_tok], W=[tttok])
                            yield
                            cur = (X[2], X[3])
                            nxt = (X[5], X[1])
                            for lev in range(nlev):
                                last = (lev == nlev - 1)
                                (B, btok), (Bt, bttok) = cur
                                (Bn, bntok), (Btn, btntok) = nxt
                                p2, p2tok = psblk.get()
                                mmr(p2[0:BS, 0:BS], Bt[0:BS, 0:BS], B[0:BS, 0:BS], R=[bttok, btok], W=[p2tok])
                                if not last:
                                    p1, p1tok = psblk.get()
                                    mmr(p1[0:BS, 0:BS], B[0:BS, 0:BS], Bt[0:BS, 0:BS], R=[bttok, btok], W=[p1tok])
                                yield
                                cp(Bn[0:BS, 0:BS], p2[0:BS, 0:BS], R=[p2tok], W=[bntok])
                                if not last:
                                    act(Btn[0:BS, 0:BS], p1[0:BS, 0:BS], AF.Copy, R=[p1tok], W=[btntok])
                                yield
                                p3, p3tok = psblk.get()
                                mmr(p3[0:BS, 0:BS], Bn[0:BS, 0:BS], TT[0:BS, 0:BS], R=[bntok, tttok], W=[p3tok])
                                yield
                                tt(TT[0:BS, 0:BS], p3[0:BS, 0:BS], TT[0:BS, 0:BS], ALU.add, R=[p3tok, tttok], W=[tttok])
                                yield
                                cur, nxt = nxt, cur
                            TTb, ttbtok = HS[1]
                            act(TTb[0:BS, 0:BS], TT[0:BS, 0:BS], AF.Copy, R=[tttok], W=[ttbtok])
                            pk, pktok = pbf.get()
                            tr(pk[0:BS, :], kT[:, cols], ident_bf[:], R=[k_tok, misc_tok], W=[pktok])
                            pvv, pvtok = pbf.get()
                            tr(pvv[0:BS, :], vT[:, cols], ident_bf[:], R=[v_tok, misc_tok], W=[pvtok])
                            yield
                            vb, vbtok = HS[2]
                            act(vb[0:BS, :], pvv[0:BS, :], AF.Copy, scale=beta_c, R=[pvtok, dk_tok], W=[vbtok])
                            kbg, kbgtok = sl["kbg"]
                            act(kbg[0:BS, :], pk[0:BS, :], AF.Copy, scale=bG_c, R=[pktok, dk_tok], W=[kbgtok])
                            if kind == "P":
                                kd, kdtok = HS[5]
                                act(kd[0:BS, :], pk[0:BS, :], AF.Copy, scale=eGL_c, R=[pktok, dk_tok], W=[kdtok])
                            else:
                                kdm, kdmtok = spad["kdm"]
                                Kpad, kp_tok = spad["Kpad"]
                                ts(kdm[0:BS, 0:ncb], rmk, eGL_c, None, ALU.mult, R=[cst_tok, dk_tok], W=[kdmtok])
                                tt(Kpad[0:BS, 0:ncb, :], pk[0:BS, :].unsqueeze(1).broadcast_to([BS, ncb, 128]),
                                   kdm[0:BS, 0:ncb].unsqueeze(2).broadcast_to([BS, ncb, 128]), ALU.mult, R=[pktok, kdmtok], W=[kp_tok])
                            yield
                            wps, wptok = psblk.get()
                            mm(wps[:, 0:BS], kbg[0:BS, :], TTb[0:BS, 0:BS], R=[kbgtok, ttbtok], W=[wptok])
                            yield
                            if kind == "P":
                                Wneg, wntok = HS[4]
                                act(Wneg[:, 0:BS], wps[:, 0:BS], AF.Copy, scale=-1.0, R=[wptok], W=[wntok])
                            else:
                                Wpad, wp_tok = spad["Wpad"]
                                stt(Wpad[:, 0:ncb, 0:BS], wps[:, 0:BS].unsqueeze(1).broadcast_to([128, ncb, BS]), -1.0, cmask[:, :, 0:BS],
                                    ALU.mult, ALU.mult, R=[wptok, cst_tok], W=[wp_tok])

                        pend = [None]

                        def flush_post():
                            if pend[0] is not None:
                                o_post(*pend[0])
                                pend[0] = None

                        def state_group(grp, par):
                            for k, bi in enumerate(grp):
                                c0, BS, kind = blks[bi]
                                sl = slots[k]
                                HS = sl["HS"][par]
                                attnT, attok = HS[0]
                                TTb, ttbtok = HS[1]
                                vb, vbtok = HS[2]
                                ops_, optok = obank.get()
                                if kind == "P":
                                    qg, qgtok = HS[3]
                                    Wneg, wntok = HS[4]
                                    kd, kdtok = HS[5]
                                    for c in range(2):
                                        rows = slice(c * 64, (c + 1) * 64)
                                        mm(u_ps[rows, 0:128], TTb[rows, rows], vb[rows, :], start=True, stop=False, R=[ttbtok, vbtok], W=[u_ptok])
                                        mm(u_ps[rows, 0:128], Wneg[:, rows], Sbf[:], start=False, stop=True, R=[wntok, Sbf_tok], W=[u_ptok])
                                        act(u_sb[rows, :], u_ps[rows, 0:128], AF.Copy, R=[u_ptok], W=[usb_tok])
                                        mm(ops_[rows, 0:128], qg[:, rows], Sbf[:], start=True, stop=False, R=[qgtok, Sbf_tok], W=[optok])
                                        mm(ops_[rows, 0:128], attnT[rows, rows], u_sb[rows, :], start=False, stop=True, R=[attok, usb_tok], W=[optok])
                                        sps, sptok = psblk.get()
                                        mm(sps[:, :], kd[rows, :], u_sb[rows, :], R=[kdtok, usb_tok], W=[sptok])
                                        gl = glb[:, h, (bi * 2 + c):(bi * 2 + c) + 1]
                                        stt(Sbf[:], Sf_all[:, h, :], gl, sps[:, :], ALU.mult, ALU.add, R=[sptok, dk_tok, Sf_tok[h]], W=[Sbf_tok])
                                        stt(Sf_all[:, h, :], Sf_all[:, h, :], gl, sps[:, :], ALU.mult, ALU.add, R=[sptok, dk_tok], W=[Sf_tok[h]])
                                        yield
                                else:
                                    ncb = 8
                                    Kpad, kp_tok = spad["Kpad"]
                                    Wpad, wp_tok = spad["Wpad"]
                                    Qpad, qp_tok = spad["Qpad"]
                                    mm(u_ps[0:BS, 0:128], TTb[0:BS, 0:BS], vb[0:BS, :], start=True, stop=False, R=[ttbtok, vbtok], W=[u_ptok])
                                    for c in range(ncb):
                                        mm(u_ps[0:BS, 0:128], Wpad[:, c, 0:BS], Ssb[:, c, :], start=False, stop=(c == ncb - 1), R=[wp_tok, Ssb_tok], W=[u_ptok])
                                    act(u_sb[0:BS, :], u_ps[0:BS, 0:128], AF.Copy, R=[u_ptok], W=[usb_tok])
                                    for c in range(ncb):
                                        mm(ops_[0:BS, 0:128], Qpad[:, c, 0:BS], Ssb[:, c, :], start=(c == 0), stop=False, R=[qp_tok, Ssb_tok], W=[optok])
                                    mm(ops_[0:BS, 0:128], attnT[0:BS, 0:BS], u_sb[0:BS, :], start=False, stop=True, R=[attok, usb_tok], W=[optok])
                                    sl2 = [pbig.get(), pbig.get()]
                                    for c in range(ncb):
                                        ps, ptok = sl2[c // 4]
                                        mm(ps[:, (c % 4) * 128:(c % 4) * 128 + 128], Kpad[0:BS, c, :], u_sb[0:BS, :], R=[kp_tok, usb_tok], W=[ptok])
                                    tt(Sn[:], Ss[:], glb[:, h, 16:24].unsqueeze(2).broadcast_to([128, 8, 128]), ALU.mult, R=[Ss_tok, dk_tok], W=[Sn_tok])
                                    for hb in range(2):
                                        ps, ptok = sl2[hb]
                                        tt(Sn[:, hb * 4:hb * 4 + 4, :], Sn[:, hb * 4:hb * 4 + 4, :], ps[:, :].rearrange("p (s v) -> p s v", v=128), ALU.add,
                                           R=[ptok], W=[Sn_tok])
                                    K.dma("sp", sgdn_o[j, SPP * ph:SPP * ph + SPP, h].rearrange("s k v -> k s v"), Sn[:], R=[Sn_tok], semtok=Sn_tok)
                                flush_post()
                                pend[0] = (l, (ops_, optok), BS, h, c0, oT, otoks[min(c0 // 512, 2)], sgT, sg_tok, opf, opb)
                                yield

                        def run_rr(gens):
                            alive = list(gens)
                            while alive:
                                nx = []
                                for g in alive:
                                    try:
                                        next(g)
                                        nx.append(g)
                                    except StopIteration:
                                        pass
                                alive = nx

                        blks = blocks()
                        groups = [[0, 1, 2, 3], [4, 5, 6, 7], [8]]
                        stage(4.06 + 0.1 * h)
                        run_rr([pre_state(bi, blks[bi][0], blks[bi][1], blks[bi][2], slots[k], 0) for k, bi in enumerate(groups[0])])
                        run_rr([state_group(groups[0], 0)] +
                               [pre_state(bi, blks[bi][0], blks[bi][1], blks[bi][2], slots[k], 1) for k, bi in enumerate(groups[1])])
                        run_rr([state_group(groups[1], 1)] +
                               [pre_state(8, blks[8][0], blks[8][1], blks[8][2], slots[0], 0)])
                        run_rr([state_group(groups[2], 0)])
                        flush_post()
                        if ph == 1:
                            K.dma("sp", pgdn_d[j, h], Sf_all[:, h, :], R=[Sf_tok[h]], semtok=Sf_tok[h])
                        K.snapshot()
                K.dma("sp", gco_d[j, ph], gco[:].rearrange("p a s r -> p (a s r)"), R=[gco_tok], semtok=gco_tok)
                K.snapshot()

        def hgrn_phase(l, ph, hT, htoks, oT, otoks, S):
            j = l // 2
            Win = hgrn_w_in[j]
            with contextlib.ExitStack() as S1:
                rmask = sb(S1, "h_rm", [128, TPH], F32)
                rm_tok = Tok()
                eG = sb(S1, "h_eG", [128, TPH], F32)
                eG_tok = Tok()
                qgT = sb(S1, "h_qgT", [128, TPH], BF16)
                kiT = sb(S1, "h_kiT", [128, TPH], BF16)
                kdT = sb(S1, "h_kdT", [128, TPH], BF16)
                vT = sb(S1, "h_vT", [128, TPH], BF16)
                sgT = sb(S1, "h_sgT", [128, TPH], BF16)
                qg_tok, ki_tok, kd_tok, v_tok, sg_tok = Tok(), Tok(), Tok(), Tok(), Tok()
                Sbf = sb(S1, "h_Sbf", [128, 128], BF16)
                Sbf_tok = Tok()
                Ss = sb(S1, "h_Ss", [128, SPP, 128], F32)
                Ss_tok = retok(Ss_ptok)
                Ssb = sb(S1, "h_Ssb", [128, SPP, 128], BF16)
                Ssb_tok = Tok()
                Sn, Sn_tok = Ss, Ss_tok
                memset(rmask[:], 1.0, W=[rm_tok])
                memset(rmask[:, 0:TP].rearrange("p (c t) -> p c t", t=32)[:, :, 0:1], 0.0, W=[rm_tok])
                memset(rmask[:, TP:TPH].rearrange("p (c t) -> p c t", t=8)[:, :, 0:1], 0.0, W=[rm_tok])
                psblk = Ring([(psF[i % 5][:, (i // 5) * 128:(i // 5) * 128 + 128], bkt[i % 5]) for i in range(20)])
                obank = Ring([(psF[5], bkt[5]), (psF[6], bkt[6])])
                for h in range(H):
                    stage(7 + 0.1 * h)
                    lb = lbT[:, h, 0:1]
                    oml = lbT[:, h, 1:2]
                    with contextlib.ExitStack() as SPL:
                        fa = sb(SPL, "h_fa", [128, TPH], F32)
                        fa_tok = T()
                        fk = sb(SPL, "h_fk", [128, TPH], F32)
                        fk_tok = T()
                        fG = sb(SPL, "h_fG", [128, TPH], F32)
                        fG_tok = T()
                        fe = sb(SPL, "h_fe", [128, TPH], F32)
                        fe_tok = T()
                        sq_ = sb(SPL, "h_sq", [128, TPH], F32)
                        sq_tok = T()
                        for role in range(4):
                            wt, wtok = wload(Win, role * 1024 + h * 128)
                            pall = proj_all(wt, wtok, hT, htoks)
                            for ti in range(3):
                                ps, ptok, n = pall[ti]
                                t0 = TILES[ti][0]
                                if role == 0:
                                    act(sq_[:, t0:t0 + n], ps[:, 0:n], AF.Silu, R=[ptok], W=[sq_tok])
                                elif role == 1:
                                    act(fa[:, t0:t0 + n], ps[:, 0:n], AF.Sigmoid, R=[ptok], W=[fa_tok])
                                elif role == 2:
                                    act(vT[:, t0:t0 + n], ps[:, 0:n], AF.Copy, R=[ptok], W=[v_tok])
                                else:
                                    act(sgT[:, t0:t0 + n], ps[:, 0:n], AF.Silu, R=[ptok], W=[sg_tok])
                        ts(fa[:], fa[:], oml, lb, ALU.mult, ALU.add, R=[fa_tok, lb_tok], W=[fa_tok])
                        ts(fk[:], fa[:], -1.0, 1.0, ALU.mult, ALU.add, R=[fa_tok], W=[fk_tok])
                        act(fa[:], fa[:], AF.Ln, R=[fa_tok], W=[fa_tok])
                        K.op("dve", lambda e, fG=fG, fa=fa: e.tensor_tensor_scan(out=fG[:], data0=rmask[:], data1=fa[:], initial=0.0, op0=ALU.mult, op1=ALU.add),
                             [fa_tok, rm_tok], [fG_tok])
                        act(eG[:], fG[:], AF.Exp, R=[fG_tok], W=[eG_tok])
                        tt(qgT[:], sq_[:], eG[:], ALU.mult, R=[sq_tok, eG_tok], W=[qg_tok])
                        act(fe[:], fG[:], AF.Exp, scale=-1.0, R=[fG_tok], W=[fe_tok])
                        tt(kiT[:], fk[:], fe[:], ALU.mult, R=[fk_tok, fe_tok], W=[ki_tok])
                        gp = fG[:, 0:TP].rearrange("p (c t) -> p c t", t=32)
                        tt(fa[:, 0:TP].rearrange("p (c t) -> p c t", t=32), gp[:, :, 31:32].broadcast_to([128, 32, 32]), gp, ALU.subtract,
                           R=[fG_tok], W=[fa_tok])
                        gs = fG[:, TP:TPH].rearrange("p (c t) -> p c t", t=8)
                        tt(fa[:, TP:TPH].rearrange("p (c t) -> p c t", t=8), gs[:, :, 7:8].broadcast_to([128, 8, 8]), gs, ALU.subtract,
                           R=[fG_tok], W=[fa_tok])
                        act(fe[:], fa[:], AF.Exp, R=[fa_tok], W=[fe_tok])
                        tt(kdT[:], fk[:], fe[:], ALU.mult, R=[fk_tok, fe_tok], W=[kd_tok])
                        K.snapshot()

                    stage(7.05 + 0.1 * h)
                    K.dma("pool", Ss[:], shgrn_d[j, SPP * ph:SPP * ph + SPP, h].rearrange("s k v -> k s v"), W=[Ss_tok], semtok=Ss_tok)
                    act(Ssb[:], Ss[:], AF.Copy, R=[Ss_tok], W=[Ssb_tok])
                    if ph == 0:
                        memset(Sf_all[:, h, :], 0.0, W=[Sf_tok[h]])
                    cp(Sbf[:], Sf_all[:, h, :], R=[Sf_tok[h]], W=[Sbf_tok])

                    with contextlib.ExitStack() as SBK:
                        blks = blocks()
                        bufs = []
                        for bi, (c0, BS, kind) in enumerate(blks):
                            npad = 4 if kind == "P" else 8
                            bufs.append(dict(
                                attnT=(sb(SBK, "h_at%d" % bi, [128, 128], BF16), T()),
                                vtk=(sb(SBK, "h_vt%d" % bi, [128, 128], BF16), T()),
                                Kpad=(sb(SBK, "h_kp%d" % bi, [128, npad, 128], BF16), T()),
                                Qpad=(sb(SBK, "h_qp%d" % bi, [128, npad, 128], BF16), T())))
                        opf = Ring([(sb(SBK, "h_of%d" % i, [128, 128], F32), T()) for i in range(4)])
                        opb = Ring([(sb(SBK, "h_ob%d" % i, [128, 128], BF16), T()) for i in range(2)])

                        def consts_for(kind):
                            if kind == "P":
                                return (4, 32, cst[:, C_MIP:C_MIP + 128],
                                        cst[:, C_CM4:C_CM4 + 512].rearrange("p (c t) -> p c t", c=4), cst[:, C_RM4:C_RM4 + 4])
                            return (8, 8, cst[0:64, C_MIS:C_MIS + 64],
                                    cst[:, C_CM8:C_CM8 + 512].rearrange("p (c t) -> p c t", c=8), cst[0:64, C_RM8:C_RM8 + 8])

                        for bi, (c0, BS, kind) in enumerate(blks):
                            ncb, C, mi, cmask, rmk = consts_for(kind)
                            bf = bufs[bi]
                            cols = slice(c0, c0 + BS)
                            aps, aptok = psblk.get()
                            mm(aps[0:BS, 0:BS], kiT[:, cols], qgT[:, cols], R=[ki_tok, qg_tok], W=[aptok])
                            attnT, attok = bf["attnT"]
                            tt(attnT[0:BS, 0:BS], aps[0:BS, 0:BS], mi, ALU.mult, R=[aptok, cst_tok], W=[attok])
                            pvv, pvtok = pbf.get()
                            tr(pvv[0:BS, :], vT[:, cols], ident_bf[:], R=[v_tok, misc_tok], W=[pvtok])
                            vtk, vtktok = bf["vtk"]
                            act(vtk[0:BS, :], pvv[0:BS, :], AF.Copy, R=[pvtok], W=[vtktok])
                            pk, pktok = pbf.get()
                            tr(pk[0:BS, :], kdT[:, cols], ident_bf[:], R=[kd_tok, misc_tok], W=[pktok])
                            Kpad, kp_tok = bf["Kpad"]
                            tt(Kpad[0:BS, 0:ncb, :], pk[0:BS, :].unsqueeze(1).broadcast_to([BS, ncb, 128]),
                               rmk.unsqueeze(2).broadcast_to([BS, ncb, 128]), ALU.mult, R=[pktok, cst_tok], W=[kp_tok])
                            Qpad, qp_tok = bf["Qpad"]
                            tt(Qpad[:, 0:ncb, 0:BS], cmask[:, :, 0:BS], qgT[:, cols].unsqueeze(1).broadcast_to([128, ncb, BS]), ALU.mult,
                               R=[cst_tok, qg_tok], W=[qp_tok])

                        pending = None
                        for bi, (c0, BS, kind) in enumerate(blks):
                            ncb, C, mi, cmask, rmk = consts_for(kind)
                            bf = bufs[bi]
                            attnT, attok = bf["attnT"]
                            vtk, vtktok = bf["vtk"]
                            Kpad, kp_tok = bf["Kpad"]
                            Qpad, qp_tok = bf["Qpad"]
                            ops_, optok = obank.get()
                            if kind == "P":
                                for c in range(ncb):
                                    mm(ops_[0:BS, 0:128], Qpad[:, c, 0:BS], Sbf[:], start=(c == 0), stop=False, R=[qp_tok, Sbf_tok], W=[optok])
                                    sps, sptok = psblk.get()
                                    mm(sps[:, :], Kpad[0:BS, c, :], vtk[0:BS, :], R=[kp_tok, vtktok], W=[sptok])
                                    ce = c0 + (c + 1) * C - 1
                                    stt(Sbf[:], Sf_all[:, h, :], eG[:, ce:ce + 1], sps[:, :], ALU.mult, ALU.add,
                                        R=[sptok, eG_tok, Sf_tok[h]], W=[Sbf_tok])
                                    stt(Sf_all[:, h, :], Sf_all[:, h, :], eG[:, ce:ce + 1], sps[:, :], ALU.mult, ALU.add,
                                        R=[sptok, eG_tok], W=[Sf_tok[h]])
                                mm(ops_[0:BS, 0:128], attnT[0:BS, 0:BS], vtk[0:BS, :], start=False, stop=True, R=[attok, vtktok], W=[optok])
                            else:
                                for c in range(ncb):
                                    mm(ops_[0:BS, 0:128], Qpad[:, c, 0:BS], Ssb[:, c, :], start=(c == 0), stop=False, R=[qp_tok, Ssb_tok], W=[optok])
                                mm(ops_[0:BS, 0:128], attnT[0:BS, 0:BS], vtk[0:BS, :], start=False, stop=True, R=[attok, vtktok], W=[optok])
                                sl2 = [pbig.get(), pbig.get()]
                                for c in range(ncb):
                                    ps, ptok = sl2[c // 4]
                                    mm(ps[:, (c % 4) * 128:(c % 4) * 128 + 128], Kpad[0:BS, c, :], vtk[0:BS, :], R=[kp_tok, vtktok], W=[ptok])
                                ege = eG[:, TP:TPH].rearrange("p (s t) -> p s t", t=8)[:, :, 7:8]
                                tt(Sn[:], Ss[:], ege.broadcast_to([128, 8, 128]), ALU.mult, R=[Ss_tok, eG_tok], W=[Sn_tok])
                                for hb in range(2):
                                    ps, ptok = sl2[hb]
                                    tt(Sn[:, hb * 4:hb * 4 + 4, :], Sn[:, hb * 4:hb * 4 + 4, :], ps[:, :].rearrange("p (s v) -> p s v", v=128), ALU.add,
                                       R=[ptok], W=[Sn_tok])
                                K.dma("sp", shgrn_o[j, SPP * ph:SPP * ph + SPP, h].rearrange("s k v -> k s v"), Sn[:], R=[Sn_tok], semtok=Sn_tok)
                            if pending is not None:
                                o_post(*pending)
                            pending = (l, (ops_, optok), BS, h, c0, oT, otoks[min(c0 // 512, 2)], sgT, sg_tok, opf, opb)
                        o_post(*pending)
                        if ph == 1:
                            K.dma("sp", phgrn_d[j, h], Sf_all[:, h, :], R=[Sf_tok[h]], semtok=Sf_tok[h])
                        K.snapshot()
                K.snapshot()

        pw5 = Ring([(psF[i], bkt[i]) for i in range(5)])

        def ffn_phase(l, ph, want_ada=False):
            agen = None
            modA = modsets[l % 2]
            A2, B2, G2 = modA[3], modA[4], modA[5]
            with contextlib.ExitStack() as SF:
                aT = sb(SF, "f_aT", [128, NFF, TPH], BF16)
                a_toks = [Tok() for _ in range(3)]
                fci = sb(SF, "f_fci", [128, NFF, 8, 2], F32)
                fci_tok = retok(fci_ptok)
                fco = sb(SF, "f_fco", [128, NFF, 9, 2], F32)
                fco_tok = retok(fco_ptok)
                K.dma("sp", fci[:].rearrange("p a s r -> p (a s r)"), fci_d[l, ph], W=[fci_tok], semtok=fci_tok)
                memset(fco[:], 0.0, W=[fco_tok])
                with contextlib.ExitStack() as S1:
                    hT = sb(S1, "f_hT", [128, KC, TPH], BF16)
                    htoks = [Tok() for _ in range(3)]
                    with contextlib.ExitStack() as SPN:
                        prenorm(ph, A2, B2, hT, htoks, SPN)
                        K.snapshot()
                    gpre_r = Ring([(sb(S1, "f_gpre%d" % i, [128, 2 + TP + SPP * 10], F32), Tok()) for i in range(2)])
                    up_r = Ring([(sb(S1, "f_up%d" % i, [128, TPH], F32), Tok()) for i in range(2)])
                    gc_r = Ring([(sb(S1, "f_gc%d" % i, [128, TPH], F32), Tok()) for i in range(2)])
                    if want_ada:
                        agen = adaln_gen(l + 1, S1)
                    for jj in range(NFF):
                        gpre, gp_tok = gpre_r.get()
                        up, up_tok = up_r.get()
                        gc, gc_tok = gc_r.get()
                        pv = gpre[:, 2 + TP:2 + TP + SPP * 10].rearrange("p (s r) -> p s r", r=10)
                        wg, wgtok = wload(ffn_w_gu[l], jj * 128)
                        wu, wutok = wload(ffn_w_gu[l], DFF + jj * 128)
                        if ph == 0:
                            memset(gpre[:, 0:2], 0.0, W=[gp_tok])
                        else:
                            cp(gpre[:, 0:2], fcar[:, jj, :], R=[fcar_tok], W=[gp_tok])
                        cp(pv[:, :, 0:2], fci[:, jj, :, :], R=[fci_tok], W=[gp_tok])
                        pg = proj_all(wg, wgtok, hT, htoks)
                        for ti in range(3):
                            t0 = TILES[ti][0]
                            ps, ptok, n = pg[ti]
                            if ti < 2:
                                act(gpre[:, 2 + t0:2 + t0 + n], ps[:, 0:n], AF.Copy, R=[ptok], W=[gp_tok])
                            else:
                                act(pv[:, :, 2:10], ps[:, 0:n].rearrange("p (s r) -> p s r", r=LS), AF.Copy, R=[ptok], W=[gp_tok])
                        pu = proj_all(wu, wutok, hT, htoks)
                        for ti in range(3):
                            t0 = TILES[ti][0]
                            ps, ptok, n = pu[ti]
                            act(up[:, t0:t0 + n], ps[:, 0:n], AF.Copy, R=[ptok], W=[up_tok])
                        if ph == 0:
                            cp(fcar[:, jj, :], gpre[:, TP:TP + 2], R=[gp_tok], W=[fcar_tok])
                        else:
                            cp(fco[:, jj, 8, :], gpre[:, TP:TP + 2], R=[gp_tok], W=[fco_tok])
                        cp(fco[:, jj, 0:8, :], pv[:, :, 8:10], R=[gp_tok], W=[fco_tok])
                        wc = prm[:, l, P_FCW + jj * 3:P_FCW + jj * 3 + 3]
                        bc = prm[:, l, P_FCB + jj:P_FCB + jj + 1]
                        gcs = gc[:, TP:TPH].rearrange("p (s r) -> p s r", r=LS)
                        ts(gc[:, 0:TP], gpre[:, 0:TP], wc[:, 0:1], bc, ALU.mult, ALU.add, R=[gp_tok, prm_tok], W=[gc_tok])
                        ts(gcs, pv[:, :, 0:8], wc[:, 0:1], bc, ALU.mult, ALU.add, R=[gp_tok, prm_tok], W=[gc_tok])
                        for tap in range(1, 3):
                            stt(gc[:, 0:TP], gpre[:, tap:tap + TP], wc[:, tap:tap + 1], gc[:, 0:TP], ALU.mult, ALU.add,
                                R=[gp_tok, prm_tok], W=[gc_tok])
                            stt(gcs, pv[:, :, tap:tap + 8], wc[:, tap:tap + 1], gcs, ALU.mult, ALU.add, R=[gp_tok, prm_tok], W=[gc_tok])
                        act(gc[:], gc[:], AF.Silu, R=[gc_tok], W=[gc_tok])
                        for ti, (t0, n) in enumerate(TILES):
                            tt(aT[:, jj, t0:t0 + n], gc[:, t0:t0 + n], up[:, t0:t0 + n], ALU.mult, R=[gc_tok, up_tok], W=[a_toks[ti]])
                        if agen is not None:
                            for _ in range(3):
                                next(agen, None)
                    if agen is not None:
                        for _ in agen:
                            pass
                    K.dma("sp", fco_d[l, ph], fco[:].rearrange("p a s r -> p (a s r)"), R=[fco_tok], semtok=fco_tok)
                    K.snapshot()
                with contextlib.ExitStack() as S2:
                    y = sb(S2, "f_y", [128, KC, TPH], F32)
                    ytok = Tok()
                    wdn = Ring([(sb(S2, "f_wd%d" % i, [128, NFF, 128], BF16), retok(wdn_ptoks[i])) for i in range(2)])
                    sqt = (sb(S2, "f_sq", [128, KC, 256], BF16), Tok())
                    rs = (sb(S2, "f_rs", [128, 256], F32), Tok())
                    tmps = Ring([(sb(S2, "f_t%d" % i, [128, 256], F32), Tok()) for i in range(2)])
                    for oc in range(KC):
                        wt, wtok = wdn.get()
                        K.dma("pool", wt[:], ffn_w_down[l][:, oc * 128:(oc + 1) * 128].rearrange("(j p) n -> p j n", p=128), W=[wtok], semtok=wtok)
                        sl3 = [pwide.get() for _ in range(3)]
                        for jj in range(NFF):
                            for ti, (t0, n) in enumerate(TILES):
                                ps, ptok = sl3[ti]
                                mm(ps[:, 0:n], wt[:, jj, :], aT[:, jj, t0:t0 + n], start=(jj == 0), stop=(jj == NFF - 1),
                                   R=[wtok, a_toks[ti]], W=[ptok])
                        for ti, (t0, n) in enumerate(TILES):
                            ps, ptok = sl3[ti]
                            cp(y[:, oc, t0:t0 + n], ps[:, 0:n], R=[ptok], W=[ytok])
                    for ti, (t0, n) in enumerate([(0, 256), (256, 256), (512, 256), (768, 256), (1024, 64)]):
                        rstd_fm(y[:, :, t0:t0 + n], n, [ytok], sqt, rs)
                        for kc in range(KC):
                            tmp, ttok = tmps.get()
                            tt(tmp[:, 0:n], y[:, kc, t0:t0 + n], rs[0][:, 0:n], ALU.mult, R=[ytok, rs[1]], W=[ttok])
                            if t0 < TP:
                                stt(xT[:, kc, ph, t0:t0 + n], tmp[:, 0:n], G2[:, kc, 0:1], xT[:, kc, ph, t0:t0 + n], ALU.mult, ALU.add,
                                    R=[ttok, mod_tok], W=[xtok[ph]])
                            else:
                                sc = seqcols(ph)
                                v3 = tmp[:, 0:n].rearrange("p (s j) -> p s j", j=LS)
                                tt(v3, v3, G2[:, kc, sc].unsqueeze(2).broadcast_to([128, SPP, LS]), ALU.mult, R=[ttok, mod_tok], W=[ttok])
                                x3 = xT[:, kc, ph, t0:t0 + n].rearrange("p (s j) -> p s j", j=LS)
                                tt(x3, x3, v3, ALU.add, R=[ttok], W=[xtok[ph]])
                    K.snapshot()

        def main_program():
            for l in range(depth):
                modA = modsets[l % 2]
                with contextlib.ExitStack() as SL:
                    stage(1)
                    for _ in adaln_gen(l, SL):
                        pass
                    if l % 2 == 1:
                        jj_ = l // 2
                        if jj_ == 0:
                            memset(lbT[:, :, 0:1], 0.0, W=[lb_tok])
                            memset(lbT[:, :, 1:2], 1.0, W=[lb_tok])
                        else:
                            hl = prm[:, l, P_HLB:P_HLB + 16].rearrange("p (h t) -> p h t", t=2)
                            tt(lbT[:, :, 0:1], hl[:, :, 1:2], hl[:, :, 0:1], ALU.subtract, R=[prm_tok], W=[lb_tok])
                            act(lbT[:, :, 0:1], lbT[:, :, 0:1], AF.Sigmoid, R=[lb_tok], W=[lb_tok])
                            ts(lbT[:, :, 1:2], lbT[:, :, 0:1], -1.0, 1.0, ALU.mult, ALU.add, R=[lb_tok], W=[lb_tok])
                    K.snapshot()
                for ph in range(2):
                    with contextlib.ExitStack() as SA:
                        hT = sb(SA, "m_hT", [128, KC, TPH], BF16)
                        htoks = [Tok() for _ in range(3)]
                        oT = sb(SA, "m_oT", [128, KC, TPH], BF16)
                        otoks = [Tok() for _ in range(3)]
                        with contextlib.ExitStack() as SP:
                            stage(2)
                            prenorm(ph, modA[0], modA[1], hT, htoks, SP)
                            K.snapshot()
                        with contextlib.ExitStack() as SM:
                            if l % 2 == 0:
                                gdn_phase(l, ph, hT, htoks, oT, otoks, SM)
                            else:
                                hgrn_phase(l, ph, hT, htoks, oT, otoks, SM)
                            K.snapshot()
                        with contextlib.ExitStack() as SO:
                            stage(5)
                            Wout = gdn_w_out[l // 2] if l % 2 == 0 else hgrn_w_out[l // 2]
                            outproj_postnorm(l, ph, Wout, oT, otoks, modA[2], SO)
                            K.snapshot()
                    stage(6)
                    ffn_phase(l, ph, False)
            for ph in range(2):
                K.dma("sp", yT_d.rearrange("p (k h t) -> p k h t", k=KC, h=2)[:, :, ph, :], xT[:, :, ph, :], R=[xtok[ph]], semtok=xtok[ph])

        try:
            main_program()
        except StopBuild:
            K.barrier()
        K.final_wait("sp")

        with nc.Block() as block:
            K.replay(block)
    return nc


def _consts():
    c = np.zeros((128, NCST), np.float32)
    c[:, C_IDENT:C_IDENT + 128] = np.eye(128, dtype=np.float32)
    i = np.arange(128)[:, None]
    jx = np.arange(128)[None, :]
    BIG = 1.0e4
    valid = (i // 64 == jx // 64) & (i > jx)
    c[:, C_PENP:C_PENP + 128] = np.where(valid, 0.0, BIG)
    i8 = np.arange(64)[:, None]
    j8 = np.arange(64)[None, :]
    valid = (i8 // 8 == j8 // 8) & (i8 > j8)
    c[0:64, C_PENS:C_PENS + 64] = np.where(valid, 0.0, BIG)
    for ncb, off, bs in ((2, C_CM2, 128), (4, C_CM4, 128), (8, C_CM8, 64)):
        C = bs // ncb
        m = np.zeros((ncb, bs), np.float32)
        for cc in range(ncb):
            m[cc, cc * C:(cc + 1) * C] = 1.0
        c[:, off:off + ncb * bs] = m.reshape(1, -1)
    for ncb, off, bs in ((2, C_RM2, 128), (4, C_RM4, 128), (8, C_RM8, 64)):
        C = bs // ncb
        m = np.zeros((bs, ncb), np.float32)
        for cc in range(ncb):
            m[cc * C:(cc + 1) * C, cc] = 1.0
        c[0:bs, off:off + ncb] = m
    c[:, C_MIP:C_MIP + 128] = ((i // 32 == jx // 32) & (i <= jx)).astype(np.float32)
    c[0:64, C_MIS:C_MIS + 64] = ((i8 // 8 == j8 // 8) & (i8 <= j8)).astype(np.float32)
    e = np.zeros((8, 8, 128), np.float32)
    for h in range(8):
        e[h, h, :] = 1.0
    c[0:8, C_ESEL:C_ESEL + 1024] = e.reshape(8, 1024)
    return c


def _fm(v):
    sh = v.shape
    k = sh[-1] // 128
    v = v.reshape(sh[:-1] + (k, 128))
    return np.moveaxis(np.moveaxis(v, -1, 0), -1, 1)


_NC_CACHE = {}


def kernel(x_prompt, x_sample, state_gdn, state_gdn_conv, state_hgrn, state_ffn_conv, c_prompt, c_sample,
           ada_w, ada_b, norm_pre_mix, norm_post_mix, norm_pre_ffn, norm_post_ffn,
           gdn_w_in, gdn_conv_w, gdn_conv_b, gdn_a_log, gdn_dt_bias, gdn_norm, gdn_w_out,
           hgrn_lb, hgrn_w_in, hgrn_norm, hgrn_w_out,
           ffn_w_gu, ffn_conv_w, ffn_conv_b, ffn_w_down, _depth=DEPTH, _stage=None):
    f32 = np.float32
    A = lambda a: np.ascontiguousarray(np.asarray(a, dtype=f32))
    x_prompt, x_sample = A(x_prompt), A(x_sample)
    state_gdn, state_gdn_conv, state_hgrn, state_ffn_conv = A(state_gdn), A(state_gdn_conv), A(state_hgrn), A(state_ffn_conv)
    c_prompt, c_sample = A(c_prompt), A(c_sample)
    prm = np.zeros((128, 4, NPRM), f32)
    nrow = np.zeros((128, 4, 128), f32)
    ada_b_, gcw, gcb = A(ada_b), A(gdn_conv_w), A(gdn_conv_b)
    fcw, fcb = A(ffn_conv_w), A(ffn_conv_b)
    hlb = A(hgrn_lb)
    for l in range(4):
        prm[:, l, P_ADAB:P_ADAB + 48] = ada_b_[l].reshape(48, 128).T
        prm[:, l, P_NPRE_MIX:P_NPRE_MIX + 8] = A(norm_pre_mix)[l].reshape(8, 128).T
        prm[:, l, P_NPOST_MIX:P_NPOST_MIX + 8] = A(norm_post_mix)[l].reshape(8, 128).T
        prm[:, l, P_NPRE_FFN:P_NPRE_FFN + 8] = A(norm_pre_ffn)[l].reshape(8, 128).T
        prm[:, l, P_NPOST_FFN:P_NPOST_FFN + 8] = A(norm_post_ffn)[l].reshape(8, 128).T
        j = l // 2
        if l % 2 == 0:
            prm[:, l, P_GCW:P_GCW + 96] = gcw[j].reshape(4, 24, 128).transpose(2, 1, 0).reshape(128, 96)
            prm[:, l, P_GCB:P_GCB + 24] = gcb[j].reshape(24, 128).T
            prm[0:8, l, P_ALOG] = A(gdn_a_log)[j]
            prm[0:8, l, P_DTB] = A(gdn_dt_bias)[j]
            nrow[:, l, :] = A(gdn_norm)[j][None, :]
        else:
            nrow[:, l, :] = A(hgrn_norm)[j][None, :]
        prm[:, l, P_FCW:P_FCW + 66] = fcw[l].reshape(3, 22, 128).transpose(2, 1, 0).reshape(128, 66)
        prm[:, l, P_FCB:P_FCB + 22] = fcb[l].reshape(22, 128).T
        prm[:, l, P_HLB:P_HLB + 16] = hlb.reshape(2, 8, 128).transpose(2, 1, 0).reshape(128, 16)
    cst = _consts()
    shared = dict(prm=prm.reshape(128, -1), nrow=nrow.reshape(128, -1), cst=cst,
                  ada_w=A(ada_w), gdn_w_in=A(gdn_w_in), gdn_w_out=A(gdn_w_out), hgrn_w_in=A(hgrn_w_in),
                  hgrn_w_out=A(hgrn_w_out), ffn_w_gu=A(ffn_w_gu), ffn_w_down=A(ffn_w_down))
    in_maps = []
    for c in range(NCORE):
        xt = np.zeros((128, KC, 2, TPH), f32)
        xp = _fm(x_prompt[c])
        xs = _fm(x_sample[16 * c:16 * c + 16])
        for ph in range(2):
            xt[:, :, ph, 0:TP] = xp[:, :, TP * ph:TP * ph + TP]
            xt[:, :, ph, TP:] = xs[:, :, 8 * ph:8 * ph + 8, :].reshape(128, KC, TS)
        cc = np.concatenate([c_prompt[c:c + 1], c_sample[16 * c:16 * c + 16]], 0)
        cT = _fm(cc)
        gc = _fm(state_gdn_conv[:, 16 * c:16 * c + 16])
        gci = np.zeros((2, 2, 128, 24, 8, 3), f32)
        fc = _fm(state_ffn_conv[:, 16 * c:16 * c + 16])
        fci = np.zeros((4, 2, 128, NFF, 8, 2), f32)
        for ph in range(2):
            gci[:, ph] = gc[:, :, :, 8 * ph:8 * ph + 8, :].transpose(2, 0, 1, 3, 4)
            fci[:, ph] = fc[:, :, :, 8 * ph:8 * ph + 8, :].transpose(2, 0, 1, 3, 4)
        m = dict(shared)
        m.update(xT=xt.reshape(128, -1), cT=np.ascontiguousarray(cT).reshape(128, -1),
                 sgdn=np.ascontiguousarray(state_gdn[:, 16 * c:16 * c + 16]),
                 shgrn=np.ascontiguousarray(state_hgrn[:, 16 * c:16 * c + 16]),
                 gci=gci.reshape(2, 2, 128, -1), fci=fci.reshape(4, 2, 128, -1))
        in_maps.append(m)
    ck = (_depth, _stage)
    if ck not in _NC_CACHE:
        _STAGE_LIMIT[0] = _stage
        _STAGE_LIMIT[1] = False
        _NC_CACHE[ck] = build_nc(_depth)
        _STAGE_LIMIT[0] = None
        _STAGE_LIMIT[1] = False
    nc = _NC_CACHE[ck]
    res = run_bass_kernel_spmd(nc, in_maps, core_ids=list(range(NCORE)))
    R = res.results
    y_prompt = np.zeros((8, 2048, D), f32)
    y_sample = np.zeros((128, 8, D), f32)
    p_gdn = np.zeros((2, 8, H, 128, 128), f32)
    p_hgrn = np.zeros((2, 8, H, 128, 128), f32)
    s_gdn = np.zeros((2, 128, H, 128, 128), f32)
    s_hgrn = np.zeros((2, 128, H, 128, 128), f32)
    p_gconv = np.zeros((2, 8, 3, 3072), f32)
    s_gconv = np.zeros((2, 128, 3, 3072), f32)
    p_fconv = np.zeros((4, 8, 2, DFF), f32)
    s_fconv = np.zeros((4, 128, 2, DFF), f32)
    for c in range(NCORE):
        r = R[c]
        yt = r["yT"].reshape(128, KC, 2, TPH)
        for ph in range(2):
            y_prompt[c, TP * ph:TP * ph + TP] = yt[:, :, ph, 0:TP].transpose(2, 1, 0).reshape(TP, D)
            ys = yt[:, :, ph, TP:].reshape(128, KC, 8, 8)
            y_sample[16 * c + 8 * ph:16 * c + 8 * ph + 8] = ys.transpose(2, 3, 1, 0).reshape(8, 8, D)
        p_gdn[:, c] = r["pgdn"]
        p_hgrn[:, c] = r["phgrn"]
        s_gdn[:, 16 * c:16 * c + 16] = r["sgdn_o"]
        s_hgrn[:, 16 * c:16 * c + 16] = r["shgrn_o"]
        g = r["gco"].reshape(2, 2, 128, 24, 9, 3)
        f = r["fco"].reshape(4, 2, 128, NFF, 9, 2)
        for ph in range(2):
            gs = g[:, ph, :, :, 0:8, :].transpose(0, 3, 4, 2, 1).reshape(2, 8, 3, 3072)
            s_gconv[:, 16 * c + 8 * ph:16 * c + 8 * ph + 8] = gs
            fs = f[:, ph, :, :, 0:8, :].transpose(0, 3, 4, 2, 1).reshape(4, 8, 2, DFF)
            s_fconv[:, 16 * c + 8 * ph:16 * c + 8 * ph + 8] = fs
        p_gconv[:, c] = g[:, 1, :, :, 8, :].transpose(0, 3, 2, 1).reshape(2, 3, 3072)
        p_fconv[:, c] = f[:, 1, :, :, 8, :].transpose(0, 3, 2, 1).reshape(4, 2, DFF)
    return (y_prompt, y_sample, p_gdn, p_gconv, p_hgrn, p_fconv, s_gdn, s_gconv, s_hgrn, s_fconv)
```

```python
import contextlib
import numpy as np
import concourse.bass as bass
import concourse.mybir as mybir
from concourse.bass_utils import run_bass_kernel_spmd

F32 = mybir.dt.float32
BF16 = mybir.dt.bfloat16
AF = mybir.ActivationFunctionType
ALU = mybir.AluOpType

NCORE = 8
D = 1024
KC = 8
H = 8
DFF = 2816
NFF = 22
DEPTH = 4
TP = 1024
SPP = 8
LS = 8
TS = SPP * LS
TPH = TP + TS
TILES = [(0, 512), (512, 512), (1024, 64)]
EPS = 1e-6
NPRM = 48 + 32 + 96 + 24 + 66 + 22 + 16 + 2

P_ADAB = 0
P_NPRE_MIX = 48
P_NPOST_MIX = 56
P_NPRE_FFN = 64
P_NPOST_FFN = 72
P_GCW = 80
P_GCB = 176
P_FCW = 200
P_FCB = 266
P_HLB = 288
P_ALOG = 304
P_DTB = 305

C_IDENT = 0
C_PENP = 128
C_PENS = 256
C_CM2 = 320
C_CM4 = 576
C_CM8 = 1088
C_RM2 = 1600
C_RM4 = 1602
C_RM8 = 1606
C_MIP = 1614
C_MIS = 1742
C_ESEL = 1806
NCST = 1806 + 1024


_CARRY = [{}]


class Tok:
    __slots__ = ("w", "r", "dsem", "dval", "excl")

    def __init__(self, excl=False):
        self.w = {}
        self.r = dict(_CARRY[0])
        self.dsem = None
        self.dval = 0
        self.excl = excl


class Sched:
    ENGS = ("pe", "act", "dve", "pool", "sp")

    def __init__(self, nc, es):
        self.nc = nc
        self.es = es
        self.q = {k: [] for k in self.ENGS}
        self.n = {k: 0 for k in self.ENGS}
        self.seen = {k: {} for k in self.ENGS}
        self.sem = {}
        for k in ("pe", "act", "dve", "pool"):
            self.sem[k] = es.enter_context(nc.semaphore("s_" + k))
        self.dma_latest = {}
        self.nd = 0
        self.carry = {}

    def _waits(self, en, R, W):
        need = {}

        def add(ev):
            key, sem, val = ev
            if key not in need or need[key][1] < val:
                need[key] = (sem, val)

        for t in R:
            for ev in t.w.values():
                add(ev)
            if t.excl:
                for ev in t.r.values():
                    add(ev)
        for t in W:
            for ev in t.w.values():
                add(ev)
            for ev in t.r.values():
                add(ev)
        out = []
        seen = self.seen[en]
        for key, (sem, val) in need.items():
            if key == en and (en == "pe" or _NO_SAME_ENGINE_WAIT[0]):
                continue
            if seen.get(key, 0) >= val:
                continue
            seen[key] = val
            out.append((sem, val))
        return out

    def op(self, en, fn, R=(), W=()):
        if _STAGE_LIMIT[1]:
            return
        waits = self._waits(en, R, W)
        self.n[en] += 1
        sem = self.sem[en]
        self.q[en].append((waits, fn, sem, 1))
        ev = (en, sem, self.n[en])
        for t in W:
            t.w[en] = ev
        for t in R:
            t.r[en] = ev

    def dma(self, qn, out, in_, R=(), W=(), semtok=None):
        if _STAGE_LIMIT[1]:
            return
        waits = self._waits(qn, R, W)
        t = semtok
        if t.dsem is None:
            t.dsem = {}
        if qn not in t.dsem:
            self.nd += 1
            t.dsem[qn] = [self.es.enter_context(self.nc.semaphore("d%d" % self.nd)), 0]
        ent = t.dsem[qn]
        ent[1] += 16
        dsem, dval = ent[0], ent[1]
        key = "d%d_%s" % (id(t), qn)
        self.q[qn].append((waits, (lambda e: e.dma_start(out=out, in_=in_)), dsem, 16))
        ev = (key, dsem, dval)
        for x in W:
            x.w[key] = ev
        for x in R:
            x.r[key] = ev
        self.dma_latest[key] = (dsem, dval)

    def barrier(self, head=False):
        if head and _SKIP_HEAD_BARRIERS[0]:
            return
        for en in self.ENGS:
            waits = []
            seen = self.seen[en]
            for k in ("pe", "act", "dve", "pool"):
                if k == en and en == "pe":
                    continue
                v = self.n[k]
                if v > 0 and seen.get(k, 0) < v:
                    seen[k] = v
                    waits.append((self.sem[k], v))
            for key, (sem, val) in self.dma_latest.items():
                if seen.get(key, 0) < val:
                    seen[key] = val
                    waits.append((sem, val))
            if waits:
                self.q[en].append((waits, None, None, 0))

    def snapshot(self):
        snap = {}
        for k in ("pe", "act", "dve", "pool"):
            if self.n[k] > 0:
                snap[k] = (k, self.sem[k], self.n[k])
        for key, (sem, val) in self.dma_latest.items():
            snap[key] = (key, sem, val)
        self.carry = snap
        _CARRY[0] = snap

    def nsems(self):
        return self.nd + 4

    def final_wait(self, en="sp"):
        waits = []
        for key, (sem, val) in self.dma_latest.items():
            waits.append((sem, val))
        for k in ("pe", "act", "dve", "pool"):
            if self.n[k] > 0:
                waits.append((self.sem[k], self.n[k]))
        self.q[en].append((waits, None, None, 0))

    def replay(self, block):
        q = self.q

        def run(e, lst):
            for waits, fn, sem, inc in lst:
                if fn is None or not _INLINE_WAIT[0] or not waits:
                    for s, v in waits:
                        e.wait_ge(s, v)
                    if fn is not None:
                        fn(e).then_inc(sem, inc)
                else:
                    for s, v in waits[:-1]:
                        e.wait_ge(s, v)
                    s, v = waits[-1]
                    fn(e)._wait_ge(s, v).then_inc(sem, inc)

        @block.tensor
        def _(e):
            run(e, q["pe"])

        @block.scalar
        def _(e):
            run(e, q["act"])

        @block.vector
        def _(e):
            run(e, q["dve"])

        @block.gpsimd
        def _(e):
            run(e, q["pool"])

        @block.sync
        def _(e):
            run(e, q["sp"])


class StopBuild(Exception):
    pass


_STAGE_LIMIT = [None, False]
_NO_SAME_ENGINE_WAIT = [False]
_INLINE_WAIT = [True]
_FP32R = [False]
_SKIP_HEAD_BARRIERS = [False]


def stage(n):
    if _STAGE_LIMIT[0] is not None and n > _STAGE_LIMIT[0]:
        _STAGE_LIMIT[1] = True


class Ring:
    def __init__(self, items):
        self.items = items
        self.i = 0

    def get(self):
        r = self.items[self.i]
        self.i = (self.i + 1) % len(self.items)
        return r


def build_nc(depth=DEPTH):
    _CARRY[0] = {}
    nc = bass.Bass("TRN2", target_bir_lowering=False)

    def din(name, shape):
        return nc.dram_tensor(name, list(shape), F32, kind="ExternalInput").ap()

    def dout(name, shape):
        return nc.dram_tensor(name, list(shape), F32, kind="ExternalOutput").ap()

    xT_d = din("xT", [128, KC * 2 * TPH])
    cT_d = din("cT", [128, KC * 17])
    sgdn_d = din("sgdn", [2, 16, H, 128, 128])
    shgrn_d = din("shgrn", [2, 16, H, 128, 128])
    gci_d = din("gci", [2, 2, 128, 24 * 8 * 3])
    fci_d = din("fci", [4, 2, 128, NFF * 8 * 2])
    prm_d = din("prm", [128, 4 * NPRM])
    nrow_d = din("nrow", [128, 4 * 128])
    cst_d = din("cst", [128, NCST])
    ada_w = din("ada_w", [4, D, 6 * D])
    gdn_w_in = din("gdn_w_in", [2, D, 4112])
    gdn_w_out = din("gdn_w_out", [2, D, D])
    hgrn_w_in = din("hgrn_w_in", [2, D, 4096])
    hgrn_w_out = din("hgrn_w_out", [2, D, D])
    ffn_w_gu = din("ffn_w_gu", [4, D, 2 * DFF])
    ffn_w_down = din("ffn_w_down", [4, DFF, D])

    yT_d = dout("yT", [128, KC * 2 * TPH])
    pgdn_d = dout("pgdn", [2, H, 128, 128])
    sgdn_o = dout("sgdn_o", [2, 16, H, 128, 128])
    phgrn_d = dout("phgrn", [2, H, 128, 128])
    shgrn_o = dout("shgrn_o", [2, 16, H, 128, 128])
    gco_d = dout("gco", [2, 2, 128, 24 * 9 * 3])
    fco_d = dout("fco", [4, 2, 128, NFF * 9 * 2])

    with contextlib.ExitStack() as es:
        K = Sched(nc, es)

        cnt = [0]

        def T():
            t = Tok()
            t.r.update(K.carry)
            return t

        def retok(t):
            t.r.update(_CARRY[0])
            return t

        def sb(stack, name, shape, dt):
            cnt[0] += 1
            return stack.enter_context(nc.sbuf_tensor("sb%d_%s" % (cnt[0], name), list(shape), dt))

        xT = sb(es, "xT", [128, KC, 2, TPH], F32)
        xtok = [Tok(), Tok()]
        cst = sb(es, "cst", [128, NCST], F32)
        cst_tok = Tok()
        prm = sb(es, "prm", [128, 4, NPRM], F32)
        prm_tok = Tok()
        nrow = sb(es, "nrow", [128, 4, 128], F32)
        nrow_tok = Tok()
        ident_bf = sb(es, "ident_bf", [128, 128], BF16)
        ones_bf = sb(es, "ones_bf", [128, 128], BF16)
        cb = sb(es, "cb", [128, 8], F32)
        misc_tok = Tok()
        csT = sb(es, "csT", [128, KC, 17], BF16)
        cs_tok = Tok()
        _ms = [sb(es, "modA%d" % i, [128, KC, 17], F32) for i in range(6)]
        modsets = [_ms, _ms]
        mod_tok = Tok()
        Sf_all = sb(es, "Sf_all", [128, H, 128], F32)
        Sf_tok = [Tok() for _ in range(H)]
        gcar = sb(es, "gcar", [128, 24, 3], F32)
        gcar_tok = Tok()
        fcar = sb(es, "fcar", [128, NFF, 2], F32)
        fcar_tok = Tok()
        lbT = sb(es, "lbT", [128, H, 2], F32)
        lb_tok = Tok()
        wun = [(sb(es, "wun%d" % i, [128, KC, 128], BF16), Tok()) for i in range(5)]
        wring = Ring(wun)
        wres_toks = [Tok() for _ in range(KC)]
        Ss_ptok, gci_ptok, gco_ptok, fci_ptok, fco_ptok = Tok(), Tok(), Tok(), Tok(), Tok()
        wdn_ptoks = [Tok(), Tok()]

        psF = [es.enter_context(nc.psum_tensor("psF%d" % i, [128, 512], F32)) for i in range(7)]
        psB = es.enter_context(nc.psum_tensor("psB", [128, 1024], BF16))
        bkt = [Tok(excl=True) for _ in range(8)]
        pbig = Ring([(psF[0], bkt[0]), (psF[1], bkt[1])])
        psmall = Ring([(psF[2 + i % 3][:, (i // 3) * 128:(i // 3) * 128 + 128], bkt[2 + i % 3]) for i in range(12)])
        pbf = Ring([(psB[:, i * 128:(i + 1) * 128], bkt[7]) for i in range(8)])
        o_ps, o_ptok = psF[5], bkt[5]
        u_ps, u_ptok = psF[6], bkt[6]
        pwide = Ring([(psF[i], bkt[i]) for i in range(7)])

        def mm(out, lhsT, rhs, start=True, stop=True, R=(), W=()):
            K.op("pe", lambda e: e.matmul(out, lhsT, rhs, start=start, stop=stop), R, W)

        def mmr(out, lhsT, rhs, R=(), W=()):
            if _FP32R[0] and lhsT.shape[-1] == 128:
                F32R = mybir.dt.float32r
                K.op("pe", lambda e: e.matmul(out, lhsT.bitcast(F32R), rhs.bitcast(F32R), start=True, stop=True), R, W)
            else:
                K.op("pe", lambda e: e.matmul(out, lhsT, rhs, start=True, stop=True), R, W)

        def tr(out, in_, idn, R=(), W=()):
            K.op("pe", lambda e: e.transpose(out, in_, idn), R, W)

        def act(out, in_, func, bias=None, scale=None, accum=None, R=(), W=()):
            kw = {}
            if bias is not None:
                kw["bias"] = bias
            if scale is not None:
                kw["scale"] = scale
            if accum is not None:
                kw["accum_out"] = accum
            K.op("act", lambda e: e.activation(out=out, in_=in_, func=func, **kw), R, W)

        def tt(out, a, b, op, R=(), W=(), en="dve"):
            K.op(en, lambda e: e.tensor_tensor(out=out, in0=a, in1=b, op=op), R, W)

        def ts(out, a, s1, s2, op0, op1=None, R=(), W=(), en="dve"):
            if op1 is None:
                K.op(en, lambda e: e.tensor_scalar(out=out, in0=a, scalar1=s1, scalar2=None, op0=op0), R, W)
            else:
                K.op(en, lambda e: e.tensor_scalar(out=out, in0=a, scalar1=s1, scalar2=s2, op0=op0, op1=op1), R, W)

        def stt(out, a, sc, b, op0, op1, R=(), W=(), en="dve"):
            K.op(en, lambda e: e.scalar_tensor_tensor(out=out, in0=a, scalar=sc, in1=b, op0=op0, op1=op1), R, W)

        def cp(out, in_, R=(), W=(), en="dve"):
            K.op(en, lambda e: e.tensor_copy(out=out, in_=in_), R, W)

        def recip(out, in_, R=(), W=()):
            K.op("dve", lambda e: e.reciprocal(out=out, in_=in_), R, W)

        def memset(ap, val, W=(), en="dve"):
            K.op(en, lambda e: e.memset(ap, val), (), W)

        def wload(W2d, c0, ncols=128):
            t, tok = wring.get()
            K.dma("pool", t[:, :, 0:ncols], W2d[:, c0:c0 + ncols].rearrange("(k p) n -> p k n", p=128), W=[tok], semtok=tok)
            return t, tok

        K.dma("sp", cst[:], cst_d[:, :], W=[cst_tok], semtok=cst_tok)
        K.dma("sp", prm[:].rearrange("p l n -> p (l n)"), prm_d[:, :], W=[prm_tok], semtok=prm_tok)
        K.dma("sp", nrow[:].rearrange("p l n -> p (l n)"), nrow_d[:, :], W=[nrow_tok], semtok=nrow_tok)
        for ph in range(2):
            K.dma("sp", xT[:, :, ph, :], xT_d.rearrange("p (k h t) -> p k h t", k=KC, h=2)[:, :, ph, :], W=[xtok[ph]], semtok=xtok[ph])
        ident = cst[:, C_IDENT:C_IDENT + 128]
        cp(ident_bf[:], ident, R=[cst_tok], W=[misc_tok])
        memset(ones_bf[:], 1.0, W=[misc_tok])
        memset(cb[:, 0:1], 1024.0 * EPS, W=[misc_tok])
        memset(cb[:, 1:2], EPS, W=[misc_tok])
        memset(cb[:, 2:3], 128.0 * EPS, W=[misc_tok])
        memset(cb[:, 3:4], 1.0, W=[misc_tok])
        memset(cb[:, 4:5], 0.0, W=[misc_tok])
        ts(nrow[:], nrow[:], float(np.sqrt(128.0)), None, ALU.mult, R=[nrow_tok], W=[nrow_tok])
        for l in range(4):
            ts(prm[:, l, P_NPRE_MIX:P_NPRE_MIX + 32], prm[:, l, P_NPRE_MIX:P_NPRE_MIX + 32], 32.0, None, ALU.mult, R=[prm_tok], W=[prm_tok])
        with contextlib.ExitStack() as s0:
            cTt = sb(s0, "cTt", [128, KC, 17], F32)
            ctok = Tok()
            K.dma("sp", cTt[:].rearrange("p k q -> p (k q)"), cT_d[:, :], W=[ctok], semtok=ctok)
            act(csT[:], cTt[:], AF.Silu, R=[ctok], W=[cs_tok])
            K.snapshot()

        def adaln_gen(l, S):
            modA = modsets[l % 2]
            mod = sb(S, "modraw", [128, 48, 17], F32)
            mtok = Tok()
            slots = [(psF[5], bkt[5]), (psF[6], bkt[6])]
            for cc in range(48):
                wt, wtok = wload(ada_w[l], cc * 128)
                ps, ptok = slots[cc // 24]
                col = (cc % 24) * 17
                for kc in range(KC):
                    mm(ps[:, col:col + 17], wt[:, kc, :], csT[:, kc, :], start=(kc == 0), stop=(kc == KC - 1),
                       R=[wtok, cs_tok], W=[ptok])
                yield
            for bk in range(2):
                ps, ptok = slots[bk]
                tt(mod[:, 24 * bk:24 * bk + 24, :], ps[:, 0:408].rearrange("p (c q) -> p c q", q=17),
                   prm[:, l, P_ADAB + 24 * bk:P_ADAB + 24 * bk + 24].unsqueeze(2).broadcast_to([128, 24, 17]),
                   ALU.add, R=[ptok, prm_tok], W=[mtok])
            def nb(off):
                return prm[:, l, off:off + 8].unsqueeze(2).broadcast_to([128, 8, 17])
            stt(modA[0][:], mod[:, 8:16, :], 1.0, nb(P_NPRE_MIX), ALU.add, ALU.mult, R=[mtok, prm_tok], W=[mod_tok])
            cp(modA[1][:], mod[:, 0:8, :], R=[mtok], W=[mod_tok])
            stt(modA[2][:], mod[:, 16:24, :], 1.0, nb(P_NPOST_MIX), ALU.add, ALU.mult, R=[mtok, prm_tok], W=[mod_tok])
            stt(modA[3][:], mod[:, 32:40, :], 1.0, nb(P_NPRE_FFN), ALU.add, ALU.mult, R=[mtok, prm_tok], W=[mod_tok])
            cp(modA[4][:], mod[:, 24:32, :], R=[mtok], W=[mod_tok])
            stt(modA[5][:], mod[:, 40:48, :], 1.0, nb(P_NPOST_FFN), ALU.add, ALU.mult, R=[mtok, prm_tok], W=[mod_tok])

        def seqcols(ph):
            return slice(1 + SPP * ph, 1 + SPP * ph + SPP)

        def rstd_fm(src3, n, R, sqt, rs):
            sq, sqtok = sqt
            rst, rstok = rs
            act(sq[:, :, 0:n], src3, AF.Square, R=R, W=[sqtok])
            ps, ptok = pbig.get()
            for kc in range(KC):
                mm(ps[:, 0:n], ones_bf[:], sq[:, kc, 0:n], start=(kc == 0), stop=(kc == KC - 1), R=[sqtok, misc_tok], W=[ptok])
            act(rst[:, 0:n], ps[:, 0:n], AF.Ln, bias=cb[:, 0:1], scale=1.0, R=[ptok, misc_tok], W=[rstok])
            act(rst[:, 0:n], rst[:, 0:n], AF.Exp, scale=-0.5, R=[rstok], W=[rstok])

        def prenorm(ph, A, B, hT, htoks, S):
            sqt_r = Ring([(sb(S, "pn_sq%d" % i, [128, KC, 512], BF16), Tok()) for i in range(2)])
            rs_r = Ring([(sb(S, "pn_rs%d" % i, [128, 512], F32), Tok()) for i in range(2)])
            tmps = Ring([(sb(S, "pn_t%d" % i, [128, 512], F32), Tok()) for i in range(3)])
            for ti, (t0, n) in enumerate(TILES):
                sqt = sqt_r.get()
                rs = rs_r.get()
                rstd_fm(xT[:, :, ph, t0:t0 + n], n, [xtok[ph]], sqt, rs)
                for kc in range(KC):
                    tmp, ttok = tmps.get()
                    tt(tmp[:, 0:n], xT[:, kc, ph, t0:t0 + n], rs[0][:, 0:n], ALU.mult, R=[xtok[ph], rs[1]], W=[ttok])
                    if ti < 2:
                        act(hT[:, kc, t0:t0 + n], tmp[:, 0:n], AF.Identity, bias=B[:, kc, 0:1], scale=A[:, kc, 0:1],
                            R=[ttok, mod_tok], W=[htoks[ti]])
                    else:
                        sc = seqcols(ph)
                        tt(tmp[:, 0:n].rearrange("p (s j) -> p s j", j=LS), tmp[:, 0:n].rearrange("p (s j) -> p s j", j=LS),
                           A[:, kc, sc].unsqueeze(2).broadcast_to([128, SPP, LS]), ALU.mult, R=[ttok, mod_tok], W=[ttok])
                        tt(hT[:, kc, t0:t0 + n].rearrange("p (s j) -> p s j", j=LS), tmp[:, 0:n].rearrange("p (s j) -> p s j", j=LS),
                           B[:, kc, sc].unsqueeze(2).broadcast_to([128, SPP, LS]), ALU.add, R=[ttok, mod_tok], W=[htoks[ti]])

        def proj_tile(wt, wtok, hT, htoks, ti, M=128, ring=None):
            t0, n = TILES[ti]
            ps, ptok = (ring or pbig).get()
            for kc in range(KC):
                mm(ps[0:M, 0:n], wt[:, kc, 0:M], hT[:, kc, t0:t0 + n], start=(kc == 0), stop=(kc == KC - 1),
                   R=[wtok, htoks[ti]], W=[ptok])
            return ps, ptok, n

        def proj_all(wt, wtok, hT, htoks, ring=None, M=128):
            sl3 = [(ring or pwide).get() for _ in range(3)]
            for kc in range(KC):
                for ti, (t0, n) in enumerate(TILES):
                    ps, ptok = sl3[ti]
                    mm(ps[0:M, 0:n], wt[:, kc, 0:M], hT[:, kc, t0:t0 + n], start=(kc == 0), stop=(kc == KC - 1),
                       R=[wtok, htoks[ti]], W=[ptok])
            return [(sl3[ti][0], sl3[ti][1], TILES[ti][1]) for ti in range(3)]

        def outproj_postnorm(l, ph, Wout2d, oT, otoks, G, S):
            y = sb(S, "op_y", [128, KC, 512], F32)
            ytok = Tok()
            sqt = (sb(S, "op_sq", [128, KC, 512], BF16), Tok())
            rs = (sb(S, "op_rs", [128, 512], F32), Tok())
            tmps = Ring([(sb(S, "op_t%d" % i, [128, 512], F32), Tok()) for i in range(2)])
            wres = sb(S, "op_w", [128, KC, KC, 128], BF16)
            wrtok = [retok(t) for t in wres_toks]
            for oc in range(KC):
                K.dma("pool", wres[:, oc, :, :], Wout2d[:, oc * 128:(oc + 1) * 128].rearrange("(k p) n -> p k n", p=128), W=[wrtok[oc]], semtok=wrtok[oc])
            for ti, (t0, n) in enumerate(TILES):
                for oc in range(KC):
                    ps, ptok = pwide.get()
                    for kc in range(KC):
                        mm(ps[:, 0:n], wres[:, oc, kc, :], oT[:, kc, t0:t0 + n], start=(kc == 0), stop=(kc == KC - 1),
                           R=[wrtok[oc], otoks[ti]], W=[ptok])
                    cp(y[:, oc, 0:n], ps[:, 0:n], R=[ptok], W=[ytok])
                resid_update(ph, t0, n, y, ytok, G, sqt, rs, tmps)

        def resid_update(ph, t0, n, y, ytok, G, sqt, rs, tmps):
            rstd_fm(y[:, :, 0:n], n, [ytok], sqt, rs)
            for kc in range(KC):
                tmp, ttok = tmps.get()
                tt(tmp[:, 0:n], y[:, kc, 0:n], rs[0][:, 0:n], ALU.mult, R=[ytok, rs[1]], W=[ttok])
                if t0 < TP:
                    stt(xT[:, kc, ph, t0:t0 + n], tmp[:, 0:n], G[:, kc, 0:1], xT[:, kc, ph, t0:t0 + n], ALU.mult, ALU.add,
                        R=[ttok, mod_tok], W=[xtok[ph]])
                else:
                    sc = seqcols(ph)
                    v3 = tmp[:, 0:n].rearrange("p (s j) -> p s j", j=LS)
                    tt(v3, v3, G[:, kc, sc].unsqueeze(2).broadcast_to([128, SPP, LS]), ALU.mult, R=[ttok, mod_tok], W=[ttok])
                    x3 = xT[:, kc, ph, t0:t0 + n].rearrange("p (s j) -> p s j", j=LS)
                    tt(x3, x3, v3, ALU.add, R=[ttok], W=[xtok[ph]])

        def blocks():
            return [(b * 128, 128, "P") for b in range(8)] + [(TP, 64, "S")]

        def o_post(l, o_rows, BS, h, c0, oT, otok_t, sgT, sgtok, tmpf, tmpb):
            o_ps_, o_ptok_ = o_rows if o_rows is not None else (o_ps, o_ptok)
            junk, jtok = tmpf.get()
            ss, sstok = tmpf.get()
            act(junk[0:BS, :], o_ps_[0:BS, 0:128], AF.Square, accum=ss[0:BS, 0:1], R=[o_ptok_], W=[jtok, sstok])
            act(ss[0:BS, 0:1], ss[0:BS, 0:1], AF.Ln, bias=cb[0:BS, 2:3], scale=1.0, R=[sstok, misc_tok], W=[sstok])
            act(ss[0:BS, 0:1], ss[0:BS, 0:1], AF.Exp, scale=-0.5, R=[sstok], W=[sstok])
            on, ontok = tmpb.get()
            stt(on[0:BS, :], o_ps_[0:BS, 0:128], ss[0:BS, 0:1], nrow[0:BS, l, :], ALU.mult, ALU.mult,
                R=[o_ptok_, sstok, nrow_tok], W=[ontok])
            pb, pbtok = pbf.get()
            tr(pb[:, 0:BS], on[0:BS, :], ident_bf[0:BS, 0:BS], R=[ontok, misc_tok], W=[pbtok])
            tt(oT[:, h, c0:c0 + BS], pb[:, 0:BS], sgT[:, c0:c0 + BS], ALU.mult, R=[pbtok, sgtok], W=[otok_t])

        def gdn_phase(l, ph, hT, htoks, oT, otoks, S):
            j = l // 2
            Win = gdn_w_in[j]
            stage(3)
            G = 4
            dG = sb(S, "g_dG", [8, TPH], F32)
            dtok_ = Tok()
            dtk = sb(S, "g_dtk", [128, 9, 3, 8], F32)
            ex = sb(S, "g_ex", [128, 9, 4, 8], F32)
            glb = sb(S, "g_glb", [128, H, 24], F32)
            dk_tok = Tok()
            gci = sb(S, "g_gci", [128, 24, 8, 3], F32)
            gci_tok = retok(gci_ptok)
            gco = sb(S, "g_gco", [128, 24, 9, 3], F32)
            gco_tok = retok(gco_ptok)
            K.dma("sp", gci[:].rearrange("p a s r -> p (a s r)"), gci_d[j, ph], W=[gci_tok], semtok=gci_tok)
            memset(gco[:], 0.0, W=[gco_tok])
            with contextlib.ExitStack() as SD:
                dB = sb(SD, "g_dB", [8, TPH], F32)
                dL = sb(SD, "g_dL", [8, TPH], F32)
                dg = sb(SD, "g_dg", [8, TPH], F32)
                rmask = dL
                nega = sb(SD, "g_nega", [8, 1], F32)
                memset(rmask[:], 1.0, W=[dtok_])
                memset(rmask[:, 0:TP].rearrange("p (c t) -> p c t", t=64)[:, :, 0:1], 0.0, W=[dtok_])
                memset(rmask[:, TP:TPH].rearrange("p (c t) -> p c t", t=8)[:, :, 0:1], 0.0, W=[dtok_])
                act(nega[:], prm[0:8, l, P_ALOG:P_ALOG + 1], AF.Exp, R=[prm_tok], W=[dtok_])
                ts(nega[:], nega[:], -1.0, None, ALU.mult, R=[dtok_], W=[dtok_])
                wb, wbtok = wload(Win, 4096, 8)
                for ti in range(3):
                    ps, ptok, n = proj_tile(wb, wbtok, hT, htoks, ti, M=8)
                    t0 = TILES[ti][0]
                    act(dB[:, t0:t0 + n], ps[0:8, 0:n], AF.Sigmoid, R=[ptok], W=[dtok_])
                wa, watok = wload(Win, 4104, 8)
                for ti in range(3):
                    ps, ptok, n = proj_tile(wa, watok, hT, htoks, ti, M=8)
                    t0 = TILES[ti][0]
                    act(dg[:, t0:t0 + n], ps[0:8, 0:n], AF.Exp, bias=prm[0:8, l, P_DTB:P_DTB + 1], scale=1.0, R=[ptok, prm_tok], W=[dtok_])
                act(dg[:], dg[:], AF.Ln, bias=cb[0:8, 3:4], scale=1.0, R=[dtok_, misc_tok], W=[dtok_])
                ts(dg[:], dg[:], nega[:, 0:1], None, ALU.mult, R=[dtok_], W=[dtok_])
                K.op("dve", lambda e, dG=dG, rmask=rmask, dg=dg: e.tensor_tensor_scan(out=dG[:], data0=rmask[:], data1=dg[:], initial=0.0, op0=ALU.mult, op1=ALU.add),
                     [dtok_], [dtok_])
                gp = dG[:, 0:TP].rearrange("p (c t) -> p c t", t=64)
                tt(dL[:, 0:TP].rearrange("p (c t) -> p c t", t=64), gp[:, :, 63:64].broadcast_to([8, 16, 64]), gp, ALU.subtract, R=[dtok_], W=[dtok_])
                gs = dG[:, TP:TPH].rearrange("p (c t) -> p c t", t=8)
                tt(dL[:, TP:TPH].rearrange("p (c t) -> p c t", t=8), gs[:, :, 7:8].broadcast_to([8, 8, 8]), gs, ALU.subtract, R=[dtok_], W=[dtok_])
                ps, ptok = pbig.get()
                for bi, (c0, BS, kind) in enumerate(blocks()):
                    for qi, src in enumerate((dB, dG, dL)):
                        col = (bi * 3 + qi) * 8
                        tr(ps[0:BS, col:col + 8], src[:, c0:c0 + BS], cst[0:8, C_IDENT:C_IDENT + 8], R=[dtok_, cst_tok], W=[ptok])
                memset(dtk[:], 0.0, W=[dk_tok])
                cp(dtk[:, 0:8].rearrange("p b q h -> p (b q h)"), ps[:, 0:192], R=[ptok], W=[dk_tok])
                cp(dtk[0:64, 8].rearrange("p q h -> p (q h)"), ps[0:64, 192:216], R=[ptok], W=[dk_tok])
                act(ex[:, :, 0:2, :], dtk[:, :, 1:3, :], AF.Exp, R=[dk_tok], W=[dk_tok])
                tt(ex[:, :, 2, :], dtk[:, :, 0, :], ex[:, :, 0, :], ALU.mult, R=[dk_tok], W=[dk_tok])
                ts(ex[:, :, 3, :], dtk[:, :, 0, :], -1.0, None, ALU.mult, R=[dk_tok], W=[dk_tok])
                ps, ptok = pbig.get()
                for h in range(H):
                    esel_h = cst[0:8, C_ESEL + h * 128:C_ESEL + (h + 1) * 128]
                    mm(ps[:, h * 24:h * 24 + 16], esel_h, dG[:, 0:TP].rearrange("p (c t) -> p c t", t=64)[:, :, 63], R=[dtok_, cst_tok], W=[ptok])
                    mm(ps[:, h * 24 + 16:h * 24 + 24], esel_h, dG[:, TP:TPH].rearrange("p (c t) -> p c t", t=8)[:, :, 7], R=[dtok_, cst_tok], W=[ptok])
                act(glb[:].rearrange("p h c -> p (h c)"), ps[:, 0:192], AF.Exp, R=[ptok], W=[dk_tok])
                K.snapshot()

            with contextlib.ExitStack() as S1:
                qT = sb(S1, "g_qT", [128, TPH], BF16)
                kT = sb(S1, "g_kT", [128, TPH], BF16)
                vT = sb(S1, "g_vT", [128, TPH], BF16)
                sgT = sb(S1, "g_sgT", [128, TPH], BF16)
                q_tok, k_tok, v_tok, sg_tok = Tok(), Tok(), Tok(), Tok()
                Sbf = sb(S1, "g_Sbf", [128, 128], BF16)
                Sbf_tok = Tok()
                Ss = sb(S1, "g_Ss", [128, SPP, 128], F32)
                Ss_tok = retok(Ss_ptok)
                Ssb = sb(S1, "g_Ssb", [128, SPP, 128], BF16)
                Ssb_tok = Tok()
                Sn, Sn_tok = Ss, Ss_tok
                u_sb = sb(S1, "g_usb", [128, 128], BF16)
                usb_tok = Tok()
                memset(u_sb[:], 0.0, W=[usb_tok])
                psblk = Ring([(psF[i % 4][:, (i // 4) * 128:(i // 4) * 128 + 128], bkt[i % 4]) for i in range(16)])
                obank = Ring([(psF[5], bkt[5]), (psF[4], bkt[4])])

                for h in range(H):
                    stage(4 + 0.1 * h)
                    with contextlib.ExitStack() as SPJ:
                        pre_r = Ring([(sb(SPJ, "g_pre%d" % i, [128, 3 + TP + SPP * 11], F32), T()) for i in range(2)])
                        cv_r = Ring([(sb(SPJ, "g_cv%d" % i, [128, TPH], F32), T()) for i in range(2)])
                        sq_r = Ring([(sb(SPJ, "g_sqb%d" % i, [128, 512], BF16), T()) for i in range(2)])
                        ri_r = Ring([(sb(SPJ, "g_rinv%d" % i, [128, 512], F32), T()) for i in range(2)])
                        for role in range(4):
                            wt, wtok = wload(Win, role * 1024 + h * 128)
                            if role < 3:
                                pre, pre_tok = pre_r.get()
                                cv, cv_tok = cv_r.get()
                                pv = pre[:, 3 + TP:3 + TP + SPP * 11].rearrange("p (s r) -> p s r", r=11)
                                ch = role * 8 + h
                                if ph == 0:
                                    memset(pre[:, 0:3], 0.0, W=[pre_tok])
                                else:
                                    cp(pre[:, 0:3], gcar[:, ch, :], R=[gcar_tok], W=[pre_tok])
                                cp(pv[:, :, 0:3], gci[:, ch, :, :], R=[gci_tok], W=[pre_tok])
                            pall = proj_all(wt, wtok, hT, htoks)
                            for ti in range(3):
                                ps, ptok, n = pall[ti]
                                t0 = TILES[ti][0]
                                if role == 3:
                                    act(sgT[:, t0:t0 + n], ps[:, 0:n], AF.Silu, R=[ptok], W=[sg_tok])
                                elif ti < 2:
                                    act(pre[:, 3 + t0:3 + t0 + n], ps[:, 0:n], AF.Copy, R=[ptok], W=[pre_tok])
                                else:
                                    act(pv[:, :, 3:11], ps[:, 0:n].rearrange("p (s r) -> p s r", r=LS), AF.Copy, R=[ptok], W=[pre_tok])
                            if role == 3:
                                continue
                            if ph == 0:
                                cp(gcar[:, ch, :], pre[:, TP:TP + 3], R=[pre_tok], W=[gcar_tok])
                            else:
                                cp(gco[:, ch, 8, :], pre[:, TP:TP + 3], R=[pre_tok], W=[gco_tok])
                            cp(gco[:, ch, 0:8, :], pv[:, :, 8:11], R=[pre_tok], W=[gco_tok])
                            wc = prm[:, l, P_GCW + ch * 4:P_GCW + ch * 4 + 4]
                            bc = prm[:, l, P_GCB + ch:P_GCB + ch + 1]
                            cvs = cv[:, TP:TPH].rearrange("p (s r) -> p s r", r=LS)
                            ts(cv[:, 0:TP], pre[:, 0:TP], wc[:, 0:1], bc, ALU.mult, ALU.add, R=[pre_tok, prm_tok], W=[cv_tok])
                            ts(cvs, pv[:, :, 0:8], wc[:, 0:1], bc, ALU.mult, ALU.add, R=[pre_tok, prm_tok], W=[cv_tok])
                            for tap in range(1, 4):
                                stt(cv[:, 0:TP], pre[:, tap:tap + TP], wc[:, tap:tap + 1], cv[:, 0:TP], ALU.mult, ALU.add,
                                    R=[pre_tok, prm_tok], W=[cv_tok])
                                stt(cvs, pv[:, :, tap:tap + 8], wc[:, tap:tap + 1], cvs, ALU.mult, ALU.add, R=[pre_tok, prm_tok], W=[cv_tok])
                            if role == 2:
                                act(vT[:], cv[:], AF.Silu, R=[cv_tok], W=[v_tok])
                                continue
                            act(cv[:], cv[:], AF.Silu, R=[cv_tok], W=[cv_tok])
                            for ti, (t0, n) in enumerate(TILES):
                                sqb, sqb_tok = sq_r.get()
                                rinv, rinv_tok = ri_r.get()
                                act(sqb[:, 0:n], cv[:, t0:t0 + n], AF.Square, R=[cv_tok], W=[sqb_tok])
                                ps, ptok = pwide.get()
                                mm(ps[:, 0:n], ones_bf[:], sqb[:, 0:n], R=[sqb_tok, misc_tok], W=[ptok])
                                act(rinv[:, 0:n], ps[:, 0:n], AF.Ln, bias=cb[:, 1:2], scale=1.0, R=[ptok, misc_tok], W=[rinv_tok])
                                act(rinv[:, 0:n], rinv[:, 0:n], AF.Exp, scale=-0.5, R=[rinv_tok], W=[rinv_tok])
                                if role == 0:
                                    stt(qT[:, t0:t0 + n], cv[:, t0:t0 + n], float(128.0 ** -0.5), rinv[:, 0:n], ALU.mult, ALU.mult,
                                        R=[cv_tok, rinv_tok], W=[q_tok])
                                else:
                                    tt(kT[:, t0:t0 + n], cv[:, t0:t0 + n], rinv[:, 0:n], ALU.mult, R=[cv_tok, rinv_tok], W=[k_tok])
                        K.snapshot()

                    stage(4.05 + 0.1 * h)
                    K.dma("pool", Ss[:], sgdn_d[j, SPP * ph:SPP * ph + SPP, h].rearrange("s k v -> k s v"), W=[Ss_tok], semtok=Ss_tok)
                    act(Ssb[:], Ss[:], AF.Copy, R=[Ss_tok], W=[Ssb_tok])
                    if ph == 0:
                        memset(Sf_all[:, h, :], 0.0, W=[Sf_tok[h]])
                    cp(Sbf[:], Sf_all[:, h, :], R=[Sf_tok[h]], W=[Sbf_tok])

                    with contextlib.ExitStack() as SBK:
                        slots = []
                        for k in range(G):
                            sl = {}
                            sl["X"] = [(sb(SBK, "g_x%d_%d" % (k, i), [128, 128], F32), T()) for i in range(6)]
                            sl["kbg"] = (sb(SBK, "g_kbg%d" % k, [128, 128], BF16), T())
                            sl["HS"] = [[(sb(SBK, "g_hs%d_%d_%d" % (k, par, i), [128, 128], BF16), T()) for i in range(6)] for par in range(2)]
                            slots.append(sl)
                        spad = {}
                        for nm in ("Kpad", "Wpad", "Qpad", "tw"):
                            spad[nm] = (sb(SBK, "g_s%s" % nm, [128, 8, 128], BF16), T())
                        spad["kdm"] = (sb(SBK, "g_skdm", [128, 8], F32), T())
                        opf = Ring([(sb(SBK, "g_of%d" % i, [128, 128], F32), T()) for i in range(4)])
                        opb = Ring([(sb(SBK, "g_ob%d" % i, [128, 128], BF16), T()) for i in range(2)])

                        def pre_state(bi, c0, BS, kind, sl, par):
                            if kind == "P":
                                ncb, C, nlev = 2, 64, 5
                                pen = cst[:, C_PENP:C_PENP + 128]
                            else:
                                ncb, C, nlev = 8, 8, 2
                                pen = cst[0:64, C_PENS:C_PENS + 64]
                                cmask = cst[:, C_CM8:C_CM8 + 512].rearrange("p (c t) -> p c t", c=8)
                                rmk = cst[0:64, C_RM8:C_RM8 + 8]
                            cols = slice(c0, c0 + BS)
                            X = sl["X"]
                            HS = sl["HS"][par]
                            Gcol = dtk[0:BS, bi, 1, h:h + 1]
                            beta_c = dtk[0:BS, bi, 0, h:h + 1]
                            eGL_c = ex[0:BS, bi, 1, h:h + 1]
                            bG_c = ex[0:BS, bi, 2, h:h + 1]
                            nbeta_c = ex[0:BS, bi, 3, h:h + 1]
                            esel_h = cst[0:8, C_ESEL + h * 128:C_ESEL + (h + 1) * 128]
                            gps, gptok = psblk.get()
                            mm(gps[:, 0:BS], esel_h, dG[:, cols], R=[dtok_, cst_tok], W=[gptok])
                            yield
                            r_, rtok = X[0]
                            stt(r_[0:BS, 0:BS], gps[0:BS, 0:BS], Gcol, pen, ALU.subtract, ALU.max, R=[gptok, dk_tok, cst_tok], W=[rtok])
                            eGr, egtok = X[2]
                            act(eGr[:, 0:BS], gps[:, 0:BS], AF.Exp, R=[gptok], W=[egtok])
                            yield
                            Ds, dstok = X[1]
                            act(Ds[0:BS, 0:BS], r_[0:BS, 0:BS], AF.Exp, scale=-1.0, R=[rtok], W=[dstok])
                            if kind == "P":
                                qg, qgtok = HS[3]
                                tt(qg[:, 0:BS], qT[:, cols], eGr[:, 0:BS], ALU.mult, R=[q_tok, egtok], W=[qgtok])
                            else:
                                tw, twtok = spad["tw"]
                                Qpad, qp_tok = spad["Qpad"]
                                tt(tw[:, 0:ncb, 0:BS], cmask[:, :, 0:BS], eGr[:, 0:BS].unsqueeze(1).broadcast_to([128, ncb, BS]), ALU.mult,
                                   R=[cst_tok, egtok], W=[twtok])
                                tt(Qpad[:, 0:ncb, 0:BS], tw[:, 0:ncb, 0:BS], qT[:, cols].unsqueeze(1).broadcast_to([128, ncb, BS]), ALU.mult,
                                   R=[twtok, q_tok], W=[qp_tok])
                            yield
                            dps, dptok = psblk.get()
                            tr(dps[0:BS, 0:BS], Ds[0:BS, 0:BS], ident[0:BS, 0:BS], R=[dstok, cst_tok], W=[dptok])
                            kkps, kktok = psblk.get()
                            mm(kkps[0:BS, 0:BS], kT[:, cols], kT[:, cols], R=[k_tok], W=[kktok])
                            qkps, qktok = psblk.get()
                            mm(qkps[0:BS, 0:BS], kT[:, cols], qT[:, cols], R=[k_tok, q_tok], W=[qktok])
                            yield
                            DmT, dmtok = X[0]
                            tt(DmT[0:BS, 0:BS], dps[0:BS, 0:BS], ident[0:BS, 0:BS], ALU.add, R=[dptok, cst_tok], W=[dmtok])
                            B, btok = X[2]
                            stt(B[0:BS, 0:BS], kkps[0:BS, 0:BS], nbeta_c, Ds[0:BS, 0:BS], ALU.mult, ALU.mult, R=[kktok, dk_tok, dstok], W=[btok])
                            yield
                            attnT, attok = HS[0]
                            tt(attnT[0:BS, 0:BS], qkps[0:BS, 0:BS], DmT[0:BS, 0:BS], ALU.mult, R=[qktok, dmtok], W=[attok])
                            btps, btptok = psblk.get()
                            tr(btps[0:BS, 0:BS], B[0:BS, 0:BS], ident[0:BS, 0:BS], R=[btok, cst_tok], W=[btptok])
                            yield
                            Bt, bttok = X[3]
                            act(Bt[0:BS, 0:BS], btps[0:BS, 0:BS], AF.Copy, R=[btptok], W=[bttok])
                            TT, tttok = X[4]
                            tt(TT[0:BS, 0:BS], btps[0:BS, 0:BS], ident[0:BS, 0:BS], ALU.add, R=[btptok, cst_tok], W=[tttok])
                            yield
                            cur = (X[2], X[3])
                            nxt = (X[5], X[1])
                            for lev in range(nlev):
                                last = (lev == nlev - 1)
                                (B, btok), (Bt, bttok) = cur
                                (Bn, bntok), (Btn, btntok) = nxt
                                p2, p2tok = psblk.get()
                                mmr(p2[0:BS, 0:BS], Bt[0:BS, 0:BS], B[0:BS, 0:BS], R=[bttok, btok], W=[p2tok])
                                if not last:
                                    p1, p1tok = psblk.get()
                                    mmr(p1[0:BS, 0:BS], B[0:BS, 0:BS], Bt[0:BS, 0:BS], R=[bttok, btok], W=[p1tok])
                                yield
                                cp(Bn[0:BS, 0:BS], p2[0:BS, 0:BS], R=[p2tok], W=[bntok])
                                if not last:
                                    act(Btn[0:BS, 0:BS], p1[0:BS, 0:BS], AF.Copy, R=[p1tok], W=[btntok])
                                yield
                                p3, p3tok = psblk.get()
                                mmr(p3[0:BS, 0:BS], Bn[0:BS, 0:BS], TT[0:BS, 0:BS], R=[bntok, tttok], W=[p3tok])
                                yield
                                tt(TT[0:BS, 0:BS], p3[0:BS, 0:BS], TT[0:BS, 0:BS], ALU.add, R=[p3tok, tttok], W=[tttok])
                                yield
                                cur, nxt = nxt, cur
                            TTb, ttbtok = HS[1]
                            act(TTb[0:BS, 0:BS], TT[0:BS, 0:BS], AF.Copy, R=[tttok], W=[ttbtok])
                            pk, pktok = pbf.get()
                            tr(pk[0:BS, :], kT[:, cols], ident_bf[:], R=[k_tok, misc_tok], W=[pktok])
                            pvv, pvtok = pbf.get()
                            tr(pvv[0:BS, :], vT[:, cols], ident_bf[:], R=[v_tok, misc_tok], W=[pvtok])
                            yield
                            vb, vbtok = HS[2]
                            act(vb[0:BS, :], pvv[0:BS, :], AF.Copy, scale=beta_c, R=[pvtok, dk_tok], W=[vbtok])
                            kbg, kbgtok = sl["kbg"]
                            act(kbg[0:BS, :], pk[0:BS, :], AF.Copy, scale=bG_c, R=[pktok, dk_tok], W=[kbgtok])
                            if kind == "P":
                                kd, kdtok = HS[5]
                                act(kd[0:BS, :], pk[0:BS, :], AF.Copy, scale=eGL_c, R=[pktok, dk_tok], W=[kdtok])
                            else:
                                kdm, kdmtok = spad["kdm"]
                                Kpad, kp_tok = spad["Kpad"]
                                ts(kdm[0:BS, 0:ncb], rmk, eGL_c, None, ALU.mult, R=[cst_tok, dk_tok], W=[kdmtok])
                                tt(Kpad[0:BS, 0:ncb, :], pk[0:BS, :].unsqueeze(1).broadcast_to([BS, ncb, 128]),
                                   kdm[0:BS, 0:ncb].unsqueeze(2).broadcast_to([BS, ncb, 128]), ALU.mult, R=[pktok, kdmtok], W=[kp_tok])
                            yield
                            wps, wptok = psblk.get()
                            mm(wps[:, 0:BS], kbg[0:BS, :], TTb[0:BS, 0:BS], R=[kbgtok, ttbtok], W=[wptok])
                            yield
                            if kind == "P":
                                Wneg, wntok = HS[4]
                                act(Wneg[:, 0:BS], wps[:, 0:BS], AF.Copy, scale=-1.0, R=[wptok], W=[wntok])
                            else:
                                Wpad, wp_tok = spad["Wpad"]
                                stt(Wpad[:, 0:ncb, 0:BS], wps[:, 0:BS].unsqueeze(1).broadcast_to([128, ncb, BS]), -1.0, cmask[:, :, 0:BS],
                                    ALU.mult, ALU.mult, R=[wptok, cst_tok], W=[wp_tok])

                        pend = [None]

                        def flush_post():
                            if pend[0] is not None:
                                o_post(*pend[0])
                                pend[0] = None

                        def state_group(grp, par):
                            for k, bi in enumerate(grp):
                                c0, BS, kind = blks[bi]
                                sl = slots[k]
                                HS = sl["HS"][par]
                                attnT, attok = HS[0]
                                TTb, ttbtok = HS[1]
                                vb, vbtok = HS[2]
                                ops_, optok = obank.get()
                                if kind == "P":
                                    qg, qgtok = HS[3]
                                    Wneg, wntok = HS[4]
                                    kd, kdtok = HS[5]
                                    for c in range(2):
                                        rows = slice(c * 64, (c + 1) * 64)
                                        mm(u_ps[rows, 0:128], TTb[rows, rows], vb[rows, :], start=True, stop=False, R=[ttbtok, vbtok], W=[u_ptok])
                                        mm(u_ps[rows, 0:128], Wneg[:, rows], Sbf[:], start=False, stop=True, R=[wntok, Sbf_tok], W=[u_ptok])
                                        act(u_sb[rows, :], u_ps[rows, 0:128], AF.Copy, R=[u_ptok], W=[usb_tok])
                                        mm(ops_[rows, 0:128], qg[:, rows], Sbf[:], start=True, stop=False, R=[qgtok, Sbf_tok], W=[optok])
                                        mm(ops_[rows, 0:128], attnT[rows, rows], u_sb[rows, :], start=False, stop=True, R=[attok, usb_tok], W=[optok])
                                        sps, sptok = psblk.get()
                                        mm(sps[:, :], kd[rows, :], u_sb[rows, :], R=[kdtok, usb_tok], W=[sptok])
                                        gl = glb[:, h, (bi * 2 + c):(bi * 2 + c) + 1]
                                        stt(Sbf[:], Sf_all[:, h, :], gl, sps[:, :], ALU.mult, ALU.add, R=[sptok, dk_tok, Sf_tok[h]], W=[Sbf_tok])
                                        stt(Sf_all[:, h, :], Sf_all[:, h, :], gl, sps[:, :], ALU.mult, ALU.add, R=[sptok, dk_tok], W=[Sf_tok[h]])
                                        yield
                                else:
                                    ncb = 8
                                    Kpad, kp_tok = spad["Kpad"]
                                    Wpad, wp_tok = spad["Wpad"]
                                    Qpad, qp_tok = spad["Qpad"]
                                    mm(u_ps[0:BS, 0:128], TTb[0:BS, 0:BS], vb[0:BS, :], start=True, stop=False, R=[ttbtok, vbtok], W=[u_ptok])
                                    for c in range(ncb):
                                        mm(u_ps[0:BS, 0:128], Wpad[:, c, 0:BS], Ssb[:, c, :], start=False, stop=(c == ncb - 1), R=[wp_tok, Ssb_tok], W=[u_ptok])
                                    act(u_sb[0:BS, :], u_ps[0:BS, 0:128], AF.Copy, R=[u_ptok], W=[usb_tok])
                                    for c in range(ncb):
                                        mm(ops_[0:BS, 0:128], Qpad[:, c, 0:BS], Ssb[:, c, :], start=(c == 0), stop=False, R=[qp_tok, Ssb_tok], W=[optok])
                                    mm(ops_[0:BS, 0:128], attnT[0:BS, 0:BS], u_sb[0:BS, :], start=False, stop=True, R=[attok, usb_tok], W=[optok])
                                    sl2 = [pbig.get(), pbig.get()]
                                    for c in range(ncb):
                                        ps, ptok = sl2[c // 4]
                                        mm(ps[:, (c % 4) * 128:(c % 4) * 128 + 128], Kpad[0:BS, c, :], u_sb[0:BS, :], R=[kp_tok, usb_tok], W=[ptok])
                                    tt(Sn[:], Ss[:], glb[:, h, 16:24].unsqueeze(2).broadcast_to([128, 8, 128]), ALU.mult, R=[Ss_tok, dk_tok], W=[Sn_tok])
                                    for hb in range(2):
                                        ps, ptok = sl2[hb]
                                        tt(Sn[:, hb * 4:hb * 4 + 4, :], Sn[:, hb * 4:hb * 4 + 4, :], ps[:, :].rearrange("p (s v) -> p s v", v=128), ALU.add,
                                           R=[ptok], W=[Sn_tok])
                                    K.dma("sp", sgdn_o[j, SPP * ph:SPP * ph + SPP, h].rearrange("s k v -> k s v"), Sn[:], R=[Sn_tok], semtok=Sn_tok)
                                flush_post()
                                pend[0] = (l, (ops_, optok), BS, h, c0, oT, otoks[min(c0 // 512, 2)], sgT, sg_tok, opf, opb)
                                yield

                        def run_rr(gens):
                            alive = list(gens)
                            while alive:
                                nx = []
                                for g in alive:
                                    try:
                                        next(g)
                                        nx.append(g)
                                    except StopIteration:
                                        pass
                                alive = nx

                        blks = blocks()
                        groups = [[0, 1, 2, 3], [4, 5, 6, 7], [8]]
                        stage(4.06 + 0.1 * h)
                        run_rr([pre_state(bi, blks[bi][0], blks[bi][1], blks[bi][2], slots[k], 0) for k, bi in enumerate(groups[0])])
                        run_rr([state_group(groups[0], 0)] +
                               [pre_state(bi, blks[bi][0], blks[bi][1], blks[bi][2], slots[k], 1) for k, bi in enumerate(groups[1])])
                        run_rr([state_group(groups[1], 1)] +
                               [pre_state(8, blks[8][0], blks[8][1], blks[8][2], slots[0], 0)])
                        run_rr([state_group(groups[2], 0)])
                        flush_post()
                        if ph == 1:
                            K.dma("sp", pgdn_d[j, h], Sf_all[:, h, :], R=[Sf_tok[h]], semtok=Sf_tok[h])
                        K.snapshot()
                K.dma("sp", gco_d[j, ph], gco[:].rearrange("p a s r -> p (a s r)"), R=[gco_tok], semtok=gco_tok)
                K.snapshot()

        def hgrn_phase(l, ph, hT, htoks, oT, otoks, S):
            j = l // 2
            Win = hgrn_w_in[j]
            with contextlib.ExitStack() as S1:
                rmask = sb(S1, "h_rm", [128, TPH], F32)
                rm_tok = Tok()
                eG = sb(S1, "h_eG", [128, TPH], F32)
                eG_tok = Tok()
                qgT = sb(S1, "h_qgT", [128, TPH], BF16)
                kiT = sb(S1, "h_kiT", [128, TPH], BF16)
                kdT = sb(S1, "h_kdT", [128, TPH], BF16)
                vT = sb(S1, "h_vT", [128, TPH], BF16)
                sgT = sb(S1, "h_sgT", [128, TPH], BF16)
                qg_tok, ki_tok, kd_tok, v_tok, sg_tok = Tok(), Tok(), Tok(), Tok(), Tok()
                Sbf = sb(S1, "h_Sbf", [128, 128], BF16)
                Sbf_tok = Tok()
                Ss = sb(S1, "h_Ss", [128, SPP, 128], F32)
                Ss_tok = retok(Ss_ptok)
                Ssb = sb(S1, "h_Ssb", [128, SPP, 128], BF16)
                Ssb_tok = Tok()
                Sn, Sn_tok = Ss, Ss_tok
                memset(rmask[:], 1.0, W=[rm_tok])
                memset(rmask[:, 0:TP].rearrange("p (c t) -> p c t", t=32)[:, :, 0:1], 0.0, W=[rm_tok])
                memset(rmask[:, TP:TPH].rearrange("p (c t) -> p c t", t=8)[:, :, 0:1], 0.0, W=[rm_tok])
                psblk = Ring([(psF[i % 5][:, (i // 5) * 128:(i // 5) * 128 + 128], bkt[i % 5]) for i in range(20)])
                obank = Ring([(psF[5], bkt[5]), (psF[6], bkt[6])])
                for h in range(H):
                    stage(7 + 0.1 * h)
                    lb = lbT[:, h, 0:1]
                    oml = lbT[:, h, 1:2]
                    with contextlib.ExitStack() as SPL:
                        fa = sb(SPL, "h_fa", [128, TPH], F32)
                        fa_tok = T()
                        fk = sb(SPL, "h_fk", [128, TPH], F32)
                        fk_tok = T()
                        fG = sb(SPL, "h_fG", [128, TPH], F32)
                        fG_tok = T()
                        fe = sb(SPL, "h_fe", [128, TPH], F32)
                        fe_tok = T()
                        sq_ = sb(SPL, "h_sq", [128, TPH], F32)
                        sq_tok = T()
                        for role in range(4):
                            wt, wtok = wload(Win, role * 1024 + h * 128)
                            pall = proj_all(wt, wtok, hT, htoks)
                            for ti in range(3):
                                ps, ptok, n = pall[ti]
                                t0 = TILES[ti][0]
                                if role == 0:
                                    act(sq_[:, t0:t0 + n], ps[:, 0:n], AF.Silu, R=[ptok], W=[sq_tok])
                                elif role == 1:
                                    act(fa[:, t0:t0 + n], ps[:, 0:n], AF.Sigmoid, R=[ptok], W=[fa_tok])
                                elif role == 2:
                                    act(vT[:, t0:t0 + n], ps[:, 0:n], AF.Copy, R=[ptok], W=[v_tok])
                                else:
                                    act(sgT[:, t0:t0 + n], ps[:, 0:n], AF.Silu, R=[ptok], W=[sg_tok])
                        ts(fa[:], fa[:], oml, lb, ALU.mult, ALU.add, R=[fa_tok, lb_tok], W=[fa_tok])
                        ts(fk[:], fa[:], -1.0, 1.0, ALU.mult, ALU.add, R=[fa_tok], W=[fk_tok])
                        act(fa[:], fa[:], AF.Ln, R=[fa_tok], W=[fa_tok])
                        K.op("dve", lambda e, fG=fG, fa=fa: e.tensor_tensor_scan(out=fG[:], data0=rmask[:], data1=fa[:], initial=0.0, op0=ALU.mult, op1=ALU.add),
                             [fa_tok, rm_tok], [fG_tok])
                        act(eG[:], fG[:], AF.Exp, R=[fG_tok], W=[eG_tok])
                        tt(qgT[:], sq_[:], eG[:], ALU.mult, R=[sq_tok, eG_tok], W=[qg_tok])
                        act(fe[:], fG[:], AF.Exp, scale=-1.0, R=[fG_tok], W=[fe_tok])
                        tt(kiT[:], fk[:], fe[:], ALU.mult, R=[fk_tok, fe_tok], W=[ki_tok])
                        gp = fG[:, 0:TP].rearrange("p (c t) -> p c t", t=32)
                        tt(fa[:, 0:TP].rearrange("p (c t) -> p c t", t=32), gp[:, :, 31:32].broadcast_to([128, 32, 32]), gp, ALU.subtract,
                           R=[fG_tok], W=[fa_tok])
                        gs = fG[:, TP:TPH].rearrange("p (c t) -> p c t", t=8)
                        tt(fa[:, TP:TPH].rearrange("p (c t) -> p c t", t=8), gs[:, :, 7:8].broadcast_to([128, 8, 8]), gs, ALU.subtract,
                           R=[fG_tok], W=[fa_tok])
                        act(fe[:], fa[:], AF.Exp, R=[fa_tok], W=[fe_tok])
                        tt(kdT[:], fk[:], fe[:], ALU.mult, R=[fk_tok, fe_tok], W=[kd_tok])
                        K.snapshot()

                    stage(7.05 + 0.1 * h)
                    K.dma("pool", Ss[:], shgrn_d[j, SPP * ph:SPP * ph + SPP, h].rearrange("s k v -> k s v"), W=[Ss_tok], semtok=Ss_tok)
                    act(Ssb[:], Ss[:], AF.Copy, R=[Ss_tok], W=[Ssb_tok])
                    if ph == 0:
                        memset(Sf_all[:, h, :], 0.0, W=[Sf_tok[h]])
                    cp(Sbf[:], Sf_all[:, h, :], R=[Sf_tok[h]], W=[Sbf_tok])

                    with contextlib.ExitStack() as SBK:
                        blks = blocks()
                        bufs = []
                        for bi, (c0, BS, kind) in enumerate(blks):
                            npad = 4 if kind == "P" else 8
                            bufs.append(dict(
                                attnT=(sb(SBK, "h_at%d" % bi, [128, 128], BF16), T()),
                                vtk=(sb(SBK, "h_vt%d" % bi, [128, 128], BF16), T()),
                                Kpad=(sb(SBK, "h_kp%d" % bi, [128, npad, 128], BF16), T()),
                                Qpad=(sb(SBK, "h_qp%d" % bi, [128, npad, 128], BF16), T())))
                        opf = Ring([(sb(SBK, "h_of%d" % i, [128, 128], F32), T()) for i in range(4)])
                        opb = Ring([(sb(SBK, "h_ob%d" % i, [128, 128], BF16), T()) for i in range(2)])

                        def consts_for(kind):
                            if kind == "P":
                                return (4, 32, cst[:, C_MIP:C_MIP + 128],
                                        cst[:, C_CM4:C_CM4 + 512].rearrange("p (c t) -> p c t", c=4), cst[:, C_RM4:C_RM4 + 4])
                            return (8, 8, cst[0:64, C_MIS:C_MIS + 64],
                                    cst[:, C_CM8:C_CM8 + 512].rearrange("p (c t) -> p c t", c=8), cst[0:64, C_RM8:C_RM8 + 8])

                        for bi, (c0, BS, kind) in enumerate(blks):
                            ncb, C, mi, cmask, rmk = consts_for(kind)
                            bf = bufs[bi]
                            cols = slice(c0, c0 + BS)
                            aps, aptok = psblk.get()
                            mm(aps[0:BS, 0:BS], kiT[:, cols], qgT[:, cols], R=[ki_tok, qg_tok], W=[aptok])
                            attnT, attok = bf["attnT"]
                            tt(attnT[0:BS, 0:BS], aps[0:BS, 0:BS], mi, ALU.mult, R=[aptok, cst_tok], W=[attok])
                            pvv, pvtok = pbf.get()
                            tr(pvv[0:BS, :], vT[:, cols], ident_bf[:], R=[v_tok, misc_tok], W=[pvtok])
                            vtk, vtktok = bf["vtk"]
                            act(vtk[0:BS, :], pvv[0:BS, :], AF.Copy, R=[pvtok], W=[vtktok])
                            pk, pktok = pbf.get()
                            tr(pk[0:BS, :], kdT[:, cols], ident_bf[:], R=[kd_tok, misc_tok], W=[pktok])
                            Kpad, kp_tok = bf["Kpad"]
                            tt(Kpad[0:BS, 0:ncb, :], pk[0:BS, :].unsqueeze(1).broadcast_to([BS, ncb, 128]),
                               rmk.unsqueeze(2).broadcast_to([BS, ncb, 128]), ALU.mult, R=[pktok, cst_tok], W=[kp_tok])
                            Qpad, qp_tok = bf["Qpad"]
                            tt(Qpad[:, 0:ncb, 0:BS], cmask[:, :, 0:BS], qgT[:, cols].unsqueeze(1).broadcast_to([128, ncb, BS]), ALU.mult,
                               R=[cst_tok, qg_tok], W=[qp_tok])

                        pending = None
                        for bi, (c0, BS, kind) in enumerate(blks):
                            ncb, C, mi, cmask, rmk = consts_for(kind)
                            bf = bufs[bi]
                            attnT, attok = bf["attnT"]
                            vtk, vtktok = bf["vtk"]
                            Kpad, kp_tok = bf["Kpad"]
                            Qpad, qp_tok = bf["Qpad"]
                            ops_, optok = obank.get()
                            if kind == "P":
                                for c in range(ncb):
                                    mm(ops_[0:BS, 0:128], Qpad[:, c, 0:BS], Sbf[:], start=(c == 0), stop=False, R=[qp_tok, Sbf_tok], W=[optok])
                                    sps, sptok = psblk.get()
                                    mm(sps[:, :], Kpad[0:BS, c, :], vtk[0:BS, :], R=[kp_tok, vtktok], W=[sptok])
                                    ce = c0 + (c + 1) * C - 1
                                    stt(Sbf[:], Sf_all[:, h, :], eG[:, ce:ce + 1], sps[:, :], ALU.mult, ALU.add,
                                        R=[sptok, eG_tok, Sf_tok[h]], W=[Sbf_tok])
                                    stt(Sf_all[:, h, :], Sf_all[:, h, :], eG[:, ce:ce + 1], sps[:, :], ALU.mult, ALU.add,
                                        R=[sptok, eG_tok], W=[Sf_tok[h]])
                                mm(ops_[0:BS, 0:128], attnT[0:BS, 0:BS], vtk[0:BS, :], start=False, stop=True, R=[attok, vtktok], W=[optok])
                            else:
                                for c in range(ncb):
                                    mm(ops_[0:BS, 0:128], Qpad[:, c, 0:BS], Ssb[:, c, :], start=(c == 0), stop=False, R=[qp_tok, Ssb_tok], W=[optok])
                                mm(ops_[0:BS, 0:128], attnT[0:BS, 0:BS], vtk[0:BS, :], start=False, stop=True, R=[attok, vtktok], W=[optok])
                                sl2 = [pbig.get(), pbig.get()]
                                for c in range(ncb):
                                    ps, ptok = sl2[c // 4]
                                    mm(ps[:, (c % 4) * 128:(c % 4) * 128 + 128], Kpad[0:BS, c, :], vtk[0:BS, :], R=[kp_tok, vtktok], W=[ptok])
                                ege = eG[:, TP:TPH].rearrange("p (s t) -> p s t", t=8)[:, :, 7:8]
                                tt(Sn[:], Ss[:], ege.broadcast_to([128, 8, 128]), ALU.mult, R=[Ss_tok, eG_tok], W=[Sn_tok])
                                for hb in range(2):
                                    ps, ptok = sl2[hb]
                                    tt(Sn[:, hb * 4:hb * 4 + 4, :], Sn[:, hb * 4:hb * 4 + 4, :], ps[:, :].rearrange("p (s v) -> p s v", v=128), ALU.add,
                                       R=[ptok], W=[Sn_tok])
                                K.dma("sp", shgrn_o[j, SPP * ph:SPP * ph + SPP, h].rearrange("s k v -> k s v"), Sn[:], R=[Sn_tok], semtok=Sn_tok)
                            if pending is not None:
                                o_post(*pending)
                            pending = (l, (ops_, optok), BS, h, c0, oT, otoks[min(c0 // 512, 2)], sgT, sg_tok, opf, opb)
                        o_post(*pending)
                        if ph == 1:
                            K.dma("sp", phgrn_d[j, h], Sf_all[:, h, :], R=[Sf_tok[h]], semtok=Sf_tok[h])
                        K.snapshot()
                K.snapshot()

        pw5 = Ring([(psF[i], bkt[i]) for i in range(5)])

        def ffn_phase(l, ph, want_ada=False):
            agen = None
            modA = modsets[l % 2]
            A2, B2, G2 = modA[3], modA[4], modA[5]
            with contextlib.ExitStack() as SF:
                aT = sb(SF, "f_aT", [128, NFF, TPH], BF16)
                a_toks = [Tok() for _ in range(3)]
                fci = sb(SF, "f_fci", [128, NFF, 8, 2], F32)
                fci_tok = retok(fci_ptok)
                fco = sb(SF, "f_fco", [128, NFF, 9, 2], F32)
                fco_tok = retok(fco_ptok)
                K.dma("sp", fci[:].rearrange("p a s r -> p (a s r)"), fci_d[l, ph], W=[fci_tok], semtok=fci_tok)
                memset(fco[:], 0.0, W=[fco_tok])
                with contextlib.ExitStack() as S1:
                    hT = sb(S1, "f_hT", [128, KC, TPH], BF16)
                    htoks = [Tok() for _ in range(3)]
                    with contextlib.ExitStack() as SPN:
                        prenorm(ph, A2, B2, hT, htoks, SPN)
                        K.snapshot()
                    gpre_r = Ring([(sb(S1, "f_gpre%d" % i, [128, 2 + TP + SPP * 10], F32), Tok()) for i in range(2)])
                    up_r = Ring([(sb(S1, "f_up%d" % i, [128, TPH], F32), Tok()) for i in range(2)])
                    gc_r = Ring([(sb(S1, "f_gc%d" % i, [128, TPH], F32), Tok()) for i in range(2)])
                    if want_ada:
                        agen = adaln_gen(l + 1, S1)
                    def part_a(jj):
                        gpre, gp_tok = gpre_r.get()
                        up, up_tok = up_r.get()
                        gc, gc_tok = gc_r.get()
                        pv = gpre[:, 2 + TP:2 + TP + SPP * 10].rearrange("p (s r) -> p s r", r=10)
                        wg, wgtok = wload(ffn_w_gu[l], jj * 128)
                        wu, wutok = wload(ffn_w_gu[l], DFF + jj * 128)
                        if ph == 0:
                            memset(gpre[:, 0:2], 0.0, W=[gp_tok])
                        else:
                            cp(gpre[:, 0:2], fcar[:, jj, :], R=[fcar_tok], W=[gp_tok])
                        cp(pv[:, :, 0:2], fci[:, jj, :, :], R=[fci_tok], W=[gp_tok])
                        pg = proj_all(wg, wgtok, hT, htoks)
                        for ti in range(3):
                            t0 = TILES[ti][0]
                            ps, ptok, n = pg[ti]
                            if ti < 2:
                                act(gpre[:, 2 + t0:2 + t0 + n], ps[:, 0:n], AF.Copy, R=[ptok], W=[gp_tok])
                            else:
                                act(pv[:, :, 2:10], ps[:, 0:n].rearrange("p (s r) -> p s r", r=LS), AF.Copy, R=[ptok], W=[gp_tok])
                        pu = proj_all(wu, wutok, hT, htoks)
                        for ti in range(3):
                            t0 = TILES[ti][0]
                            ps, ptok, n = pu[ti]
                            act(up[:, t0:t0 + n], ps[:, 0:n], AF.Copy, R=[ptok], W=[up_tok])
                        return (gpre, gp_tok, up, up_tok, gc, gc_tok, pv)

                    def part_b(jj, c):
                        gpre, gp_tok, up, up_tok, gc, gc_tok, pv = c
                        if ph == 0:
                            cp(fcar[:, jj, :], gpre[:, TP:TP + 2], R=[gp_tok], W=[fcar_tok])
                        else:
                            cp(fco[:, jj, 8, :], gpre[:, TP:TP + 2], R=[gp_tok], W=[fco_tok])
                        cp(fco[:, jj, 0:8, :], pv[:, :, 8:10], R=[gp_tok], W=[fco_tok])
                        wc = prm[:, l, P_FCW + jj * 3:P_FCW + jj * 3 + 3]
                        bc = prm[:, l, P_FCB + jj:P_FCB + jj + 1]
                        gcs = gc[:, TP:TPH].rearrange("p (s r) -> p s r", r=LS)
                        ts(gc[:, 0:TP], gpre[:, 0:TP], wc[:, 0:1], bc, ALU.mult, ALU.add, R=[gp_tok, prm_tok], W=[gc_tok])
                        ts(gcs, pv[:, :, 0:8], wc[:, 0:1], bc, ALU.mult, ALU.add, R=[gp_tok, prm_tok], W=[gc_tok])
                        for tap in range(1, 3):
                            stt(gc[:, 0:TP], gpre[:, tap:tap + TP], wc[:, tap:tap + 1], gc[:, 0:TP], ALU.mult, ALU.add,
                                R=[gp_tok, prm_tok], W=[gc_tok])
                            stt(gcs, pv[:, :, tap:tap + 8], wc[:, tap:tap + 1], gcs, ALU.mult, ALU.add, R=[gp_tok, prm_tok], W=[gc_tok])
                        act(gc[:], gc[:], AF.Silu, R=[gc_tok], W=[gc_tok])
                        for ti, (t0, n) in enumerate(TILES):
                            tt(aT[:, jj, t0:t0 + n], gc[:, t0:t0 + n], up[:, t0:t0 + n], ALU.mult, R=[gc_tok, up_tok], W=[a_toks[ti]])

                    prev = None
                    for jj in range(NFF):
                        c = part_a(jj)
                        if prev is not None:
                            part_b(*prev)
                        prev = (jj, c)
                    part_b(*prev)
                    K.dma("sp", fco_d[l, ph], fco[:].rearrange("p a s r -> p (a s r)"), R=[fco_tok], semtok=fco_tok)
                    K.snapshot()
                with contextlib.ExitStack() as S2:
                    y = sb(S2, "f_y", [128, KC, TPH], F32)
                    ytok = Tok()
                    wdn = Ring([(sb(S2, "f_wd%d" % i, [128, NFF, 128], BF16), retok(wdn_ptoks[i])) for i in range(2)])
                    sqt = (sb(S2, "f_sq", [128, KC, 256], BF16), Tok())
                    rs = (sb(S2, "f_rs", [128, 256], F32), Tok())
                    tmps = Ring([(sb(S2, "f_t%d" % i, [128, 256], F32), Tok()) for i in range(2)])
                    for oc in range(KC):
                        wt, wtok = wdn.get()
                        K.dma("pool", wt[:], ffn_w_down[l][:, oc * 128:(oc + 1) * 128].rearrange("(j p) n -> p j n", p=128), W=[wtok], semtok=wtok)
                        sl3 = [pwide.get() for _ in range(3)]
                        for jj in range(NFF):
                            for ti, (t0, n) in enumerate(TILES):
                                ps, ptok = sl3[ti]
                                mm(ps[:, 0:n], wt[:, jj, :], aT[:, jj, t0:t0 + n], start=(jj == 0), stop=(jj == NFF - 1),
                                   R=[wtok, a_toks[ti]], W=[ptok])
                        for ti, (t0, n) in enumerate(TILES):
                            ps, ptok = sl3[ti]
                            cp(y[:, oc, t0:t0 + n], ps[:, 0:n], R=[ptok], W=[ytok])
                    for ti, (t0, n) in enumerate([(0, 256), (256, 256), (512, 256), (768, 256), (1024, 64)]):
                        rstd_fm(y[:, :, t0:t0 + n], n, [ytok], sqt, rs)
                        for kc in range(KC):
                            tmp, ttok = tmps.get()
                            tt(tmp[:, 0:n], y[:, kc, t0:t0 + n], rs[0][:, 0:n], ALU.mult, R=[ytok, rs[1]], W=[ttok])
                            if t0 < TP:
                                stt(xT[:, kc, ph, t0:t0 + n], tmp[:, 0:n], G2[:, kc, 0:1], xT[:, kc, ph, t0:t0 + n], ALU.mult, ALU.add,
                                    R=[ttok, mod_tok], W=[xtok[ph]])
                            else:
                                sc = seqcols(ph)
                                v3 = tmp[:, 0:n].rearrange("p (s j) -> p s j", j=LS)
                                tt(v3, v3, G2[:, kc, sc].unsqueeze(2).broadcast_to([128, SPP, LS]), ALU.mult, R=[ttok, mod_tok], W=[ttok])
                                x3 = xT[:, kc, ph, t0:t0 + n].rearrange("p (s j) -> p s j", j=LS)
                                tt(x3, x3, v3, ALU.add, R=[ttok], W=[xtok[ph]])
                    K.snapshot()

        def main_program():
            for l in range(depth):
                modA = modsets[l % 2]
                with contextlib.ExitStack() as SL:
                    stage(1)
                    for _ in adaln_gen(l, SL):
                        pass
                    if l % 2 == 1:
                        jj_ = l // 2
                        if jj_ == 0:
                            memset(lbT[:, :, 0:1], 0.0, W=[lb_tok])
                            memset(lbT[:, :, 1:2], 1.0, W=[lb_tok])
                        else:
                            hl = prm[:, l, P_HLB:P_HLB + 16].rearrange("p (h t) -> p h t", t=2)
                            tt(lbT[:, :, 0:1], hl[:, :, 1:2], hl[:, :, 0:1], ALU.subtract, R=[prm_tok], W=[lb_tok])
                            act(lbT[:, :, 0:1], lbT[:, :, 0:1], AF.Sigmoid, R=[lb_tok], W=[lb_tok])
                            ts(lbT[:, :, 1:2], lbT[:, :, 0:1], -1.0, 1.0, ALU.mult, ALU.add, R=[lb_tok], W=[lb_tok])
                    K.snapshot()
                for ph in range(2):
                    with contextlib.ExitStack() as SA:
                        hT = sb(SA, "m_hT", [128, KC, TPH], BF16)
                        htoks = [Tok() for _ in range(3)]
                        oT = sb(SA, "m_oT", [128, KC, TPH], BF16)
                        otoks = [Tok() for _ in range(3)]
                        with contextlib.ExitStack() as SP:
                            stage(2)
                            prenorm(ph, modA[0], modA[1], hT, htoks, SP)
                            K.snapshot()
                        with contextlib.ExitStack() as SM:
                            if l % 2 == 0:
                                gdn_phase(l, ph, hT, htoks, oT, otoks, SM)
                            else:
                                hgrn_phase(l, ph, hT, htoks, oT, otoks, SM)
                            K.snapshot()
                        with contextlib.ExitStack() as SO:
                            stage(5)
                            Wout = gdn_w_out[l // 2] if l % 2 == 0 else hgrn_w_out[l // 2]
                            outproj_postnorm(l, ph, Wout, oT, otoks, modA[2], SO)
                            K.snapshot()
                    stage(6)
                    ffn_phase(l, ph, False)
            for ph in range(2):
                K.dma("sp", yT_d.rearrange("p (k h t) -> p k h t", k=KC, h=2)[:, :, ph, :], xT[:, :, ph, :], R=[xtok[ph]], semtok=xtok[ph])

        try:
            main_program()
        except StopBuild:
            K.barrier()
        K.final_wait("sp")

        with nc.Block() as block:
            K.replay(block)
    return nc


def _consts():
    c = np.zeros((128, NCST), np.float32)
    c[:, C_IDENT:C_IDENT + 128] = np.eye(128, dtype=np.float32)
    i = np.arange(128)[:, None]
    jx = np.arange(128)[None, :]
    BIG = 1.0e4
    valid = (i // 64 == jx // 64) & (i > jx)
    c[:, C_PENP:C_PENP + 128] = np.where(valid, 0.0, BIG)
    i8 = np.arange(64)[:, None]
    j8 = np.arange(64)[None, :]
    valid = (i8 // 8 == j8 // 8) & (i8 > j8)
    c[0:64, C_PENS:C_PENS + 64] = np.where(valid, 0.0, BIG)
    for ncb, off, bs in ((2, C_CM2, 128), (4, C_CM4, 128), (8, C_CM8, 64)):
        C = bs // ncb
        m = np.zeros((ncb, bs), np.float32)
        for cc in range(ncb):
            m[cc, cc * C:(cc + 1) * C] = 1.0
        c[:, off:off + ncb * bs] = m.reshape(1, -1)
    for ncb, off, bs in ((2, C_RM2, 128), (4, C_RM4, 128), (8, C_RM8, 64)):
        C = bs // ncb
        m = np.zeros((bs, ncb), np.float32)
        for cc in range(ncb):
            m[cc * C:(cc + 1) * C, cc] = 1.0
        c[0:bs, off:off + ncb] = m
    c[:, C_MIP:C_MIP + 128] = ((i // 32 == jx // 32) & (i <= jx)).astype(np.float32)
    c[0:64, C_MIS:C_MIS + 64] = ((i8 // 8 == j8 // 8) & (i8 <= j8)).astype(np.float32)
    e = np.zeros((8, 8, 128), np.float32)
    for h in range(8):
        e[h, h, :] = 1.0
    c[0:8, C_ESEL:C_ESEL + 1024] = e.reshape(8, 1024)
    return c


def _fm(v):
    sh = v.shape
    k = sh[-1] // 128
    v = v.reshape(sh[:-1] + (k, 128))
    return np.moveaxis(np.moveaxis(v, -1, 0), -1, 1)


_NC_CACHE = {}


def kernel(x_prompt, x_sample, state_gdn, state_gdn_conv, state_hgrn, state_ffn_conv, c_prompt, c_sample,
           ada_w, ada_b, norm_pre_mix, norm_post_mix, norm_pre_ffn, norm_post_ffn,
           gdn_w_in, gdn_conv_w, gdn_conv_b, gdn_a_log, gdn_dt_bias, gdn_norm, gdn_w_out,
           hgrn_lb, hgrn_w_in, hgrn_norm, hgrn_w_out,
           ffn_w_gu, ffn_conv_w, ffn_conv_b, ffn_w_down, _depth=DEPTH, _stage=None):
    f32 = np.float32
    A = lambda a: np.ascontiguousarray(np.asarray(a, dtype=f32))
    x_prompt, x_sample = A(x_prompt), A(x_sample)
    state_gdn, state_gdn_conv, state_hgrn, state_ffn_conv = A(state_gdn), A(state_gdn_conv), A(state_hgrn), A(state_ffn_conv)
    c_prompt, c_sample = A(c_prompt), A(c_sample)
    prm = np.zeros((128, 4, NPRM), f32)
    nrow = np.zeros((128, 4, 128), f32)
    ada_b_, gcw, gcb = A(ada_b), A(gdn_conv_w), A(gdn_conv_b)
    fcw, fcb = A(ffn_conv_w), A(ffn_conv_b)
    hlb = A(hgrn_lb)
    for l in range(4):
        prm[:, l, P_ADAB:P_ADAB + 48] = ada_b_[l].reshape(48, 128).T
        prm[:, l, P_NPRE_MIX:P_NPRE_MIX + 8] = A(norm_pre_mix)[l].reshape(8, 128).T
        prm[:, l, P_NPOST_MIX:P_NPOST_MIX + 8] = A(norm_post_mix)[l].reshape(8, 128).T
        prm[:, l, P_NPRE_FFN:P_NPRE_FFN + 8] = A(norm_pre_ffn)[l].reshape(8, 128).T
        prm[:, l, P_NPOST_FFN:P_NPOST_FFN + 8] = A(norm_post_ffn)[l].reshape(8, 128).T
        j = l // 2
        if l % 2 == 0:
            prm[:, l, P_GCW:P_GCW + 96] = gcw[j].reshape(4, 24, 128).transpose(2, 1, 0).reshape(128, 96)
            prm[:, l, P_GCB:P_GCB + 24] = gcb[j].reshape(24, 128).T
            prm[0:8, l, P_ALOG] = A(gdn_a_log)[j]
            prm[0:8, l, P_DTB] = A(gdn_dt_bias)[j]
            nrow[:, l, :] = A(gdn_norm)[j][None, :]
        else:
            nrow[:, l, :] = A(hgrn_norm)[j][None, :]
        prm[:, l, P_FCW:P_FCW + 66] = fcw[l].reshape(3, 22, 128).transpose(2, 1, 0).reshape(128, 66)
        prm[:, l, P_FCB:P_FCB + 22] = fcb[l].reshape(22, 128).T
        prm[:, l, P_HLB:P_HLB + 16] = hlb.reshape(2, 8, 128).transpose(2, 1, 0).reshape(128, 16)
    cst = _consts()
    shared = dict(prm=prm.reshape(128, -1), nrow=nrow.reshape(128, -1), cst=cst,
                  ada_w=A(ada_w), gdn_w_in=A(gdn_w_in), gdn_w_out=A(gdn_w_out), hgrn_w_in=A(hgrn_w_in),
                  hgrn_w_out=A(hgrn_w_out), ffn_w_gu=A(ffn_w_gu), ffn_w_down=A(ffn_w_down))
    in_maps = []
    for c in range(NCORE):
        xt = np.zeros((128, KC, 2, TPH), f32)
        xp = _fm(x_prompt[c])
        xs = _fm(x_sample[16 * c:16 * c + 16])
        for ph in range(2):
            xt[:, :, ph, 0:TP] = xp[:, :, TP * ph:TP * ph + TP]
            xt[:, :, ph, TP:] = xs[:, :, 8 * ph:8 * ph + 8, :].reshape(128, KC, TS)
        cc = np.concatenate([c_prompt[c:c + 1], c_sample[16 * c:16 * c + 16]], 0)
        cT = _fm(cc)
        gc = _fm(state_gdn_conv[:, 16 * c:16 * c + 16])
        gci = np.zeros((2, 2, 128, 24, 8, 3), f32)
        fc = _fm(state_ffn_conv[:, 16 * c:16 * c + 16])
        fci = np.zeros((4, 2, 128, NFF, 8, 2), f32)
        for ph in range(2):
            gci[:, ph] = gc[:, :, :, 8 * ph:8 * ph + 8, :].transpose(2, 0, 1, 3, 4)
            fci[:, ph] = fc[:, :, :, 8 * ph:8 * ph + 8, :].transpose(2, 0, 1, 3, 4)
        m = dict(shared)
        m.update(xT=xt.reshape(128, -1), cT=np.ascontiguousarray(cT).reshape(128, -1),
                 sgdn=np.ascontiguousarray(state_gdn[:, 16 * c:16 * c + 16]),
                 shgrn=np.ascontiguousarray(state_hgrn[:, 16 * c:16 * c + 16]),
                 gci=gci.reshape(2, 2, 128, -1), fci=fci.reshape(4, 2, 128, -1))
        in_maps.append(m)
    ck = (_depth, _stage)
    if ck not in _NC_CACHE:
        _STAGE_LIMIT[0] = _stage
        _STAGE_LIMIT[1] = False
        _NC_CACHE[ck] = build_nc(_depth)
        _STAGE_LIMIT[0] = None
        _STAGE_LIMIT[1] = False
    nc = _NC_CACHE[ck]
    res = run_bass_kernel_spmd(nc, in_maps, core_ids=list(range(NCORE)))
    R = res.results
    y_prompt = np.zeros((8, 2048, D), f32)
    y_sample = np.zeros((128, 8, D), f32)
    p_gdn = np.zeros((2, 8, H, 128, 128), f32)
    p_hgrn = np.zeros((2, 8, H, 128, 128), f32)
    s_gdn = np.zeros((2, 128, H, 128, 128), f32)
    s_hgrn = np.zeros((2, 128, H, 128, 128), f32)
    p_gconv = np.zeros((2, 8, 3, 3072), f32)
    s_gconv = np.zeros((2, 128, 3, 3072), f32)
    p_fconv = np.zeros((4, 8, 2, DFF), f32)
    s_fconv = np.zeros((4, 128, 2, DFF), f32)
    for c in range(NCORE):
        r = R[c]
        yt = r["yT"].reshape(128, KC, 2, TPH)
        for ph in range(2):
            y_prompt[c, TP * ph:TP * ph + TP] = yt[:, :, ph, 0:TP].transpose(2, 1, 0).reshape(TP, D)
            ys = yt[:, :, ph, TP:].reshape(128, KC, 8, 8)
            y_sample[16 * c + 8 * ph:16 * c + 8 * ph + 8] = ys.transpose(2, 3, 1, 0).reshape(8, 8, D)
        p_gdn[:, c] = r["pgdn"]
        p_hgrn[:, c] = r["phgrn"]
        s_gdn[:, 16 * c:16 * c + 16] = r["sgdn_o"]
        s_hgrn[:, 16 * c:16 * c + 16] = r["shgrn_o"]
        g = r["gco"].reshape(2, 2, 128, 24, 9, 3)
        f = r["fco"].reshape(4, 2, 128, NFF, 9, 2)
        for ph in range(2):
            gs = g[:, ph, :, :, 0:8, :].transpose(0, 3, 4, 2, 1).reshape(2, 8, 3, 3072)
            s_gconv[:, 16 * c + 8 * ph:16 * c + 8 * ph + 8] = gs
            fs = f[:, ph, :, :, 0:8, :].transpose(0, 3, 4, 2, 1).reshape(4, 8, 2, DFF)
            s_fconv[:, 16 * c + 8 * ph:16 * c + 8 * ph + 8] = fs
        p_gconv[:, c] = g[:, 1, :, :, 8, :].transpose(0, 3, 2, 1).reshape(2, 3, 3072)
        p_fconv[:, c] = f[:, 1, :, :, 8, :].transpose(0, 3, 2, 1).reshape(4, 2, DFF)
    return (y_prompt, y_sample, p_gdn, p_gconv, p_hgrn, p_fconv, s_gdn, s_gconv, s_hgrn, s_fconv)
```

```python
import contextlib
import numpy as np
import concourse.bass as bass
import concourse.mybir as mybir
from concourse.bass_utils import run_bass_kernel_spmd

F32 = mybir.dt.float32
BF16 = mybir.dt.bfloat16
AF = mybir.ActivationFunctionType
ALU = mybir.AluOpType

NCORE = 8
D = 1024
KC = 8
H = 8
DFF = 2816
NFF = 22
DEPTH = 4
TP = 1024
SPP = 8
LS = 8
TS = SPP * LS
TPH = TP + TS
TILES = [(0, 512), (512, 512), (1024, 64)]
EPS = 1e-6
NPRM = 48 + 32 + 96 + 24 + 66 + 22 + 16 + 2

P_ADAB = 0
P_NPRE_MIX = 48
P_NPOST_MIX = 56
P_NPRE_FFN = 64
P_NPOST_FFN = 72
P_GCW = 80
P_GCB = 176
P_FCW = 200
P_FCB = 266
P_HLB = 288
P_ALOG = 304
P_DTB = 305

C_IDENT = 0
C_PENP = 128
C_PENS = 256
C_CM2 = 320
C_CM4 = 576
C_CM8 = 1088
C_RM2 = 1600
C_RM4 = 1602
C_RM8 = 1606
C_MIP = 1614
C_MIS = 1742
C_ESEL = 1806
NCST = 1806 + 1024


_CARRY = [{}]


class Tok:
    __slots__ = ("w", "r", "dsem", "dval", "excl")

    def __init__(self, excl=False):
        self.w = {}
        self.r = dict(_CARRY[0])
        self.dsem = None
        self.dval = 0
        self.excl = excl


class Sched:
    ENGS = ("pe", "act", "dve", "pool", "sp")

    def __init__(self, nc, es):
        self.nc = nc
        self.es = es
        self.q = {k: [] for k in self.ENGS}
        self.n = {k: 0 for k in self.ENGS}
        self.seen = {k: {} for k in self.ENGS}
        self.sem = {}
        for k in ("pe", "act", "dve", "pool"):
            self.sem[k] = es.enter_context(nc.semaphore("s_" + k))
        self.dma_latest = {}
        self.nd = 0
        self.carry = {}

    def _waits(self, en, R, W):
        need = {}

        def add(ev):
            key, sem, val = ev
            if key not in need or need[key][1] < val:
                need[key] = (sem, val)

        for t in R:
            for ev in t.w.values():
                add(ev)
            if t.excl:
                for ev in t.r.values():
                    add(ev)
        for t in W:
            for ev in t.w.values():
                add(ev)
            for ev in t.r.values():
                add(ev)
        out = []
        seen = self.seen[en]
        for key, (sem, val) in need.items():
            if key == en and (en == "pe" or _NO_SAME_ENGINE_WAIT[0]):
                continue
            if seen.get(key, 0) >= val:
                continue
            seen[key] = val
            out.append((sem, val))
        return out

    def op(self, en, fn, R=(), W=()):
        if _STAGE_LIMIT[1]:
            return
        waits = self._waits(en, R, W)
        self.n[en] += 1
        sem = self.sem[en]
        self.q[en].append((waits, fn, sem, 1))
        ev = (en, sem, self.n[en])
        for t in W:
            t.w[en] = ev
        for t in R:
            t.r[en] = ev

    def dma(self, qn, out, in_, R=(), W=(), semtok=None):
        if _STAGE_LIMIT[1]:
            return
        waits = self._waits(qn, R, W)
        t = semtok
        if t.dsem is None:
            t.dsem = {}
        if qn not in t.dsem:
            self.nd += 1
            t.dsem[qn] = [self.es.enter_context(self.nc.semaphore("d%d" % self.nd)), 0]
        ent = t.dsem[qn]
        ent[1] += 16
        dsem, dval = ent[0], ent[1]
        key = "d%d_%s" % (id(t), qn)
        self.q[qn].append((waits, (lambda e: e.dma_start(out=out, in_=in_)), dsem, 16))
        ev = (key, dsem, dval)
        for x in W:
            x.w[key] = ev
        for x in R:
            x.r[key] = ev
        self.dma_latest[key] = (dsem, dval)

    def barrier(self, head=False):
        if head and _SKIP_HEAD_BARRIERS[0]:
            return
        for en in self.ENGS:
            waits = []
            seen = self.seen[en]
            for k in ("pe", "act", "dve", "pool"):
                if k == en and en == "pe":
                    continue
                v = self.n[k]
                if v > 0 and seen.get(k, 0) < v:
                    seen[k] = v
                    waits.append((self.sem[k], v))
            for key, (sem, val) in self.dma_latest.items():
                if seen.get(key, 0) < val:
                    seen[key] = val
                    waits.append((sem, val))
            if waits:
                self.q[en].append((waits, None, None, 0))

    def snapshot(self):
        snap = {}
        for k in ("pe", "act", "dve", "pool"):
            if self.n[k] > 0:
                snap[k] = (k, self.sem[k], self.n[k])
        for key, (sem, val) in self.dma_latest.items():
            snap[key] = (key, sem, val)
        self.carry = snap
        _CARRY[0] = snap

    def nsems(self):
        return self.nd + 4

    def final_wait(self, en="sp"):
        waits = []
        for key, (sem, val) in self.dma_latest.items():
            waits.append((sem, val))
        for k in ("pe", "act", "dve", "pool"):
            if self.n[k] > 0:
                waits.append((self.sem[k], self.n[k]))
        self.q[en].append((waits, None, None, 0))

    def replay(self, block):
        q = self.q

        def run(e, lst):
            for waits, fn, sem, inc in lst:
                if fn is None or not _INLINE_WAIT[0] or not waits:
                    for s, v in waits:
                        e.wait_ge(s, v)
                    if fn is not None:
                        fn(e).then_inc(sem, inc)
                else:
                    for s, v in waits[:-1]:
                        e.wait_ge(s, v)
                    s, v = waits[-1]
                    fn(e)._wait_ge(s, v).then_inc(sem, inc)

        @block.tensor
        def _(e):
            run(e, q["pe"])

        @block.scalar
        def _(e):
            run(e, q["act"])

        @block.vector
        def _(e):
            run(e, q["dve"])

        @block.gpsimd
        def _(e):
            run(e, q["pool"])

        @block.sync
        def _(e):
            run(e, q["sp"])


class StopBuild(Exception):
    pass


_STAGE_LIMIT = [None, False]
_NO_SAME_ENGINE_WAIT = [False]
_INLINE_WAIT = [True]
_FP32R = [False]
_SKIP_HEAD_BARRIERS = [False]


def stage(n):
    if _STAGE_LIMIT[0] is not None and n > _STAGE_LIMIT[0]:
        _STAGE_LIMIT[1] = True


class Ring:
    def __init__(self, items):
        self.items = items
        self.i = 0

    def get(self):
        r = self.items[self.i]
        self.i = (self.i + 1) % len(self.items)
        return r


def build_nc(depth=DEPTH):
    _CARRY[0] = {}
    nc = bass.Bass("TRN2", target_bir_lowering=False)

    def din(name, shape):
        return nc.dram_tensor(name, list(shape), F32, kind="ExternalInput").ap()

    def dout(name, shape):
        return nc.dram_tensor(name, list(shape), F32, kind="ExternalOutput").ap()

    xT_d = din("xT", [128, KC * 2 * TPH])
    cT_d = din("cT", [128, KC * 17])
    sgdn_d = din("sgdn", [2, 16, H, 128, 128])
    shgrn_d = din("shgrn", [2, 16, H, 128, 128])
    gci_d = din("gci", [2, 2, 128, 24 * 8 * 3])
    fci_d = din("fci", [4, 2, 128, NFF * 8 * 2])
    prm_d = din("prm", [128, 4 * NPRM])
    nrow_d = din("nrow", [128, 4 * 128])
    cst_d = din("cst", [128, NCST])
    ada_w = din("ada_w", [4, D, 6 * D])
    gdn_w_in = din("gdn_w_in", [2, D, 4112])
    gdn_w_out = din("gdn_w_out", [2, D, D])
    hgrn_w_in = din("hgrn_w_in", [2, D, 4096])
    hgrn_w_out = din("hgrn_w_out", [2, D, D])
    ffn_w_gu = din("ffn_w_gu", [4, D, 2 * DFF])
    ffn_w_down = din("ffn_w_down", [4, DFF, D])

    yT_d = dout("yT", [128, KC * 2 * TPH])
    pgdn_d = dout("pgdn", [2, H, 128, 128])
    sgdn_o = dout("sgdn_o", [2, 16, H, 128, 128])
    phgrn_d = dout("phgrn", [2, H, 128, 128])
    shgrn_o = dout("shgrn_o", [2, 16, H, 128, 128])
    gco_d = dout("gco", [2, 2, 128, 24 * 9 * 3])
    fco_d = dout("fco", [4, 2, 128, NFF * 9 * 2])

    with contextlib.ExitStack() as es:
        K = Sched(nc, es)

        cnt = [0]

        def T():
            t = Tok()
            t.r.update(K.carry)
            return t

        def retok(t):
            t.r.update(_CARRY[0])
            return t

        def sb(stack, name, shape, dt):
            cnt[0] += 1
            return stack.enter_context(nc.sbuf_tensor("sb%d_%s" % (cnt[0], name), list(shape), dt))

        xT = sb(es, "xT", [128, KC, 2, TPH], F32)
        xtok = [Tok(), Tok()]
        cst = sb(es, "cst", [128, NCST], F32)
        cst_tok = Tok()
        prm = sb(es, "prm", [128, 4, NPRM], F32)
        prm_tok = Tok()
        nrow = sb(es, "nrow", [128, 4, 128], F32)
        nrow_tok = Tok()
        ident_bf = sb(es, "ident_bf", [128, 128], BF16)
        ones_bf = sb(es, "ones_bf", [128, 128], BF16)
        cb = sb(es, "cb", [128, 8], F32)
        misc_tok = Tok()
        csT = sb(es, "csT", [128, KC, 17], BF16)
        cs_tok = Tok()
        _ms = [sb(es, "modA%d" % i, [128, KC, 17], F32) for i in range(6)]
        modsets = [_ms, _ms]
        mod_tok = Tok()
        Sf_all = sb(es, "Sf_all", [128, H, 128], F32)
        Sf_tok = [Tok() for _ in range(H)]
        gcar = sb(es, "gcar", [128, 24, 3], F32)
        gcar_tok = Tok()
        fcar = sb(es, "fcar", [128, NFF, 2], F32)
        fcar_tok = Tok()
        lbT = sb(es, "lbT", [128, H, 2], F32)
        lb_tok = Tok()
        wun = [(sb(es, "wun%d" % i, [128, KC, 128], BF16), Tok()) for i in range(5)]
        wring = Ring(wun)
        wres_toks = [Tok() for _ in range(KC)]
        Ss_ptok, gci_ptok, gco_ptok, fci_ptok, fco_ptok = Tok(), Tok(), Tok(), Tok(), Tok()
        wdn_ptoks = [Tok(), Tok()]

        psF = [es.enter_context(nc.psum_tensor("psF%d" % i, [128, 512], F32)) for i in range(7)]
        psB = es.enter_context(nc.psum_tensor("psB", [128, 1024], BF16))
        bkt = [Tok(excl=True) for _ in range(8)]
        pbig = Ring([(psF[0], bkt[0]), (psF[1], bkt[1])])
        psmall = Ring([(psF[2 + i % 3][:, (i // 3) * 128:(i // 3) * 128 + 128], bkt[2 + i % 3]) for i in range(12)])
        pbf = Ring([(psB[:, i * 128:(i + 1) * 128], bkt[7]) for i in range(8)])
        o_ps, o_ptok = psF[5], bkt[5]
        u_ps, u_ptok = psF[6], bkt[6]
        pwide = Ring([(psF[i], bkt[i]) for i in range(7)])

        def mm(out, lhsT, rhs, start=True, stop=True, R=(), W=()):
            K.op("pe", lambda e: e.matmul(out, lhsT, rhs, start=start, stop=stop), R, W)

        def mmr(out, lhsT, rhs, R=(), W=()):
            if _FP32R[0] and lhsT.shape[-1] == 128:
                F32R = mybir.dt.float32r
                K.op("pe", lambda e: e.matmul(out, lhsT.bitcast(F32R), rhs.bitcast(F32R), start=True, stop=True), R, W)
            else:
                K.op("pe", lambda e: e.matmul(out, lhsT, rhs, start=True, stop=True), R, W)

        def tr(out, in_, idn, R=(), W=()):
            K.op("pe", lambda e: e.transpose(out, in_, idn), R, W)

        def act(out, in_, func, bias=None, scale=None, accum=None, R=(), W=()):
            kw = {}
            if bias is not None:
                kw["bias"] = bias
            if scale is not None:
                kw["scale"] = scale
            if accum is not None:
                kw["accum_out"] = accum
            K.op("act", lambda e: e.activation(out=out, in_=in_, func=func, **kw), R, W)

        def tt(out, a, b, op, R=(), W=(), en="dve"):
            K.op(en, lambda e: e.tensor_tensor(out=out, in0=a, in1=b, op=op), R, W)

        def ts(out, a, s1, s2, op0, op1=None, R=(), W=(), en="dve"):
            if op1 is None:
                K.op(en, lambda e: e.tensor_scalar(out=out, in0=a, scalar1=s1, scalar2=None, op0=op0), R, W)
            else:
                K.op(en, lambda e: e.tensor_scalar(out=out, in0=a, scalar1=s1, scalar2=s2, op0=op0, op1=op1), R, W)

        def stt(out, a, sc, b, op0, op1, R=(), W=(), en="dve"):
            K.op(en, lambda e: e.scalar_tensor_tensor(out=out, in0=a, scalar=sc, in1=b, op0=op0, op1=op1), R, W)

        def cp(out, in_, R=(), W=(), en="dve"):
            K.op(en, lambda e: e.tensor_copy(out=out, in_=in_), R, W)

        def recip(out, in_, R=(), W=()):
            K.op("dve", lambda e: e.reciprocal(out=out, in_=in_), R, W)

        def memset(ap, val, W=(), en="dve"):
            K.op(en, lambda e: e.memset(ap, val), (), W)

        def wload(W2d, c0, ncols=128):
            t, tok = wring.get()
            K.dma("pool", t[:, :, 0:ncols], W2d[:, c0:c0 + ncols].rearrange("(k p) n -> p k n", p=128), W=[tok], semtok=tok)
            return t, tok

        K.dma("sp", cst[:], cst_d[:, :], W=[cst_tok], semtok=cst_tok)
        K.dma("sp", prm[:].rearrange("p l n -> p (l n)"), prm_d[:, :], W=[prm_tok], semtok=prm_tok)
        K.dma("sp", nrow[:].rearrange("p l n -> p (l n)"), nrow_d[:, :], W=[nrow_tok], semtok=nrow_tok)
        for ph in range(2):
            K.dma("sp", xT[:, :, ph, :], xT_d.rearrange("p (k h t) -> p k h t", k=KC, h=2)[:, :, ph, :], W=[xtok[ph]], semtok=xtok[ph])
        ident = cst[:, C_IDENT:C_IDENT + 128]
        cp(ident_bf[:], ident, R=[cst_tok], W=[misc_tok])
        memset(ones_bf[:], 1.0, W=[misc_tok])
        memset(cb[:, 0:1], 1024.0 * EPS, W=[misc_tok])
        memset(cb[:, 1:2], EPS, W=[misc_tok])
        memset(cb[:, 2:3], 128.0 * EPS, W=[misc_tok])
        memset(cb[:, 3:4], 1.0, W=[misc_tok])
        memset(cb[:, 4:5], 0.0, W=[misc_tok])
        ts(nrow[:], nrow[:], float(np.sqrt(128.0)), None, ALU.mult, R=[nrow_tok], W=[nrow_tok])
        for l in range(4):
            ts(prm[:, l, P_NPRE_MIX:P_NPRE_MIX + 32], prm[:, l, P_NPRE_MIX:P_NPRE_MIX + 32], 32.0, None, ALU.mult, R=[prm_tok], W=[prm_tok])
        with contextlib.ExitStack() as s0:
            cTt = sb(s0, "cTt", [128, KC, 17], F32)
            ctok = Tok()
            K.dma("sp", cTt[:].rearrange("p k q -> p (k q)"), cT_d[:, :], W=[ctok], semtok=ctok)
            act(csT[:], cTt[:], AF.Silu, R=[ctok], W=[cs_tok])
            K.snapshot()

        def adaln_gen(l, S):
            modA = modsets[l % 2]
            mod = sb(S, "modraw", [128, 48, 17], F32)
            mtok = Tok()
            slots = [(psF[5], bkt[5]), (psF[6], bkt[6])]
            for cc in range(48):
                wt, wtok = wload(ada_w[l], cc * 128)
                ps, ptok = slots[cc // 24]
                col = (cc % 24) * 17
                for kc in range(KC):
                    mm(ps[:, col:col + 17], wt[:, kc, :], csT[:, kc, :], start=(kc == 0), stop=(kc == KC - 1),
                       R=[wtok, cs_tok], W=[ptok])
                yield
            for bk in range(2):
                ps, ptok = slots[bk]
                tt(mod[:, 24 * bk:24 * bk + 24, :], ps[:, 0:408].rearrange("p (c q) -> p c q", q=17),
                   prm[:, l, P_ADAB + 24 * bk:P_ADAB + 24 * bk + 24].unsqueeze(2).broadcast_to([128, 24, 17]),
                   ALU.add, R=[ptok, prm_tok], W=[mtok])
            def nb(off):
                return prm[:, l, off:off + 8].unsqueeze(2).broadcast_to([128, 8, 17])
            stt(modA[0][:], mod[:, 8:16, :], 1.0, nb(P_NPRE_MIX), ALU.add, ALU.mult, R=[mtok, prm_tok], W=[mod_tok])
            cp(modA[1][:], mod[:, 0:8, :], R=[mtok], W=[mod_tok])
            stt(modA[2][:], mod[:, 16:24, :], 1.0, nb(P_NPOST_MIX), ALU.add, ALU.mult, R=[mtok, prm_tok], W=[mod_tok])
            stt(modA[3][:], mod[:, 32:40, :], 1.0, nb(P_NPRE_FFN), ALU.add, ALU.mult, R=[mtok, prm_tok], W=[mod_tok])
            cp(modA[4][:], mod[:, 24:32, :], R=[mtok], W=[mod_tok])
            stt(modA[5][:], mod[:, 40:48, :], 1.0, nb(P_NPOST_FFN), ALU.add, ALU.mult, R=[mtok, prm_tok], W=[mod_tok])

        def seqcols(ph):
            return slice(1 + SPP * ph, 1 + SPP * ph + SPP)

        def rstd_fm(src3, n, R, sqt, rs):
            sq, sqtok = sqt
            rst, rstok = rs
            act(sq[:, :, 0:n], src3, AF.Square, R=R, W=[sqtok])
            ps, ptok = pbig.get()
            for kc in range(KC):
                mm(ps[:, 0:n], ones_bf[:], sq[:, kc, 0:n], start=(kc == 0), stop=(kc == KC - 1), R=[sqtok, misc_tok], W=[ptok])
            act(rst[:, 0:n], ps[:, 0:n], AF.Ln, bias=cb[:, 0:1], scale=1.0, R=[ptok, misc_tok], W=[rstok])
            act(rst[:, 0:n], rst[:, 0:n], AF.Exp, scale=-0.5, R=[rstok], W=[rstok])

        def prenorm(ph, A, B, hT, htoks, S):
            sqt_r = Ring([(sb(S, "pn_sq%d" % i, [128, KC, 512], BF16), Tok()) for i in range(2)])
            rs_r = Ring([(sb(S, "pn_rs%d" % i, [128, 512], F32), Tok()) for i in range(2)])
            tmps = Ring([(sb(S, "pn_t%d" % i, [128, 512], F32), Tok()) for i in range(3)])
            for ti, (t0, n) in enumerate(TILES):
                sqt = sqt_r.get()
                rs = rs_r.get()
                rstd_fm(xT[:, :, ph, t0:t0 + n], n, [xtok[ph]], sqt, rs)
                for kc in range(KC):
                    tmp, ttok = tmps.get()
                    tt(tmp[:, 0:n], xT[:, kc, ph, t0:t0 + n], rs[0][:, 0:n], ALU.mult, R=[xtok[ph], rs[1]], W=[ttok])
                    if ti < 2:
                        act(hT[:, kc, t0:t0 + n], tmp[:, 0:n], AF.Identity, bias=B[:, kc, 0:1], scale=A[:, kc, 0:1],
                            R=[ttok, mod_tok], W=[htoks[ti]])
                    else:
                        sc = seqcols(ph)
                        tt(tmp[:, 0:n].rearrange("p (s j) -> p s j", j=LS), tmp[:, 0:n].rearrange("p (s j) -> p s j", j=LS),
                           A[:, kc, sc].unsqueeze(2).broadcast_to([128, SPP, LS]), ALU.mult, R=[ttok, mod_tok], W=[ttok])
                        tt(hT[:, kc, t0:t0 + n].rearrange("p (s j) -> p s j", j=LS), tmp[:, 0:n].rearrange("p (s j) -> p s j", j=LS),
                           B[:, kc, sc].unsqueeze(2).broadcast_to([128, SPP, LS]), ALU.add, R=[ttok, mod_tok], W=[htoks[ti]])

        def proj_tile(wt, wtok, hT, htoks, ti, M=128, ring=None):
            t0, n = TILES[ti]
            ps, ptok = (ring or pbig).get()
            for kc in range(KC):
                mm(ps[0:M, 0:n], wt[:, kc, 0:M], hT[:, kc, t0:t0 + n], start=(kc == 0), stop=(kc == KC - 1),
                   R=[wtok, htoks[ti]], W=[ptok])
            return ps, ptok, n

        def proj_all(wt, wtok, hT, htoks, ring=None, M=128):
            sl3 = [(ring or pwide).get() for _ in range(3)]
            for kc in range(KC):
                for ti, (t0, n) in enumerate(TILES):
                    ps, ptok = sl3[ti]
                    mm(ps[0:M, 0:n], wt[:, kc, 0:M], hT[:, kc, t0:t0 + n], start=(kc == 0), stop=(kc == KC - 1),
                       R=[wtok, htoks[ti]], W=[ptok])
            return [(sl3[ti][0], sl3[ti][1], TILES[ti][1]) for ti in range(3)]

        def outproj_postnorm(l, ph, Wout2d, oT, otoks, G, S):
            y = sb(S, "op_y", [128, KC, 512], F32)
            ytok = Tok()
            sqt = (sb(S, "op_sq", [128, KC, 512], BF16), Tok())
            rs = (sb(S, "op_rs", [128, 512], F32), Tok())
            tmps = Ring([(sb(S, "op_t%d" % i, [128, 512], F32), Tok()) for i in range(2)])
            wres = sb(S, "op_w", [128, KC, KC, 128], BF16)
            wrtok = [retok(t) for t in wres_toks]
            for oc in range(KC):
                K.dma("pool", wres[:, oc, :, :], Wout2d[:, oc * 128:(oc + 1) * 128].rearrange("(k p) n -> p k n", p=128), W=[wrtok[oc]], semtok=wrtok[oc])
            for ti, (t0, n) in enumerate(TILES):
                for oc in range(KC):
                    ps, ptok = pwide.get()
                    for kc in range(KC):
                        mm(ps[:, 0:n], wres[:, oc, kc, :], oT[:, kc, t0:t0 + n], start=(kc == 0), stop=(kc == KC - 1),
                           R=[wrtok[oc], otoks[ti]], W=[ptok])
                    cp(y[:, oc, 0:n], ps[:, 0:n], R=[ptok], W=[ytok])
                resid_update(ph, t0, n, y, ytok, G, sqt, rs, tmps)

        def resid_update(ph, t0, n, y, ytok, G, sqt, rs, tmps):
            rstd_fm(y[:, :, 0:n], n, [ytok], sqt, rs)
            for kc in range(KC):
                tmp, ttok = tmps.get()
                tt(tmp[:, 0:n], y[:, kc, 0:n], rs[0][:, 0:n], ALU.mult, R=[ytok, rs[1]], W=[ttok])
                if t0 < TP:
                    stt(xT[:, kc, ph, t0:t0 + n], tmp[:, 0:n], G[:, kc, 0:1], xT[:, kc, ph, t0:t0 + n], ALU.mult, ALU.add,
                        R=[ttok, mod_tok], W=[xtok[ph]])
                else:
                    sc = seqcols(ph)
                    v3 = tmp[:, 0:n].rearrange("p (s j) -> p s j", j=LS)
                    tt(v3, v3, G[:, kc, sc].unsqueeze(2).broadcast_to([128, SPP, LS]), ALU.mult, R=[ttok, mod_tok], W=[ttok])
                    x3 = xT[:, kc, ph, t0:t0 + n].rearrange("p (s j) -> p s j", j=LS)
                    tt(x3, x3, v3, ALU.add, R=[ttok], W=[xtok[ph]])

        def blocks():
            return [(b * 128, 128, "P") for b in range(8)] + [(TP, 64, "S")]

        def o_post(l, o_rows, BS, h, c0, oT, otok_t, sgT, sgtok, tmpf, tmpb):
            o_ps_, o_ptok_ = o_rows if o_rows is not None else (o_ps, o_ptok)
            junk, jtok = tmpf.get()
            ss, sstok = tmpf.get()
            act(junk[0:BS, :], o_ps_[0:BS, 0:128], AF.Square, accum=ss[0:BS, 0:1], R=[o_ptok_], W=[jtok, sstok])
            act(ss[0:BS, 0:1], ss[0:BS, 0:1], AF.Ln, bias=cb[0:BS, 2:3], scale=1.0, R=[sstok, misc_tok], W=[sstok])
            act(ss[0:BS, 0:1], ss[0:BS, 0:1], AF.Exp, scale=-0.5, R=[sstok], W=[sstok])
            on, ontok = tmpb.get()
            stt(on[0:BS, :], o_ps_[0:BS, 0:128], ss[0:BS, 0:1], nrow[0:BS, l, :], ALU.mult, ALU.mult,
                R=[o_ptok_, sstok, nrow_tok], W=[ontok])
            pb, pbtok = pbf.get()
            tr(pb[:, 0:BS], on[0:BS, :], ident_bf[0:BS, 0:BS], R=[ontok, misc_tok], W=[pbtok])
            tt(oT[:, h, c0:c0 + BS], pb[:, 0:BS], sgT[:, c0:c0 + BS], ALU.mult, R=[pbtok, sgtok], W=[otok_t])

        def gdn_phase(l, ph, hT, htoks, oT, otoks, S):
            j = l // 2
            Win = gdn_w_in[j]
            stage(3)
            G = 4
            dG = sb(S, "g_dG", [8, TPH], F32)
            dtok_ = Tok()
            dtk = sb(S, "g_dtk", [128, 9, 3, 8], F32)
            ex = sb(S, "g_ex", [128, 9, 4, 8], F32)
            glb = sb(S, "g_glb", [128, H, 24], F32)
            dk_tok = Tok()
            gci = sb(S, "g_gci", [128, 24, 8, 3], F32)
            gci_tok = retok(gci_ptok)
            gco = sb(S, "g_gco", [128, 24, 9, 3], F32)
            gco_tok = retok(gco_ptok)
            K.dma("sp", gci[:].rearrange("p a s r -> p (a s r)"), gci_d[j, ph], W=[gci_tok], semtok=gci_tok)
            memset(gco[:], 0.0, W=[gco_tok])
            with contextlib.ExitStack() as SD:
                dB = sb(SD, "g_dB", [8, TPH], F32)
                dL = sb(SD, "g_dL", [8, TPH], F32)
                dg = sb(SD, "g_dg", [8, TPH], F32)
                rmask = dL
                nega = sb(SD, "g_nega", [8, 1], F32)
                memset(rmask[:], 1.0, W=[dtok_])
                memset(rmask[:, 0:TP].rearrange("p (c t) -> p c t", t=64)[:, :, 0:1], 0.0, W=[dtok_])
                memset(rmask[:, TP:TPH].rearrange("p (c t) -> p c t", t=8)[:, :, 0:1], 0.0, W=[dtok_])
                act(nega[:], prm[0:8, l, P_ALOG:P_ALOG + 1], AF.Exp, R=[prm_tok], W=[dtok_])
                ts(nega[:], nega[:], -1.0, None, ALU.mult, R=[dtok_], W=[dtok_])
                wb, wbtok = wload(Win, 4096, 8)
                for ti in range(3):
                    ps, ptok, n = proj_tile(wb, wbtok, hT, htoks, ti, M=8)
                    t0 = TILES[ti][0]
                    act(dB[:, t0:t0 + n], ps[0:8, 0:n], AF.Sigmoid, R=[ptok], W=[dtok_])
                wa, watok = wload(Win, 4104, 8)
                for ti in range(3):
                    ps, ptok, n = proj_tile(wa, watok, hT, htoks, ti, M=8)
                    t0 = TILES[ti][0]
                    act(dg[:, t0:t0 + n], ps[0:8, 0:n], AF.Exp, bias=prm[0:8, l, P_DTB:P_DTB + 1], scale=1.0, R=[ptok, prm_tok], W=[dtok_])
                act(dg[:], dg[:], AF.Ln, bias=cb[0:8, 3:4], scale=1.0, R=[dtok_, misc_tok], W=[dtok_])
                ts(dg[:], dg[:], nega[:, 0:1], None, ALU.mult, R=[dtok_], W=[dtok_])
                K.op("dve", lambda e, dG=dG, rmask=rmask, dg=dg: e.tensor_tensor_scan(out=dG[:], data0=rmask[:], data1=dg[:], initial=0.0, op0=ALU.mult, op1=ALU.add),
                     [dtok_], [dtok_])
                gp = dG[:, 0:TP].rearrange("p (c t) -> p c t", t=64)
                tt(dL[:, 0:TP].rearrange("p (c t) -> p c t", t=64), gp[:, :, 63:64].broadcast_to([8, 16, 64]), gp, ALU.subtract, R=[dtok_], W=[dtok_])
                gs = dG[:, TP:TPH].rearrange("p (c t) -> p c t", t=8)
                tt(dL[:, TP:TPH].rearrange("p (c t) -> p c t", t=8), gs[:, :, 7:8].broadcast_to([8, 8, 8]), gs, ALU.subtract, R=[dtok_], W=[dtok_])
                ps, ptok = pbig.get()
                for bi, (c0, BS, kind) in enumerate(blocks()):
                    for qi, src in enumerate((dB, dG, dL)):
                        col = (bi * 3 + qi) * 8
                        tr(ps[0:BS, col:col + 8], src[:, c0:c0 + BS], cst[0:8, C_IDENT:C_IDENT + 8], R=[dtok_, cst_tok], W=[ptok])
                memset(dtk[:], 0.0, W=[dk_tok])
                cp(dtk[:, 0:8].rearrange("p b q h -> p (b q h)"), ps[:, 0:192], R=[ptok], W=[dk_tok])
                cp(dtk[0:64, 8].rearrange("p q h -> p (q h)"), ps[0:64, 192:216], R=[ptok], W=[dk_tok])
                act(ex[:, :, 0:2, :], dtk[:, :, 1:3, :], AF.Exp, R=[dk_tok], W=[dk_tok])
                tt(ex[:, :, 2, :], dtk[:, :, 0, :], ex[:, :, 0, :], ALU.mult, R=[dk_tok], W=[dk_tok])
                ts(ex[:, :, 3, :], dtk[:, :, 0, :], -1.0, None, ALU.mult, R=[dk_tok], W=[dk_tok])
                ps, ptok = pbig.get()
                for h in range(H):
                    esel_h = cst[0:8, C_ESEL + h * 128:C_ESEL + (h + 1) * 128]
                    mm(ps[:, h * 24:h * 24 + 16], esel_h, dG[:, 0:TP].rearrange("p (c t) -> p c t", t=64)[:, :, 63], R=[dtok_, cst_tok], W=[ptok])
                    mm(ps[:, h * 24 + 16:h * 24 + 24], esel_h, dG[:, TP:TPH].rearrange("p (c t) -> p c t", t=8)[:, :, 7], R=[dtok_, cst_tok], W=[ptok])
                act(glb[:].rearrange("p h c -> p (h c)"), ps[:, 0:192], AF.Exp, R=[ptok], W=[dk_tok])
                K.snapshot()

            with contextlib.ExitStack() as S1:
                qT = sb(S1, "g_qT", [128, TPH], BF16)
                kT = sb(S1, "g_kT", [128, TPH], BF16)
                vT = sb(S1, "g_vT", [128, TPH], BF16)
                sgT = sb(S1, "g_sgT", [128, TPH], BF16)
                q_tok, k_tok, v_tok, sg_tok = Tok(), Tok(), Tok(), Tok()
                Sbf = sb(S1, "g_Sbf", [128, 128], BF16)
                Sbf_tok = Tok()
                Ss = sb(S1, "g_Ss", [128, SPP, 128], F32)
                Ss_tok = retok(Ss_ptok)
                Ssb = sb(S1, "g_Ssb", [128, SPP, 128], BF16)
                Ssb_tok = Tok()
                Sn, Sn_tok = Ss, Ss_tok
                u_sb = sb(S1, "g_usb", [128, 128], BF16)
                usb_tok = Tok()
                memset(u_sb[:], 0.0, W=[usb_tok])
                psblk = Ring([(psF[i % 4][:, (i // 4) * 128:(i // 4) * 128 + 128], bkt[i % 4]) for i in range(16)])
                obank = Ring([(psF[5], bkt[5]), (psF[4], bkt[4])])

                for h in range(H):
                    stage(4 + 0.1 * h)
                    with contextlib.ExitStack() as SPJ:
                        pre_r = Ring([(sb(SPJ, "g_pre%d" % i, [128, 3 + TP + SPP * 11], F32), T()) for i in range(2)])
                        cv_r = Ring([(sb(SPJ, "g_cv%d" % i, [128, TPH], F32), T()) for i in range(2)])
                        sq_r = Ring([(sb(SPJ, "g_sqb%d" % i, [128, 512], BF16), T()) for i in range(2)])
                        ri_r = Ring([(sb(SPJ, "g_rinv%d" % i, [128, 512], F32), T()) for i in range(2)])
                        def role_a(role):
                            wt, wtok = wload(Win, role * 1024 + h * 128)
                            c = None
                            if role < 3:
                                pre, pre_tok = pre_r.get()
                                pv = pre[:, 3 + TP:3 + TP + SPP * 11].rearrange("p (s r) -> p s r", r=11)
                                ch = role * 8 + h
                                if ph == 0:
                                    memset(pre[:, 0:3], 0.0, W=[pre_tok])
                                else:
                                    cp(pre[:, 0:3], gcar[:, ch, :], R=[gcar_tok], W=[pre_tok])
                                cp(pv[:, :, 0:3], gci[:, ch, :, :], R=[gci_tok], W=[pre_tok])
                                c = (pre, pre_tok, pv, ch)
                            pall = proj_all(wt, wtok, hT, htoks)
                            for ti in range(3):
                                ps, ptok, n = pall[ti]
                                t0 = TILES[ti][0]
                                if role == 3:
                                    act(sgT[:, t0:t0 + n], ps[:, 0:n], AF.Silu, R=[ptok], W=[sg_tok])
                                elif ti < 2:
                                    act(pre[:, 3 + t0:3 + t0 + n], ps[:, 0:n], AF.Copy, R=[ptok], W=[pre_tok])
                                else:
                                    act(pv[:, :, 3:11], ps[:, 0:n].rearrange("p (s r) -> p s r", r=LS), AF.Copy, R=[ptok], W=[pre_tok])
                            return c

                        def role_b(role, c):
                            pre, pre_tok, pv, ch = c
                            cv, cv_tok = cv_r.get()
                            if ph == 0:
                                cp(gcar[:, ch, :], pre[:, TP:TP + 3], R=[pre_tok], W=[gcar_tok])
                            else:
                                cp(gco[:, ch, 8, :], pre[:, TP:TP + 3], R=[pre_tok], W=[gco_tok])
                            cp(gco[:, ch, 0:8, :], pv[:, :, 8:11], R=[pre_tok], W=[gco_tok])
                            wc = prm[:, l, P_GCW + ch * 4:P_GCW + ch * 4 + 4]
                            bc = prm[:, l, P_GCB + ch:P_GCB + ch + 1]
                            cvs = cv[:, TP:TPH].rearrange("p (s r) -> p s r", r=LS)
                            ts(cv[:, 0:TP], pre[:, 0:TP], wc[:, 0:1], bc, ALU.mult, ALU.add, R=[pre_tok, prm_tok], W=[cv_tok])
                            ts(cvs, pv[:, :, 0:8], wc[:, 0:1], bc, ALU.mult, ALU.add, R=[pre_tok, prm_tok], W=[cv_tok])
                            for tap in range(1, 4):
                                stt(cv[:, 0:TP], pre[:, tap:tap + TP], wc[:, tap:tap + 1], cv[:, 0:TP], ALU.mult, ALU.add,
                                    R=[pre_tok, prm_tok], W=[cv_tok])
                                stt(cvs, pv[:, :, tap:tap + 8], wc[:, tap:tap + 1], cvs, ALU.mult, ALU.add, R=[pre_tok, prm_tok], W=[cv_tok])
                            if role == 2:
                                act(vT[:], cv[:], AF.Silu, R=[cv_tok], W=[v_tok])
                                return
                            act(cv[:], cv[:], AF.Silu, R=[cv_tok], W=[cv_tok])
                            for ti, (t0, n) in enumerate(TILES):
                                sqb, sqb_tok = sq_r.get()
                                rinv, rinv_tok = ri_r.get()
                                act(sqb[:, 0:n], cv[:, t0:t0 + n], AF.Square, R=[cv_tok], W=[sqb_tok])
                                ps, ptok = pwide.get()
                                mm(ps[:, 0:n], ones_bf[:], sqb[:, 0:n], R=[sqb_tok, misc_tok], W=[ptok])
                                act(rinv[:, 0:n], ps[:, 0:n], AF.Ln, bias=cb[:, 1:2], scale=1.0, R=[ptok, misc_tok], W=[rinv_tok])
                                act(rinv[:, 0:n], rinv[:, 0:n], AF.Exp, scale=-0.5, R=[rinv_tok], W=[rinv_tok])
                                if role == 0:
                                    stt(qT[:, t0:t0 + n], cv[:, t0:t0 + n], float(128.0 ** -0.5), rinv[:, 0:n], ALU.mult, ALU.mult,
                                        R=[cv_tok, rinv_tok], W=[q_tok])
                                else:
                                    tt(kT[:, t0:t0 + n], cv[:, t0:t0 + n], rinv[:, 0:n], ALU.mult, R=[cv_tok, rinv_tok], W=[k_tok])

                        c0_ = role_a(0)
                        c1_ = role_a(1)
                        role_b(0, c0_)
                        c2_ = role_a(2)
                        role_b(1, c1_)
                        role_a(3)
                        role_b(2, c2_)
                        K.snapshot()

                    stage(4.05 + 0.1 * h)
                    K.dma("pool", Ss[:], sgdn_d[j, SPP * ph:SPP * ph + SPP, h].rearrange("s k v -> k s v"), W=[Ss_tok], semtok=Ss_tok)
                    act(Ssb[:], Ss[:], AF.Copy, R=[Ss_tok], W=[Ssb_tok])
                    if ph == 0:
                        memset(Sf_all[:, h, :], 0.0, W=[Sf_tok[h]])
                    cp(Sbf[:], Sf_all[:, h, :], R=[Sf_tok[h]], W=[Sbf_tok])

                    with contextlib.ExitStack() as SBK:
                        slots = []
                        for k in range(G):
                            sl = {}
                            sl["X"] = [(sb(SBK, "g_x%d_%d" % (k, i), [128, 128], F32), T()) for i in range(6)]
                            sl["kbg"] = (sb(SBK, "g_kbg%d" % k, [128, 128], BF16), T())
                            sl["HS"] = [[(sb(SBK, "g_hs%d_%d_%d" % (k, par, i), [128, 128], BF16), T()) for i in range(6)] for par in range(2)]
                            slots.append(sl)
                        spad = {}
                        for nm in ("Kpad", "Wpad", "Qpad", "tw"):
                            spad[nm] = (sb(SBK, "g_s%s" % nm, [128, 8, 128], BF16), T())
                        spad["kdm"] = (sb(SBK, "g_skdm", [128, 8], F32), T())
                        opf = Ring([(sb(SBK, "g_of%d" % i, [128, 128], F32), T()) for i in range(4)])
                        opb = Ring([(sb(SBK, "g_ob%d" % i, [128, 128], BF16), T()) for i in range(2)])

                        def pre_state(bi, c0, BS, kind, sl, par):
                            if kind == "P":
                                ncb, C, nlev = 2, 64, 5
                                pen = cst[:, C_PENP:C_PENP + 128]
                            else:
                                ncb, C, nlev = 8, 8, 2
                                pen = cst[0:64, C_PENS:C_PENS + 64]
                                cmask = cst[:, C_CM8:C_CM8 + 512].rearrange("p (c t) -> p c t", c=8)
                                rmk = cst[0:64, C_RM8:C_RM8 + 8]
                            cols = slice(c0, c0 + BS)
                            X = sl["X"]
                            HS = sl["HS"][par]
                            Gcol = dtk[0:BS, bi, 1, h:h + 1]
                            beta_c = dtk[0:BS, bi, 0, h:h + 1]
                            eGL_c = ex[0:BS, bi, 1, h:h + 1]
                            bG_c = ex[0:BS, bi, 2, h:h + 1]
                            nbeta_c = ex[0:BS, bi, 3, h:h + 1]
                            esel_h = cst[0:8, C_ESEL + h * 128:C_ESEL + (h + 1) * 128]
                            gps, gptok = psblk.get()
                            mm(gps[:, 0:BS], esel_h, dG[:, cols], R=[dtok_, cst_tok], W=[gptok])
                            yield
                            r_, rtok = X[0]
                            stt(r_[0:BS, 0:BS], gps[0:BS, 0:BS], Gcol, pen, ALU.subtract, ALU.max, R=[gptok, dk_tok, cst_tok], W=[rtok])
                            eGr, egtok = X[2]
                            act(eGr[:, 0:BS], gps[:, 0:BS], AF.Exp, R=[gptok], W=[egtok])
                            yield
                            Ds, dstok = X[1]
                            act(Ds[0:BS, 0:BS], r_[0:BS, 0:BS], AF.Exp, scale=-1.0, R=[rtok], W=[dstok])
                            if kind == "P":
                                qg, qgtok = HS[3]
                                tt(qg[:, 0:BS], qT[:, cols], eGr[:, 0:BS], ALU.mult, R=[q_tok, egtok], W=[qgtok])
                            else:
                                tw, twtok = spad["tw"]
                                Qpad, qp_tok = spad["Qpad"]
                                tt(tw[:, 0:ncb, 0:BS], cmask[:, :, 0:BS], eGr[:, 0:BS].unsqueeze(1).broadcast_to([128, ncb, BS]), ALU.mult,
                                   R=[cst_tok, egtok], W=[twtok])
                                tt(Qpad[:, 0:ncb, 0:BS], tw[:, 0:ncb, 0:BS], qT[:, cols].unsqueeze(1).broadcast_to([128, ncb, BS]), ALU.mult,
                                   R=[twtok, q_tok], W=[qp_tok])
                            yield
                            dps, dptok = psblk.get()
                            tr(dps[0:BS, 0:BS], Ds[0:BS, 0:BS], ident[0:BS, 0:BS], R=[dstok, cst_tok], W=[dptok])
                            kkps, kktok = psblk.get()
                            mm(kkps[0:BS, 0:BS], kT[:, cols], kT[:, cols], R=[k_tok], W=[kktok])
                            qkps, qktok = psblk.get()
                            mm(qkps[0:BS, 0:BS], kT[:, cols], qT[:, cols], R=[k_tok, q_tok], W=[qktok])
                            yield
                            DmT, dmtok = X[0]
                            tt(DmT[0:BS, 0:BS], dps[0:BS, 0:BS], ident[0:BS, 0:BS], ALU.add, R=[dptok, cst_tok], W=[dmtok])
                            B, btok = X[2]
                            stt(B[0:BS, 0:BS], kkps[0:BS, 0:BS], nbeta_c, Ds[0:BS, 0:BS], ALU.mult, ALU.mult, R=[kktok, dk_tok, dstok], W=[btok])
                            yield
                            attnT, attok = HS[0]
                            tt(attnT[0:BS, 0:BS], qkps[0:BS, 0:BS], DmT[0:BS, 0:BS], ALU.mult, R=[qktok, dmtok], W=[attok])
                            btps, btptok = psblk.get()
                            tr(btps[0:BS, 0:BS], B[0:BS, 0:BS], ident[0:BS, 0:BS], R=[btok, cst_tok], W=[btptok])
                            yield
                            Bt, bttok = X[3]
                            act(Bt[0:BS, 0:BS], btps[0:BS, 0:BS], AF.Copy, R=[btptok], W=[bttok])
                            TT, tttok = X[4]
                            tt(TT[0:BS, 0:BS], btps[0:BS, 0:BS], ident[0:BS, 0:BS], ALU.add, R=[btptok, cst_tok], W=[tttok])
                            yield
                            cur = (X[2], X[3])
                            nxt = (X[5], X[1])
                            for lev in range(nlev):
                                last = (lev == nlev - 1)
                                (B, btok), (Bt, bttok) = cur
                                (Bn, bntok), (Btn, btntok) = nxt
                                p2, p2tok = psblk.get()
                                mmr(p2[0:BS, 0:BS], Bt[0:BS, 0:BS], B[0:BS, 0:BS], R=[bttok, btok], W=[p2tok])
                                if not last:
                                    p1, p1tok = psblk.get()
                                    mmr(p1[0:BS, 0:BS], B[0:BS, 0:BS], Bt[0:BS, 0:BS], R=[bttok, btok], W=[p1tok])
                                yield
                                cp(Bn[0:BS, 0:BS], p2[0:BS, 0:BS], R=[p2tok], W=[bntok])
                                if not last:
                                    act(Btn[0:BS, 0:BS], p1[0:BS, 0:BS], AF.Copy, R=[p1tok], W=[btntok])
                                yield
                                p3, p3tok = psblk.get()
                                mmr(p3[0:BS, 0:BS], Bn[0:BS, 0:BS], TT[0:BS, 0:BS], R=[bntok, tttok], W=[p3tok])
                                yield
                                tt(TT[0:BS, 0:BS], p3[0:BS, 0:BS], TT[0:BS, 0:BS], ALU.add, R=[p3tok, tttok], W=[tttok])
                                yield
                                cur, nxt = nxt, cur
                            TTb, ttbtok = HS[1]
                            act(TTb[0:BS, 0:BS], TT[0:BS, 0:BS], AF.Copy, R=[tttok], W=[ttbtok])
                            pk, pktok = pbf.get()
                            tr(pk[0:BS, :], kT[:, cols], ident_bf[:], R=[k_tok, misc_tok], W=[pktok])
                            pvv, pvtok = pbf.get()
                            tr(pvv[0:BS, :], vT[:, cols], ident_bf[:], R=[v_tok, misc_tok], W=[pvtok])
                            yield
                            vb, vbtok = HS[2]
                            act(vb[0:BS, :], pvv[0:BS, :], AF.Copy, scale=beta_c, R=[pvtok, dk_tok], W=[vbtok])
                            kbg, kbgtok = sl["kbg"]
                            act(kbg[0:BS, :], pk[0:BS, :], AF.Copy, scale=bG_c, R=[pktok, dk_tok], W=[kbgtok])
                            if kind == "P":
                                kd, kdtok = HS[5]
                                act(kd[0:BS, :], pk[0:BS, :], AF.Copy, scale=eGL_c, R=[pktok, dk_tok], W=[kdtok])
                            else:
                                kdm, kdmtok = spad["kdm"]
                                Kpad, kp_tok = spad["Kpad"]
                                ts(kdm[0:BS, 0:ncb], rmk, eGL_c, None, ALU.mult, R=[cst_tok, dk_tok], W=[kdmtok])
                                tt(Kpad[0:BS, 0:ncb, :], pk[0:BS, :].unsqueeze(1).broadcast_to([BS, ncb, 128]),
                                   kdm[0:BS, 0:ncb].unsqueeze(2).broadcast_to([BS, ncb, 128]), ALU.mult, R=[pktok, kdmtok], W=[kp_tok])
                            yield
                            wps, wptok = psblk.get()
                            mm(wps[:, 0:BS], kbg[0:BS, :], TTb[0:BS, 0:BS], R=[kbgtok, ttbtok], W=[wptok])
                            yield
                            if kind == "P":
                                Wneg, wntok = HS[4]
                                act(Wneg[:, 0:BS], wps[:, 0:BS], AF.Copy, scale=-1.0, R=[wptok], W=[wntok])
                            else:
                                Wpad, wp_tok = spad["Wpad"]
                                stt(Wpad[:, 0:ncb, 0:BS], wps[:, 0:BS].unsqueeze(1).broadcast_to([128, ncb, BS]), -1.0, cmask[:, :, 0:BS],
                                    ALU.mult, ALU.mult, R=[wptok, cst_tok], W=[wp_tok])

                        pend = [None]

                        def flush_post():
                            if pend[0] is not None:
                                o_post(*pend[0])
                                pend[0] = None

                        def state_group(grp, par):
                            for k, bi in enumerate(grp):
                                c0, BS, kind = blks[bi]
                                sl = slots[k]
                                HS = sl["HS"][par]
                                attnT, attok = HS[0]
                                TTb, ttbtok = HS[1]
                                vb, vbtok = HS[2]
                                ops_, optok = obank.get()
                                if kind == "P":
                                    qg, qgtok = HS[3]
                                    Wneg, wntok = HS[4]
                                    kd, kdtok = HS[5]
                                    for c in range(2):
                                        rows = slice(c * 64, (c + 1) * 64)
                                        mm(u_ps[rows, 0:128], TTb[rows, rows], vb[rows, :], start=True, stop=False, R=[ttbtok, vbtok], W=[u_ptok])
                                        mm(u_ps[rows, 0:128], Wneg[:, rows], Sbf[:], start=False, stop=True, R=[wntok, Sbf_tok], W=[u_ptok])
                                        act(u_sb[rows, :], u_ps[rows, 0:128], AF.Copy, R=[u_ptok], W=[usb_tok])
                                        mm(ops_[rows, 0:128], qg[:, rows], Sbf[:], start=True, stop=False, R=[qgtok, Sbf_tok], W=[optok])
                                        mm(ops_[rows, 0:128], attnT[rows, rows], u_sb[rows, :], start=False, stop=True, R=[attok, usb_tok], W=[optok])
                                        sps, sptok = psblk.get()
                                        mm(sps[:, :], kd[rows, :], u_sb[rows, :], R=[kdtok, usb_tok], W=[sptok])
                                        gl = glb[:, h, (bi * 2 + c):(bi * 2 + c) + 1]
                                        stt(Sbf[:], Sf_all[:, h, :], gl, sps[:, :], ALU.mult, ALU.add, R=[sptok, dk_tok, Sf_tok[h]], W=[Sbf_tok])
                                        stt(Sf_all[:, h, :], Sf_all[:, h, :], gl, sps[:, :], ALU.mult, ALU.add, R=[sptok, dk_tok], W=[Sf_tok[h]])
                                        yield
                                else:
                                    ncb = 8
                                    Kpad, kp_tok = spad["Kpad"]
                                    Wpad, wp_tok = spad["Wpad"]
                                    Qpad, qp_tok = spad["Qpad"]
                                    mm(u_ps[0:BS, 0:128], TTb[0:BS, 0:BS], vb[0:BS, :], start=True, stop=False, R=[ttbtok, vbtok], W=[u_ptok])
                                    for c in range(ncb):
                                        mm(u_ps[0:BS, 0:128], Wpad[:, c, 0:BS], Ssb[:, c, :], start=False, stop=(c == ncb - 1), R=[wp_tok, Ssb_tok], W=[u_ptok])
                                    act(u_sb[0:BS, :], u_ps[0:BS, 0:128], AF.Copy, R=[u_ptok], W=[usb_tok])
                                    for c in range(ncb):
                                        mm(ops_[0:BS, 0:128], Qpad[:, c, 0:BS], Ssb[:, c, :], start=(c == 0), stop=False, R=[qp_tok, Ssb_tok], W=[optok])
                                    mm(ops_[0:BS, 0:128], attnT[0:BS, 0:BS], u_sb[0:BS, :], start=False, stop=True, R=[attok, usb_tok], W=[optok])
                                    sl2 = [pbig.get(), pbig.get()]
                                    for c in range(ncb):
                                        ps, ptok = sl2[c // 4]
                                        mm(ps[:, (c % 4) * 128:(c % 4) * 128 + 128], Kpad[0:BS, c, :], u_sb[0:BS, :], R=[kp_tok, usb_tok], W=[ptok])
                                    tt(Sn[:], Ss[:], glb[:, h, 16:24].unsqueeze(2).broadcast_to([128, 8, 128]), ALU.mult, R=[Ss_tok, dk_tok], W=[Sn_tok])
                                    for hb in range(2):
                                        ps, ptok = sl2[hb]
                                        tt(Sn[:, hb * 4:hb * 4 + 4, :], Sn[:, hb * 4:hb * 4 + 4, :], ps[:, :].rearrange("p (s v) -> p s v", v=128), ALU.add,
                                           R=[ptok], W=[Sn_tok])
                                    K.dma("sp", sgdn_o[j, SPP * ph:SPP * ph + SPP, h].rearrange("s k v -> k s v"), Sn[:], R=[Sn_tok], semtok=Sn_tok)
                                flush_post()
                                pend[0] = (l, (ops_, optok), BS, h, c0, oT, otoks[min(c0 // 512, 2)], sgT, sg_tok, opf, opb)
                                yield

                        def run_rr(gens):
                            alive = list(gens)
                            while alive:
                                nx = []
                                for g in alive:
                                    try:
                                        next(g)
                                        nx.append(g)
                                    except StopIteration:
                                        pass
                                alive = nx

                        blks = blocks()
                        groups = [[0, 1, 2, 3], [4, 5, 6, 7], [8]]
                        stage(4.06 + 0.1 * h)
                        run_rr([pre_state(bi, blks[bi][0], blks[bi][1], blks[bi][2], slots[k], 0) for k, bi in enumerate(groups[0])])
                        run_rr([state_group(groups[0], 0)] +
                               [pre_state(bi, blks[bi][0], blks[bi][1], blks[bi][2], slots[k], 1) for k, bi in enumerate(groups[1])])
                        run_rr([state_group(groups[1], 1)] +
                               [pre_state(8, blks[8][0], blks[8][1], blks[8][2], slots[0], 0)])
                        run_rr([state_group(groups[2], 0)])
                        flush_post()
                        if ph == 1:
                            K.dma("sp", pgdn_d[j, h], Sf_all[:, h, :], R=[Sf_tok[h]], semtok=Sf_tok[h])
                        K.snapshot()
                K.dma("sp", gco_d[j, ph], gco[:].rearrange("p a s r -> p (a s r)"), R=[gco_tok], semtok=gco_tok)
                K.snapshot()

        def hgrn_phase(l, ph, hT, htoks, oT, otoks, S):
            j = l // 2
            Win = hgrn_w_in[j]
            with contextlib.ExitStack() as S1:
                rmask = sb(S1, "h_rm", [128, TPH], F32)
                rm_tok = Tok()
                eG = sb(S1, "h_eG", [128, TPH], F32)
                eG_tok = Tok()
                qgT = sb(S1, "h_qgT", [128, TPH], BF16)
                kiT = sb(S1, "h_kiT", [128, TPH], BF16)
                kdT = sb(S1, "h_kdT", [128, TPH], BF16)
                vT = sb(S1, "h_vT", [128, TPH], BF16)
                sgT = sb(S1, "h_sgT", [128, TPH], BF16)
                qg_tok, ki_tok, kd_tok, v_tok, sg_tok = Tok(), Tok(), Tok(), Tok(), Tok()
                Sbf = sb(S1, "h_Sbf", [128, 128], BF16)
                Sbf_tok = Tok()
                Ss = sb(S1, "h_Ss", [128, SPP, 128], F32)
                Ss_tok = retok(Ss_ptok)
                Ssb = sb(S1, "h_Ssb", [128, SPP, 128], BF16)
                Ssb_tok = Tok()
                Sn, Sn_tok = Ss, Ss_tok
                memset(rmask[:], 1.0, W=[rm_tok])
                memset(rmask[:, 0:TP].rearrange("p (c t) -> p c t", t=32)[:, :, 0:1], 0.0, W=[rm_tok])
                memset(rmask[:, TP:TPH].rearrange("p (c t) -> p c t", t=8)[:, :, 0:1], 0.0, W=[rm_tok])
                psblk = Ring([(psF[i % 5][:, (i // 5) * 128:(i // 5) * 128 + 128], bkt[i % 5]) for i in range(20)])
                obank = Ring([(psF[5], bkt[5]), (psF[6], bkt[6])])
                for h in range(H):
                    stage(7 + 0.1 * h)
                    lb = lbT[:, h, 0:1]
                    oml = lbT[:, h, 1:2]
                    with contextlib.ExitStack() as SPL:
                        fa = sb(SPL, "h_fa", [128, TPH], F32)
                        fa_tok = T()
                        fk = sb(SPL, "h_fk", [128, TPH], F32)
                        fk_tok = T()
                        fG = sb(SPL, "h_fG", [128, TPH], F32)
                        fG_tok = T()
                        fe = sb(SPL, "h_fe", [128, TPH], F32)
                        fe_tok = T()
                        sq_ = sb(SPL, "h_sq", [128, TPH], F32)
                        sq_tok = T()
                        for role in range(4):
                            wt, wtok = wload(Win, role * 1024 + h * 128)
                            pall = proj_all(wt, wtok, hT, htoks)
                            for ti in range(3):
                                ps, ptok, n = pall[ti]
                                t0 = TILES[ti][0]
                                if role == 0:
                                    act(sq_[:, t0:t0 + n], ps[:, 0:n], AF.Silu, R=[ptok], W=[sq_tok])
                                elif role == 1:
                                    act(fa[:, t0:t0 + n], ps[:, 0:n], AF.Sigmoid, R=[ptok], W=[fa_tok])
                                elif role == 2:
                                    act(vT[:, t0:t0 + n], ps[:, 0:n], AF.Copy, R=[ptok], W=[v_tok])
                                else:
                                    act(sgT[:, t0:t0 + n], ps[:, 0:n], AF.Silu, R=[ptok], W=[sg_tok])
                        ts(fa[:], fa[:], oml, lb, ALU.mult, ALU.add, R=[fa_tok, lb_tok], W=[fa_tok])
                        ts(fk[:], fa[:], -1.0, 1.0, ALU.mult, ALU.add, R=[fa_tok], W=[fk_tok])
                        act(fa[:], fa[:], AF.Ln, R=[fa_tok], W=[fa_tok])
                        K.op("dve", lambda e, fG=fG, fa=fa: e.tensor_tensor_scan(out=fG[:], data0=rmask[:], data1=fa[:], initial=0.0, op0=ALU.mult, op1=ALU.add),
                             [fa_tok, rm_tok], [fG_tok])
                        act(eG[:], fG[:], AF.Exp, R=[fG_tok], W=[eG_tok])
                        tt(qgT[:], sq_[:], eG[:], ALU.mult, R=[sq_tok, eG_tok], W=[qg_tok])
                        act(fe[:], fG[:], AF.Exp, scale=-1.0, R=[fG_tok], W=[fe_tok])
                        tt(kiT[:], fk[:], fe[:], ALU.mult, R=[fk_tok, fe_tok], W=[ki_tok])
                        gp = fG[:, 0:TP].rearrange("p (c t) -> p c t", t=32)
                        tt(fa[:, 0:TP].rearrange("p (c t) -> p c t", t=32), gp[:, :, 31:32].broadcast_to([128, 32, 32]), gp, ALU.subtract,
                           R=[fG_tok], W=[fa_tok])
                        gs = fG[:, TP:TPH].rearrange("p (c t) -> p c t", t=8)
                        tt(fa[:, TP:TPH].rearrange("p (c t) -> p c t", t=8), gs[:, :, 7:8].broadcast_to([128, 8, 8]), gs, ALU.subtract,
                           R=[fG_tok], W=[fa_tok])
                        act(fe[:], fa[:], AF.Exp, R=[fa_tok], W=[fe_tok])
                        tt(kdT[:], fk[:], fe[:], ALU.mult, R=[fk_tok, fe_tok], W=[kd_tok])
                        K.snapshot()

                    stage(7.05 + 0.1 * h)
                    K.dma("pool", Ss[:], shgrn_d[j, SPP * ph:SPP * ph + SPP, h].rearrange("s k v -> k s v"), W=[Ss_tok], semtok=Ss_tok)
                    act(Ssb[:], Ss[:], AF.Copy, R=[Ss_tok], W=[Ssb_tok])
                    if ph == 0:
                        memset(Sf_all[:, h, :], 0.0, W=[Sf_tok[h]])
                    cp(Sbf[:], Sf_all[:, h, :], R=[Sf_tok[h]], W=[Sbf_tok])

                    with contextlib.ExitStack() as SBK:
                        blks = blocks()
                        bufs = []
                        for bi, (c0, BS, kind) in enumerate(blks):
                            npad = 4 if kind == "P" else 8
                            bufs.append(dict(
                                attnT=(sb(SBK, "h_at%d" % bi, [128, 128], BF16), T()),
                                vtk=(sb(SBK, "h_vt%d" % bi, [128, 128], BF16), T()),
                                Kpad=(sb(SBK, "h_kp%d" % bi, [128, npad, 128], BF16), T()),
                                Qpad=(sb(SBK, "h_qp%d" % bi, [128, npad, 128], BF16), T())))
                        opf = Ring([(sb(SBK, "h_of%d" % i, [128, 128], F32), T()) for i in range(4)])
                        opb = Ring([(sb(SBK, "h_ob%d" % i, [128, 128], BF16), T()) for i in range(2)])

                        def consts_for(kind):
                            if kind == "P":
                                return (4, 32, cst[:, C_MIP:C_MIP + 128],
                                        cst[:, C_CM4:C_CM4 + 512].rearrange("p (c t) -> p c t", c=4), cst[:, C_RM4:C_RM4 + 4])
                            return (8, 8, cst[0:64, C_MIS:C_MIS + 64],
                                    cst[:, C_CM8:C_CM8 + 512].rearrange("p (c t) -> p c t", c=8), cst[0:64, C_RM8:C_RM8 + 8])

                        for bi, (c0, BS, kind) in enumerate(blks):
                            ncb, C, mi, cmask, rmk = consts_for(kind)
                            bf = bufs[bi]
                            cols = slice(c0, c0 + BS)
                            aps, aptok = psblk.get()
                            mm(aps[0:BS, 0:BS], kiT[:, cols], qgT[:, cols], R=[ki_tok, qg_tok], W=[aptok])
                            attnT, attok = bf["attnT"]
                            tt(attnT[0:BS, 0:BS], aps[0:BS, 0:BS], mi, ALU.mult, R=[aptok, cst_tok], W=[attok])
                            pvv, pvtok = pbf.get()
                            tr(pvv[0:BS, :], vT[:, cols], ident_bf[:], R=[v_tok, misc_tok], W=[pvtok])
                            vtk, vtktok = bf["vtk"]
                            act(vtk[0:BS, :], pvv[0:BS, :], AF.Copy, R=[pvtok], W=[vtktok])
                            pk, pktok = pbf.get()
                            tr(pk[0:BS, :], kdT[:, cols], ident_bf[:], R=[kd_tok, misc_tok], W=[pktok])
                            Kpad, kp_tok = bf["Kpad"]
                            tt(Kpad[0:BS, 0:ncb, :], pk[0:BS, :].unsqueeze(1).broadcast_to([BS, ncb, 128]),
                               rmk.unsqueeze(2).broadcast_to([BS, ncb, 128]), ALU.mult, R=[pktok, cst_tok], W=[kp_tok])
                            Qpad, qp_tok = bf["Qpad"]
                            tt(Qpad[:, 0:ncb, 0:BS], cmask[:, :, 0:BS], qgT[:, cols].unsqueeze(1).broadcast_to([128, ncb, BS]), ALU.mult,
                               R=[cst_tok, qg_tok], W=[qp_tok])

                        pending = None
                        for bi, (c0, BS, kind) in enumerate(blks):
                            ncb, C, mi, cmask, rmk = consts_for(kind)
                            bf = bufs[bi]
                            attnT, attok = bf["attnT"]
                            vtk, vtktok = bf["vtk"]
                            Kpad, kp_tok = bf["Kpad"]
                            Qpad, qp_tok = bf["Qpad"]
                            ops_, optok = obank.get()
                            if kind == "P":
                                for c in range(ncb):
                                    mm(ops_[0:BS, 0:128], Qpad[:, c, 0:BS], Sbf[:], start=(c == 0), stop=False, R=[qp_tok, Sbf_tok], W=[optok])
                                    sps, sptok = psblk.get()
                                    mm(sps[:, :], Kpad[0:BS, c, :], vtk[0:BS, :], R=[kp_tok, vtktok], W=[sptok])
                                    ce = c0 + (c + 1) * C - 1
                                    stt(Sbf[:], Sf_all[:, h, :], eG[:, ce:ce + 1], sps[:, :], ALU.mult, ALU.add,
                                        R=[sptok, eG_tok, Sf_tok[h]], W=[Sbf_tok])
                                    stt(Sf_all[:, h, :], Sf_all[:, h, :], eG[:, ce:ce + 1], sps[:, :], ALU.mult, ALU.add,
                                        R=[sptok, eG_tok], W=[Sf_tok[h]])
                                mm(ops_[0:BS, 0:128], attnT[0:BS, 0:BS], vtk[0:BS, :], start=False, stop=True, R=[attok, vtktok], W=[optok])
                            else:
                                for c in range(ncb):
                                    mm(ops_[0:BS, 0:128], Qpad[:, c, 0:BS], Ssb[:, c, :], start=(c == 0), stop=False, R=[qp_tok, Ssb_tok], W=[optok])
                                mm(ops_[0:BS, 0:128], attnT[0:BS, 0:BS], vtk[0:BS, :], start=False, stop=True, R=[attok, vtktok], W=[optok])
                                sl2 = [pbig.get(), pbig.get()]
                                for c in range(ncb):
                                    ps, ptok = sl2[c // 4]
                                    mm(ps[:, (c % 4) * 128:(c % 4) * 128 + 128], Kpad[0:BS, c, :], vtk[0:BS, :], R=[kp_tok, vtktok], W=[ptok])
                                ege = eG[:, TP:TPH].rearrange("p (s t) -> p s t", t=8)[:, :, 7:8]
                                tt(Sn[:], Ss[:], ege.broadcast_to([128, 8, 128]), ALU.mult, R=[Ss_tok, eG_tok], W=[Sn_tok])
                                for hb in range(2):
                                    ps, ptok = sl2[hb]
                                    tt(Sn[:, hb * 4:hb * 4 + 4, :], Sn[:, hb * 4:hb * 4 + 4, :], ps[:, :].rearrange("p (s v) -> p s v", v=128), ALU.add,
                                       R=[ptok], W=[Sn_tok])
                                K.dma("sp", shgrn_o[j, SPP * ph:SPP * ph + SPP, h].rearrange("s k v -> k s v"), Sn[:], R=[Sn_tok], semtok=Sn_tok)
                            if pending is not None:
                                o_post(*pending)
                            pending = (l, (ops_, optok), BS, h, c0, oT, otoks[min(c0 // 512, 2)], sgT, sg_tok, opf, opb)
                        o_post(*pending)
                        if ph == 1:
                            K.dma("sp", phgrn_d[j, h], Sf_all[:, h, :], R=[Sf_tok[h]], semtok=Sf_tok[h])
                        K.snapshot()
                K.snapshot()

        pw5 = Ring([(psF[i], bkt[i]) for i in range(5)])

        def ffn_phase(l, ph, want_ada=False):
            agen = None
            modA = modsets[l % 2]
            A2, B2, G2 = modA[3], modA[4], modA[5]
            with contextlib.ExitStack() as SF:
                aT = sb(SF, "f_aT", [128, NFF, TPH], BF16)
                a_toks = [Tok() for _ in range(3)]
                fci = sb(SF, "f_fci", [128, NFF, 8, 2], F32)
                fci_tok = retok(fci_ptok)
                fco = sb(SF, "f_fco", [128, NFF, 9, 2], F32)
                fco_tok = retok(fco_ptok)
                K.dma("sp", fci[:].rearrange("p a s r -> p (a s r)"), fci_d[l, ph], W=[fci_tok], semtok=fci_tok)
                memset(fco[:], 0.0, W=[fco_tok])
                with contextlib.ExitStack() as S1:
                    hT = sb(S1, "f_hT", [128, KC, TPH], BF16)
                    htoks = [Tok() for _ in range(3)]
                    with contextlib.ExitStack() as SPN:
                        prenorm(ph, A2, B2, hT, htoks, SPN)
                        K.snapshot()
                    gpre_r = Ring([(sb(S1, "f_gpre%d" % i, [128, 2 + TP + SPP * 10], F32), Tok()) for i in range(2)])
                    up_r = Ring([(sb(S1, "f_up%d" % i, [128, TPH], F32), Tok()) for i in range(2)])
                    gc_r = Ring([(sb(S1, "f_gc%d" % i, [128, TPH], F32), Tok()) for i in range(2)])
                    if want_ada:
                        agen = adaln_gen(l + 1, S1)
                    def part_a(jj):
                        gpre, gp_tok = gpre_r.get()
                        up, up_tok = up_r.get()
                        gc, gc_tok = gc_r.get()
                        pv = gpre[:, 2 + TP:2 + TP + SPP * 10].rearrange("p (s r) -> p s r", r=10)
                        wg, wgtok = wload(ffn_w_gu[l], jj * 128)
                        wu, wutok = wload(ffn_w_gu[l], DFF + jj * 128)
                        if ph == 0:
                            memset(gpre[:, 0:2], 0.0, W=[gp_tok])
                        else:
                            cp(gpre[:, 0:2], fcar[:, jj, :], R=[fcar_tok], W=[gp_tok])
                        cp(pv[:, :, 0:2], fci[:, jj, :, :], R=[fci_tok], W=[gp_tok])
                        pg = proj_all(wg, wgtok, hT, htoks)
                        for ti in range(3):
                            t0 = TILES[ti][0]
                            ps, ptok, n = pg[ti]
                            if ti < 2:
                                act(gpre[:, 2 + t0:2 + t0 + n], ps[:, 0:n], AF.Copy, R=[ptok], W=[gp_tok])
                            else:
                                act(pv[:, :, 2:10], ps[:, 0:n].rearrange("p (s r) -> p s r", r=LS), AF.Copy, R=[ptok], W=[gp_tok])
                        pu = proj_all(wu, wutok, hT, htoks)
                        for ti in range(3):
                            t0 = TILES[ti][0]
                            ps, ptok, n = pu[ti]
                            act(up[:, t0:t0 + n], ps[:, 0:n], AF.Copy, R=[ptok], W=[up_tok])
                        return (gpre, gp_tok, up, up_tok, gc, gc_tok, pv)

                    def part_b(jj, c):
                        gpre, gp_tok, up, up_tok, gc, gc_tok, pv = c
                        if ph == 0:
                            cp(fcar[:, jj, :], gpre[:, TP:TP + 2], R=[gp_tok], W=[fcar_tok])
                        else:
                            cp(fco[:, jj, 8, :], gpre[:, TP:TP + 2], R=[gp_tok], W=[fco_tok])
                        cp(fco[:, jj, 0:8, :], pv[:, :, 8:10], R=[gp_tok], W=[fco_tok])
                        wc = prm[:, l, P_FCW + jj * 3:P_FCW + jj * 3 + 3]
                        bc = prm[:, l, P_FCB + jj:P_FCB + jj + 1]
                        gcs = gc[:, TP:TPH].rearrange("p (s r) -> p s r", r=LS)
                        ts(gc[:, 0:TP], gpre[:, 0:TP], wc[:, 0:1], bc, ALU.mult, ALU.add, R=[gp_tok, prm_tok], W=[gc_tok])
                        ts(gcs, pv[:, :, 0:8], wc[:, 0:1], bc, ALU.mult, ALU.add, R=[gp_tok, prm_tok], W=[gc_tok])
                        for tap in range(1, 3):
                            stt(gc[:, 0:TP], gpre[:, tap:tap + TP], wc[:, tap:tap + 1], gc[:, 0:TP], ALU.mult, ALU.add,
                                R=[gp_tok, prm_tok], W=[gc_tok])
                            stt(gcs, pv[:, :, tap:tap + 8], wc[:, tap:tap + 1], gcs, ALU.mult, ALU.add, R=[gp_tok, prm_tok], W=[gc_tok])
                        act(gc[:], gc[:], AF.Silu, R=[gc_tok], W=[gc_tok])
                        for ti, (t0, n) in enumerate(TILES):
                            tt(aT[:, jj, t0:t0 + n], gc[:, t0:t0 + n], up[:, t0:t0 + n], ALU.mult, R=[gc_tok, up_tok], W=[a_toks[ti]])

                    prev = None
                    for jj in range(NFF):
                        c = part_a(jj)
                        if prev is not None:
                            part_b(*prev)
                        prev = (jj, c)
                    part_b(*prev)
                    K.dma("sp", fco_d[l, ph], fco[:].rearrange("p a s r -> p (a s r)"), R=[fco_tok], semtok=fco_tok)
                    K.snapshot()
                with contextlib.ExitStack() as S2:
                    y = sb(S2, "f_y", [128, KC, TPH], F32)
                    ytok = Tok()
                    wdn = Ring([(sb(S2, "f_wd%d" % i, [128, NFF, 128], BF16), retok(wdn_ptoks[i])) for i in range(2)])
                    sqt = (sb(S2, "f_sq", [128, KC, 256], BF16), Tok())
                    rs = (sb(S2, "f_rs", [128, 256], F32), Tok())
                    tmps = Ring([(sb(S2, "f_t%d" % i, [128, 256], F32), Tok()) for i in range(2)])
                    for oc in range(KC):
                        wt, wtok = wdn.get()
                        K.dma("pool", wt[:], ffn_w_down[l][:, oc * 128:(oc + 1) * 128].rearrange("(j p) n -> p j n", p=128), W=[wtok], semtok=wtok)
                        sl3 = [pwide.get() for _ in range(3)]
                        for jj in range(NFF):
                            for ti, (t0, n) in enumerate(TILES):
                                ps, ptok = sl3[ti]
                                mm(ps[:, 0:n], wt[:, jj, :], aT[:, jj, t0:t0 + n], start=(jj == 0), stop=(jj == NFF - 1),
                                   R=[wtok, a_toks[ti]], W=[ptok])
                        for ti, (t0, n) in enumerate(TILES):
                            ps, ptok = sl3[ti]
                            cp(y[:, oc, t0:t0 + n], ps[:, 0:n], R=[ptok], W=[ytok])
                    for ti, (t0, n) in enumerate([(0, 256), (256, 256), (512, 256), (768, 256), (1024, 64)]):
                        rstd_fm(y[:, :, t0:t0 + n], n, [ytok], sqt, rs)
                        for kc in range(KC):
                            tmp, ttok = tmps.get()
                            tt(tmp[:, 0:n], y[:, kc, t0:t0 + n], rs[0][:, 0:n], ALU.mult, R=[ytok, rs[1]], W=[ttok])
                            if t0 < TP:
                                stt(xT[:, kc, ph, t0:t0 + n], tmp[:, 0:n], G2[:, kc, 0:1], xT[:, kc, ph, t0:t0 + n], ALU.mult, ALU.add,
                                    R=[ttok, mod_tok], W=[xtok[ph]])
                            else:
                                sc = seqcols(ph)
                                v3 = tmp[:, 0:n].rearrange("p (s j) -> p s j", j=LS)
                                tt(v3, v3, G2[:, kc, sc].unsqueeze(2).broadcast_to([128, SPP, LS]), ALU.mult, R=[ttok, mod_tok], W=[ttok])
                                x3 = xT[:, kc, ph, t0:t0 + n].rearrange("p (s j) -> p s j", j=LS)
                                tt(x3, x3, v3, ALU.add, R=[ttok], W=[xtok[ph]])
                    K.snapshot()

        def main_program():
            for l in range(depth):
                modA = modsets[l % 2]
                with contextlib.ExitStack() as SL:
                    stage(1)
                    for _ in adaln_gen(l, SL):
                        pass
                    if l % 2 == 1:
                        jj_ = l // 2
                        if jj_ == 0:
                            memset(lbT[:, :, 0:1], 0.0, W=[lb_tok])
                            memset(lbT[:, :, 1:2], 1.0, W=[lb_tok])
                        else:
                            hl = prm[:, l, P_HLB:P_HLB + 16].rearrange("p (h t) -> p h t", t=2)
                            tt(lbT[:, :, 0:1], hl[:, :, 1:2], hl[:, :, 0:1], ALU.subtract, R=[prm_tok], W=[lb_tok])
                            act(lbT[:, :, 0:1], lbT[:, :, 0:1], AF.Sigmoid, R=[lb_tok], W=[lb_tok])
                            ts(lbT[:, :, 1:2], lbT[:, :, 0:1], -1.0, 1.0, ALU.mult, ALU.add, R=[lb_tok], W=[lb_tok])
                    K.snapshot()
                for ph in range(2):
                    with contextlib.ExitStack() as SA:
                        hT = sb(SA, "m_hT", [128, KC, TPH], BF16)
                        htoks = [Tok() for _ in range(3)]
                        oT = sb(SA, "m_oT", [128, KC, TPH], BF16)
                        otoks = [Tok() for _ in range(3)]
                        with contextlib.ExitStack() as SP:
                            stage(2)
                            prenorm(ph, modA[0], modA[1], hT, htoks, SP)
                            K.snapshot()
                        with contextlib.ExitStack() as SM:
                            if l % 2 == 0:
                                gdn_phase(l, ph, hT, htoks, oT, otoks, SM)
                            else:
                                hgrn_phase(l, ph, hT, htoks, oT, otoks, SM)
                            K.snapshot()
                        with contextlib.ExitStack() as SO:
                            stage(5)
                            Wout = gdn_w_out[l // 2] if l % 2 == 0 else hgrn_w_out[l // 2]
                            outproj_postnorm(l, ph, Wout, oT, otoks, modA[2], SO)
                            K.snapshot()
                    stage(6)
                    ffn_phase(l, ph, False)
            for ph in range(2):
                K.dma("sp", yT_d.rearrange("p (k h t) -> p k h t", k=KC, h=2)[:, :, ph, :], xT[:, :, ph, :], R=[xtok[ph]], semtok=xtok[ph])

        try:
            main_program()
        except StopBuild:
            K.barrier()
        K.final_wait("sp")

        with nc.Block() as block:
            K.replay(block)
    return nc


def _consts():
    c = np.zeros((128, NCST), np.float32)
    c[:, C_IDENT:C_IDENT + 128] = np.eye(128, dtype=np.float32)
    i = np.arange(128)[:, None]
    jx = np.arange(128)[None, :]
    BIG = 1.0e4
    valid = (i // 64 == jx // 64) & (i > jx)
    c[:, C_PENP:C_PENP + 128] = np.where(valid, 0.0, BIG)
    i8 = np.arange(64)[:, None]
    j8 = np.arange(64)[None, :]
    valid = (i8 // 8 == j8 // 8) & (i8 > j8)
    c[0:64, C_PENS:C_PENS + 64] = np.where(valid, 0.0, BIG)
    for ncb, off, bs in ((2, C_CM2, 128), (4, C_CM4, 128), (8, C_CM8, 64)):
        C = bs // ncb
        m = np.zeros((ncb, bs), np.float32)
        for cc in range(ncb):
            m[cc, cc * C:(cc + 1) * C] = 1.0
        c[:, off:off + ncb * bs] = m.reshape(1, -1)
    for ncb, off, bs in ((2, C_RM2, 128), (4, C_RM4, 128), (8, C_RM8, 64)):
        C = bs // ncb
        m = np.zeros((bs, ncb), np.float32)
        for cc in range(ncb):
            m[cc * C:(cc + 1) * C, cc] = 1.0
        c[0:bs, off:off + ncb] = m
    c[:, C_MIP:C_MIP + 128] = ((i // 32 == jx // 32) & (i <= jx)).astype(np.float32)
    c[0:64, C_MIS:C_MIS + 64] = ((i8 // 8 == j8 // 8) & (i8 <= j8)).astype(np.float32)
    e = np.zeros((8, 8, 128), np.float32)
    for h in range(8):
        e[h, h, :] = 1.0
    c[0:8, C_ESEL:C_ESEL + 1024] = e.reshape(8, 1024)
    return c


def _fm(v):
    sh = v.shape
    k = sh[-1] // 128
    v = v.reshape(sh[:-1] + (k, 128))
    return np.moveaxis(np.moveaxis(v, -1, 0), -1, 1)


_NC_CACHE = {}


def kernel(x_prompt, x_sample, state_gdn, state_gdn_conv, state_hgrn, state_ffn_conv, c_prompt, c_sample,
           ada_w, ada_b, norm_pre_mix, norm_post_mix, norm_pre_ffn, norm_post_ffn,
           gdn_w_in, gdn_conv_w, gdn_conv_b, gdn_a_log, gdn_dt_bias, gdn_norm, gdn_w_out,
           hgrn_lb, hgrn_w_in, hgrn_norm, hgrn_w_out,
           ffn_w_gu, ffn_conv_w, ffn_conv_b, ffn_w_down, _depth=DEPTH, _stage=None):
    f32 = np.float32
    A = lambda a: np.ascontiguousarray(np.asarray(a, dtype=f32))
    x_prompt, x_sample = A(x_prompt), A(x_sample)
    state_gdn, state_gdn_conv, state_hgrn, state_ffn_conv = A(state_gdn), A(state_gdn_conv), A(state_hgrn), A(state_ffn_conv)
    c_prompt, c_sample = A(c_prompt), A(c_sample)
    prm = np.zeros((128, 4, NPRM), f32)
    nrow = np.zeros((128, 4, 128), f32)
    ada_b_, gcw, gcb = A(ada_b), A(gdn_conv_w), A(gdn_conv_b)
    fcw, fcb = A(ffn_conv_w), A(ffn_conv_b)
    hlb = A(hgrn_lb)
    for l in range(4):
        prm[:, l, P_ADAB:P_ADAB + 48] = ada_b_[l].reshape(48, 128).T
        prm[:, l, P_NPRE_MIX:P_NPRE_MIX + 8] = A(norm_pre_mix)[l].reshape(8, 128).T
        prm[:, l, P_NPOST_MIX:P_NPOST_MIX + 8] = A(norm_post_mix)[l].reshape(8, 128).T
        prm[:, l, P_NPRE_FFN:P_NPRE_FFN + 8] = A(norm_pre_ffn)[l].reshape(8, 128).T
        prm[:, l, P_NPOST_FFN:P_NPOST_FFN + 8] = A(norm_post_ffn)[l].reshape(8, 128).T
        j = l // 2
        if l % 2 == 0:
            prm[:, l, P_GCW:P_GCW + 96] = gcw[j].reshape(4, 24, 128).transpose(2, 1, 0).reshape(128, 96)
            prm[:, l, P_GCB:P_GCB + 24] = gcb[j].reshape(24, 128).T
            prm[0:8, l, P_ALOG] = A(gdn_a_log)[j]
            prm[0:8, l, P_DTB] = A(gdn_dt_bias)[j]
            nrow[:, l, :] = A(gdn_norm)[j][None, :]
        else:
            nrow[:, l, :] = A(hgrn_norm)[j][None, :]
        prm[:, l, P_FCW:P_FCW + 66] = fcw[l].reshape(3, 22, 128).transpose(2, 1, 0).reshape(128, 66)
        prm[:, l, P_FCB:P_FCB + 22] = fcb[l].reshape(22, 128).T
        prm[:, l, P_HLB:P_HLB + 16] = hlb.reshape(2, 8, 128).transpose(2, 1, 0).reshape(128, 16)
    cst = _consts()
    shared = dict(prm=prm.reshape(128, -1), nrow=nrow.reshape(128, -1), cst=cst,
                  ada_w=A(ada_w), gdn_w_in=A(gdn_w_in), gdn_w_out=A(gdn_w_out), hgrn_w_in=A(hgrn_w_in),
                  hgrn_w_out=A(hgrn_w_out), ffn_w_gu=A(ffn_w_gu), ffn_w_down=A(ffn_w_down))
    in_maps = []
    for c in range(NCORE):
        xt = np.zeros((128, KC, 2, TPH), f32)
        xp = _fm(x_prompt[c])
        xs = _fm(x_sample[16 * c:16 * c + 16])
        for ph in range(2):
            xt[:, :, ph, 0:TP] = xp[:, :, TP * ph:TP * ph + TP]
            xt[:, :, ph, TP:] = xs[:, :, 8 * ph:8 * ph + 8, :].reshape(128, KC, TS)
        cc = np.concatenate([c_prompt[c:c + 1], c_sample[16 * c:16 * c + 16]], 0)
        cT = _fm(cc)
        gc = _fm(state_gdn_conv[:, 16 * c:16 * c + 16])
        gci = np.zeros((2, 2, 128, 24, 8, 3), f32)
        fc = _fm(state_ffn_conv[:, 16 * c:16 * c + 16])
        fci = np.zeros((4, 2, 128, NFF, 8, 2), f32)
        for ph in range(2):
            gci[:, ph] = gc[:, :, :, 8 * ph:8 * ph + 8, :].transpose(2, 0, 1, 3, 4)
            fci[:, ph] = fc[:, :, :, 8 * ph:8 * ph + 8, :].transpose(2, 0, 1, 3, 4)
        m = dict(shared)
        m.update(xT=xt.reshape(128, -1), cT=np.ascontiguousarray(cT).reshape(128, -1),
                 sgdn=np.ascontiguousarray(state_gdn[:, 16 * c:16 * c + 16]),
                 shgrn=np.ascontiguousarray(state_hgrn[:, 16 * c:16 * c + 16]),
                 gci=gci.reshape(2, 2, 128, -1), fci=fci.reshape(4, 2, 128, -1))
        in_maps.append(m)
    ck = (_depth, _stage)
    if ck not in _NC_CACHE:
        _STAGE_LIMIT[0] = _stage
        _STAGE_LIMIT[1] = False
        _NC_CACHE[ck] = build_nc(_depth)
        _STAGE_LIMIT[0] = None
        _STAGE_LIMIT[1] = False
    nc = _NC_CACHE[ck]
    res = run_bass_kernel_spmd(nc, in_maps, core_ids=list(range(NCORE)))
    R = res.results
    y_prompt = np.zeros((8, 2048, D), f32)
    y_sample = np.zeros((128, 8, D), f32)
    p_gdn = np.zeros((2, 8, H, 128, 128), f32)
    p_hgrn = np.zeros((2, 8, H, 128, 128), f32)
    s_gdn = np.zeros((2, 128, H, 128, 128), f32)
    s_hgrn = np.zeros((2, 128, H, 128, 128), f32)
    p_gconv = np.zeros((2, 8, 3, 3072), f32)
    s_gconv = np.zeros((2, 128, 3, 3072), f32)
    p_fconv = np.zeros((4, 8, 2, DFF), f32)
    s_fconv = np.zeros((4, 128, 2, DFF), f32)
    for c in range(NCORE):
        r = R[c]
        yt = r["yT"].reshape(128, KC, 2, TPH)
        for ph in range(2):
            y_prompt[c, TP * ph:TP * ph + TP] = yt[:, :, ph, 0:TP].transpose(2, 1, 0).reshape(TP, D)
            ys = yt[:, :, ph, TP:].reshape(128, KC, 8, 8)
            y_sample[16 * c + 8 * ph:16 * c + 8 * ph + 8] = ys.transpose(2, 3, 1, 0).reshape(8, 8, D)
        p_gdn[:, c] = r["pgdn"]
        p_hgrn[:, c] = r["phgrn"]
        s_gdn[:, 16 * c:16 * c + 16] = r["sgdn_o"]
        s_hgrn[:, 16 * c:16 * c + 16] = r["shgrn_o"]
        g = r["gco"].reshape(2, 2, 128, 24, 9, 3)
        f = r["fco"].reshape(4, 2, 128, NFF, 9, 2)
        for ph in range(2):
            gs = g[:, ph, :, :, 0:8, :].transpose(0, 3, 4, 2, 1).reshape(2, 8, 3, 3072)
            s_gconv[:, 16 * c + 8 * ph:16 * c + 8 * ph + 8] = gs
            fs = f[:, ph, :, :, 0:8, :].transpose(0, 3, 4, 2, 1).reshape(4, 8, 2, DFF)
            s_fconv[:, 16 * c + 8 * ph:16 * c + 8 * ph + 8] = fs
        p_gconv[:, c] = g[:, 1, :, :, 8, :].transpose(0, 3, 2, 1).reshape(2, 3, 3072)
        p_fconv[:, c] = f[:, 1, :, :, 8, :].transpose(0, 3, 2, 1).reshape(4, 2, DFF)
    return (y_prompt, y_sample, p_gdn, p_gconv, p_hgrn, p_fconv, s_gdn, s_gconv, s_hgrn, s_fconv)
```
